# Optimizing a Trainium2 kernel written in Bass

```python
import math
import jax
import jax.numpy as jnp
from jax import lax
import numpy as np


D_MODEL = 1024
BATCH = 8
SEQ = 2048
DEPTH = 2

GRID_W = 64
PLE_DIM = 256
N_EVEN = (DEPTH + 1) // 2
N_ODD = DEPTH // 2

NA_HEADS = 8
NA_HEAD_DIM = 64
NA_WIDTH = NA_HEADS * NA_HEAD_DIM
NA_WIN_ROWS_MAX = 8
NA_WIN_COLS = 16

HY_WIDTH = D_MODEL - NA_WIDTH
HY_ORDER = 2
HY_SHORT_K = 3
HY_EMB_DIM = 33
HY_FILTER_HIDDEN = 64
HY_FAST_DECAY_PCT = 0.3
HY_SLOW_DECAY_PCT = 1.5
HY_DECAY_TARGET = 1e-2
HY_FILTER_OUT = HY_ORDER * 2 * HY_WIDTH

AB_IN_WIDTH = 3 * NA_WIDTH + (HY_ORDER + 1) * HY_WIDTH

MLA_HEADS = 16
MLA_Q_LORA = 384
MLA_KV_LORA = 256
MLA_NOPE = 64
MLA_ROPE = 32
MLA_V = 64
MLA_IN_WIDTH = MLA_Q_LORA + MLA_KV_LORA + MLA_ROPE
ROPE_THETA = 10000.0
Q_BLOCK = 128

N_EXPERTS = 16
EC_CAPACITY_FACTOR = 2
D_FF_EXPERT = 2048

DN_ALPHA = (2 * DEPTH) ** 0.25
DN_BETA = (8 * DEPTH) ** -0.25
NORM_EPS = 1e-5
NEG_INF = -1e30

kernel_name = 'hybrid_na_hyena_mla_ec_encoder'


def layer_norm(x, g, b):
    xf = x.astype(jnp.float32)
    mu = jnp.mean(xf, axis=-1, keepdims=True)
    var = jnp.mean(jnp.square(xf - mu), axis=-1, keepdims=True)
    return ((xf - mu) * lax.rsqrt(var + NORM_EPS) * g + b).astype(x.dtype)


def rms_norm(x, g):
    xf = x.astype(jnp.float32)
    return (xf * lax.rsqrt(jnp.mean(jnp.square(xf), axis=-1, keepdims=True) + NORM_EPS) * g).astype(x.dtype)


def neighbourhood_attention(q, k, v, rpb):
    B, L, H, Dh = q.shape
    rows = L // GRID_W
    wr = min(NA_WIN_ROWS_MAX, rows)
    qg = q.reshape(B, rows, GRID_W, H, Dh)
    kg = k.reshape(B, rows, GRID_W, H, Dh)
    vg = v.reshape(B, rows, GRID_W, H, Dh)
    r = np.arange(rows)
    r0 = np.clip(r - wr // 2, 0, rows - wr)
    row_idx = r0[:, None] + np.arange(wr)[None, :]
    c = np.arange(GRID_W)
    c0 = np.clip(c - NA_WIN_COLS // 2, 0, GRID_W - NA_WIN_COLS)
    kc = np.arange(GRID_W)
    col_ok = (kc[None, :] >= c0[:, None]) & (kc[None, :] < c0[:, None] + NA_WIN_COLS)
    dr_idx = row_idx - r[:, None] + (NA_WIN_ROWS_MAX - 1)
    dc_idx = np.clip(kc[None, :] - c[:, None], -(NA_WIN_COLS - 1), NA_WIN_COLS - 1) + (NA_WIN_COLS - 1)
    kr = kg[:, row_idx]
    vr = vg[:, row_idx]
    s = jnp.einsum('brchd,brwkhd->bhrcwk', qg, kr).astype(jnp.float32) * (Dh ** -0.5)
    bias = rpb[:, dr_idx[:, None, :, None], dc_idx[None, :, None, :]].astype(jnp.float32)
    s = jnp.where(col_ok[None, None, None, :, None, :], s + bias[None], NEG_INF)
    sh = s.shape
    prob = jax.nn.softmax(s.reshape(sh[:4] + (wr * GRID_W,)), axis=-1).reshape(sh).astype(v.dtype)
    out = jnp.einsum('bhrcwk,brwkhd->brchd', prob, vr)
    return out.reshape(B, L, H * Dh)


def short_conv_centred(x, w, b):
    L = x.shape[1]
    pad = HY_SHORT_K // 2
    xp = jnp.pad(x, ((0, 0), (pad, pad), (0, 0)))
    y = b
    for j in range(HY_SHORT_K):
        y = y + xp[:, j:j + L] * w[j]
    return y


def hyena_filters(L, w1, b1, freq, w2, b2, w3):
    f32 = jnp.float32
    bands = (HY_EMB_DIM - 1) // 2
    t = jnp.linspace(0.0, 1.0, L, dtype=f32)[:, None]
    w = 2.0 * math.pi * jnp.arange(L, dtype=f32)[:, None] / L
    f = jnp.linspace(1e-4, bands - 1, bands, dtype=f32)[None, :]
    z = jnp.concatenate([t, jnp.cos(f * w), -jnp.sin(f * w)], axis=-1)
    h = jnp.sin(freq * (z @ w1 + b1))
    h = jnp.sin(freq * (h @ w2 + b2))
    h = (h @ w3).astype(f32).reshape(L, HY_ORDER, 2, HY_WIDTH)
    min_decay = math.log(HY_DECAY_TARGET) / HY_SLOW_DECAY_PCT
    max_decay = math.log(HY_DECAY_TARGET) / HY_FAST_DECAY_PCT
    deltas = jnp.abs(jnp.linspace(min_decay, max_decay, HY_WIDTH, dtype=f32))
    h = h * jnp.exp(-t * deltas)[:, None, None, :]
    k_fwd = h[:, :, 0]
    k_bwd = h[:, :, 1]
    K = jnp.concatenate([k_fwd, jnp.zeros((1, HY_ORDER, HY_WIDTH), f32), k_bwd[1:][::-1]], axis=0)
    return K * lax.rsqrt(jnp.sum(jnp.square(K), axis=0, keepdims=True) + 1e-12)


def long_conv_bidirectional(u, K, skip):
    L = u.shape[1]
    uf = jnp.fft.rfft(u.astype(jnp.float32), n=2 * L, axis=1)
    kf = jnp.fft.rfft(K, n=2 * L, axis=0)
    y = jnp.fft.irfft(uf * kf[None], n=2 * L, axis=1)[:, :L]
    return (y + u.astype(jnp.float32) * skip.astype(jnp.float32)).astype(u.dtype)


def na_hyena_mixer(x, w_in, rpb, conv_w, conv_b, f_w1, f_b1, f_freq, f_w2, f_b2, f_w3, skip, w_out):
    B, L, _ = x.shape
    h = x @ w_in
    qa, ka, va, hb = jnp.split(h, [NA_WIDTH, 2 * NA_WIDTH, 3 * NA_WIDTH], axis=-1)
    shp = (B, L, NA_HEADS, NA_HEAD_DIM)
    y_a = neighbourhood_attention(qa.reshape(shp), ka.reshape(shp), va.reshape(shp), rpb)
    hb = short_conv_centred(hb, conv_w, conv_b)
    parts = jnp.split(hb, HY_ORDER + 1, axis=-1)
    K = hyena_filters(L, f_w1, f_b1, f_freq, f_w2, f_b2, f_w3)
    z = parts[0]
    for o in range(HY_ORDER):
        z = parts[o + 1] * long_conv_bidirectional(z, K[:, o], skip[o])
    return jnp.concatenate([y_a, z], axis=-1) @ w_out


def rope_tables(L, dim):
    inv = 1.0 / (ROPE_THETA ** (jnp.arange(0, dim, 2, dtype=jnp.float32) / dim))
    ang = jnp.arange(L, dtype=jnp.float32)[:, None] * inv[None, :]
    return jnp.cos(ang), jnp.sin(ang)


def apply_rope(x, cos, sin):
    x1, x2 = jnp.split(x.astype(jnp.float32), 2, axis=-1)
    c = cos[None, :, None, :]
    s = sin[None, :, None, :]
    return jnp.concatenate([x1 * c - x2 * s, x1 * s + x2 * c], axis=-1).astype(x.dtype)


def dense_attention_blocked(q, k, v):
    B, L, H, Dq = q.shape
    nb = L // Q_BLOCK
    qb = q.reshape(B, nb, Q_BLOCK, H, Dq).transpose(1, 0, 2, 3, 4)
    scale = Dq ** -0.5

    def one_block(qi):
        s = jnp.einsum('bqhd,bkhd->bhqk', qi, k).astype(jnp.float32) * scale
        prob = jax.nn.softmax(s, axis=-1).astype(v.dtype)
        return jnp.einsum('bhqk,bkhd->bqhd', prob, v)

    out = lax.map(one_block, qb)
    return out.transpose(1, 0, 2, 3, 4).reshape(B, L, H * v.shape[-1])


def mla_mixer(x, w_in, q_norm_g, w_q_up, kv_norm_g, w_kv_up, w_out):
    B, L, _ = x.shape
    h = x @ w_in
    cq, ckv, k_rope = jnp.split(h, [MLA_Q_LORA, MLA_Q_LORA + MLA_KV_LORA], axis=-1)
    cos, sin = rope_tables(L, MLA_ROPE)
    q = (rms_norm(cq, q_norm_g) @ w_q_up).reshape(B, L, MLA_HEADS, MLA_NOPE + MLA_ROPE)
    q_nope, q_rope = jnp.split(q, [MLA_NOPE], axis=-1)
    q = jnp.concatenate([q_nope, apply_rope(q_rope, cos, sin)], axis=-1)
    kv = (rms_norm(ckv, kv_norm_g) @ w_kv_up).reshape(B, L, MLA_HEADS, MLA_NOPE + MLA_V)
    k_nope, v = jnp.split(kv, [MLA_NOPE], axis=-1)
    k_r = apply_rope(k_rope[:, :, None, :], cos, sin)
    k = jnp.concatenate([k_nope, jnp.broadcast_to(k_r, (B, L, MLA_HEADS, MLA_ROPE))], axis=-1)
    return dense_attention_blocked(q, k, v) @ w_out


def expert_choice_moe(x, w_router, w_gate, w_up, w_down):
    B, L, _ = x.shape
    cap = EC_CAPACITY_FACTOR * L // N_EXPERTS
    aff = jax.nn.softmax((x @ w_router).astype(jnp.float32), axis=-1)
    g, idx = lax.top_k(aff.transpose(0, 2, 1), cap)
    bidx = jnp.arange(B)[:, None, None]
    xe = x[bidx, idx]
    hg = jnp.einsum('becd,edf->becf', xe, w_gate)
    hu = jnp.einsum('becd,edf->becf', xe, w_up)
    ye = jnp.einsum('becf,efd->becd', jax.nn.silu(hg) * hu, w_down)
    ye = ye * g[..., None].astype(ye.dtype)
    return jnp.zeros_like(x).at[bidx, idx].add(ye)


def setup_inputs(seed: int = 0) -> dict:
    keys = iter(jax.random.split(jax.random.key(seed), 64))
    f32 = jnp.float32

    def nrm(shape, scale):
        return jax.random.normal(next(keys), shape, f32) * scale

    D = D_MODEL
    return {
        'x': nrm((BATCH, SEQ, D), 1.0),
        'p': nrm((DEPTH, BATCH, SEQ, PLE_DIM), 1.0),
        'ab_w_in': nrm((N_EVEN, D, AB_IN_WIDTH), D ** -0.5),
        'na_rpb': nrm((N_EVEN, NA_HEADS, 2 * NA_WIN_ROWS_MAX - 1, 2 * NA_WIN_COLS - 1), 0.02),
        'hy_conv_w': nrm((N_EVEN, HY_SHORT_K, (HY_ORDER + 1) * HY_WIDTH), HY_SHORT_K ** -0.5),
        'hy_conv_b': nrm((N_EVEN, (HY_ORDER + 1) * HY_WIDTH), 0.02),
        'hy_f_w1': nrm((N_EVEN, HY_EMB_DIM, HY_FILTER_HIDDEN), HY_EMB_DIM ** -0.5),
        'hy_f_b1': nrm((N_EVEN, HY_FILTER_HIDDEN), 0.02),
        'hy_f_freq': 1.0 + nrm((N_EVEN, HY_FILTER_HIDDEN), 0.02),
        'hy_f_w2': nrm((N_EVEN, HY_FILTER_HIDDEN, HY_FILTER_HIDDEN), HY_FILTER_HIDDEN ** -0.5),
        'hy_f_b2': nrm((N_EVEN, HY_FILTER_HIDDEN), 0.02),
        'hy_f_w3': nrm((N_EVEN, HY_FILTER_HIDDEN, HY_FILTER_OUT), HY_FILTER_HIDDEN ** -0.5),
        'hy_skip': nrm((N_EVEN, HY_ORDER, HY_WIDTH), 1.0),
        'ab_w_out': nrm((N_EVEN, NA_WIDTH + HY_WIDTH, D), DN_BETA * (NA_WIDTH + HY_WIDTH) ** -0.5),
        'mla_w_in': nrm((N_ODD, D, MLA_IN_WIDTH), D ** -0.5),
        'mla_q_norm': 1.0 + nrm((N_ODD, MLA_Q_LORA), 0.02),
        'mla_w_q_up': nrm((N_ODD, MLA_Q_LORA, MLA_HEADS * (MLA_NOPE + MLA_ROPE)), MLA_Q_LORA ** -0.5),
        'mla_kv_norm': 1.0 + nrm((N_ODD, MLA_KV_LORA), 0.02),
        'mla_w_kv_up': nrm((N_ODD, MLA_KV_LORA, MLA_HEADS * (MLA_NOPE + MLA_V)), MLA_KV_LORA ** -0.5),
        'mla_w_out': nrm((N_ODD, MLA_HEADS * MLA_V, D), DN_BETA * (MLA_HEADS * MLA_V) ** -0.5),
        'ln1_g': 1.0 + nrm((DEPTH, D), 0.02),
        'ln1_b': nrm((DEPTH, D), 0.02),
        'ln2_g': 1.0 + nrm((DEPTH, D), 0.02),
        'ln2_b': nrm((DEPTH, D), 0.02),
        'moe_router': nrm((DEPTH, D, N_EXPERTS), D ** -0.5),
        'moe_w_gate': nrm((DEPTH, N_EXPERTS, D, D_FF_EXPERT), D ** -0.5),
        'moe_w_up': nrm((DEPTH, N_EXPERTS, D, D_FF_EXPERT), D ** -0.5),
        'moe_w_down': nrm((DEPTH, N_EXPERTS, D_FF_EXPERT, D), DN_BETA * D_FF_EXPERT ** -0.5),
        'ple_gate': nrm((DEPTH, D, D), D ** -0.5),
        'ple_proj': nrm((DEPTH, PLE_DIM, D), PLE_DIM ** -0.5),
    }


def reference(x, p, ab_w_in, na_rpb, hy_conv_w, hy_conv_b, hy_f_w1, hy_f_b1, hy_f_freq, hy_f_w2, hy_f_b2, hy_f_w3, hy_skip, ab_w_out, mla_w_in, mla_q_norm, mla_w_q_up, mla_kv_norm, mla_w_kv_up, mla_w_out, ln1_g, ln1_b, ln2_g, ln2_b, moe_router, moe_w_gate, moe_w_up, moe_w_down, ple_gate, ple_proj):
    for i in range(DEPTH):
        j = i // 2
        if i % 2 == 0:
            m = na_hyena_mixer(x, ab_w_in[j], na_rpb[j], hy_conv_w[j], hy_conv_b[j], hy_f_w1[j], hy_f_b1[j],
                               hy_f_freq[j], hy_f_w2[j], hy_f_b2[j], hy_f_w3[j], hy_skip[j], ab_w_out[j])
        else:
            m = mla_mixer(x, mla_w_in[j], mla_q_norm[j], mla_w_q_up[j], mla_kv_norm[j], mla_w_kv_up[j], mla_w_out[j])
        x = layer_norm(DN_ALPHA * x + m, ln1_g[i], ln1_b[i])
        f = expert_choice_moe(x, moe_router[i], moe_w_gate[i], moe_w_up[i], moe_w_down[i])
        x = layer_norm(DN_ALPHA * x + f, ln2_g[i], ln2_b[i])
        x = x + jax.nn.sigmoid(x @ ple_gate[i]) * (p[i] @ ple_proj[i])
    return x
```

```python
from contextlib import ExitStack
import math
import numpy as np
import ml_dtypes
import concourse.bass as bass
import concourse.mybir as mybir
from concourse.bass_utils import run_bass_kernel_spmd

F32 = mybir.dt.float32
BF16 = mybir.dt.bfloat16
I32 = mybir.dt.int32
U32 = mybir.dt.uint32
AF = mybir.ActivationFunctionType
ALU = mybir.AluOpType
AX = mybir.AxisListType
NPBF = ml_dtypes.bfloat16

D_MODEL = 1024
SEQ = 2048
NT = SEQ // 128
ALPHA = 4.0 ** 0.25
EPS = 1e-5

ENGS = ("pe", "act", "dve", "pool", "sp")
N_DMA_SEMS = 16


class Dep:
    __slots__ = ("w", "r", "name")

    def __init__(self, name=""):
        self.w = {}
        self.r = {}
        self.name = name


class Prog:
    def __init__(self, nc, strict=True):
        self.nc = nc
        self.es = ExitStack()
        self.q = {e: [] for e in ENGS}
        self.cnt = {e: 0 for e in ENGS}
        self.seen = {e: {} for e in ENGS}
        self.sem = {}
        self.strict = strict
        for e in ENGS:
            self.sem[e] = self.es.enter_context(nc.semaphore("s_" + e))
        self.dma_sems = {}
        self.dma_tot = {}
        self.dma_rr = {}
        for e in ("sp", "pool", "act"):
            self.dma_sems[e] = [self.es.enter_context(nc.semaphore(f"d_{e}{i}")) for i in range(N_DMA_SEMS)]
            self.dma_tot[e] = [0] * N_DMA_SEMS
            self.dma_rr[e] = 0
        self.all_events = {}
        self.n_ops = 0
        self.stage_es = None
        self.uid = 0

    def begin_stage(self):
        self.barrier()
        if self.stage_es is not None:
            self.stage_es.close()
        self.stage_es = ExitStack()

    def sb(self, name, shape, dt):
        self.uid += 1
        t = self.stage_es.enter_context(self.nc.sbuf_tensor(f"{name}_{self.uid}", list(shape), dt))
        return t

    def ps(self, name, shape, dt=F32):
        return self.es.enter_context(self.nc.psum_tensor(name, list(shape), dt))

    def _semobj(self, key):
        if isinstance(key, str):
            return self.sem[key]
        e, i = key
        return self.dma_sems[e][i]

    def _need(self, eng, reads, writes, merge):
        need = {}

        def add(k, v):
            if k == eng and not self.strict:
                return
            if self.seen[eng].get(k, 0) >= v:
                return
            if need.get(k, 0) < v:
                need[k] = v
        for d in reads:
            for k, v in d.w.items():
                add(k, v)
        for d in writes:
            if not merge:
                for k, v in d.w.items():
                    add(k, v)
            for k, v in d.r.items():
                add(k, v)
        for k, v in need.items():
            self.seen[eng][k] = v
        return list(need.items())

    def _commit(self, ev, reads, writes, merge):
        k, v = ev
        for d in reads:
            if d.r.get(k, 0) < v:
                d.r[k] = v
        for d in writes:
            if merge:
                d.w[k] = v
            else:
                d.w = {k: v}
                d.r = {}
        self.all_events[k] = v

    def op(self, eng, fn, reads=(), writes=(), merge=False):
        waits = self._need(eng, reads, writes, merge)
        self.cnt[eng] += 1
        ev = (eng, self.cnt[eng])
        sem = self.sem[eng]
        waitobjs = [(self._semobj(k), v) for k, v in waits]

        def emit(e, fn=fn, waitobjs=waitobjs, sem=sem):
            for s, v in waitobjs:
                e.wait_ge(s, v)
            fn(e).then_inc(sem, 1)
        self.q[eng].append(emit)
        self._commit(ev, reads, writes, merge)
        self.n_ops += 1
        return ev

    def dma(self, eng, out, in_, reads=(), writes=(), merge=False, **kw):
        i = self.dma_rr[eng]
        self.dma_rr[eng] = (i + 1) % N_DMA_SEMS
        key = (eng, i)
        prev = self.dma_tot[eng][i]
        waits = self._need(eng, reads, writes, merge)
        if prev > 0 and self.seen[eng].get(key, 0) < prev:
            waits.append((key, prev))
            self.seen[eng][key] = prev
        self.dma_tot[eng][i] = prev + 16
        ev = (key, prev + 16)
        sem = self.dma_sems[eng][i]
        waitobjs = [(self._semobj(k), v) for k, v in waits]

        def emit(e, waitobjs=waitobjs, sem=sem, out=out, in_=in_, kw=kw):
            for s, v in waitobjs:
                e.wait_ge(s, v)
            e.dma_start(out=out, in_=in_, **kw).then_inc(sem, 16)
        self.q[eng].append(emit)
        self._commit(ev, reads, writes, merge)
        self.n_ops += 1
        return ev

    def coll(self, kind, out, in_, reads=(), writes=()):
        eng = "pool"
        i = self.dma_rr[eng]
        self.dma_rr[eng] = (i + 1) % N_DMA_SEMS
        key = (eng, i)
        prev = self.dma_tot[eng][i]
        waits = self._need(eng, reads, writes, False)
        if prev > 0 and self.seen[eng].get(key, 0) < prev:
            waits.append((key, prev))
            self.seen[eng][key] = prev
        self.dma_tot[eng][i] = prev + 16
        ev = (key, prev + 16)
        sem = self.dma_sems[eng][i]
        waitobjs = [(self._semobj(k), v) for k, v in waits]

        def emit(e, waitobjs=waitobjs, sem=sem, out=out, in_=in_, kind=kind):
            for s_, v in waitobjs:
                e.wait_ge(s_, v)
            e.collective_compute(kind, ALU.bypass, replica_groups=[list(range(8))], ins=[in_], outs=[out]).then_inc(sem, 16)
        self.q[eng].append(emit)
        self._commit(ev, reads, writes, False)
        self.n_ops += 1
        return ev

    def barrier(self):
        snap = dict(self.all_events)
        for eng in ENGS:
            waits = []
            for k, v in snap.items():
                if k == eng:
                    continue
                if self.seen[eng].get(k, 0) >= v:
                    continue
                waits.append((self._semobj(k), v))
                self.seen[eng][k] = v
            if waits:
                def emit(e, waits=waits):
                    for s, v in waits:
                        e.wait_ge(s, v)
                self.q[eng].append(emit)

    def finish(self):
        self.barrier()
        nc = self.nc
        q = self.q
        with nc.Block() as block:
            @block.tensor
            def _(e):
                for f in q["pe"]:
                    f(e)

            @block.scalar
            def _(e):
                for f in q["act"]:
                    f(e)

            @block.vector
            def _(e):
                for f in q["dve"]:
                    f(e)

            @block.gpsimd
            def _(e):
                for f in q["pool"]:
                    f(e)

            @block.sync
            def _(e):
                for f in q["sp"]:
                    f(e)
        if self.stage_es is not None:
            self.stage_es.close()
        self.es.close()


class Ring:
    def __init__(self, P, name, n, shape, dt):
        self.bufs = [(P.sb(f"{name}{i}", shape, dt), Dep(f"{name}{i}")) for i in range(n)]
        self.i = 0

    def next(self):
        b = self.bufs[self.i]
        self.i = (self.i + 1) % len(self.bufs)
        return b


class Ctx:
    def __init__(self, ext_in, ext_out):
        self.nc = bass.Bass("TRN2", target_bir_lowering=False)
        self.P = Prog(self.nc)
        self.ext_in = set(ext_in)
        self.ext_out = set(ext_out)
        self.dram = {}
        self.ddep = {}
        P = self.P
        self.banks = [(P.ps(f"bank{i}", [128, 512], F32), Dep(f"bank{i}")) for i in range(8)]
        self.bank_i = 0
        self.held = set()

    def D(self, name, shape=None, dt=F32):
        if name in self.dram:
            return self.dram[name]
        kind = "Internal"
        if name in self.ext_in:
            kind = "ExternalInput"
        elif name in self.ext_out:
            kind = "ExternalOutput"
        t = self.nc.dram_tensor(name, list(shape), dt, kind=kind).ap()
        self.dram[name] = t
        self.ddep[name] = Dep(name)
        return t

    def dd(self, name):
        return self.ddep[name]

    def bank(self, hold=False):
        for _ in range(8):
            i = self.bank_i
            self.bank_i = (self.bank_i + 1) % 8
            if i not in self.held:
                if hold:
                    self.held.add(i)
                return self.banks[i]
        raise RuntimeError("no free PSUM bank")

    def release_all(self):
        self.held = set()

    def release(self, b):
        for i, bb in enumerate(self.banks):
            if bb[0] is b[0]:
                self.held.discard(i)


def mm_acc(P, out, pairs, reads, dwrite):
    n = len(pairs)
    for i, (l, r) in enumerate(pairs):
        P.op("pe", lambda e, l=l, r=r, i=i: e.matmul(out, lhsT=l, rhs=r, start=(i == 0), stop=(i == n - 1)),
             reads=reads, writes=[dwrite], merge=(i > 0))


def load_fm_bf16(C, name, src, kc_n, width, eng="pool"):
    P = C.P
    t = P.sb(name, [128, kc_n, width], BF16)
    deps = [Dep(f"{name}{k}") for k in range(kc_n)]
    for k in range(kc_n):
        P.dma(eng, t[:, k, :], src[k * 128:(k + 1) * 128, :], writes=[deps[k]])
    return t, deps


def stage_a1(C):
    P = C.P
    P.begin_stage()
    xT = C.D("xT", [1024, 2048], F32)
    w_in = C.D("ab_w_in", [1024, 3072], F32)
    qkT = C.D("qkT", [1024, 2048], BF16)
    v_tm = C.D("v_tm", [2048, 512], BF16)
    hbT = C.D("hbT", [1536, 2048], F32)
    xs, dxs = load_fm_bf16(C, "xTb", xT, 8, 2048)
    ws, dws = load_fm_bf16(C, "winb", w_in, 8, 3072)
    st_b = Ring(P, "a1sb", 3, [128, 2048], BF16)
    st_f = Ring(P, "a1sf", 3, [128, 2048], F32)
    ev_i = 0
    for mc in list(range(8)) + list(range(12, 24)):
        is_hb = mc >= 12
        stg, dstg = (st_f if is_hb else st_b).next()
        for nt in range(4):
            bk, dbk = C.bank()
            mm_acc(P, bk[:], [(ws[:, kc, mc * 128:(mc + 1) * 128], xs[:, kc, nt * 512:(nt + 1) * 512]) for kc in range(8)],
                   reads=dxs + dws, dwrite=dbk)
            eng = "act" if ev_i % 2 == 0 else "dve"
            ev_i += 1
            o = stg[:, nt * 512:(nt + 1) * 512]
            if eng == "act":
                P.op("act", lambda e, o=o, bk=bk: e.copy(out=o, in_=bk[:]), reads=[dbk], writes=[dstg], merge=(nt > 0))
            else:
                P.op("dve", lambda e, o=o, bk=bk: e.tensor_copy(out=o, in_=bk[:]), reads=[dbk], writes=[dstg], merge=(nt > 0))
        if is_hb:
            P.dma("sp", hbT[(mc - 12) * 128:(mc - 11) * 128, :], stg[:], reads=[dstg], writes=[C.dd("hbT")], merge=True)
        else:
            P.dma("sp", qkT[mc * 128:(mc + 1) * 128, :], stg[:], reads=[dstg], writes=[C.dd("qkT")], merge=True)
    st_v = Ring(P, "a1sv", 3, [128, 512], BF16)
    for tt in range(NT):
        bk, dbk = C.bank()
        mm_acc(P, bk[:], [(xs[:, kc, tt * 128:(tt + 1) * 128], ws[:, kc, 1024:1536]) for kc in range(8)],
               reads=dxs + dws, dwrite=dbk)
        stg, dstg = st_v.next()
        if tt % 2 == 0:
            P.op("act", lambda e, stg=stg, bk=bk: e.copy(out=stg[:], in_=bk[:]), reads=[dbk], writes=[dstg])
        else:
            P.op("dve", lambda e, stg=stg, bk=bk: e.tensor_copy(out=stg[:], in_=bk[:]), reads=[dbk], writes=[dstg])
        P.dma("sp", v_tm[tt * 128:(tt + 1) * 128, :], stg[:], reads=[dstg], writes=[C.dd("v_tm")], merge=True)


def na_plan():
    rows, wr = 32, 8
    r0 = np.clip(np.arange(rows) - wr // 2, 0, rows - wr)
    plan = []
    keys = {}
    for i in range(16):
        lo = r0[2 * i] // 2
        hi = (r0[2 * i + 1] + 7) // 2
        lst = []
        for j in range(lo, hi + 1):
            val = []
            for ak in range(2):
                for aq in range(2):
                    r = 2 * i + aq
                    kr = 2 * j + ak
                    val.append(bool(r0[r] <= kr <= r0[r] + 7))
            key = (j - i, tuple(val))
            if key not in keys:
                keys[key] = len(keys)
            lst.append((j, keys[key]))
        plan.append(lst)
    return plan, keys


def na_tables(rpb):
    plan, keys = na_plan()
    c = np.arange(64)
    c0 = np.clip(c - 8, 0, 48)
    col_ok = (c[None, :] >= c0[:, None]) & (c[None, :] < c0[:, None] + 16)
    dc_idx = np.clip(c[None, :] - c[:, None], -15, 15) + 15
    tab = np.full((len(keys), 2, 64, 8, 2, 64), -1e30, np.float32)
    for (delta, val), tid in keys.items():
        vi = 0
        for ak in range(2):
            for aq in range(2):
                ok = val[vi]
                vi += 1
                if not ok:
                    continue
                dr = 2 * delta + ak - aq
                b = rpb[:, dr + 7, :][:, dc_idx]
                b = np.where(col_ok[None], b, np.float32(-1e30))
                tab[tid, ak, :, :, aq, :] = b.transpose(2, 0, 1)
    return tab.reshape(len(keys), 128, 8, 128)


def stage_a2(C):
    P = C.P
    P.begin_stage()
    plan, keys = na_plan()
    ntab = len(keys)
    qkT = C.D("qkT", [1024, 2048], BF16)
    v_tm = C.D("v_tm", [2048, 512], BF16)
    tab_d = C.D("na_tab", [ntab, 128, 8, 128], F32)
    mixT = C.D("mixT", [1024, 2048], BF16)
    QT = P.sb("QT", [128, 4, 2048], BF16)
    KT = P.sb("KT", [128, 4, 2048], BF16)
    dQ = [Dep() for _ in range(4)]
    dK = [Dep() for _ in range(4)]
    for hp in range(4):
        P.dma("sp", QT[:, hp, :], qkT[hp * 128:(hp + 1) * 128, :], reads=[C.dd("qkT")], writes=[dQ[hp]])
        P.dma("sp", KT[:, hp, :], qkT[512 + hp * 128:512 + (hp + 1) * 128, :], reads=[C.dd("qkT")], writes=[dK[hp]])
    tab = P.sb("natab", [128, ntab, 8, 128], F32)
    dtab = Dep()
    for t in range(ntab):
        P.dma("sp", tab[:, t, :, :], tab_d[t], writes=[dtab], merge=True)
    Vx = P.sb("Vx", [128, 8, NT, 128], BF16)
    dV = Dep()
    P.op("pool", lambda e: e.memset(Vx[:], 0.0), writes=[dV])
    vv = v_tm.rearrange("(t p) c -> p t c", p=128)
    for h in range(8):
        a = h % 2
        P.dma("sp", Vx[:, h, :, a * 64:(a + 1) * 64], vv[:, :, h * 64:(h + 1) * 64], reads=[C.dd("v_tm")], writes=[dV], merge=(h > 0))
    ones2 = P.sb("ones2", [128, 2, 128], BF16)
    dones = Dep()
    P.op("pool", lambda e: e.memset(ones2[:], 0.0), writes=[dones])
    P.op("pool", lambda e: e.memset(ones2[:, 0, 0:64], 1.0), writes=[dones])
    P.op("pool", lambda e: e.memset(ones2[:, 1, 64:128], 1.0), writes=[dones])
    yaT = P.sb("yaT", [128, 4, 2048], BF16)
    dya = [Dep() for _ in range(4)]
    s_ring = Ring(P, "na_s", 3, [128, 640], F32)
    p_ring = Ring(P, "na_p", 4, [128, 640], BF16)
    rd_ring = Ring(P, "na_rd", 2, [128, 128], F32)
    units = [(i, hp, a) for i in range(16) for hp in range(4) for a in range(2)]
    pair = {}

    def phase1(u):
        i, hp, a = u
        q0 = i * 128
        h = hp * 2 + a
        pa = slice(a * 64, (a + 1) * 64)
        lst = plan[i]
        nkb = len(lst)
        bA, dbA = C.bank()
        bB, dbB = (C.bank() if nkb > 4 else (None, None))
        ssb, dss = s_ring.next()
        for jj, (j, tid) in enumerate(lst):
            bk, dbk = (bA, dbA) if jj < 4 else (bB, dbB)
            o = bk[:, (jj % 4) * 128:(jj % 4 + 1) * 128]
            P.op("pe", lambda e, o=o, j=j, pa=pa, hp=hp, q0=q0: e.matmul(
                o, lhsT=KT[pa, hp, j * 128:(j + 1) * 128], rhs=QT[pa, hp, q0:q0 + 128], start=True, stop=True),
                reads=[dK[hp], dQ[hp]], writes=[dbk], merge=(jj % 4 > 0))
        for jj, (j, tid) in enumerate(lst):
            bk, dbk = (bA, dbA) if jj < 4 else (bB, dbB)
            o = bk[:, (jj % 4) * 128:(jj % 4 + 1) * 128]
            P.op("dve", lambda e, o=o, jj=jj, tid=tid, h=h, ssb=ssb: e.scalar_tensor_tensor(
                out=ssb[:, jj * 128:(jj + 1) * 128], in0=o, scalar=0.125, in1=tab[:, tid, h, :],
                op0=ALU.mult, op1=ALU.add), reads=[dbk, dtab], writes=[dss], merge=(jj > 0))
        pT, dpT = p_ring.next()
        P.op("act", lambda e, pT=pT, ssb=ssb, nkb=nkb: e.activation(
            out=pT[:, 0:nkb * 128], in_=ssb[:, 0:nkb * 128], func=AF.Exp), reads=[dss], writes=[dpT])
        return (pT, dpT)

    def phase2(u, pp):
        i, hp, a = u
        q0 = i * 128
        h = hp * 2 + a
        pT, dpT = pp
        lst = plan[i]
        nkb = len(lst)
        if a == 0:
            pair[(i, hp)] = (C.bank(hold=True), C.bank(hold=True))
        (bo, dbo), (bd, dbd) = pair[(i, hp)]
        for jj, (j, tid) in enumerate(lst):
            first = (a == 0 and jj == 0)
            last = (a == 1 and jj == nkb - 1)
            P.op("pe", lambda e, bo=bo, h=h, j=j, pT=pT, jj=jj, first=first, last=last: e.matmul(
                bo[:, 0:128], lhsT=Vx[:, h, j, :], rhs=pT[:, jj * 128:(jj + 1) * 128], start=first, stop=last),
                reads=[dV, dpT], writes=[dbo], merge=(not first))
            P.op("pe", lambda e, bd=bd, a=a, pT=pT, jj=jj, first=first, last=last: e.matmul(
                bd[:, 0:128], lhsT=ones2[:, a, :], rhs=pT[:, jj * 128:(jj + 1) * 128], start=first, stop=last),
                reads=[dones, dpT], writes=[dbd], merge=(not first))
        if a == 1:
            rd, drd = rd_ring.next()
            P.op("dve", lambda e, rd=rd, bd=bd: e.reciprocal(out=rd[:], in_=bd[:, 0:128]), reads=[dbd], writes=[drd])
            P.op("dve", lambda e, rd=rd, bo=bo, hp=hp, q0=q0: e.tensor_tensor(
                out=yaT[:, hp, q0:q0 + 128], in0=bo[:, 0:128], in1=rd[:], op=ALU.mult),
                reads=[dbo, drd], writes=[dya[hp]], merge=True)
            C.release(pair[(i, hp)][0])
            C.release(pair[(i, hp)][1])
            del pair[(i, hp)]

    pend = None
    for u in units:
        pp = phase1(u)
        if pend is not None:
            phase2(*pend)
        pend = (u, pp)
    phase2(*pend)
    for hp in range(4):
        P.dma("sp", mixT[hp * 128:(hp + 1) * 128, :], yaT[:, hp, :], reads=[dya[hp]], writes=[C.dd("mixT")], merge=True)


class LNBufs:
    def __init__(self, P, name):
        self.stats = Ring(P, name + "st", 2, [128, 2, 6], F32)
        self.mv = Ring(P, name + "mv", 2, [128, 2], F32)
        self.rstd = Ring(P, name + "rs", 2, [128, 1], F32)
        self.nmr = Ring(P, name + "nm", 2, [128, 1], F32)


def layer_norm_tile(P, lb, r, dr, gb, bb, dgb, y, dy):
    st, dst = lb.stats.next()
    mv, dmv = lb.mv.next()
    rs, drs = lb.rstd.next()
    nm, dnm = lb.nmr.next()
    P.op("dve", lambda e: e.bn_stats(out=st[:, 0, :], in_=r[:, 0:512]), reads=[dr], writes=[dst])
    P.op("dve", lambda e: e.bn_stats(out=st[:, 1, :], in_=r[:, 512:1024]), reads=[dr], writes=[dst], merge=True)
    P.op("dve", lambda e: e.bn_aggr(out=mv[:], in_=st[:]), reads=[dst], writes=[dmv])
    P.op("dve", lambda e: e.tensor_scalar(out=rs[:], in0=mv[:, 1:2], scalar1=EPS, scalar2=None, op0=ALU.add), reads=[dmv], writes=[drs])
    P.op("act", lambda e: e.activation(out=rs[:], in_=rs[:], func=AF.Sqrt), reads=[drs], writes=[drs])
    P.op("dve", lambda e: e.reciprocal(out=rs[:], in_=rs[:]), reads=[drs], writes=[drs])
    P.op("dve", lambda e: e.scalar_tensor_tensor(out=nm[:], in0=mv[:, 0:1], scalar=-1.0, in1=rs[:], op0=ALU.mult, op1=ALU.mult),
         reads=[dmv, drs], writes=[dnm])
    P.op("act", lambda e: e.activation(out=y[:], in_=r[:], func=AF.Identity, bias=nm[:], scale=rs[:]),
         reads=[dr, drs, dnm], writes=[dy])
    P.op("pool", lambda e: e.tensor_tensor(out=y[:], in0=y[:], in1=gb[:], op=ALU.mult), reads=[dy, dgb], writes=[dy])
    P.op("pool", lambda e: e.tensor_tensor(out=y[:], in0=y[:], in1=bb[:], op=ALU.add), reads=[dy, dgb], writes=[dy])


def stage_proj_ln(C, w_name, x_name, lnname, li, out_name):
    P = C.P
    P.begin_stage()
    mixT = C.D("mixT", [1024, 2048], BF16)
    w = C.D(w_name, [1024, 1024], F32)
    x = C.D(x_name, [2048, 1024], F32)
    g_d = C.D(f"{lnname}_g{li}", [128, 1024], F32)
    b_d = C.D(f"{lnname}_b{li}", [128, 1024], F32)
    out = C.D(out_name, [2048, 1024], F32)
    ms = P.sb("ms", [128, 8, 2048], BF16)
    dms = [Dep() for _ in range(8)]
    for kc in range(8):
        P.dma("sp", ms[:, kc, :], mixT[kc * 128:(kc + 1) * 128, :], reads=[C.dd("mixT")], writes=[dms[kc]])
    ws, dws = load_fm_bf16(C, "wout", w, 8, 1024)
    gb = P.sb("gb", [128, 1024], F32)
    bb = P.sb("bb", [128, 1024], F32)
    dgb = Dep()
    P.dma("sp", gb[:], g_d, writes=[dgb])
    P.dma("sp", bb[:], b_d, writes=[dgb], merge=True)
    lb = LNBufs(P, "ln")
    xr = Ring(P, "xr", 3, [128, 1024], F32)
    rr = Ring(P, "rr", 2, [128, 1024], F32)
    yr = Ring(P, "yr", 2, [128, 1024], F32)
    for tt in range(NT):
        xt, dxt = xr.next()
        P.dma("sp", xt[:], x[tt * 128:(tt + 1) * 128, :], reads=[C.dd(x_name)], writes=[dxt])
        r, dr = rr.next()
        for half in range(2):
            bk, dbk = C.bank()
            hs = slice(half * 512, (half + 1) * 512)
            mm_acc(P, bk[:], [(ms[:, kc, tt * 128:(tt + 1) * 128], ws[:, kc, hs]) for kc in range(8)], reads=dms + dws, dwrite=dbk)
            P.op("dve", lambda e, r=r, xt=xt, bk=bk, hs=hs: e.scalar_tensor_tensor(
                out=r[:, hs], in0=xt[:, hs], scalar=ALPHA, in1=bk[:], op0=ALU.mult, op1=ALU.add),
                reads=[dxt, dbk], writes=[dr], merge=(half > 0))
        y, dy = yr.next()
        layer_norm_tile(P, lb, r, dr, gb, bb, dgb, y, dy)
        P.dma("sp", out[tt * 128:(tt + 1) * 128, :], y[:], reads=[dy], writes=[C.dd(out_name)], merge=True)


def stage_moe1(C, li, x_name):
    P = C.P
    P.begin_stage()
    x1 = C.D(x_name, [2048, 1024], F32)
    wr_d = C.D(f"moe_router{li}", [1024, 16], F32)
    ident_d = C.D("ident", [128, 128], F32)
    esel_d = C.D("esel", [16, 16, 128], F32)
    iotac_d = C.D("iota_col", [128, 16], F32)
    xe_all = C.D("xeT_all", [16, 128, 8, 256], BF16)
    idxc_d = C.D("moe_idxc", [128, 2, 16], F32)
    gc_d = C.D("moe_gc", [128, 2, 16], F32)
    wr = P.sb("wr", [128, 8, 16], F32)
    dcst = Dep()
    P.dma("sp", wr[:], wr_d.rearrange("(kc p) e -> p kc e", p=128), writes=[dcst])
    ident = P.sb("ident", [128, 128], F32)
    P.dma("sp", ident[:], ident_d, writes=[dcst], merge=True)
    esel = P.sb("esel", [16, 16, 128], F32)
    P.dma("sp", esel[:], esel_d, writes=[dcst], merge=True)
    iotac = P.sb("iotac", [128, 16], F32)
    P.dma("sp", iotac[:], iotac_d, writes=[dcst], merge=True)
    x1b = P.sb("x1b", [128, NT, 1024], BF16)
    dx1b = [Dep() for _ in range(NT)]
    affT = P.sb("affT", [16, 2048], F32)
    daffT = Dep()
    xr = Ring(P, "m1x", 2, [128, 1024], F32)
    xTr = Ring(P, "m1xT", 2, [128, 8, 128], F32)
    sm_r = Ring(P, "m1sm", 2, [128, 4], F32)
    ex_r = Ring(P, "m1ex", 2, [128, 16], F32)
    af_r = Ring(P, "m1af", 2, [128, 16], F32)
    for tt in range(NT):
        xt, dxt = xr.next()
        P.dma("sp", xt[:], x1[tt * 128:(tt + 1) * 128, :], reads=[C.dd(x_name)], writes=[dxt])
        P.op("act", lambda e, xt=xt, tt=tt: e.copy(out=x1b[:, tt, :], in_=xt[:]), reads=[dxt], writes=[dx1b[tt]])
        xT, dxT = xTr.next()
        for hb in range(2):
            bk, dbk = C.bank()
            for k4 in range(4):
                kc = hb * 4 + k4
                P.op("pe", lambda e, bk=bk, k4=k4, kc=kc, xt=xt: e.transpose(bk[:, k4 * 128:(k4 + 1) * 128], xt[:, kc * 128:(kc + 1) * 128], ident[:]),
                     reads=[dxt, dcst], writes=[dbk], merge=(k4 > 0))
            P.op("dve", lambda e, xT=xT, hb=hb, bk=bk: e.tensor_copy(out=xT[:, hb * 4:(hb + 1) * 4, :], in_=bk[:].rearrange("p (k t) -> p k t", k=4)),
                 reads=[dbk], writes=[dxT], merge=(hb > 0))
        bk, dbk = C.bank()
        mm_acc(P, bk[:, 0:16], [(xT[:, kc, :], wr[:, kc, :]) for kc in range(8)], reads=[dxT, dcst], dwrite=dbk)
        sm, dsm = sm_r.next()
        ex, dex = ex_r.next()
        af, daf = af_r.next()
        P.op("dve", lambda e, sm=sm, bk=bk: e.reduce_max(out=sm[:, 0:1], in_=bk[:, 0:16], axis=AX.X), reads=[dbk], writes=[dsm])
        P.op("dve", lambda e, sm=sm: e.tensor_scalar(out=sm[:, 1:2], in0=sm[:, 0:1], scalar1=-1.0, scalar2=None, op0=ALU.mult),
             reads=[dsm], writes=[dsm])
        P.op("act", lambda e, ex=ex, bk=bk, sm=sm: e.activation(out=ex[:], in_=bk[:, 0:16], func=AF.Exp, bias=sm[:, 1:2], accum_out=sm[:, 2:3]),
             reads=[dbk, dsm], writes=[dex, dsm])
        P.op("dve", lambda e, sm=sm: e.reciprocal(out=sm[:, 3:4], in_=sm[:, 2:3]), reads=[dsm], writes=[dsm])
        P.op("dve", lambda e, af=af, ex=ex, sm=sm: e.tensor_scalar(out=af[:], in0=ex[:], scalar1=sm[:, 3:4], scalar2=None, op0=ALU.mult),
             reads=[dex, dsm], writes=[daf])
        bk2, dbk2 = C.bank()
        P.op("pe", lambda e, bk2=bk2, af=af: e.transpose(bk2[0:16, 0:128], af[:], ident[:]), reads=[daf, dcst], writes=[dbk2])
        P.op("act", lambda e, bk2=bk2, tt=tt: e.copy(out=affT[:, tt * 128:(tt + 1) * 128], in_=bk2[0:16, 0:128]),
             reads=[dbk2], writes=[daffT], merge=(tt > 0))
    work = P.sb("work", [16, 2048], F32)
    dwork = Dep()
    g_all = P.sb("g_all", [16, 256], F32)
    idx_all = P.sb("idx_all", [16, 256], U32)
    dg = Dep()
    di = Dep()
    for r in range(32):
        src, dsrc = (affT, daffT) if r == 0 else (work, dwork)
        sl = slice(r * 8, (r + 1) * 8)
        P.op("dve", lambda e, src=src, sl=sl: e.max(out=g_all[:, sl], in_=src[:]), reads=[dsrc], writes=[dg], merge=(r > 0))
        P.op("dve", lambda e, src=src, sl=sl: e.max_index(out=idx_all[:, sl], in_max=g_all[:, sl], in_values=src[:]),
             reads=[dsrc, dg], writes=[di], merge=(r > 0))
        if r < 31:
            P.op("dve", lambda e, src=src, sl=sl: e.match_replace(out=work[:], in_to_replace=g_all[:, sl], in_values=src[:], imm_value=-1.0),
                 reads=[dsrc, dg], writes=[dwork])
    idxf = P.sb("idxf", [16, 256], F32)
    didxf = Dep()
    P.op("dve", lambda e: e.tensor_copy(out=idxf[:], in_=idx_all[:]), reads=[di], writes=[didxf])
    colt = P.sb("colt", [128, 2, 2, 16], F32)
    dcol = Dep()
    for which, (src, dsrc) in enumerate(((idxf, didxf), (g_all, dg))):
        for cc in range(2):
            bk, dbk = C.bank()
            P.op("pe", lambda e, bk=bk, src=src, cc=cc: e.transpose(bk[:, 0:16], src[:, cc * 128:(cc + 1) * 128], ident[0:16, 0:16]),
                 reads=[dsrc, dcst], writes=[dbk])
            P.op("act", lambda e, bk=bk, which=which, cc=cc: e.copy(out=colt[:, which, cc, :], in_=bk[:, 0:16]),
                 reads=[dbk], writes=[dcol], merge=True)
    P.dma("sp", idxc_d, colt[:, 0, :, :], reads=[dcol], writes=[C.dd("moe_idxc")])
    P.dma("sp", gc_d, colt[:, 1, :, :], reads=[dcol], writes=[C.dd("moe_gc")])
    sel_r = Ring(P, "sel", 2, [128, NT, 256], BF16)
    xe_r = Ring(P, "xe", 2, [128, 8, 256], BF16)
    for ex_i in range(16):
        bk, dbk = C.bank()
        P.op("pe", lambda e, bk=bk, ex_i=ex_i: e.matmul(bk[:, 0:256], lhsT=esel[:, ex_i, :], rhs=idxf[:], start=True, stop=True),
             reads=[dcst, didxf], writes=[dbk])
        sel, dsel = sel_r.next()
        for tt in range(NT):
            P.op("dve", lambda e, sel=sel, bk=bk, tt=tt: e.tensor_scalar(out=sel[:, tt, :], in0=bk[:, 0:256], scalar1=iotac[:, tt:tt + 1], scalar2=None, op0=ALU.is_equal),
                 reads=[dbk, dcst], writes=[dsel], merge=(tt > 0))
        xe, dxe = xe_r.next()
        for dc in range(8):
            bk2, dbk2 = C.bank()
            mm_acc(P, bk2[:, 0:256], [(x1b[:, tt, dc * 128:(dc + 1) * 128], sel[:, tt, :]) for tt in range(NT)],
                   reads=dx1b + [dsel], dwrite=dbk2)
            if dc % 2 == 0:
                P.op("act", lambda e, xe=xe, dc=dc, bk2=bk2: e.copy(out=xe[:, dc, :], in_=bk2[:, 0:256]), reads=[dbk2], writes=[dxe], merge=(dc > 0))
            else:
                P.op("dve", lambda e, xe=xe, dc=dc, bk2=bk2: e.tensor_copy(out=xe[:, dc, :], in_=bk2[:, 0:256]), reads=[dbk2], writes=[dxe], merge=True)
        P.dma("sp", xe_all[ex_i], xe[:], reads=[dxe], writes=[C.dd("xeT_all")], merge=True)


def stage_moe2(C, li, x_name, out_name):
    P = C.P
    P.begin_stage()
    x1 = C.D(x_name, [2048, 1024], F32)
    wg_d = C.D(f"moe_w_gate{li}", [16, 1024, 2048], F32)
    wu_d = C.D(f"moe_w_up{li}", [16, 1024, 2048], F32)
    wd_d = C.D(f"moe_w_down{li}", [16, 2048, 1024], F32)
    xe_all = C.D("xeT_all", [16, 128, 8, 256], BF16)
    idxc_d = C.D("moe_idxc", [128, 2, 16], F32)
    gc_d = C.D("moe_gc", [128, 2, 16], F32)
    iotar_d = C.D("iota_row", [128, 2048], F32)
    ident_d = C.D("ident", [128, 128], F32)
    g_d = C.D(f"ln2_g{li}", [128, 1024], F32)
    b_d = C.D(f"ln2_b{li}", [128, 1024], F32)
    out = C.D(out_name, [2048, 1024], F32)
    outT = C.D(out_name + "T", [1024, 2048], BF16)
    dcst = Dep()
    idxc = P.sb("idxc", [128, 2, 16], F32)
    gc = P.sb("gc", [128, 2, 16], F32)
    iotar = P.sb("iotar", [128, 2048], F32)
    P.dma("sp", idxc[:], idxc_d, reads=[C.dd("moe_idxc")], writes=[dcst])
    P.dma("sp", gc[:], gc_d, reads=[C.dd("moe_gc")], writes=[dcst], merge=True)
    P.dma("sp", iotar[:], iotar_d, writes=[dcst], merge=True)
    f_acc = P.sb("f_acc", [128, NT, 1024], F32)
    dfa = [Dep() for _ in range(NT)]
    xe_r = Ring(P, "m2xe", 2, [128, 8, 256], BF16)
    selT_r = Ring(P, "selT", 2, [128, 2, 2048], BF16)
    wg_r = Ring(P, "wg", 2, [128, 8, 512], BF16)
    wu_r = Ring(P, "wu", 2, [128, 8, 512], BF16)
    wd_r = Ring(P, "wd", 2, [128, 4, 1024], BF16)
    sg_r = Ring(P, "sg", 2, [128, 256], F32)
    hT_r = Ring(P, "hT", 2, [128, 16, 256], BF16)
    ye_r = Ring(P, "ye", 2, [128, 2, 1024], BF16)
    ev = 0
    for e_i in range(16):
        xe, dxe = xe_r.next()
        P.dma("sp", xe[:], xe_all[e_i], reads=[C.dd("xeT_all")], writes=[dxe])
        selT, dselT = selT_r.next()
        for cc in range(2):
            P.op("pool", lambda e, selT=selT, cc=cc, e_i=e_i: e.tensor_scalar(
                out=selT[:, cc, :], in0=iotar[:], scalar1=idxc[:, cc, e_i:e_i + 1], scalar2=gc[:, cc, e_i:e_i + 1],
                op0=ALU.is_equal, op1=ALU.mult), reads=[dcst], writes=[dselT], merge=(cc > 0))
        hT, dhT = hT_r.next()
        wgv = wg_d[e_i].rearrange("(kc p) f -> p kc f", p=128)
        wuv = wu_d[e_i].rearrange("(kc p) f -> p kc f", p=128)
        wdv = wd_d[e_i].rearrange("(fc p) d -> p fc d", p=128)
        for q in range(4):
            wg, dwg = wg_r.next()
            wu, dwu = wu_r.next()
            P.dma("pool", wg[:], wgv[:, :, q * 512:(q + 1) * 512], writes=[dwg])
            P.dma("pool", wu[:], wuv[:, :, q * 512:(q + 1) * 512], writes=[dwu])
            for fcl in range(4):
                fc = q * 4 + fcl
                fs = slice(fcl * 128, (fcl + 1) * 128)
                bg, dbg = C.bank()
                bu, dbu = C.bank()
                mm_acc(P, bg[:, 0:256], [(wg[:, kc, fs], xe[:, kc, :]) for kc in range(8)], reads=[dwg, dxe], dwrite=dbg)
                mm_acc(P, bu[:, 0:256], [(wu[:, kc, fs], xe[:, kc, :]) for kc in range(8)], reads=[dwu, dxe], dwrite=dbu)
                sg, dsg = sg_r.next()
                P.op("act", lambda e, sg=sg, bg=bg: e.activation(out=sg[:], in_=bg[:, 0:256], func=AF.Silu), reads=[dbg], writes=[dsg])
                P.op("dve", lambda e, hT=hT, fc=fc, sg=sg, bu=bu: e.tensor_tensor(out=hT[:, fc, :], in0=sg[:], in1=bu[:, 0:256], op=ALU.mult),
                     reads=[dsg, dbu], writes=[dhT], merge=(fc > 0))
        ye, dye = ye_r.next()
        dbanks = [C.bank() for _ in range(4)]
        for r in range(4):
            wd, dwd = wd_r.next()
            P.dma("pool", wd[:], wdv[:, r * 4:(r + 1) * 4, :], writes=[dwd])
            for ct in range(2):
                for dh in range(2):
                    bk, dbk = dbanks[ct * 2 + dh]
                    for f4 in range(4):
                        fc = r * 4 + f4
                        first = (fc == 0)
                        last = (fc == 15)
                        P.op("pe", lambda e, bk=bk, hT=hT, fc=fc, ct=ct, wd=wd, f4=f4, dh=dh, first=first, last=last: e.matmul(
                            bk[:], lhsT=hT[:, fc, ct * 128:(ct + 1) * 128], rhs=wd[:, f4, dh * 512:(dh + 1) * 512], start=first, stop=last),
                            reads=[dhT, dwd], writes=[dbk], merge=(not first))
        for ct in range(2):
            for dh in range(2):
                bk, dbk = dbanks[ct * 2 + dh]
                P.op("act", lambda e, ye=ye, ct=ct, dh=dh, bk=bk: e.copy(out=ye[:, ct, dh * 512:(dh + 1) * 512], in_=bk[:]),
                     reads=[dbk], writes=[dye], merge=(ct + dh > 0))
        for tt in range(NT):
            for dh in range(2):
                bk, dbk = C.bank()
                ds = slice(dh * 512, (dh + 1) * 512)
                mm_acc(P, bk[:], [(selT[:, ct, tt * 128:(tt + 1) * 128], ye[:, ct, ds]) for ct in range(2)], reads=[dselT, dye], dwrite=dbk)
                if e_i == 0:
                    P.op("dve", lambda e, tt=tt, ds=ds, bk=bk: e.tensor_copy(out=f_acc[:, tt, ds], in_=bk[:]), reads=[dbk], writes=[dfa[tt]], merge=(dh > 0))
                else:
                    P.op("dve", lambda e, tt=tt, ds=ds, bk=bk: e.tensor_tensor(out=f_acc[:, tt, ds], in0=f_acc[:, tt, ds], in1=bk[:], op=ALU.add),
                         reads=[dbk, dfa[tt]], writes=[dfa[tt]])
    gb = P.sb("gb2", [128, 1024], F32)
    bb = P.sb("bb2", [128, 1024], F32)
    ident = P.sb("ident2", [128, 128], F32)
    dgb = Dep()
    P.dma("sp", gb[:], g_d, writes=[dgb])
    P.dma("sp", bb[:], b_d, writes=[dgb], merge=True)
    P.dma("sp", ident[:], ident_d, writes=[dgb], merge=True)
    lb = LNBufs(P, "ln2")
    xr = Ring(P, "m2x", 2, [128, 1024], F32)
    yr = Ring(P, "m2y", 2, [128, 1024], F32)
    yT_r = Ring(P, "m2yT", 2, [128, 8, 128], BF16)
    for tt in range(NT):
        xt, dxt = xr.next()
        P.dma("sp", xt[:], x1[tt * 128:(tt + 1) * 128, :], reads=[C.dd(x_name)], writes=[dxt])
        P.op("dve", lambda e, xt=xt, tt=tt: e.scalar_tensor_tensor(out=xt[:], in0=xt[:], scalar=ALPHA, in1=f_acc[:, tt, :], op0=ALU.mult, op1=ALU.add),
             reads=[dxt, dfa[tt]], writes=[dxt])
        y, dy = yr.next()
        layer_norm_tile(P, lb, xt, dxt, gb, bb, dgb, y, dy)
        P.dma("sp", out[tt * 128:(tt + 1) * 128, :], y[:], reads=[dy], writes=[C.dd(out_name)], merge=True)
        transpose_tile_to_dram(C, y, dy, ident, dgb, yT_r, outT, out_name + "T", tt)


def transpose_tile_to_dram(C, y, dy, ident, dident, yT_r, outT, outT_name, tt):
    P = C.P
    yT, dyT = yT_r.next()
    for hb in range(2):
        bk, dbk = C.bank()
        for k4 in range(4):
            kc = hb * 4 + k4
            P.op("pe", lambda e, bk=bk, k4=k4, kc=kc: e.transpose(bk[:, k4 * 128:(k4 + 1) * 128], y[:, kc * 128:(kc + 1) * 128], ident[:]),
                 reads=[dy, dident], writes=[dbk], merge=(k4 > 0))
        P.op("act", lambda e, hb=hb, bk=bk: e.copy(out=yT[:, hb * 4:(hb + 1) * 4, :], in_=bk[:].rearrange("p (k t) -> p k t", k=4)),
             reads=[dbk], writes=[dyT], merge=(hb > 0))
    P.dma("sp", outT.rearrange("(kc p) t -> p kc t", p=128)[:, :, tt * 128:(tt + 1) * 128], yT[:], reads=[dyT], writes=[C.dd(outT_name)], merge=True)


def stage_ple(C, li, x_name, out_name, want_T):
    P = C.P
    P.begin_stage()
    x2 = C.D(x_name, [2048, 1024], F32)
    x2T = C.D(x_name + "T", [1024, 2048], BF16)
    pT_d = C.D(f"pT{li}", [256, 2048], F32)
    wg_d = C.D(f"ple_gate{li}", [1024, 1024], F32)
    wp_d = C.D(f"ple_proj{li}", [256, 1024], F32)
    ident_d = C.D("ident", [128, 128], F32)
    out = C.D(out_name, [2048, 1024], F32)
    xs = P.sb("plx", [128, 8, 2048], BF16)
    dxs = [Dep() for _ in range(8)]
    for kc in range(8):
        P.dma("sp", xs[:, kc, :], x2T[kc * 128:(kc + 1) * 128, :], reads=[C.dd(x_name + "T")], writes=[dxs[kc]])
    ps_, dps_ = load_fm_bf16(C, "plp", pT_d, 2, 2048)
    wg, dwg = load_fm_bf16(C, "plwg", wg_d, 8, 1024)
    wp, dwp = load_fm_bf16(C, "plwp", wp_d, 2, 1024)
    ident = P.sb("ident3", [128, 128], F32)
    dident = Dep()
    P.dma("sp", ident[:], ident_d, writes=[dident])
    if want_T:
        outT = C.D(out_name + "T", [1024, 2048], BF16)
        yT_r = Ring(P, "plyT", 2, [128, 8, 128], BF16)
    xr = Ring(P, "plxr", 2, [128, 1024], F32)
    gr = Ring(P, "plg", 2, [128, 1024], F32)
    yr = Ring(P, "ply", 2, [128, 1024], F32)
    for tt in range(NT):
        ts_ = slice(tt * 128, (tt + 1) * 128)
        xt, dxt = xr.next()
        P.dma("sp", xt[:], x2[ts_, :], reads=[C.dd(x_name)], writes=[dxt])
        gt, dgt = gr.next()
        y, dy = yr.next()
        for half in range(2):
            hs = slice(half * 512, (half + 1) * 512)
            bk, dbk = C.bank()
            mm_acc(P, bk[:], [(xs[:, kc, ts_], wg[:, kc, hs]) for kc in range(8)], reads=dxs + dwg, dwrite=dbk)
            P.op("act", lambda e, gt=gt, hs=hs, bk=bk: e.activation(out=gt[:, hs], in_=bk[:], func=AF.Sigmoid), reads=[dbk], writes=[dgt], merge=(half > 0))
            bk2, dbk2 = C.bank()
            mm_acc(P, bk2[:], [(ps_[:, kc, ts_], wp[:, kc, hs]) for kc in range(2)], reads=dps_ + dwp, dwrite=dbk2)
            P.op("dve", lambda e, gt=gt, hs=hs, bk2=bk2: e.tensor_tensor(out=gt[:, hs], in0=gt[:, hs], in1=bk2[:], op=ALU.mult),
                 reads=[dgt, dbk2], writes=[dgt])
        P.op("pool", lambda e, y=y, xt=xt, gt=gt: e.tensor_tensor(out=y[:], in0=xt[:], in1=gt[:], op=ALU.add), reads=[dxt, dgt], writes=[dy])
        P.dma("sp", out[ts_, :], y[:], reads=[dy], writes=[C.dd(out_name)], merge=True)
        if want_T:
            transpose_tile_to_dram(C, y, dy, ident, dident, yT_r, outT, out_name + "T", tt)


def hyena_consts():
    L, N = 2048, 4096
    R = np.arange(N)
    f = np.where(R <= 2048, R, R - 2048).astype(np.int64)
    is_im = R > 2048
    t = np.arange(L, dtype=np.int64)
    k = (t[:, None] * f[None, :]) % N
    ang = 2.0 * np.pi * k.astype(np.float64) / N
    Wf = np.where(is_im[None, :], -np.sin(ang), np.cos(ang))
    Wb = np.where(is_im[None, :], np.sin(ang), np.cos(ang))
    Wb[0, :] = 0.0
    cR = np.full(N, 2.0 / N)
    cR[0] = 1.0 / N
    cR[2048] = 1.0 / N
    WfT = np.ascontiguousarray(Wf.T)
    Wf_d = Wf.reshape(16, 128, 32, 128).transpose(2, 1, 0, 3)
    Wb_d = Wb.reshape(16, 128, 32, 128).transpose(2, 1, 0, 3)
    WA_d = WfT.reshape(32, 128, 16, 128).transpose(2, 1, 0, 3)
    WB_d = WfT.reshape(2, 16, 128, 4, 512).transpose(3, 0, 2, 1, 4)
    tl = np.linspace(0.0, 1.0, L, dtype=np.float32)[:, None]
    w = (2.0 * np.float32(math.pi) * np.arange(L, dtype=np.float32)[:, None] / np.float32(L)).astype(np.float32)
    fb = np.linspace(1e-4, 15, 16, dtype=np.float32)[None, :]
    z = np.concatenate([tl, np.cos(fb * w), -np.sin(fb * w)], axis=-1).astype(np.float32)
    min_decay = math.log(1e-2) / 1.5
    max_decay = math.log(1e-2) / 0.3
    deltas = np.abs(np.linspace(min_decay, max_decay, 512, dtype=np.float32))
    decay = np.exp(-tl * deltas[None, :]).astype(np.float32)
    return {
        "hy_Wf": np.ascontiguousarray(Wf_d).astype(NPBF), "hy_Wb": np.ascontiguousarray(Wb_d).astype(NPBF),
        "hy_WA": np.ascontiguousarray(WA_d).astype(NPBF), "hy_WB": np.ascontiguousarray(WB_d).astype(NPBF),
        "hy_cR": np.ascontiguousarray(cR.reshape(32, 128).T).astype(np.float32),
        "hy_zT": np.ascontiguousarray(z.T), "hy_decay": np.ascontiguousarray(decay.reshape(16, 128, 512)),
    }


TWO_PI = 2.0 * math.pi


def stage_hy_filter(C):
    P = C.P
    P.begin_stage()
    zT_d = C.D("hy_zT", [33, 2048], F32)
    w1_d = C.D("hy_f_w1", [33, 64], F32)
    w2_d = C.D("hy_f_w2", [64, 64], F32)
    w3_d = C.D("hy_f_w3", [64, 2048], F32)
    cols_d = C.D("hy_cols", [64, 3], F32)
    dec_d = C.D("hy_decay", [16, 128, 512], F32)
    Wf_d = C.D("hy_Wf", [32, 128, 16, 128], BF16)
    Wb_d = C.D("hy_Wb", [32, 128, 16, 128], BF16)
    cR_d = C.D("hy_cR", [128, 32], F32)
    skip_d = C.D("hy_skipb", [2, 128, 512], F32)
    Kf_d = C.D("hy_Kf", [32, 2, 128, 512], F32)
    dc = Dep()
    zT = P.sb("zT", [33, 2048], F32)
    w1 = P.sb("w1", [33, 64], F32)
    w2 = P.sb("w2", [64, 64], F32)
    w3 = P.sb("w3", [64, 2048], BF16)
    cols = P.sb("cols", [64, 8], F32)
    cR = P.sb("cR", [128, 32], F32)
    skipb = P.sb("skipb", [128, 2, 512], F32)
    P.dma("sp", zT[:], zT_d, writes=[dc])
    P.dma("sp", w1[:], w1_d, writes=[dc], merge=True)
    P.dma("sp", w2[:], w2_d, writes=[dc], merge=True)
    P.dma("pool", w3[:], w3_d, writes=[dc], merge=True)
    P.dma("sp", cols[:, 0:3], cols_d, writes=[dc], merge=True)
    P.dma("sp", cR[:], cR_d, writes=[dc], merge=True)
    for o in range(2):
        P.dma("sp", skipb[:, o, :], skip_d[o], writes=[dc], merge=True)
    dcol = Dep()
    P.op("dve", lambda e: e.tensor_tensor(out=cols[:, 3:4], in0=cols[:, 0:1], in1=cols[:, 1:2], op=ALU.mult), reads=[dc], writes=[dcol])
    P.op("dve", lambda e: e.tensor_tensor(out=cols[:, 4:5], in0=cols[:, 2:3], in1=cols[:, 1:2], op=ALU.mult), reads=[dc], writes=[dcol], merge=True)
    P.op("pool", lambda e: e.memset(cols[:, 5:6], -math.pi), reads=[dc], writes=[dcol], merge=True)
    h1T = P.sb("h1T", [64, 2048], F32)
    h2T = P.sb("h2T", [64, 2048], BF16)
    dh1 = Dep()
    dh2 = Dep()
    u_r = Ring(P, "hyu", 2, [64, 512], F32)
    s_r = Ring(P, "hys", 4, [64, 512], F32)
    for layer in range(2):
        for nt in range(4):
            ns = slice(nt * 512, (nt + 1) * 512)
            bk, dbk = C.bank()
            if layer == 0:
                P.op("pe", lambda e, bk=bk, ns=ns: e.matmul(bk[0:64, :], lhsT=w1[:], rhs=zT[:, ns], start=True, stop=True), reads=[dc], writes=[dbk])
            else:
                P.op("pe", lambda e, bk=bk, ns=ns: e.matmul(bk[0:64, :], lhsT=w2[:], rhs=h1T[:, ns], start=True, stop=True), reads=[dc, dh1], writes=[dbk])
            u, du = u_r.next()
            fbc = 3 + layer
            P.op("dve", lambda e, u=u, bk=bk, fbc=fbc: e.tensor_scalar(out=u[:], in0=bk[0:64, :], scalar1=cols[:, 1:2], scalar2=cols[:, fbc:fbc + 1], op0=ALU.mult, op1=ALU.add),
                 reads=[dbk, dcol, dc], writes=[du])
            s2, ds2 = s_r.next()
            s4, ds4 = s_r.next()
            P.op("act", lambda e, u=u, s2=s2: e.activation(out=s2[:], in_=u[:], func=AF.Sin, scale=0.5), reads=[du], writes=[ds2])
            P.op("act", lambda e, u=u, s4=s4: e.activation(out=s4[:], in_=u[:], func=AF.Sin, scale=0.25), reads=[du], writes=[ds4])
            P.op("dve", lambda e, s4=s4: e.tensor_tensor(out=s4[:], in0=s4[:], in1=s4[:], op=ALU.mult), reads=[ds4], writes=[ds4])
            P.op("dve", lambda e, s4=s4: e.tensor_scalar(out=s4[:], in0=s4[:], scalar1=-2.0, scalar2=1.0, op0=ALU.mult, op1=ALU.add), reads=[ds4], writes=[ds4])
            if layer == 0:
                P.op("dve", lambda e, s2=s2, s4=s4, ns=ns: e.scalar_tensor_tensor(out=h1T[:, ns], in0=s2[:], scalar=2.0, in1=s4[:], op0=ALU.mult, op1=ALU.mult),
                     reads=[ds2, ds4], writes=[dh1], merge=(nt > 0))
            else:
                P.op("dve", lambda e, s2=s2, s4=s4, ns=ns: e.scalar_tensor_tensor(out=h2T[:, ns], in0=s2[:], scalar=2.0, in1=s4[:], op0=ALU.mult, op1=ALU.mult),
                     reads=[ds2, ds4], writes=[dh2], merge=(nt > 0))
    Kt = P.sb("Kt", [128, 16, 4, 512], BF16)
    dKt = [Dep() for _ in range(16)]
    ones = P.sb("onesb", [128, 128], BF16)
    dones = Dep()
    P.op("pool", lambda e: e.memset(ones[:], 1.0), writes=[dones])
    dec_r = Ring(P, "dec", 2, [128, 512], F32)
    sq_r = Ring(P, "sq", 3, [128, 512], BF16)
    ssq = [C.bank(hold=True), C.bank(hold=True)]
    for tt in range(16):
        dec, ddec = dec_r.next()
        P.dma("sp", dec[:], dec_d[tt], writes=[ddec])
        for q in range(4):
            bk, dbk = C.bank()
            P.op("pe", lambda e, bk=bk, tt=tt, q=q: e.matmul(bk[:], lhsT=h2T[:, tt * 128:(tt + 1) * 128], rhs=w3[:, q * 512:(q + 1) * 512], start=True, stop=True),
                 reads=[dh2, dc], writes=[dbk])
            P.op("dve", lambda e, bk=bk, tt=tt, q=q, dec=dec: e.tensor_tensor(out=Kt[:, tt, q, :], in0=bk[:], in1=dec[:], op=ALU.mult),
                 reads=[dbk, ddec], writes=[dKt[tt]], merge=(q > 0))
        if tt == 0:
            for q in (1, 3):
                P.op("pool", lambda e, q=q: e.memset(Kt[0:1, 0, q, :], 0.0), reads=[dKt[0]], writes=[dKt[0]])
        for q in range(4):
            sq, dsq = sq_r.next()
            P.op("act", lambda e, sq=sq, tt=tt, q=q: e.activation(out=sq[:], in_=Kt[:, tt, q, :], func=AF.Square), reads=[dKt[tt]], writes=[dsq])
            sb_, dsb_ = ssq[q // 2]
            first = (tt == 0 and q % 2 == 0)
            last = (tt == 15 and q % 2 == 1)
            P.op("pe", lambda e, sb_=sb_, sq=sq, first=first, last=last: e.matmul(sb_[:], lhsT=ones[:], rhs=sq[:], start=first, stop=last),
                 reads=[dsq, dones], writes=[dsb_], merge=(not first))
    rs = P.sb("hyrs", [128, 2, 512], F32)
    drs = Dep()
    for o in range(2):
        sb_, dsb_ = ssq[o]
        P.op("dve", lambda e, o=o, sb_=sb_: e.tensor_scalar(out=rs[:, o, :], in0=sb_[:], scalar1=1e-12, scalar2=None, op0=ALU.add), reads=[dsb_], writes=[drs], merge=(o > 0))
    C.release_all()
    P.op("act", lambda e: e.activation(out=rs[:], in_=rs[:], func=AF.Sqrt), reads=[drs], writes=[drs])
    P.op("dve", lambda e: e.reciprocal(out=rs[:], in_=rs[:]), reads=[drs], writes=[drs])
    wf_r = Ring(P, "wf", 2, [128, 16, 128], BF16)
    wb_r = Ring(P, "wb", 2, [128, 16, 128], BF16)
    kf_r = Ring(P, "kf", 3, [128, 512], F32)
    for ft in range(32):
        wf, dwf = wf_r.next()
        wb, dwb = wb_r.next()
        P.dma("sp", wf[:], Wf_d[ft], writes=[dwf])
        P.dma("sp", wb[:], Wb_d[ft], writes=[dwb])
        for o in range(2):
            bk, dbk = C.bank()
            pairs = [(wf[:, tt, :], Kt[:, tt, 2 * o, :]) for tt in range(16)] + [(wb[:, tt, :], Kt[:, tt, 2 * o + 1, :]) for tt in range(16)]
            mm_acc(P, bk[:], pairs, reads=[dwf, dwb] + dKt, dwrite=dbk)
            kf, dkf = kf_r.next()
            P.op("dve", lambda e, kf=kf, bk=bk, o=o: e.tensor_tensor(out=kf[:], in0=bk[:], in1=rs[:, o, :], op=ALU.mult), reads=[dbk, drs], writes=[dkf])
            if ft < 16:
                P.op("pool", lambda e, kf=kf, o=o: e.tensor_tensor(out=kf[:], in0=kf[:], in1=skipb[:, o, :], op=ALU.add), reads=[dkf, dc], writes=[dkf])
            elif ft == 16:
                P.op("pool", lambda e, kf=kf, o=o: e.tensor_tensor(out=kf[0:1, :], in0=kf[0:1, :], in1=skipb[0:1, o, :], op=ALU.add), reads=[dkf, dc], writes=[dkf])
            P.op("pool", lambda e, kf=kf, ft=ft: e.tensor_scalar(out=kf[:], in0=kf[:], scalar1=cR[:, ft:ft + 1], scalar2=None, op0=ALU.mult), reads=[dkf, dc], writes=[dkf])
            P.dma("sp", Kf_d[ft, o], kf[:], reads=[dkf], writes=[C.dd("hy_Kf")], merge=True)


def stage_hy_prep(C):
    P = C.P
    P.begin_stage()
    hbT = C.D("hbT", [1536, 2048], F32)
    cw_d = C.D("hy_cw", [128, 12, 3], F32)
    cb_d = C.D("hy_cb", [128, 12], F32)
    ident_d = C.D("ident", [128, 128], F32)
    hv = C.D("hv_tm", [2048, 512], BF16)
    hx1 = C.D("hx1_tm", [2048, 512], F32)
    hx2T = C.D("hx2T", [512, 2048], F32)
    dc = Dep()
    cw = P.sb("cw", [128, 12, 3], F32)
    cb = P.sb("cb", [128, 12], F32)
    ident = P.sb("identh", [128, 128], F32)
    P.dma("sp", cw[:], cw_d, writes=[dc])
    P.dma("sp", cb[:], cb_d, writes=[dc], merge=True)
    P.dma("sp", ident[:], ident_d, writes=[dc], merge=True)
    xin_r = Ring(P, "hxin", 2, [128, 2048], F32)
    y_r = Ring(P, "hy", 2, [128, 2048], F32)
    sv_r = Ring(P, "hsv", 2, [128, 16, 128], BF16)
    sx_r = Ring(P, "hsx", 2, [128, 16, 128], F32)
    for ch in range(12):
        xin, dxin = xin_r.next()
        P.dma("sp", xin[:], hbT[ch * 128:(ch + 1) * 128, :], reads=[C.dd("hbT")], writes=[dxin])
        y, dy = y_r.next()
        P.op("act", lambda e, y=y, xin=xin, ch=ch: e.activation(out=y[:], in_=xin[:], func=AF.Identity, bias=cb[:, ch:ch + 1], scale=cw[:, ch, 1:2]),
             reads=[dxin, dc], writes=[dy])
        P.op("dve", lambda e, y=y, xin=xin, ch=ch: e.scalar_tensor_tensor(out=y[:, 1:2048], in0=xin[:, 0:2047], scalar=cw[:, ch, 0:1], in1=y[:, 1:2048], op0=ALU.mult, op1=ALU.add),
             reads=[dxin, dc, dy], writes=[dy])
        P.op("dve", lambda e, y=y, xin=xin, ch=ch: e.scalar_tensor_tensor(out=y[:, 0:2047], in0=xin[:, 1:2048], scalar=cw[:, ch, 2:3], in1=y[:, 0:2047], op0=ALU.mult, op1=ALU.add),
             reads=[dxin, dc, dy], writes=[dy])
        if ch >= 8:
            P.dma("sp", hx2T[(ch - 8) * 128:(ch - 7) * 128, :], y[:], reads=[dy], writes=[C.dd("hx2T")], merge=True)
            continue
        stg, dstg = (sv_r if ch < 4 else sx_r).next()
        for g in range(4):
            bk, dbk = C.bank()
            for k4 in range(4):
                tt = g * 4 + k4
                P.op("pe", lambda e, bk=bk, k4=k4, tt=tt, y=y: e.transpose(bk[:, k4 * 128:(k4 + 1) * 128], y[:, tt * 128:(tt + 1) * 128], ident[:]),
                     reads=[dy, dc], writes=[dbk], merge=(k4 > 0))
            P.op("act", lambda e, stg=stg, g=g, bk=bk: e.copy(out=stg[:, g * 4:(g + 1) * 4, :], in_=bk[:].rearrange("p (k t) -> p k t", k=4)),
                 reads=[dbk], writes=[dstg], merge=(g > 0))
        if ch < 4:
            P.dma("sp", hv.rearrange("(tt p) c -> p tt c", p=128)[:, :, ch * 128:(ch + 1) * 128], stg[:], reads=[dstg], writes=[C.dd("hv_tm")], merge=True)
        else:
            P.dma("sp", hx1.rearrange("(tt p) c -> p tt c", p=128)[:, :, (ch - 4) * 128:(ch - 3) * 128], stg[:], reads=[dstg], writes=[C.dd("hx1_tm")], merge=True)


def stage_hy_conv(C):
    P = C.P
    P.begin_stage()
    hv = C.D("hv_tm", [2048, 512], BF16)
    hx1 = C.D("hx1_tm", [2048, 512], F32)
    hx2T = C.D("hx2T", [512, 2048], F32)
    Kf_d = C.D("hy_Kf", [32, 2, 128, 512], F32)
    Wf_d = C.D("hy_Wf", [32, 128, 16, 128], BF16)
    WA_d = C.D("hy_WA", [16, 128, 32, 128], BF16)
    WB_d = C.D("hy_WB", [4, 2, 128, 16, 512], BF16)
    mixT = C.D("mixT", [1024, 2048], BF16)
    ztm = P.sb("ztm", [128, 16, 512], BF16)
    dz = [Dep() for _ in range(16)]
    hvv = hv.rearrange("(tt p) c -> p tt c", p=128)
    for tt in range(16):
        P.dma("sp", ztm[:, tt, :], hvv[:, tt, :], reads=[C.dd("hv_tm")], writes=[dz[tt]])
    Yt = P.sb("Yt", [128, 32, 512], BF16)
    dY = [Dep() for _ in range(32)]
    wf_r = Ring(P, "cwf", 3, [128, 16, 128], BF16)
    kf_r = Ring(P, "ckf", 4, [128, 512], F32)
    t_r = Ring(P, "ct", 4, [128, 512], F32)
    wa_r = Ring(P, "cwa", 2, [128, 32, 128], BF16)
    wb_r = Ring(P, "cwb", 2, [128, 16, 512], BF16)
    x_r = Ring(P, "cx", 3, [128, 512], F32)
    zo_r = Ring(P, "czo", 3, [128, 512], BF16)
    for o in range(2):
        for j in range(16):
            ub = []
            for part in range(2):
                ft = part * 16 + j
                wf, dwf = wf_r.next()
                P.dma("sp", wf[:], Wf_d[ft], writes=[dwf])
                bk, dbk = C.bank()
                mm_acc(P, bk[:], [(wf[:, tt, :], ztm[:, tt, :]) for tt in range(16)], reads=[dwf] + dz, dwrite=dbk)
                ub.append((bk, dbk))
            kre, dkre = kf_r.next()
            kim, dkim = kf_r.next()
            P.dma("sp", kre[:], Kf_d[j, o], reads=[C.dd("hy_Kf")], writes=[dkre])
            P.dma("sp", kim[:], Kf_d[16 + j, o], reads=[C.dd("hy_Kf")], writes=[dkim])
            (ure, dure), (uim, duim) = ub
            t1, dt1 = t_r.next()
            t2, dt2 = t_r.next()
            P.op("dve", lambda e, t1=t1, ure=ure, kre=kre: e.tensor_tensor(out=t1[:], in0=ure[:], in1=kre[:], op=ALU.mult), reads=[dure, dkre], writes=[dt1])
            P.op("dve", lambda e, t2=t2, uim=uim, kim=kim: e.tensor_tensor(out=t2[:], in0=uim[:], in1=kim[:], op=ALU.mult), reads=[duim, dkim], writes=[dt2])
            P.op("pool", lambda e, j=j, t1=t1, t2=t2: e.tensor_tensor(out=Yt[:, j, :], in0=t1[:], in1=t2[:], op=ALU.subtract), reads=[dt1, dt2], writes=[dY[j]])
            if j == 0:
                P.op("pool", lambda e, t1=t1: e.tensor_copy(out=Yt[0:1, 0, :], in_=t1[0:1, :]), reads=[dt1, dY[0]], writes=[dY[0]])
            t3, dt3 = t_r.next()
            t4, dt4 = t_r.next()
            P.op("dve", lambda e, t3=t3, ure=ure, kim=kim: e.tensor_tensor(out=t3[:], in0=ure[:], in1=kim[:], op=ALU.mult), reads=[dure, dkim], writes=[dt3])
            P.op("dve", lambda e, t4=t4, uim=uim, kre=kre: e.tensor_tensor(out=t4[:], in0=uim[:], in1=kre[:], op=ALU.mult), reads=[duim, dkre], writes=[dt4])
            P.op("pool", lambda e, j=j, t3=t3, t4=t4: e.tensor_tensor(out=Yt[:, 16 + j, :], in0=t3[:], in1=t4[:], op=ALU.add), reads=[dt3, dt4], writes=[dY[16 + j]])
            if j == 0:
                P.op("pool", lambda e, t2=t2: e.tensor_copy(out=Yt[0:1, 16, :], in_=t2[0:1, :]), reads=[dt2, dY[16]], writes=[dY[16]])
        if o == 0:
            for tt in range(16):
                wa, dwa = wa_r.next()
                P.dma("sp", wa[:], WA_d[tt], writes=[dwa])
                bk, dbk = C.bank()
                mm_acc(P, bk[:], [(wa[:, kt, :], Yt[:, kt, :]) for kt in range(32)], reads=[dwa] + dY, dwrite=dbk)
                xt, dxt = x_r.next()
                P.dma("sp", xt[:], hx1[tt * 128:(tt + 1) * 128, :], reads=[C.dd("hx1_tm")], writes=[dxt])
                P.op("dve", lambda e, tt=tt, bk=bk, xt=xt: e.tensor_tensor(out=ztm[:, tt, :], in0=bk[:], in1=xt[:], op=ALU.mult), reads=[dbk, dxt], writes=[dz[tt]])
        else:
            for nt in range(4):
                banks = [C.bank() for _ in range(4)]
                for hf in range(2):
                    wb, dwb = wb_r.next()
                    P.dma("sp", wb[:], WB_d[nt, hf], writes=[dwb])
                    for cc in range(4):
                        bk, dbk = banks[cc]
                        for k in range(16):
                            first = (hf == 0 and k == 0)
                            last = (hf == 1 and k == 15)
                            kt = hf * 16 + k
                            P.op("pe", lambda e, bk=bk, kt=kt, cc=cc, wb=wb, k=k, first=first, last=last: e.matmul(
                                bk[:], lhsT=Yt[:, kt, cc * 128:(cc + 1) * 128], rhs=wb[:, k, :], start=first, stop=last),
                                reads=[dY[kt], dwb], writes=[dbk], merge=(not first))
                for cc in range(4):
                    bk, dbk = banks[cc]
                    xt, dxt = x_r.next()
                    P.dma("sp", xt[:], hx2T[cc * 128:(cc + 1) * 128, nt * 512:(nt + 1) * 512], reads=[C.dd("hx2T")], writes=[dxt])
                    zo, dzo = zo_r.next()
                    P.op("dve", lambda e, zo=zo, bk=bk, xt=xt: e.tensor_tensor(out=zo[:], in0=bk[:], in1=xt[:], op=ALU.mult), reads=[dbk, dxt], writes=[dzo])
                    P.dma("sp", mixT[512 + cc * 128:512 + (cc + 1) * 128, nt * 512:(nt + 1) * 512], zo[:], reads=[dzo], writes=[C.dd("mixT")], merge=True)


MLA_SCALE = 96.0 ** -0.5


def mla_consts():
    inv = 1.0 / (10000.0 ** (np.arange(0, 32, 2, dtype=np.float32) / 32.0))
    ang = np.arange(2048, dtype=np.float32)[:, None] * inv[None, :].astype(np.float32)
    cos = np.cos(ang).astype(np.float32).T
    sin = np.sin(ang).astype(np.float32).T
    cos2 = np.concatenate([cos, cos], axis=0)
    sin2 = np.concatenate([-sin, sin], axis=0)
    return {"mla_cs2": np.ascontiguousarray(np.stack([cos2, sin2], axis=1)).astype(np.float32)}


def stage_mla1(C, xT_name):
    P = C.P
    P.begin_stage()
    xT = C.D(xT_name, [1024, 2048], BF16)
    wi_d = C.D("mla_w_in", [1024, 672], F32)
    wsw_d = C.D("mla_w_in_sw", [1024, 96], F32)
    gc_d = C.D("mla_gcols", [128, 5], F32)
    cs_d = C.D("mla_cs2", [32, 2, 2048], F32)
    nT_d = C.D("mla_nT", [640, 2048], BF16)
    kr_d = C.D("mla_krT", [32, 2048], BF16)
    xs = P.sb("mxs", [128, 8, 2048], BF16)
    dxs = [Dep() for _ in range(8)]
    for kc in range(8):
        P.dma("sp", xs[:, kc, :], xT[kc * 128:(kc + 1) * 128, :], reads=[C.dd(xT_name)], writes=[dxs[kc]])
    wi, dwi = load_fm_bf16(C, "mwi", wi_d, 8, 672)
    wsw, dwsw = load_fm_bf16(C, "mwsw", wsw_d, 8, 96)
    dc = Dep()
    gcol = P.sb("mgc", [128, 5], F32)
    P.dma("sp", gcol[:], gc_d, writes=[dc])
    cs = P.sb("mcs", [96, 2, 2048], F32)
    P.dma("sp", cs[64:96, :, :], cs_d, writes=[dc], merge=True)
    ones = P.sb("mones", [128, 128], BF16)
    P.op("pool", lambda e: e.memset(ones[:], 1.0), writes=[dc], merge=True)
    hT = P.sb("mhT", [128, 5, 2048], F32)
    nT = P.sb("mnT", [128, 5, 2048], BF16)
    dhT = Dep()
    dnT = [Dep() for _ in range(5)]
    sq_r = Ring(P, "msq", 3, [128, 512], BF16)
    r_r = Ring(P, "mr", 2, [128, 512], F32)
    for (chunks, n) in (((0, 1, 2), 384.0), ((3, 4), 256.0)):
        for nt in range(4):
            ns = slice(nt * 512, (nt + 1) * 512)
            sbk, dsbk = C.bank(hold=True)
            for ci, c in enumerate(chunks):
                bk, dbk = C.bank()
                mm_acc(P, bk[:], [(wi[:, kc, c * 128:(c + 1) * 128], xs[:, kc, ns]) for kc in range(8)], reads=dxs + dwi, dwrite=dbk)
                P.op("act", lambda e, c=c, ns=ns, bk=bk: e.copy(out=hT[:, c, ns], in_=bk[:]), reads=[dbk], writes=[dhT], merge=True)
                sq, dsq = sq_r.next()
                P.op("act", lambda e, sq=sq, bk=bk: e.activation(out=sq[:], in_=bk[:], func=AF.Square), reads=[dbk], writes=[dsq])
                P.op("pe", lambda e, sbk=sbk, sq=sq, ci=ci, chunks=chunks: e.matmul(sbk[:], lhsT=ones[:], rhs=sq[:], start=(ci == 0), stop=(ci == len(chunks) - 1)),
                     reads=[dsq, dc], writes=[dsbk], merge=(ci > 0))
            r, dr = r_r.next()
            P.op("dve", lambda e, r=r, sbk=sbk, n=n: e.tensor_scalar(out=r[:], in0=sbk[:], scalar1=1.0 / n, scalar2=EPS, op0=ALU.mult, op1=ALU.add), reads=[dsbk], writes=[dr])
            C.release_all()
            P.op("act", lambda e, r=r: e.activation(out=r[:], in_=r[:], func=AF.Sqrt), reads=[dr], writes=[dr])
            P.op("dve", lambda e, r=r: e.reciprocal(out=r[:], in_=r[:]), reads=[dr], writes=[dr])
            for c in chunks:
                P.op("dve", lambda e, c=c, ns=ns, r=r: e.scalar_tensor_tensor(out=nT[:, c, ns], in0=hT[:, c, ns], scalar=gcol[:, c:c + 1], in1=r[:], op0=ALU.mult, op1=ALU.mult),
                     reads=[dhT, dr, dc], writes=[dnT[c]], merge=True)
    for c in range(5):
        P.dma("sp", nT_d[c * 128:(c + 1) * 128, :], nT[:, c, :], reads=[dnT[c]], writes=[C.dd("mla_nT")], merge=True)
    krT = P.sb("mkr", [96, 2048], BF16)
    dkr = Dep()
    ta_r = Ring(P, "mta", 2, [96, 512], F32)
    tb_r = Ring(P, "mtb", 2, [96, 512], F32)
    for nt in range(4):
        ns = slice(nt * 512, (nt + 1) * 512)
        bk, dbk = C.bank()
        bs, dbs = C.bank()
        mm_acc(P, bk[0:96, :], [(wi[:, kc, 576:672], xs[:, kc, ns]) for kc in range(8)], reads=dxs + dwi, dwrite=dbk)
        mm_acc(P, bs[0:96, :], [(wsw[:, kc, :], xs[:, kc, ns]) for kc in range(8)], reads=dxs + dwsw, dwrite=dbs)
        ta, dta = ta_r.next()
        tb, dtb = tb_r.next()
        P.op("dve", lambda e, ta=ta, bk=bk, ns=ns: e.tensor_tensor(out=ta[64:96, :], in0=bk[64:96, :], in1=cs[64:96, 0, ns], op=ALU.mult), reads=[dbk, dc], writes=[dta])
        P.op("dve", lambda e, tb=tb, bs=bs, ns=ns: e.tensor_tensor(out=tb[64:96, :], in0=bs[64:96, :], in1=cs[64:96, 1, ns], op=ALU.mult), reads=[dbs, dc], writes=[dtb])
        P.op("pool", lambda e, ta=ta, tb=tb, ns=ns: e.tensor_tensor(out=krT[64:96, ns], in0=ta[64:96, :], in1=tb[64:96, :], op=ALU.add), reads=[dta, dtb], writes=[dkr], merge=(nt > 0))
    P.dma("sp", kr_d, krT[64:96, :], reads=[dkr], writes=[C.dd("mla_krT")])


def stage_mla2(C):
    P = C.P
    P.begin_stage()
    nT_d = C.D("mla_nT", [640, 2048], BF16)
    kr_d = C.D("mla_krT", [32, 2048], BF16)
    cs_d = C.D("mla_cs2", [32, 2, 2048], F32)
    wq_d = C.D("mla_w_q_up", [384, 1536], F32)
    wqs_d = C.D("mla_w_q_sw", [384, 1536], F32)
    wk_d = C.D("mla_w_kv_k", [256, 1024], F32)
    wv_d = C.D("mla_w_kv_v", [256, 1024], F32)
    mixT = C.D("mixT", [1024, 2048], BF16)
    nT = P.sb("anT", [128, 5, 2048], BF16)
    dnT = [Dep() for _ in range(5)]
    for c in range(5):
        P.dma("sp", nT[:, c, :], nT_d[c * 128:(c + 1) * 128, :], reads=[C.dd("mla_nT")], writes=[dnT[c]])
    dq = dnT[0:3]
    dkv = dnT[3:5]
    dc = Dep()
    KRT = P.sb("aKRT", [96, 2048], BF16)
    P.dma("sp", KRT[64:96, :], kr_d, reads=[C.dd("mla_krT")], writes=[dc])
    cs = P.sb("acs", [96, 2, 2048], F32)
    P.dma("sp", cs[64:96, :, :], cs_d, writes=[dc], merge=True)
    wq, dwq = load_fm_bf16(C, "awq", wq_d, 3, 1536)
    wqs, dwqs = load_fm_bf16(C, "awqs", wqs_d, 3, 1536)
    wk, dwk = load_fm_bf16(C, "awk", wk_d, 2, 1024)
    wv, dwv = load_fm_bf16(C, "awv", wv_d, 2, 1024)
    onesf = P.sb("aones", [128, 64], F32)
    P.op("pool", lambda e: e.memset(onesf[:], 1.0), writes=[dc], merge=True)
    Vx = P.sb("aVx", [128, 16, 16, 65], BF16)
    dV = Dep()
    P.op("pool", lambda e: e.memset(Vx[:], 1.0), writes=[dV])
    for tt in range(16):
        for half in range(2):
            bk, dbk = C.bank()
            mm_acc(P, bk[:], [(nT[:, 3 + kc, tt * 128:(tt + 1) * 128], wv[:, kc, half * 512:(half + 1) * 512]) for kc in range(2)], reads=dkv + dwv, dwrite=dbk)
            P.op("act", lambda e, tt=tt, half=half, bk=bk: e.copy(out=Vx[:, tt, half * 8:(half + 1) * 8, 0:64], in_=bk[:].rearrange("p (h d) -> p h d", h=8)),
                 reads=[dbk], writes=[dV], merge=True)
    QT_r = Ring(P, "aQT", 2, [96, 2048], BF16)
    KT_r = Ring(P, "aKT", 2, [96, 2048], BF16)
    ta_r = Ring(P, "ata", 2, [96, 512], F32)
    tb_r = Ring(P, "atb", 2, [96, 512], F32)
    p_r = Ring(P, "apT", 4, [128, 512], BF16)
    rd_r = Ring(P, "ard", 2, [65, 512], F32)
    bs_r = Ring(P, "absb", 2, [64, 512], F32)
    yo_r = Ring(P, "ayo", 3, [64, 512], BF16)
    for h in range(16):
        QT, dQT = QT_r.next()
        KT, dKT = KT_r.next()
        for nt in range(4):
            ns = slice(nt * 512, (nt + 1) * 512)
            bq, dbq = C.bank()
            bs, dbs = C.bank()
            mm_acc(P, bq[0:96, :], [(wq[:, kc, h * 96:(h + 1) * 96], nT[:, kc, ns]) for kc in range(3)], reads=dq + dwq, dwrite=dbq)
            mm_acc(P, bs[0:96, :], [(wqs[:, kc, h * 96:(h + 1) * 96], nT[:, kc, ns]) for kc in range(3)], reads=dq + dwqs, dwrite=dbs)
            P.op("act", lambda e, QT=QT, ns=ns, bq=bq: e.copy(out=QT[0:64, ns], in_=bq[0:64, :]), reads=[dbq], writes=[dQT], merge=(nt > 0))
            ta, dta = ta_r.next()
            tb, dtb = tb_r.next()
            P.op("dve", lambda e, ta=ta, bq=bq, ns=ns: e.tensor_tensor(out=ta[64:96, :], in0=bq[64:96, :], in1=cs[64:96, 0, ns], op=ALU.mult), reads=[dbq, dc], writes=[dta])
            P.op("dve", lambda e, tb=tb, bs=bs, ns=ns: e.tensor_tensor(out=tb[64:96, :], in0=bs[64:96, :], in1=cs[64:96, 1, ns], op=ALU.mult), reads=[dbs, dc], writes=[dtb])
            P.op("pool", lambda e, QT=QT, ta=ta, tb=tb, ns=ns: e.tensor_tensor(out=QT[64:96, ns], in0=ta[64:96, :], in1=tb[64:96, :], op=ALU.add), reads=[dta, dtb], writes=[dQT], merge=True)
            bkk, dbkk = C.bank()
            mm_acc(P, bkk[0:64, :], [(wk[:, kc, h * 64:(h + 1) * 64], nT[:, 3 + kc, ns]) for kc in range(2)], reads=dkv + dwk, dwrite=dbkk)
            P.op("act", lambda e, KT=KT, ns=ns, bkk=bkk: e.copy(out=KT[0:64, ns], in_=bkk[0:64, :]), reads=[dbkk], writes=[dKT], merge=(nt > 0))
        P.op("pool", lambda e, KT=KT: e.tensor_copy(out=KT[64:96, :], in_=KRT[64:96, :]), reads=[dc], writes=[dKT], merge=True)
        for qc in range(4):
            qs = slice(qc * 512, (qc + 1) * 512)
            acc, dacc = C.bank(hold=True)

            def pv(kt, pT, dpT, acc=acc, dacc=dacc, h=h):
                P.op("pe", lambda e, acc=acc, kt=kt, h=h, pT=pT: e.matmul(acc[0:65, :], lhsT=Vx[:, kt, h, :], rhs=pT[:], start=(kt == 0), stop=(kt == 15)),
                     reads=[dV, dpT], writes=[dacc], merge=(kt > 0))
            pend = None
            for kt in range(16):
                sb_, dsb_ = C.bank()
                P.op("pe", lambda e, sb_=sb_, KT=KT, QT=QT, kt=kt, qs=qs: e.matmul(sb_[:], lhsT=KT[0:96, kt * 128:(kt + 1) * 128], rhs=QT[0:96, qs], start=True, stop=True),
                     reads=[dKT, dQT], writes=[dsb_])
                pT, dpT = p_r.next()
                P.op("act", lambda e, pT=pT, sb_=sb_: e.activation(out=pT[:], in_=sb_[:], func=AF.Exp, scale=MLA_SCALE), reads=[dsb_], writes=[dpT])
                if pend is not None:
                    pv(*pend)
                pend = (kt, pT, dpT)
            pv(*pend)
            rd, drd = rd_r.next()
            P.op("dve", lambda e, rd=rd, acc=acc: e.reciprocal(out=rd[64:65, :], in_=acc[64:65, :]), reads=[dacc], writes=[drd])
            bb, dbb = C.bank()
            P.op("pe", lambda e, bb=bb, rd=rd: e.matmul(bb[0:64, :], lhsT=onesf[64:65, 0:64], rhs=rd[64:65, :], start=True, stop=True), reads=[drd, dc], writes=[dbb])
            bsb, dbsb = bs_r.next()
            P.op("act", lambda e, bsb=bsb, bb=bb: e.copy(out=bsb[:], in_=bb[0:64, :]), reads=[dbb], writes=[dbsb])
            yo, dyo = yo_r.next()
            P.op("dve", lambda e, yo=yo, acc=acc, bsb=bsb: e.tensor_tensor(out=yo[:], in0=acc[0:64, :], in1=bsb[:], op=ALU.mult), reads=[dacc, dbsb], writes=[dyo])
            C.release_all()
            P.dma("sp", mixT[h * 64:(h + 1) * 64, qs], yo[:], reads=[dyo], writes=[C.dd("mixT")], merge=True)


def _rep128(v):
    v = np.asarray(v, np.float32)
    return np.ascontiguousarray(np.broadcast_to(v[None, :], (128, v.shape[0])))


def shared_inputs(inp):
    f32 = lambda a: np.ascontiguousarray(np.asarray(a, np.float32))
    s = {}
    s["ab_w_in"] = f32(inp["ab_w_in"][0])
    s["na_tab"] = na_tables(np.asarray(inp["na_rpb"][0], np.float32))
    s.update(hyena_consts())
    s["hy_f_w1"] = f32(inp["hy_f_w1"][0])
    s["hy_f_w2"] = f32(inp["hy_f_w2"][0])
    s["hy_f_w3"] = f32(inp["hy_f_w3"][0])
    s["hy_cols"] = f32(np.stack([inp["hy_f_b1"][0], inp["hy_f_freq"][0], inp["hy_f_b2"][0]], axis=1))
    s["hy_skipb"] = np.stack([_rep128(inp["hy_skip"][0][0]), _rep128(inp["hy_skip"][0][1])])
    s["hy_cw"] = f32(np.asarray(inp["hy_conv_w"][0]).reshape(3, 12, 128).transpose(2, 1, 0))
    s["hy_cb"] = f32(np.asarray(inp["hy_conv_b"][0]).reshape(12, 128).T)
    s["ident"] = np.eye(128, dtype=np.float32)
    esel = np.zeros((16, 16, 128), np.float32)
    for e in range(16):
        esel[e, e, :] = 1.0
    s["esel"] = esel
    s["iota_col"] = (np.arange(16)[None, :] * 128 + np.arange(128)[:, None]).astype(np.float32)
    s["iota_row"] = _rep128(np.arange(2048, dtype=np.float32))
    s["ab_w_out"] = f32(inp["ab_w_out"][0])
    w_in = np.asarray(inp["mla_w_in"][0], np.float32)
    perm = np.concatenate([np.arange(16, 32), np.arange(0, 16)])
    s["mla_w_in"] = f32(w_in)
    s["mla_w_in_sw"] = f32(np.concatenate([w_in[:, 576:640], w_in[:, 640 + perm]], axis=1))
    wq = np.asarray(inp["mla_w_q_up"][0], np.float32)
    wqs = wq.reshape(384, 16, 96).copy()
    wqs[:, :, 64:] = wqs[:, :, 64 + perm]
    s["mla_w_q_up"] = f32(wq)
    s["mla_w_q_sw"] = f32(wqs.reshape(384, 1536))
    wkv = np.asarray(inp["mla_w_kv_up"][0], np.float32).reshape(256, 16, 128)
    s["mla_w_kv_k"] = f32(wkv[:, :, :64].reshape(256, 1024))
    s["mla_w_kv_v"] = f32(wkv[:, :, 64:].reshape(256, 1024))
    s["mla_gcols"] = f32(np.concatenate([np.asarray(inp["mla_q_norm"][0]).reshape(3, 128).T,
                                         np.asarray(inp["mla_kv_norm"][0]).reshape(2, 128).T], axis=1))
    s.update(mla_consts())
    s["mla_w_out"] = f32(inp["mla_w_out"][0])
    for li in range(2):
        s[f"ln1_g{li}"] = _rep128(inp["ln1_g"][li])
        s[f"ln1_b{li}"] = _rep128(inp["ln1_b"][li])
        s[f"ln2_g{li}"] = _rep128(inp["ln2_g"][li])
        s[f"ln2_b{li}"] = _rep128(inp["ln2_b"][li])
        s[f"moe_router{li}"] = f32(inp["moe_router"][li])
        s[f"moe_w_gate{li}"] = f32(inp["moe_w_gate"][li])
        s[f"moe_w_up{li}"] = f32(inp["moe_w_up"][li])
        s[f"moe_w_down{li}"] = f32(inp["moe_w_down"][li])
        s[f"ple_gate{li}"] = f32(inp["ple_gate"][li])
        s[f"ple_proj{li}"] = f32(inp["ple_proj"][li])
    return s


PER_CORE = ("x_tm", "xT", "pT0", "pT1")


def build_full(shared_names):
    C = Ctx(ext_in=set(shared_names) | set(PER_CORE), ext_out={"out"})
    stage_a1(C)
    stage_a2(C)
    stage_hy_filter(C)
    stage_hy_prep(C)
    stage_hy_conv(C)
    stage_proj_ln(C, "ab_w_out", "x_tm", "ln1", 0, "x1_0")
    stage_moe1(C, 0, "x1_0")
    stage_moe2(C, 0, "x1_0", "x2_0")
    stage_ple(C, 0, "x2_0", "x3_0", True)
    stage_mla1(C, "x3_0T")
    stage_mla2(C)
    stage_proj_ln(C, "mla_w_out", "x3_0", "ln1", 1, "x1_1")
    stage_moe1(C, 1, "x1_1")
    stage_moe2(C, 1, "x1_1", "x2_1")
    stage_ple(C, 1, "x2_1", "out", False)
    C.P.finish()
    return C


def kernel(**inputs):
    inp = {k: np.asarray(v) for k, v in inputs.items()}
    shared = shared_inputs(inp)
    x = np.asarray(inp["x"], np.float32)
    p = np.asarray(inp["p"], np.float32)
    C = build_full(shared.keys())
    used = set(C.dram.keys())
    in_maps = []
    for b in range(8):
        m = {k: v for k, v in shared.items() if k in used}
        m["x_tm"] = np.ascontiguousarray(x[b])
        m["xT"] = np.ascontiguousarray(x[b].T)
        m["pT0"] = np.ascontiguousarray(p[0, b].T)
        m["pT1"] = np.ascontiguousarray(p[1, b].T)
        in_maps.append(m)
    res = run_bass_kernel_spmd(C.nc, in_maps, core_ids=list(range(8)))
    return np.stack([np.asarray(r["out"], np.float32) for r in res.results], axis=0)
```

```python
from contextlib import ExitStack
import math
import numpy as np
import ml_dtypes
import concourse.bass as bass
import concourse.mybir as mybir
from concourse.bass_utils import run_bass_kernel_spmd

F32 = mybir.dt.float32
BF16 = mybir.dt.bfloat16
I32 = mybir.dt.int32
U32 = mybir.dt.uint32
AF = mybir.ActivationFunctionType
ALU = mybir.AluOpType
AX = mybir.AxisListType
NPBF = ml_dtypes.bfloat16

D_MODEL = 1024
SEQ = 2048
NT = SEQ // 128
ALPHA = 4.0 ** 0.25
EPS = 1e-5

ENGS = ("pe", "act", "dve", "pool", "sp")
N_DMA_SEMS = 16


class Dep:
    __slots__ = ("w", "r", "name")

    def __init__(self, name=""):
        self.w = {}
        self.r = {}
        self.name = name


class Prog:
    def __init__(self, nc, strict=True):
        self.nc = nc
        self.es = ExitStack()
        self.q = {e: [] for e in ENGS}
        self.cnt = {e: 0 for e in ENGS}
        self.seen = {e: {} for e in ENGS}
        self.sem = {}
        self.strict = strict
        for e in ENGS:
            self.sem[e] = self.es.enter_context(nc.semaphore("s_" + e))
        self.dma_sems = {}
        self.dma_tot = {}
        self.dma_rr = {}
        for e in ("sp", "pool", "act"):
            self.dma_sems[e] = [self.es.enter_context(nc.semaphore(f"d_{e}{i}")) for i in range(N_DMA_SEMS)]
            self.dma_tot[e] = [0] * N_DMA_SEMS
            self.dma_rr[e] = 0
        self.all_events = {}
        self.n_ops = 0
        self.stage_es = None
        self.uid = 0

    def begin_stage(self):
        self.barrier()
        if self.stage_es is not None:
            self.stage_es.close()
        self.stage_es = ExitStack()

    def sb(self, name, shape, dt):
        self.uid += 1
        t = self.stage_es.enter_context(self.nc.sbuf_tensor(f"{name}_{self.uid}", list(shape), dt))
        return t

    def ps(self, name, shape, dt=F32):
        return self.es.enter_context(self.nc.psum_tensor(name, list(shape), dt))

    def _semobj(self, key):
        if isinstance(key, str):
            return self.sem[key]
        e, i = key
        return self.dma_sems[e][i]

    def _need(self, eng, reads, writes, merge):
        need = {}

        def add(k, v):
            if k == eng and not self.strict:
                return
            if self.seen[eng].get(k, 0) >= v:
                return
            if need.get(k, 0) < v:
                need[k] = v
        for d in reads:
            for k, v in d.w.items():
                add(k, v)
        for d in writes:
            if not merge:
                for k, v in d.w.items():
                    add(k, v)
            for k, v in d.r.items():
                add(k, v)
        for k, v in need.items():
            self.seen[eng][k] = v
        return list(need.items())

    def _commit(self, ev, reads, writes, merge):
        k, v = ev
        for d in reads:
            if d.r.get(k, 0) < v:
                d.r[k] = v
        for d in writes:
            if merge:
                d.w[k] = v
            else:
                d.w = {k: v}
                d.r = {}
        self.all_events[k] = v

    def op(self, eng, fn, reads=(), writes=(), merge=False):
        waits = self._need(eng, reads, writes, merge)
        self.cnt[eng] += 1
        ev = (eng, self.cnt[eng])
        sem = self.sem[eng]
        waitobjs = [(self._semobj(k), v) for k, v in waits]

        def emit(e, fn=fn, waitobjs=waitobjs, sem=sem):
            for s, v in waitobjs:
                e.wait_ge(s, v)
            fn(e).then_inc(sem, 1)
        self.q[eng].append(emit)
        self._commit(ev, reads, writes, merge)
        self.n_ops += 1
        return ev

    def dma(self, eng, out, in_, reads=(), writes=(), merge=False, **kw):
        i = self.dma_rr[eng]
        self.dma_rr[eng] = (i + 1) % N_DMA_SEMS
        key = (eng, i)
        prev = self.dma_tot[eng][i]
        waits = self._need(eng, reads, writes, merge)
        if prev > 0 and self.seen[eng].get(key, 0) < prev:
            waits.append((key, prev))
            self.seen[eng][key] = prev
        self.dma_tot[eng][i] = prev + 16
        ev = (key, prev + 16)
        sem = self.dma_sems[eng][i]
        waitobjs = [(self._semobj(k), v) for k, v in waits]

        def emit(e, waitobjs=waitobjs, sem=sem, out=out, in_=in_, kw=kw):
            for s, v in waitobjs:
                e.wait_ge(s, v)
            e.dma_start(out=out, in_=in_, **kw).then_inc(sem, 16)
        self.q[eng].append(emit)
        self._commit(ev, reads, writes, merge)
        self.n_ops += 1
        return ev

    def coll(self, kind, out, in_, reads=(), writes=()):
        eng = "pool"
        i = self.dma_rr[eng]
        self.dma_rr[eng] = (i + 1) % N_DMA_SEMS
        key = (eng, i)
        prev = self.dma_tot[eng][i]
        waits = self._need(eng, reads, writes, False)
        if prev > 0 and self.seen[eng].get(key, 0) < prev:
            waits.append((key, prev))
            self.seen[eng][key] = prev
        self.dma_tot[eng][i] = prev + 16
        ev = (key, prev + 16)
        sem = self.dma_sems[eng][i]
        waitobjs = [(self._semobj(k), v) for k, v in waits]

        def emit(e, waitobjs=waitobjs, sem=sem, out=out, in_=in_, kind=kind):
            for s_, v in waitobjs:
                e.wait_ge(s_, v)
            e.collective_compute(kind, ALU.bypass, replica_groups=[list(range(8))], ins=[in_], outs=[out]).then_inc(sem, 16)
        self.q[eng].append(emit)
        self._commit(ev, reads, writes, False)
        self.n_ops += 1
        return ev

    def barrier(self):
        snap = dict(self.all_events)
        for eng in ENGS:
            waits = []
            for k, v in snap.items():
                if k == eng:
                    continue
                if self.seen[eng].get(k, 0) >= v:
                    continue
                waits.append((self._semobj(k), v))
                self.seen[eng][k] = v
            if waits:
                def emit(e, waits=waits):
                    for s, v in waits:
                        e.wait_ge(s, v)
                self.q[eng].append(emit)

    def finish(self):
        self.barrier()
        nc = self.nc
        q = self.q
        with nc.Block() as block:
            @block.tensor
            def _(e):
                for f in q["pe"]:
                    f(e)

            @block.scalar
            def _(e):
                for f in q["act"]:
                    f(e)

            @block.vector
            def _(e):
                for f in q["dve"]:
                    f(e)

            @block.gpsimd
            def _(e):
                for f in q["pool"]:
                    f(e)

            @block.sync
            def _(e):
                for f in q["sp"]:
                    f(e)
        if self.stage_es is not None:
            self.stage_es.close()
        self.es.close()


class Ring:
    def __init__(self, P, name, n, shape, dt):
        self.bufs = [(P.sb(f"{name}{i}", shape, dt), Dep(f"{name}{i}")) for i in range(n)]
        self.i = 0

    def next(self):
        b = self.bufs[self.i]
        self.i = (self.i + 1) % len(self.bufs)
        return b


class Ctx:
    def __init__(self, ext_in, ext_out):
        self.nc = bass.Bass("TRN2", target_bir_lowering=False)
        self.P = Prog(self.nc)
        self.ext_in = set(ext_in)
        self.ext_out = set(ext_out)
        self.dram = {}
        self.ddep = {}
        P = self.P
        self.banks = [(P.ps(f"bank{i}", [128, 512], F32), Dep(f"bank{i}")) for i in range(8)]
        self.bank_i = 0
        self.held = set()

    def D(self, name, shape=None, dt=F32):
        if name in self.dram:
            return self.dram[name]
        kind = "Internal"
        if name in self.ext_in:
            kind = "ExternalInput"
        elif name in self.ext_out:
            kind = "ExternalOutput"
        t = self.nc.dram_tensor(name, list(shape), dt, kind=kind).ap()
        self.dram[name] = t
        self.ddep[name] = Dep(name)
        return t

    def dd(self, name):
        return self.ddep[name]

    def bank(self, hold=False):
        for _ in range(8):
            i = self.bank_i
            self.bank_i = (self.bank_i + 1) % 8
            if i not in self.held:
                if hold:
                    self.held.add(i)
                return self.banks[i]
        raise RuntimeError("no free PSUM bank")

    def release_all(self):
        self.held = set()

    def release(self, b):
        for i, bb in enumerate(self.banks):
            if bb[0] is b[0]:
                self.held.discard(i)


def mm_acc(P, out, pairs, reads, dwrite):
    n = len(pairs)
    for i, (l, r) in enumerate(pairs):
        P.op("pe", lambda e, l=l, r=r, i=i: e.matmul(out, lhsT=l, rhs=r, start=(i == 0), stop=(i == n - 1)),
             reads=reads, writes=[dwrite], merge=(i > 0))


def load_fm_bf16(C, name, src, kc_n, width, eng="pool"):
    P = C.P
    t = P.sb(name, [128, kc_n, width], BF16)
    deps = [Dep(f"{name}{k}") for k in range(kc_n)]
    for k in range(kc_n):
        P.dma(eng, t[:, k, :], src[k * 128:(k + 1) * 128, :], writes=[deps[k]])
    return t, deps


def stage_a1(C):
    P = C.P
    P.begin_stage()
    xT = C.D("xT", [1024, 2048], F32)
    w_in = C.D("ab_w_in", [1024, 3072], F32)
    qkT = C.D("qkT", [1024, 2048], BF16)
    v_tm = C.D("v_tm", [2048, 512], BF16)
    hbT = C.D("hbT", [1536, 2048], F32)
    xs, dxs = load_fm_bf16(C, "xTb", xT, 8, 2048)
    ws, dws = load_fm_bf16(C, "winb", w_in, 8, 3072)
    st_b = Ring(P, "a1sb", 3, [128, 2048], BF16)
    st_f = Ring(P, "a1sf", 3, [128, 2048], F32)
    ev_i = 0
    for mc in list(range(8)) + list(range(12, 24)):
        is_hb = mc >= 12
        stg, dstg = (st_f if is_hb else st_b).next()
        for nt in range(4):
            bk, dbk = C.bank()
            mm_acc(P, bk[:], [(ws[:, kc, mc * 128:(mc + 1) * 128], xs[:, kc, nt * 512:(nt + 1) * 512]) for kc in range(8)],
                   reads=dxs + dws, dwrite=dbk)
            eng = "act" if ev_i % 2 == 0 else "dve"
            ev_i += 1
            o = stg[:, nt * 512:(nt + 1) * 512]
            if eng == "act":
                P.op("act", lambda e, o=o, bk=bk: e.copy(out=o, in_=bk[:]), reads=[dbk], writes=[dstg], merge=(nt > 0))
            else:
                P.op("dve", lambda e, o=o, bk=bk: e.tensor_copy(out=o, in_=bk[:]), reads=[dbk], writes=[dstg], merge=(nt > 0))
        if is_hb:
            P.dma("sp", hbT[(mc - 12) * 128:(mc - 11) * 128, :], stg[:], reads=[dstg], writes=[C.dd("hbT")], merge=True)
        else:
            P.dma("sp", qkT[mc * 128:(mc + 1) * 128, :], stg[:], reads=[dstg], writes=[C.dd("qkT")], merge=True)
    st_v = Ring(P, "a1sv", 3, [128, 512], BF16)
    for tt in range(NT):
        bk, dbk = C.bank()
        mm_acc(P, bk[:], [(xs[:, kc, tt * 128:(tt + 1) * 128], ws[:, kc, 1024:1536]) for kc in range(8)],
               reads=dxs + dws, dwrite=dbk)
        stg, dstg = st_v.next()
        if tt % 2 == 0:
            P.op("act", lambda e, stg=stg, bk=bk: e.copy(out=stg[:], in_=bk[:]), reads=[dbk], writes=[dstg])
        else:
            P.op("dve", lambda e, stg=stg, bk=bk: e.tensor_copy(out=stg[:], in_=bk[:]), reads=[dbk], writes=[dstg])
        P.dma("sp", v_tm[tt * 128:(tt + 1) * 128, :], stg[:], reads=[dstg], writes=[C.dd("v_tm")], merge=True)


def na_plan():
    rows, wr = 32, 8
    r0 = np.clip(np.arange(rows) - wr // 2, 0, rows - wr)
    plan = []
    keys = {}
    for i in range(16):
        lo = r0[2 * i] // 2
        hi = (r0[2 * i + 1] + 7) // 2
        lst = []
        for j in range(lo, hi + 1):
            val = []
            for ak in range(2):
                for aq in range(2):
                    r = 2 * i + aq
                    kr = 2 * j + ak
                    val.append(bool(r0[r] <= kr <= r0[r] + 7))
            key = (j - i, tuple(val))
            if key not in keys:
                keys[key] = len(keys)
            lst.append((j, keys[key]))
        plan.append(lst)
    return plan, keys


def na_tables(rpb):
    plan, keys = na_plan()
    c = np.arange(64)
    c0 = np.clip(c - 8, 0, 48)
    col_ok = (c[None, :] >= c0[:, None]) & (c[None, :] < c0[:, None] + 16)
    dc_idx = np.clip(c[None, :] - c[:, None], -15, 15) + 15
    tab = np.full((len(keys), 2, 64, 8, 2, 64), -1e30, np.float32)
    for (delta, val), tid in keys.items():
        vi = 0
        for ak in range(2):
            for aq in range(2):
                ok = val[vi]
                vi += 1
                if not ok:
                    continue
                dr = 2 * delta + ak - aq
                b = rpb[:, dr + 7, :][:, dc_idx]
                b = np.where(col_ok[None], b, np.float32(-1e30))
                tab[tid, ak, :, :, aq, :] = b.transpose(2, 0, 1)
    return tab.reshape(len(keys), 128, 8, 128)


def stage_a2(C):
    P = C.P
    P.begin_stage()
    plan, keys = na_plan()
    ntab = len(keys)
    qkT = C.D("qkT", [1024, 2048], BF16)
    v_tm = C.D("v_tm", [2048, 512], BF16)
    tab_d = C.D("na_tab", [ntab, 128, 8, 128], F32)
    mixT = C.D("mixT", [1024, 2048], BF16)
    QT = P.sb("QT", [128, 4, 2048], BF16)
    KT = P.sb("KT", [128, 4, 2048], BF16)
    dQ = [Dep() for _ in range(4)]
    dK = [Dep() for _ in range(4)]
    for hp in range(4):
        P.dma("sp", QT[:, hp, :], qkT[hp * 128:(hp + 1) * 128, :], reads=[C.dd("qkT")], writes=[dQ[hp]])
        P.dma("sp", KT[:, hp, :], qkT[512 + hp * 128:512 + (hp + 1) * 128, :], reads=[C.dd("qkT")], writes=[dK[hp]])
    tab = P.sb("natab", [128, ntab, 8, 128], F32)
    dtab = Dep()
    for t in range(ntab):
        P.dma("sp", tab[:, t, :, :], tab_d[t], writes=[dtab], merge=True)
    Vx = P.sb("Vx", [128, 8, NT, 128], BF16)
    dV = Dep()
    P.op("pool", lambda e: e.memset(Vx[:], 0.0), writes=[dV])
    vv = v_tm.rearrange("(t p) c -> p t c", p=128)
    for h in range(8):
        a = h % 2
        P.dma("sp", Vx[:, h, :, a * 64:(a + 1) * 64], vv[:, :, h * 64:(h + 1) * 64], reads=[C.dd("v_tm")], writes=[dV], merge=(h > 0))
    ones2 = P.sb("ones2", [128, 2, 128], BF16)
    dones = Dep()
    P.op("pool", lambda e: e.memset(ones2[:], 0.0), writes=[dones])
    P.op("pool", lambda e: e.memset(ones2[:, 0, 0:64], 1.0), writes=[dones])
    P.op("pool", lambda e: e.memset(ones2[:, 1, 64:128], 1.0), writes=[dones])
    yaT = P.sb("yaT", [128, 4, 2048], BF16)
    dya = [Dep() for _ in range(4)]
    s_ring = Ring(P, "na_s", 3, [128, 640], F32)
    p_ring = Ring(P, "na_p", 4, [128, 640], BF16)
    rd_ring = Ring(P, "na_rd", 2, [128, 128], F32)
    units = [(i, hp, a) for i in range(16) for hp in range(4) for a in range(2)]
    pair = {}

    def phase1(u):
        i, hp, a = u
        q0 = i * 128
        h = hp * 2 + a
        pa = slice(a * 64, (a + 1) * 64)
        lst = plan[i]
        nkb = len(lst)
        bA, dbA = C.bank()
        bB, dbB = (C.bank() if nkb > 4 else (None, None))
        ssb, dss = s_ring.next()
        for jj, (j, tid) in enumerate(lst):
            bk, dbk = (bA, dbA) if jj < 4 else (bB, dbB)
            o = bk[:, (jj % 4) * 128:(jj % 4 + 1) * 128]
            P.op("pe", lambda e, o=o, j=j, pa=pa, hp=hp, q0=q0: e.matmul(
                o, lhsT=KT[pa, hp, j * 128:(j + 1) * 128], rhs=QT[pa, hp, q0:q0 + 128], start=True, stop=True),
                reads=[dK[hp], dQ[hp]], writes=[dbk], merge=(jj % 4 > 0))
        for jj, (j, tid) in enumerate(lst):
            bk, dbk = (bA, dbA) if jj < 4 else (bB, dbB)
            o = bk[:, (jj % 4) * 128:(jj % 4 + 1) * 128]
            P.op("dve", lambda e, o=o, jj=jj, tid=tid, h=h, ssb=ssb: e.scalar_tensor_tensor(
                out=ssb[:, jj * 128:(jj + 1) * 128], in0=o, scalar=0.125, in1=tab[:, tid, h, :],
                op0=ALU.mult, op1=ALU.add), reads=[dbk, dtab], writes=[dss], merge=(jj > 0))
        pT, dpT = p_ring.next()
        P.op("act", lambda e, pT=pT, ssb=ssb, nkb=nkb: e.activation(
            out=pT[:, 0:nkb * 128], in_=ssb[:, 0:nkb * 128], func=AF.Exp), reads=[dss], writes=[dpT])
        return (pT, dpT)

    def phase2(u, pp):
        i, hp, a = u
        q0 = i * 128
        h = hp * 2 + a
        pT, dpT = pp
        lst = plan[i]
        nkb = len(lst)
        if a == 0:
            pair[(i, hp)] = (C.bank(hold=True), C.bank(hold=True))
        (bo, dbo), (bd, dbd) = pair[(i, hp)]
        for jj, (j, tid) in enumerate(lst):
            first = (a == 0 and jj == 0)
            last = (a == 1 and jj == nkb - 1)
            P.op("pe", lambda e, bo=bo, h=h, j=j, pT=pT, jj=jj, first=first, last=last: e.matmul(
                bo[:, 0:128], lhsT=Vx[:, h, j, :], rhs=pT[:, jj * 128:(jj + 1) * 128], start=first, stop=last),
                reads=[dV, dpT], writes=[dbo], merge=(not first))
            P.op("pe", lambda e, bd=bd, a=a, pT=pT, jj=jj, first=first, last=last: e.matmul(
                bd[:, 0:128], lhsT=ones2[:, a, :], rhs=pT[:, jj * 128:(jj + 1) * 128], start=first, stop=last),
                reads=[dones, dpT], writes=[dbd], merge=(not first))
        if a == 1:
            rd, drd = rd_ring.next()
            P.op("dve", lambda e, rd=rd, bd=bd: e.reciprocal(out=rd[:], in_=bd[:, 0:128]), reads=[dbd], writes=[drd])
            P.op("dve", lambda e, rd=rd, bo=bo, hp=hp, q0=q0: e.tensor_tensor(
                out=yaT[:, hp, q0:q0 + 128], in0=bo[:, 0:128], in1=rd[:], op=ALU.mult),
                reads=[dbo, drd], writes=[dya[hp]], merge=True)
            C.release(pair[(i, hp)][0])
            C.release(pair[(i, hp)][1])
            del pair[(i, hp)]

    pend = None
    for u in units:
        pp = phase1(u)
        if pend is not None:
            phase2(*pend)
        pend = (u, pp)
    phase2(*pend)
    for hp in range(4):
        P.dma("sp", mixT[hp * 128:(hp + 1) * 128, :], yaT[:, hp, :], reads=[dya[hp]], writes=[C.dd("mixT")], merge=True)


class LNBufs:
    def __init__(self, P, name):
        self.stats = Ring(P, name + "st", 2, [128, 2, 6], F32)
        self.mv = Ring(P, name + "mv", 2, [128, 2], F32)
        self.rstd = Ring(P, name + "rs", 2, [128, 1], F32)
        self.nmr = Ring(P, name + "nm", 2, [128, 1], F32)


def layer_norm_tile(P, lb, r, dr, gb, bb, dgb, y, dy):
    st, dst = lb.stats.next()
    mv, dmv = lb.mv.next()
    rs, drs = lb.rstd.next()
    nm, dnm = lb.nmr.next()
    P.op("dve", lambda e: e.bn_stats(out=st[:, 0, :], in_=r[:, 0:512]), reads=[dr], writes=[dst])
    P.op("dve", lambda e: e.bn_stats(out=st[:, 1, :], in_=r[:, 512:1024]), reads=[dr], writes=[dst], merge=True)
    P.op("dve", lambda e: e.bn_aggr(out=mv[:], in_=st[:]), reads=[dst], writes=[dmv])
    P.op("dve", lambda e: e.tensor_scalar(out=rs[:], in0=mv[:, 1:2], scalar1=EPS, scalar2=None, op0=ALU.add), reads=[dmv], writes=[drs])
    P.op("act", lambda e: e.activation(out=rs[:], in_=rs[:], func=AF.Sqrt), reads=[drs], writes=[drs])
    P.op("dve", lambda e: e.reciprocal(out=rs[:], in_=rs[:]), reads=[drs], writes=[drs])
    P.op("dve", lambda e: e.scalar_tensor_tensor(out=nm[:], in0=mv[:, 0:1], scalar=-1.0, in1=rs[:], op0=ALU.mult, op1=ALU.mult),
         reads=[dmv, drs], writes=[dnm])
    P.op("act", lambda e: e.activation(out=y[:], in_=r[:], func=AF.Identity, bias=nm[:], scale=rs[:]),
         reads=[dr, drs, dnm], writes=[dy])
    P.op("pool", lambda e: e.tensor_tensor(out=y[:], in0=y[:], in1=gb[:], op=ALU.mult), reads=[dy, dgb], writes=[dy])
    P.op("pool", lambda e: e.tensor_tensor(out=y[:], in0=y[:], in1=bb[:], op=ALU.add), reads=[dy, dgb], writes=[dy])


def stage_proj_ln(C, w_name, x_name, lnname, li, out_name):
    P = C.P
    P.begin_stage()
    mixT = C.D("mixT", [1024, 2048], BF16)
    w = C.D(w_name, [1024, 1024], F32)
    x = C.D(x_name, [2048, 1024], F32)
    g_d = C.D(f"{lnname}_g{li}", [128, 1024], F32)
    b_d = C.D(f"{lnname}_b{li}", [128, 1024], F32)
    out = C.D(out_name, [2048, 1024], F32)
    ms = P.sb("ms", [128, 8, 2048], BF16)
    dms = [Dep() for _ in range(8)]
    for kc in range(8):
        P.dma("sp", ms[:, kc, :], mixT[kc * 128:(kc + 1) * 128, :], reads=[C.dd("mixT")], writes=[dms[kc]])
    ws, dws = load_fm_bf16(C, "wout", w, 8, 1024)
    gb = P.sb("gb", [128, 1024], F32)
    bb = P.sb("bb", [128, 1024], F32)
    dgb = Dep()
    P.dma("sp", gb[:], g_d, writes=[dgb])
    P.dma("sp", bb[:], b_d, writes=[dgb], merge=True)
    lb = LNBufs(P, "ln")
    xr = Ring(P, "xr", 3, [128, 1024], F32)
    rr = Ring(P, "rr", 2, [128, 1024], F32)
    yr = Ring(P, "yr", 2, [128, 1024], F32)
    for tt in range(NT):
        xt, dxt = xr.next()
        P.dma("sp", xt[:], x[tt * 128:(tt + 1) * 128, :], reads=[C.dd(x_name)], writes=[dxt])
        r, dr = rr.next()
        for half in range(2):
            bk, dbk = C.bank()
            hs = slice(half * 512, (half + 1) * 512)
            mm_acc(P, bk[:], [(ms[:, kc, tt * 128:(tt + 1) * 128], ws[:, kc, hs]) for kc in range(8)], reads=dms + dws, dwrite=dbk)
            P.op("dve", lambda e, r=r, xt=xt, bk=bk, hs=hs: e.scalar_tensor_tensor(
                out=r[:, hs], in0=xt[:, hs], scalar=ALPHA, in1=bk[:], op0=ALU.mult, op1=ALU.add),
                reads=[dxt, dbk], writes=[dr], merge=(half > 0))
        y, dy = yr.next()
        layer_norm_tile(P, lb, r, dr, gb, bb, dgb, y, dy)
        P.dma("sp", out[tt * 128:(tt + 1) * 128, :], y[:], reads=[dy], writes=[C.dd(out_name)], merge=True)


def stage_moe1(C, li, x_name):
    P = C.P
    P.begin_stage()
    x1 = C.D(x_name, [2048, 1024], F32)
    wr_d = C.D(f"moe_router{li}", [1024, 16], F32)
    ident_d = C.D("ident", [128, 128], F32)
    esel_d = C.D("esel", [16, 16, 128], F32)
    iotac_d = C.D("iota_col", [128, 16], F32)
    xe_all = C.D("xeT_all", [16, 128, 8, 256], BF16)
    idxc_d = C.D("moe_idxc", [128, 2, 16], F32)
    gc_d = C.D("moe_gc", [128, 2, 16], F32)
    wr = P.sb("wr", [128, 8, 16], F32)
    dcst = Dep()
    P.dma("sp", wr[:], wr_d.rearrange("(kc p) e -> p kc e", p=128), writes=[dcst])
    ident = P.sb("ident", [128, 128], F32)
    P.dma("sp", ident[:], ident_d, writes=[dcst], merge=True)
    esel = P.sb("esel", [16, 16, 128], F32)
    P.dma("sp", esel[:], esel_d, writes=[dcst], merge=True)
    iotac = P.sb("iotac", [128, 16], F32)
    P.dma("sp", iotac[:], iotac_d, writes=[dcst], merge=True)
    x1b = P.sb("x1b", [128, NT, 1024], BF16)
    dx1b = [Dep() for _ in range(NT)]
    affT = P.sb("affT", [16, 2048], F32)
    daffT = Dep()
    xr = Ring(P, "m1x", 2, [128, 1024], F32)
    xTr = Ring(P, "m1xT", 2, [128, 8, 128], F32)
    sm_r = Ring(P, "m1sm", 2, [128, 4], F32)
    ex_r = Ring(P, "m1ex", 2, [128, 16], F32)
    af_r = Ring(P, "m1af", 2, [128, 16], F32)
    for tt in range(NT):
        xt, dxt = xr.next()
        P.dma("sp", xt[:], x1[tt * 128:(tt + 1) * 128, :], reads=[C.dd(x_name)], writes=[dxt])
        P.op("act", lambda e, xt=xt, tt=tt: e.copy(out=x1b[:, tt, :], in_=xt[:]), reads=[dxt], writes=[dx1b[tt]])
        xT, dxT = xTr.next()
        for hb in range(2):
            bk, dbk = C.bank()
            for k4 in range(4):
                kc = hb * 4 + k4
                P.op("pe", lambda e, bk=bk, k4=k4, kc=kc, xt=xt: e.transpose(bk[:, k4 * 128:(k4 + 1) * 128], xt[:, kc * 128:(kc + 1) * 128], ident[:]),
                     reads=[dxt, dcst], writes=[dbk], merge=(k4 > 0))
            P.op("dve", lambda e, xT=xT, hb=hb, bk=bk: e.tensor_copy(out=xT[:, hb * 4:(hb + 1) * 4, :], in_=bk[:].rearrange("p (k t) -> p k t", k=4)),
                 reads=[dbk], writes=[dxT], merge=(hb > 0))
        bk, dbk = C.bank()
        mm_acc(P, bk[:, 0:16], [(xT[:, kc, :], wr[:, kc, :]) for kc in range(8)], reads=[dxT, dcst], dwrite=dbk)
        sm, dsm = sm_r.next()
        ex, dex = ex_r.next()
        af, daf = af_r.next()
        P.op("dve", lambda e, sm=sm, bk=bk: e.reduce_max(out=sm[:, 0:1], in_=bk[:, 0:16], axis=AX.X), reads=[dbk], writes=[dsm])
        P.op("dve", lambda e, sm=sm: e.tensor_scalar(out=sm[:, 1:2], in0=sm[:, 0:1], scalar1=-1.0, scalar2=None, op0=ALU.mult),
             reads=[dsm], writes=[dsm])
        P.op("act", lambda e, ex=ex, bk=bk, sm=sm: e.activation(out=ex[:], in_=bk[:, 0:16], func=AF.Exp, bias=sm[:, 1:2], accum_out=sm[:, 2:3]),
             reads=[dbk, dsm], writes=[dex, dsm])
        P.op("dve", lambda e, sm=sm: e.reciprocal(out=sm[:, 3:4], in_=sm[:, 2:3]), reads=[dsm], writes=[dsm])
        P.op("dve", lambda e, af=af, ex=ex, sm=sm: e.tensor_scalar(out=af[:], in0=ex[:], scalar1=sm[:, 3:4], scalar2=None, op0=ALU.mult),
             reads=[dex, dsm], writes=[daf])
        bk2, dbk2 = C.bank()
        P.op("pe", lambda e, bk2=bk2, af=af: e.transpose(bk2[0:16, 0:128], af[:], ident[:]), reads=[daf, dcst], writes=[dbk2])
        P.op("act", lambda e, bk2=bk2, tt=tt: e.copy(out=affT[:, tt * 128:(tt + 1) * 128], in_=bk2[0:16, 0:128]),
             reads=[dbk2], writes=[daffT], merge=(tt > 0))
    work = P.sb("work", [16, 2048], F32)
    dwork = Dep()
    g_all = P.sb("g_all", [16, 256], F32)
    idx_all = P.sb("idx_all", [16, 256], U32)
    dg = Dep()
    di = Dep()
    for r in range(32):
        src, dsrc = (affT, daffT) if r == 0 else (work, dwork)
        sl = slice(r * 8, (r + 1) * 8)
        P.op("dve", lambda e, src=src, sl=sl: e.max(out=g_all[:, sl], in_=src[:]), reads=[dsrc], writes=[dg], merge=(r > 0))
        P.op("dve", lambda e, src=src, sl=sl: e.max_index(out=idx_all[:, sl], in_max=g_all[:, sl], in_values=src[:]),
             reads=[dsrc, dg], writes=[di], merge=(r > 0))
        if r < 31:
            P.op("dve", lambda e, src=src, sl=sl: e.match_replace(out=work[:], in_to_replace=g_all[:, sl], in_values=src[:], imm_value=-1.0),
                 reads=[dsrc, dg], writes=[dwork])
    idxf = P.sb("idxf", [16, 256], F32)
    didxf = Dep()
    P.op("dve", lambda e: e.tensor_copy(out=idxf[:], in_=idx_all[:]), reads=[di], writes=[didxf])
    colt = P.sb("colt", [128, 2, 2, 16], F32)
    dcol = Dep()
    for which, (src, dsrc) in enumerate(((idxf, didxf), (g_all, dg))):
        for cc in range(2):
            bk, dbk = C.bank()
            P.op("pe", lambda e, bk=bk, src=src, cc=cc: e.transpose(bk[:, 0:16], src[:, cc * 128:(cc + 1) * 128], ident[0:16, 0:16]),
                 reads=[dsrc, dcst], writes=[dbk])
            P.op("act", lambda e, bk=bk, which=which, cc=cc: e.copy(out=colt[:, which, cc, :], in_=bk[:, 0:16]),
                 reads=[dbk], writes=[dcol], merge=True)
    P.dma("sp", idxc_d, colt[:, 0, :, :], reads=[dcol], writes=[C.dd("moe_idxc")])
    P.dma("sp", gc_d, colt[:, 1, :, :], reads=[dcol], writes=[C.dd("moe_gc")])
    sel_r = Ring(P, "sel", 2, [128, NT, 256], BF16)
    xe_r = Ring(P, "xe", 2, [128, 8, 256], BF16)
    for ex_i in range(16):
        bk, dbk = C.bank()
        P.op("pe", lambda e, bk=bk, ex_i=ex_i: e.matmul(bk[:, 0:256], lhsT=esel[:, ex_i, :], rhs=idxf[:], start=True, stop=True),
             reads=[dcst, didxf], writes=[dbk])
        sel, dsel = sel_r.next()
        for tt in range(NT):
            P.op("dve", lambda e, sel=sel, bk=bk, tt=tt: e.tensor_scalar(out=sel[:, tt, :], in0=bk[:, 0:256], scalar1=iotac[:, tt:tt + 1], scalar2=None, op0=ALU.is_equal),
                 reads=[dbk, dcst], writes=[dsel], merge=(tt > 0))
        xe, dxe = xe_r.next()
        for dc in range(8):
            bk2, dbk2 = C.bank()
            mm_acc(P, bk2[:, 0:256], [(x1b[:, tt, dc * 128:(dc + 1) * 128], sel[:, tt, :]) for tt in range(NT)],
                   reads=dx1b + [dsel], dwrite=dbk2)
            if dc % 2 == 0:
                P.op("act", lambda e, xe=xe, dc=dc, bk2=bk2: e.copy(out=xe[:, dc, :], in_=bk2[:, 0:256]), reads=[dbk2], writes=[dxe], merge=(dc > 0))
            else:
                P.op("dve", lambda e, xe=xe, dc=dc, bk2=bk2: e.tensor_copy(out=xe[:, dc, :], in_=bk2[:, 0:256]), reads=[dbk2], writes=[dxe], merge=True)
        P.dma("sp", xe_all[ex_i], xe[:], reads=[dxe], writes=[C.dd("xeT_all")], merge=True)


def stage_moe2(C, li, x_name, out_name):
    P = C.P
    P.begin_stage()
    x1 = C.D(x_name, [2048, 1024], F32)
    wg_d = C.D(f"moe_w_gate{li}", [16, 1024, 2048], F32)
    wu_d = C.D(f"moe_w_up{li}", [16, 1024, 2048], F32)
    wd_d = C.D(f"moe_w_down{li}", [16, 2048, 1024], F32)
    xe_all = C.D("xeT_all", [16, 128, 8, 256], BF16)
    idxc_d = C.D("moe_idxc", [128, 2, 16], F32)
    gc_d = C.D("moe_gc", [128, 2, 16], F32)
    iotar_d = C.D("iota_row", [128, 2048], F32)
    ident_d = C.D("ident", [128, 128], F32)
    g_d = C.D(f"ln2_g{li}", [128, 1024], F32)
    b_d = C.D(f"ln2_b{li}", [128, 1024], F32)
    out = C.D(out_name, [2048, 1024], F32)
    outT = C.D(out_name + "T", [1024, 2048], BF16)
    dcst = Dep()
    idxc = P.sb("idxc", [128, 2, 16], F32)
    gc = P.sb("gc", [128, 2, 16], F32)
    iotar = P.sb("iotar", [128, 2048], F32)
    P.dma("sp", idxc[:], idxc_d, reads=[C.dd("moe_idxc")], writes=[dcst])
    P.dma("sp", gc[:], gc_d, reads=[C.dd("moe_gc")], writes=[dcst], merge=True)
    P.dma("sp", iotar[:], iotar_d, writes=[dcst], merge=True)
    f_acc = P.sb("f_acc", [128, NT, 1024], F32)
    dfa = [Dep() for _ in range(NT)]
    xe_r = Ring(P, "m2xe", 2, [128, 8, 256], BF16)
    selT_r = Ring(P, "selT", 2, [128, 2, 2048], BF16)
    wg_r = Ring(P, "wg", 2, [128, 8, 512], BF16)
    wu_r = Ring(P, "wu", 2, [128, 8, 512], BF16)
    wd_r = Ring(P, "wd", 2, [128, 4, 1024], BF16)
    sg_r = Ring(P, "sg", 2, [128, 256], F32)
    hT_r = Ring(P, "hT", 2, [128, 16, 256], BF16)
    ye_r = Ring(P, "ye", 2, [128, 2, 1024], BF16)
    ev = 0
    for e_i in range(16):
        xe, dxe = xe_r.next()
        P.dma("sp", xe[:], xe_all[e_i], reads=[C.dd("xeT_all")], writes=[dxe])
        selT, dselT = selT_r.next()
        for cc in range(2):
            P.op("pool", lambda e, selT=selT, cc=cc, e_i=e_i: e.tensor_scalar(
                out=selT[:, cc, :], in0=iotar[:], scalar1=idxc[:, cc, e_i:e_i + 1], scalar2=gc[:, cc, e_i:e_i + 1],
                op0=ALU.is_equal, op1=ALU.mult), reads=[dcst], writes=[dselT], merge=(cc > 0))
        hT, dhT = hT_r.next()
        wgv = wg_d[e_i].rearrange("(kc p) f -> p kc f", p=128)
        wuv = wu_d[e_i].rearrange("(kc p) f -> p kc f", p=128)
        wdv = wd_d[e_i].rearrange("(fc p) d -> p fc d", p=128)
        for q in range(4):
            wg, dwg = wg_r.next()
            wu, dwu = wu_r.next()
            P.dma("pool", wg[:], wgv[:, :, q * 512:(q + 1) * 512], writes=[dwg])
            P.dma("pool", wu[:], wuv[:, :, q * 512:(q + 1) * 512], writes=[dwu])
            for fcl in range(4):
                fc = q * 4 + fcl
                fs = slice(fcl * 128, (fcl + 1) * 128)
                bg, dbg = C.bank()
                bu, dbu = C.bank()
                mm_acc(P, bg[:, 0:256], [(wg[:, kc, fs], xe[:, kc, :]) for kc in range(8)], reads=[dwg, dxe], dwrite=dbg)
                mm_acc(P, bu[:, 0:256], [(wu[:, kc, fs], xe[:, kc, :]) for kc in range(8)], reads=[dwu, dxe], dwrite=dbu)
                sg, dsg = sg_r.next()
                P.op("act", lambda e, sg=sg, bg=bg: e.activation(out=sg[:], in_=bg[:, 0:256], func=AF.Silu), reads=[dbg], writes=[dsg])
                P.op("dve", lambda e, hT=hT, fc=fc, sg=sg, bu=bu: e.tensor_tensor(out=hT[:, fc, :], in0=sg[:], in1=bu[:, 0:256], op=ALU.mult),
                     reads=[dsg, dbu], writes=[dhT], merge=(fc > 0))
        ye, dye = ye_r.next()
        dbanks = [C.bank() for _ in range(4)]
        for r in range(4):
            wd, dwd = wd_r.next()
            P.dma("pool", wd[:], wdv[:, r * 4:(r + 1) * 4, :], writes=[dwd])
            for ct in range(2):
                for dh in range(2):
                    bk, dbk = dbanks[ct * 2 + dh]
                    for f4 in range(4):
                        fc = r * 4 + f4
                        first = (fc == 0)
                        last = (fc == 15)
                        P.op("pe", lambda e, bk=bk, hT=hT, fc=fc, ct=ct, wd=wd, f4=f4, dh=dh, first=first, last=last: e.matmul(
                            bk[:], lhsT=hT[:, fc, ct * 128:(ct + 1) * 128], rhs=wd[:, f4, dh * 512:(dh + 1) * 512], start=first, stop=last),
                            reads=[dhT, dwd], writes=[dbk], merge=(not first))
        for ct in range(2):
            for dh in range(2):
                bk, dbk = dbanks[ct * 2 + dh]
                P.op("act", lambda e, ye=ye, ct=ct, dh=dh, bk=bk: e.copy(out=ye[:, ct, dh * 512:(dh + 1) * 512], in_=bk[:]),
                     reads=[dbk], writes=[dye], merge=(ct + dh > 0))
        for tt in range(NT):
            for dh in range(2):
                bk, dbk = C.bank()
                ds = slice(dh * 512, (dh + 1) * 512)
                mm_acc(P, bk[:], [(selT[:, ct, tt * 128:(tt + 1) * 128], ye[:, ct, ds]) for ct in range(2)], reads=[dselT, dye], dwrite=dbk)
                if e_i == 0:
                    P.op("dve", lambda e, tt=tt, ds=ds, bk=bk: e.tensor_copy(out=f_acc[:, tt, ds], in_=bk[:]), reads=[dbk], writes=[dfa[tt]], merge=(dh > 0))
                else:
                    P.op("dve", lambda e, tt=tt, ds=ds, bk=bk: e.tensor_tensor(out=f_acc[:, tt, ds], in0=f_acc[:, tt, ds], in1=bk[:], op=ALU.add),
                         reads=[dbk, dfa[tt]], writes=[dfa[tt]])
    gb = P.sb("gb2", [128, 1024], F32)
    bb = P.sb("bb2", [128, 1024], F32)
    ident = P.sb("ident2", [128, 128], F32)
    dgb = Dep()
    P.dma("sp", gb[:], g_d, writes=[dgb])
    P.dma("sp", bb[:], b_d, writes=[dgb], merge=True)
    P.dma("sp", ident[:], ident_d, writes=[dgb], merge=True)
    lb = LNBufs(P, "ln2")
    xr = Ring(P, "m2x", 2, [128, 1024], F32)
    yr = Ring(P, "m2y", 2, [128, 1024], F32)
    yT_r = Ring(P, "m2yT", 2, [128, 8, 128], BF16)
    for tt in range(NT):
        xt, dxt = xr.next()
        P.dma("sp", xt[:], x1[tt * 128:(tt + 1) * 128, :], reads=[C.dd(x_name)], writes=[dxt])
        P.op("dve", lambda e, xt=xt, tt=tt: e.scalar_tensor_tensor(out=xt[:], in0=xt[:], scalar=ALPHA, in1=f_acc[:, tt, :], op0=ALU.mult, op1=ALU.add),
             reads=[dxt, dfa[tt]], writes=[dxt])
        y, dy = yr.next()
        layer_norm_tile(P, lb, xt, dxt, gb, bb, dgb, y, dy)
        P.dma("sp", out[tt * 128:(tt + 1) * 128, :], y[:], reads=[dy], writes=[C.dd(out_name)], merge=True)
        transpose_tile_to_dram(C, y, dy, ident, dgb, yT_r, outT, out_name + "T", tt)


def transpose_tile_to_dram(C, y, dy, ident, dident, yT_r, outT, outT_name, tt):
    P = C.P
    yT, dyT = yT_r.next()
    for hb in range(2):
        bk, dbk = C.bank()
        for k4 in range(4):
            kc = hb * 4 + k4
            P.op("pe", lambda e, bk=bk, k4=k4, kc=kc: e.transpose(bk[:, k4 * 128:(k4 + 1) * 128], y[:, kc * 128:(kc + 1) * 128], ident[:]),
                 reads=[dy, dident], writes=[dbk], merge=(k4 > 0))
        P.op("act", lambda e, hb=hb, bk=bk: e.copy(out=yT[:, hb * 4:(hb + 1) * 4, :], in_=bk[:].rearrange("p (k t) -> p k t", k=4)),
             reads=[dbk], writes=[dyT], merge=(hb > 0))
    P.dma("sp", outT.rearrange("(kc p) t -> p kc t", p=128)[:, :, tt * 128:(tt + 1) * 128], yT[:], reads=[dyT], writes=[C.dd(outT_name)], merge=True)


def stage_ple(C, li, x_name, out_name, want_T):
    P = C.P
    P.begin_stage()
    x2 = C.D(x_name, [2048, 1024], F32)
    x2T = C.D(x_name + "T", [1024, 2048], BF16)
    pT_d = C.D(f"pT{li}", [256, 2048], F32)
    wg_d = C.D(f"ple_gate{li}", [1024, 1024], F32)
    wp_d = C.D(f"ple_proj{li}", [256, 1024], F32)
    ident_d = C.D("ident", [128, 128], F32)
    out = C.D(out_name, [2048, 1024], F32)
    xs = P.sb("plx", [128, 8, 2048], BF16)
    dxs = [Dep() for _ in range(8)]
    for kc in range(8):
        P.dma("sp", xs[:, kc, :], x2T[kc * 128:(kc + 1) * 128, :], reads=[C.dd(x_name + "T")], writes=[dxs[kc]])
    ps_, dps_ = load_fm_bf16(C, "plp", pT_d, 2, 2048)
    wg, dwg = load_fm_bf16(C, "plwg", wg_d, 8, 1024)
    wp, dwp = load_fm_bf16(C, "plwp", wp_d, 2, 1024)
    ident = P.sb("ident3", [128, 128], F32)
    dident = Dep()
    P.dma("sp", ident[:], ident_d, writes=[dident])
    if want_T:
        outT = C.D(out_name + "T", [1024, 2048], BF16)
        yT_r = Ring(P, "plyT", 2, [128, 8, 128], BF16)
    xr = Ring(P, "plxr", 2, [128, 1024], F32)
    gr = Ring(P, "plg", 2, [128, 1024], F32)
    yr = Ring(P, "ply", 2, [128, 1024], F32)
    for tt in range(NT):
        ts_ = slice(tt * 128, (tt + 1) * 128)
        xt, dxt = xr.next()
        P.dma("sp", xt[:], x2[ts_, :], reads=[C.dd(x_name)], writes=[dxt])
        gt, dgt = gr.next()
        y, dy = yr.next()
        for half in range(2):
            hs = slice(half * 512, (half + 1) * 512)
            bk, dbk = C.bank()
            mm_acc(P, bk[:], [(xs[:, kc, ts_], wg[:, kc, hs]) for kc in range(8)], reads=dxs + dwg, dwrite=dbk)
            P.op("act", lambda e, gt=gt, hs=hs, bk=bk: e.activation(out=gt[:, hs], in_=bk[:], func=AF.Sigmoid), reads=[dbk], writes=[dgt], merge=(half > 0))
            bk2, dbk2 = C.bank()
            mm_acc(P, bk2[:], [(ps_[:, kc, ts_], wp[:, kc, hs]) for kc in range(2)], reads=dps_ + dwp, dwrite=dbk2)
            P.op("dve", lambda e, gt=gt, hs=hs, bk2=bk2: e.tensor_tensor(out=gt[:, hs], in0=gt[:, hs], in1=bk2[:], op=ALU.mult),
                 reads=[dgt, dbk2], writes=[dgt])
        P.op("pool", lambda e, y=y, xt=xt, gt=gt: e.tensor_tensor(out=y[:], in0=xt[:], in1=gt[:], op=ALU.add), reads=[dxt, dgt], writes=[dy])
        P.dma("sp", out[ts_, :], y[:], reads=[dy], writes=[C.dd(out_name)], merge=True)
        if want_T:
            transpose_tile_to_dram(C, y, dy, ident, dident, yT_r, outT, out_name + "T", tt)


def hyena_consts():
    L, N = 2048, 4096
    R = np.arange(N)
    f = np.where(R <= 2048, R, R - 2048).astype(np.int64)
    is_im = R > 2048
    t = np.arange(L, dtype=np.int64)
    k = (t[:, None] * f[None, :]) % N
    ang = 2.0 * np.pi * k.astype(np.float64) / N
    Wf = np.where(is_im[None, :], -np.sin(ang), np.cos(ang))
    cR = np.full(N, 2.0 / N)
    cR[0] = 1.0 / N
    cR[2048] = 1.0 / N
    WfT = np.ascontiguousarray(Wf.T)
    Wf_d = Wf.reshape(16, 128, 32, 128).transpose(2, 1, 0, 3)
    WA_d = WfT.reshape(32, 128, 16, 128).transpose(2, 1, 0, 3)
    WB_d = WfT.reshape(2, 16, 128, 4, 512).transpose(3, 0, 2, 1, 4)
    tl = np.linspace(0.0, 1.0, L, dtype=np.float32)[:, None]
    w = (2.0 * np.float32(math.pi) * np.arange(L, dtype=np.float32)[:, None] / np.float32(L)).astype(np.float32)
    fb = np.linspace(1e-4, 15, 16, dtype=np.float32)[None, :]
    z = np.concatenate([tl, np.cos(fb * w), -np.sin(fb * w)], axis=-1).astype(np.float32)
    min_decay = math.log(1e-2) / 1.5
    max_decay = math.log(1e-2) / 0.3
    deltas = np.abs(np.linspace(min_decay, max_decay, 512, dtype=np.float32))
    decay = np.exp(-tl * deltas[None, :]).astype(np.float32)
    return {
        "hy_Wf": np.ascontiguousarray(Wf_d).astype(NPBF),
        "hy_WA": np.ascontiguousarray(WA_d).astype(NPBF), "hy_WB": np.ascontiguousarray(WB_d).astype(NPBF),
        "hy_cR": np.ascontiguousarray(cR.reshape(32, 128).T).astype(np.float32),
        "hy_zT": np.ascontiguousarray(z.T), "hy_decay": np.ascontiguousarray(decay.reshape(16, 128, 512)),
    }


TWO_PI = 2.0 * math.pi


def stage_hy_filter(C):
    P = C.P
    P.begin_stage()
    zT_d = C.D("hy_zT", [33, 2048], F32)
    w1_d = C.D("hy_f_w1", [33, 64], F32)
    w2_d = C.D("hy_f_w2", [64, 64], F32)
    w3_d = C.D("hy_f_w3", [64, 2048], F32)
    cols_d = C.D("hy_cols", [64, 3], F32)
    dec_d = C.D("hy_decay", [16, 128, 512], F32)
    Wf_d = C.D("hy_Wf", [32, 128, 16, 128], BF16)
    cR_d = C.D("hy_cR", [128, 32], F32)
    skip_d = C.D("hy_skipb", [2, 128, 512], F32)
    Kf_d = C.D("hy_Kf", [32, 2, 128, 512], F32)
    dc = Dep()
    zT = P.sb("zT", [33, 2048], F32)
    w1 = P.sb("w1", [33, 64], F32)
    w2 = P.sb("w2", [64, 64], F32)
    w3 = P.sb("w3", [64, 2048], BF16)
    cols = P.sb("cols", [64, 8], F32)
    cR = P.sb("cR", [128, 32], F32)
    skipb = P.sb("skipb", [128, 2, 512], F32)
    P.dma("sp", zT[:], zT_d, writes=[dc])
    P.dma("sp", w1[:], w1_d, writes=[dc], merge=True)
    P.dma("sp", w2[:], w2_d, writes=[dc], merge=True)
    P.dma("pool", w3[:], w3_d, writes=[dc], merge=True)
    P.dma("sp", cols[:, 0:3], cols_d, writes=[dc], merge=True)
    P.dma("sp", cR[:], cR_d, writes=[dc], merge=True)
    for o in range(2):
        P.dma("sp", skipb[:, o, :], skip_d[o], writes=[dc], merge=True)
    dcol = Dep()
    P.op("dve", lambda e: e.tensor_tensor(out=cols[:, 3:4], in0=cols[:, 0:1], in1=cols[:, 1:2], op=ALU.mult), reads=[dc], writes=[dcol])
    P.op("dve", lambda e: e.tensor_tensor(out=cols[:, 4:5], in0=cols[:, 2:3], in1=cols[:, 1:2], op=ALU.mult), reads=[dc], writes=[dcol], merge=True)
    P.op("pool", lambda e: e.memset(cols[:, 5:6], -math.pi), reads=[dc], writes=[dcol], merge=True)
    h1T = P.sb("h1T", [64, 2048], F32)
    h2T = P.sb("h2T", [64, 2048], BF16)
    dh1 = Dep()
    dh2 = Dep()
    u_r = Ring(P, "hyu", 2, [64, 512], F32)
    s_r = Ring(P, "hys", 4, [64, 512], F32)
    for layer in range(2):
        for nt in range(4):
            ns = slice(nt * 512, (nt + 1) * 512)
            bk, dbk = C.bank()
            if layer == 0:
                P.op("pe", lambda e, bk=bk, ns=ns: e.matmul(bk[0:64, :], lhsT=w1[:], rhs=zT[:, ns], start=True, stop=True), reads=[dc], writes=[dbk])
            else:
                P.op("pe", lambda e, bk=bk, ns=ns: e.matmul(bk[0:64, :], lhsT=w2[:], rhs=h1T[:, ns], start=True, stop=True), reads=[dc, dh1], writes=[dbk])
            u, du = u_r.next()
            fbc = 3 + layer
            P.op("dve", lambda e, u=u, bk=bk, fbc=fbc: e.tensor_scalar(out=u[:], in0=bk[0:64, :], scalar1=cols[:, 1:2], scalar2=cols[:, fbc:fbc + 1], op0=ALU.mult, op1=ALU.add),
                 reads=[dbk, dcol, dc], writes=[du])
            s2, ds2 = s_r.next()
            s4, ds4 = s_r.next()
            P.op("act", lambda e, u=u, s2=s2: e.activation(out=s2[:], in_=u[:], func=AF.Sin, scale=0.5), reads=[du], writes=[ds2])
            P.op("act", lambda e, u=u, s4=s4: e.activation(out=s4[:], in_=u[:], func=AF.Sin, scale=0.25), reads=[du], writes=[ds4])
            P.op("dve", lambda e, s4=s4: e.tensor_tensor(out=s4[:], in0=s4[:], in1=s4[:], op=ALU.mult), reads=[ds4], writes=[ds4])
            P.op("dve", lambda e, s4=s4: e.tensor_scalar(out=s4[:], in0=s4[:], scalar1=-2.0, scalar2=1.0, op0=ALU.mult, op1=ALU.add), reads=[ds4], writes=[ds4])
            if layer == 0:
                P.op("dve", lambda e, s2=s2, s4=s4, ns=ns: e.scalar_tensor_tensor(out=h1T[:, ns], in0=s2[:], scalar=2.0, in1=s4[:], op0=ALU.mult, op1=ALU.mult),
                     reads=[ds2, ds4], writes=[dh1], merge=(nt > 0))
            else:
                P.op("dve", lambda e, s2=s2, s4=s4, ns=ns: e.scalar_tensor_tensor(out=h2T[:, ns], in0=s2[:], scalar=2.0, in1=s4[:], op0=ALU.mult, op1=ALU.mult),
                     reads=[ds2, ds4], writes=[dh2], merge=(nt > 0))
    Kt = P.sb("Ksd", [128, 16, 4, 512], BF16)
    dKt = [Dep() for _ in range(16)]
    ones = P.sb("onesb", [128, 128], BF16)
    dones = Dep()
    P.op("pool", lambda e: e.memset(ones[:], 1.0), writes=[dones])
    dec_r = Ring(P, "dec", 2, [128, 512], F32)
    sq_r = Ring(P, "sq", 3, [128, 512], BF16)
    kf32_r = Ring(P, "kf32", 2, [128, 512], F32)
    kb32_r = Ring(P, "kb32", 2, [128, 512], F32)
    ssq = [C.bank(hold=True), C.bank(hold=True)]
    for tt in range(16):
        dec, ddec = dec_r.next()
        P.dma("sp", dec[:], dec_d[tt], writes=[ddec])
        for o in range(2):
            bf_, dbf_ = C.bank()
            bb_, dbb_ = C.bank()
            for (bk, dbk, q) in ((bf_, dbf_, 2 * o), (bb_, dbb_, 2 * o + 1)):
                P.op("pe", lambda e, bk=bk, tt=tt, q=q: e.matmul(bk[:], lhsT=h2T[:, tt * 128:(tt + 1) * 128], rhs=w3[:, q * 512:(q + 1) * 512], start=True, stop=True),
                     reads=[dh2, dc], writes=[dbk])
            kf32, dkf32 = kf32_r.next()
            kb32, dkb32 = kb32_r.next()
            P.op("dve", lambda e, kf32=kf32, bf_=bf_, dec=dec: e.tensor_tensor(out=kf32[:], in0=bf_[:], in1=dec[:], op=ALU.mult), reads=[dbf_, ddec], writes=[dkf32])
            P.op("dve", lambda e, kb32=kb32, bb_=bb_, dec=dec: e.tensor_tensor(out=kb32[:], in0=bb_[:], in1=dec[:], op=ALU.mult), reads=[dbb_, ddec], writes=[dkb32])
            if tt == 0:
                P.op("pool", lambda e, kb32=kb32: e.memset(kb32[0:1, :], 0.0), reads=[dkb32], writes=[dkb32])
            P.op("pool", lambda e, tt=tt, o=o, kf32=kf32, kb32=kb32: e.tensor_tensor(out=Kt[:, tt, 2 * o, :], in0=kf32[:], in1=kb32[:], op=ALU.add),
                 reads=[dkf32, dkb32], writes=[dKt[tt]], merge=True)
            P.op("pool", lambda e, tt=tt, o=o, kf32=kf32, kb32=kb32: e.tensor_tensor(out=Kt[:, tt, 2 * o + 1, :], in0=kf32[:], in1=kb32[:], op=ALU.subtract),
                 reads=[dkf32, dkb32], writes=[dKt[tt]], merge=True)
        for q in range(4):
            sq, dsq = sq_r.next()
            P.op("act", lambda e, sq=sq, tt=tt, q=q: e.activation(out=sq[:], in_=Kt[:, tt, q, :], func=AF.Square), reads=[dKt[tt]], writes=[dsq])
            sb_, dsb_ = ssq[q // 2]
            first = (tt == 0 and q % 2 == 0)
            last = (tt == 15 and q % 2 == 1)
            P.op("pe", lambda e, sb_=sb_, sq=sq, first=first, last=last: e.matmul(sb_[:], lhsT=ones[:], rhs=sq[:], start=first, stop=last),
                 reads=[dsq, dones], writes=[dsb_], merge=(not first))
    rs = P.sb("hyrs", [128, 2, 512], F32)
    drs = Dep()
    for o in range(2):
        sb_, dsb_ = ssq[o]
        P.op("dve", lambda e, o=o, sb_=sb_: e.tensor_scalar(out=rs[:, o, :], in0=sb_[:], scalar1=0.5, scalar2=1e-12, op0=ALU.mult, op1=ALU.add), reads=[dsb_], writes=[drs], merge=(o > 0))
    C.release_all()
    P.op("act", lambda e: e.activation(out=rs[:], in_=rs[:], func=AF.Sqrt), reads=[drs], writes=[drs])
    P.op("dve", lambda e: e.reciprocal(out=rs[:], in_=rs[:]), reads=[drs], writes=[drs])
    wf_r = Ring(P, "wf", 2, [128, 16, 128], BF16)
    kf_r = Ring(P, "kf", 3, [128, 512], F32)
    for ft in range(32):
        wf, dwf = wf_r.next()
        P.dma("sp", wf[:], Wf_d[ft], writes=[dwf])
        for o in range(2):
            bk, dbk = C.bank()
            sel = 2 * o if ft < 16 else 2 * o + 1
            mm_acc(P, bk[:], [(wf[:, tt, :], Kt[:, tt, sel, :]) for tt in range(16)], reads=[dwf] + dKt, dwrite=dbk)
            kf, dkf = kf_r.next()
            P.op("dve", lambda e, kf=kf, bk=bk, o=o: e.tensor_tensor(out=kf[:], in0=bk[:], in1=rs[:, o, :], op=ALU.mult), reads=[dbk, drs], writes=[dkf])
            if ft < 16:
                P.op("pool", lambda e, kf=kf, o=o: e.tensor_tensor(out=kf[:], in0=kf[:], in1=skipb[:, o, :], op=ALU.add), reads=[dkf, dc], writes=[dkf])
            elif ft == 16:
                bn, dbn = C.bank()
                mm_acc(P, bn[0:32, :], [(wf[:, tt, 0:32], Kt[:, tt, 2 * o, :]) for tt in range(16)], reads=[dwf] + dKt, dwrite=dbn)
                P.op("dve", lambda e, kf=kf, bn=bn, o=o: e.tensor_tensor(out=kf[0:1, :], in0=bn[0:1, :], in1=rs[0:1, o, :], op=ALU.mult), reads=[dbn, drs, dkf], writes=[dkf])
                P.op("pool", lambda e, kf=kf, o=o: e.tensor_tensor(out=kf[0:1, :], in0=kf[0:1, :], in1=skipb[0:1, o, :], op=ALU.add), reads=[dkf, dc], writes=[dkf])
            P.op("pool", lambda e, kf=kf, ft=ft: e.tensor_scalar(out=kf[:], in0=kf[:], scalar1=cR[:, ft:ft + 1], scalar2=None, op0=ALU.mult), reads=[dkf, dc], writes=[dkf])
            P.dma("sp", Kf_d[ft, o], kf[:], reads=[dkf], writes=[C.dd("hy_Kf")], merge=True)


def stage_hy_prep(C):
    P = C.P
    P.begin_stage()
    hbT = C.D("hbT", [1536, 2048], F32)
    cw_d = C.D("hy_cw", [128, 12, 3], F32)
    cb_d = C.D("hy_cb", [128, 12], F32)
    ident_d = C.D("ident", [128, 128], F32)
    hv = C.D("hv_tm", [2048, 512], BF16)
    hx1 = C.D("hx1_tm", [2048, 512], F32)
    hx2T = C.D("hx2T", [512, 2048], F32)
    dc = Dep()
    cw = P.sb("cw", [128, 12, 3], F32)
    cb = P.sb("cb", [128, 12], F32)
    ident = P.sb("identh", [128, 128], F32)
    P.dma("sp", cw[:], cw_d, writes=[dc])
    P.dma("sp", cb[:], cb_d, writes=[dc], merge=True)
    P.dma("sp", ident[:], ident_d, writes=[dc], merge=True)
    xin_r = Ring(P, "hxin", 2, [128, 2048], F32)
    y_r = Ring(P, "hy", 2, [128, 2048], F32)
    sv_r = Ring(P, "hsv", 2, [128, 16, 128], BF16)
    sx_r = Ring(P, "hsx", 2, [128, 16, 128], F32)
    for ch in range(12):
        xin, dxin = xin_r.next()
        P.dma("sp", xin[:], hbT[ch * 128:(ch + 1) * 128, :], reads=[C.dd("hbT")], writes=[dxin])
        y, dy = y_r.next()
        P.op("act", lambda e, y=y, xin=xin, ch=ch: e.activation(out=y[:], in_=xin[:], func=AF.Identity, bias=cb[:, ch:ch + 1], scale=cw[:, ch, 1:2]),
             reads=[dxin, dc], writes=[dy])
        P.op("dve", lambda e, y=y, xin=xin, ch=ch: e.scalar_tensor_tensor(out=y[:, 1:2048], in0=xin[:, 0:2047], scalar=cw[:, ch, 0:1], in1=y[:, 1:2048], op0=ALU.mult, op1=ALU.add),
             reads=[dxin, dc, dy], writes=[dy])
        P.op("dve", lambda e, y=y, xin=xin, ch=ch: e.scalar_tensor_tensor(out=y[:, 0:2047], in0=xin[:, 1:2048], scalar=cw[:, ch, 2:3], in1=y[:, 0:2047], op0=ALU.mult, op1=ALU.add),
             reads=[dxin, dc, dy], writes=[dy])
        if ch >= 8:
            P.dma("sp", hx2T[(ch - 8) * 128:(ch - 7) * 128, :], y[:], reads=[dy], writes=[C.dd("hx2T")], merge=True)
            continue
        stg, dstg = (sv_r if ch < 4 else sx_r).next()
        for g in range(4):
            bk, dbk = C.bank()
            for k4 in range(4):
                tt = g * 4 + k4
                P.op("pe", lambda e, bk=bk, k4=k4, tt=tt, y=y: e.transpose(bk[:, k4 * 128:(k4 + 1) * 128], y[:, tt * 128:(tt + 1) * 128], ident[:]),
                     reads=[dy, dc], writes=[dbk], merge=(k4 > 0))
            P.op("act", lambda e, stg=stg, g=g, bk=bk: e.copy(out=stg[:, g * 4:(g + 1) * 4, :], in_=bk[:].rearrange("p (k t) -> p k t", k=4)),
                 reads=[dbk], writes=[dstg], merge=(g > 0))
        if ch < 4:
            P.dma("sp", hv.rearrange("(tt p) c -> p tt c", p=128)[:, :, ch * 128:(ch + 1) * 128], stg[:], reads=[dstg], writes=[C.dd("hv_tm")], merge=True)
        else:
            P.dma("sp", hx1.rearrange("(tt p) c -> p tt c", p=128)[:, :, (ch - 4) * 128:(ch - 3) * 128], stg[:], reads=[dstg], writes=[C.dd("hx1_tm")], merge=True)


def stage_hy_conv(C):
    P = C.P
    P.begin_stage()
    hv = C.D("hv_tm", [2048, 512], BF16)
    hx1 = C.D("hx1_tm", [2048, 512], F32)
    hx2T = C.D("hx2T", [512, 2048], F32)
    Kf_d = C.D("hy_Kf", [32, 2, 128, 512], F32)
    Wf_d = C.D("hy_Wf", [32, 128, 16, 128], BF16)
    WA_d = C.D("hy_WA", [16, 128, 32, 128], BF16)
    WB_d = C.D("hy_WB", [4, 2, 128, 16, 512], BF16)
    mixT = C.D("mixT", [1024, 2048], BF16)
    ztm = P.sb("ztm", [128, 16, 512], BF16)
    dz = [Dep() for _ in range(16)]
    hvv = hv.rearrange("(tt p) c -> p tt c", p=128)
    for tt in range(16):
        P.dma("sp", ztm[:, tt, :], hvv[:, tt, :], reads=[C.dd("hv_tm")], writes=[dz[tt]])
    Yt = P.sb("Yt", [128, 32, 512], BF16)
    dY = [Dep() for _ in range(32)]
    wf_r = Ring(P, "cwf", 3, [128, 16, 128], BF16)
    kf_r = Ring(P, "ckf", 4, [128, 512], F32)
    t_r = Ring(P, "ct", 4, [128, 512], F32)
    wa_r = Ring(P, "cwa", 2, [128, 32, 128], BF16)
    wb_r = Ring(P, "cwb", 2, [128, 16, 512], BF16)
    x_r = Ring(P, "cx", 3, [128, 512], F32)
    zo_r = Ring(P, "czo", 3, [128, 512], BF16)
    for o in range(2):
        for j in range(16):
            ub = []
            for part in range(2):
                ft = part * 16 + j
                wf, dwf = wf_r.next()
                P.dma("sp", wf[:], Wf_d[ft], writes=[dwf])
                bk, dbk = C.bank()
                mm_acc(P, bk[:], [(wf[:, tt, :], ztm[:, tt, :]) for tt in range(16)], reads=[dwf] + dz, dwrite=dbk)
                ub.append((bk, dbk))
            kre, dkre = kf_r.next()
            kim, dkim = kf_r.next()
            P.dma("sp", kre[:], Kf_d[j, o], reads=[C.dd("hy_Kf")], writes=[dkre])
            P.dma("sp", kim[:], Kf_d[16 + j, o], reads=[C.dd("hy_Kf")], writes=[dkim])
            (ure, dure), (uim, duim) = ub
            t1, dt1 = t_r.next()
            t2, dt2 = t_r.next()
            P.op("dve", lambda e, t1=t1, ure=ure, kre=kre: e.tensor_tensor(out=t1[:], in0=ure[:], in1=kre[:], op=ALU.mult), reads=[dure, dkre], writes=[dt1])
            P.op("dve", lambda e, t2=t2, uim=uim, kim=kim: e.tensor_tensor(out=t2[:], in0=uim[:], in1=kim[:], op=ALU.mult), reads=[duim, dkim], writes=[dt2])
            P.op("pool", lambda e, j=j, t1=t1, t2=t2: e.tensor_tensor(out=Yt[:, j, :], in0=t1[:], in1=t2[:], op=ALU.subtract), reads=[dt1, dt2], writes=[dY[j]])
            if j == 0:
                P.op("pool", lambda e, t1=t1: e.tensor_copy(out=Yt[0:1, 0, :], in_=t1[0:1, :]), reads=[dt1, dY[0]], writes=[dY[0]])
            t3, dt3 = t_r.next()
            t4, dt4 = t_r.next()
            P.op("dve", lambda e, t3=t3, ure=ure, kim=kim: e.tensor_tensor(out=t3[:], in0=ure[:], in1=kim[:], op=ALU.mult), reads=[dure, dkim], writes=[dt3])
            P.op("dve", lambda e, t4=t4, uim=uim, kre=kre: e.tensor_tensor(out=t4[:], in0=uim[:], in1=kre[:], op=ALU.mult), reads=[duim, dkre], writes=[dt4])
            P.op("pool", lambda e, j=j, t3=t3, t4=t4: e.tensor_tensor(out=Yt[:, 16 + j, :], in0=t3[:], in1=t4[:], op=ALU.add), reads=[dt3, dt4], writes=[dY[16 + j]])
            if j == 0:
                P.op("pool", lambda e, t2=t2: e.tensor_copy(out=Yt[0:1, 16, :], in_=t2[0:1, :]), reads=[dt2, dY[16]], writes=[dY[16]])
        if o == 0:
            for tt in range(16):
                wa, dwa = wa_r.next()
                P.dma("sp", wa[:], WA_d[tt], writes=[dwa])
                bk, dbk = C.bank()
                mm_acc(P, bk[:], [(wa[:, kt, :], Yt[:, kt, :]) for kt in range(32)], reads=[dwa] + dY, dwrite=dbk)
                xt, dxt = x_r.next()
                P.dma("sp", xt[:], hx1[tt * 128:(tt + 1) * 128, :], reads=[C.dd("hx1_tm")], writes=[dxt])
                P.op("dve", lambda e, tt=tt, bk=bk, xt=xt: e.tensor_tensor(out=ztm[:, tt, :], in0=bk[:], in1=xt[:], op=ALU.mult), reads=[dbk, dxt], writes=[dz[tt]])
        else:
            for nt in range(4):
                banks = [C.bank() for _ in range(4)]
                for hf in range(2):
                    wb, dwb = wb_r.next()
                    P.dma("sp", wb[:], WB_d[nt, hf], writes=[dwb])
                    for cc in range(4):
                        bk, dbk = banks[cc]
                        for k in range(16):
                            first = (hf == 0 and k == 0)
                            last = (hf == 1 and k == 15)
                            kt = hf * 16 + k
                            P.op("pe", lambda e, bk=bk, kt=kt, cc=cc, wb=wb, k=k, first=first, last=last: e.matmul(
                                bk[:], lhsT=Yt[:, kt, cc * 128:(cc + 1) * 128], rhs=wb[:, k, :], start=first, stop=last),
                                reads=[dY[kt], dwb], writes=[dbk], merge=(not first))
                for cc in range(4):
                    bk, dbk = banks[cc]
                    xt, dxt = x_r.next()
                    P.dma("sp", xt[:], hx2T[cc * 128:(cc + 1) * 128, nt * 512:(nt + 1) * 512], reads=[C.dd("hx2T")], writes=[dxt])
                    zo, dzo = zo_r.next()
                    P.op("dve", lambda e, zo=zo, bk=bk, xt=xt: e.tensor_tensor(out=zo[:], in0=bk[:], in1=xt[:], op=ALU.mult), reads=[dbk, dxt], writes=[dzo])
                    P.dma("sp", mixT[512 + cc * 128:512 + (cc + 1) * 128, nt * 512:(nt + 1) * 512], zo[:], reads=[dzo], writes=[C.dd("mixT")], merge=True)


MLA_SCALE = 96.0 ** -0.5


def mla_consts():
    inv = 1.0 / (10000.0 ** (np.arange(0, 32, 2, dtype=np.float32) / 32.0))
    ang = np.arange(2048, dtype=np.float32)[:, None] * inv[None, :].astype(np.float32)
    cos = np.cos(ang).astype(np.float32).T
    sin = np.sin(ang).astype(np.float32).T
    cos2 = np.concatenate([cos, cos], axis=0)
    sin2 = np.concatenate([-sin, sin], axis=0)
    return {"mla_cs2": np.ascontiguousarray(np.stack([cos2, sin2], axis=1)).astype(np.float32)}


def stage_mla1(C, xT_name):
    P = C.P
    P.begin_stage()
    xT = C.D(xT_name, [1024, 2048], BF16)
    wi_d = C.D("mla_w_in", [1024, 672], F32)
    wsw_d = C.D("mla_w_in_sw", [1024, 96], F32)
    gc_d = C.D("mla_gcols", [128, 5], F32)
    cs_d = C.D("mla_cs2", [32, 2, 2048], F32)
    nT_d = C.D("mla_nT", [640, 2048], BF16)
    kr_d = C.D("mla_krT", [32, 2048], BF16)
    xs = P.sb("mxs", [128, 8, 2048], BF16)
    dxs = [Dep() for _ in range(8)]
    for kc in range(8):
        P.dma("sp", xs[:, kc, :], xT[kc * 128:(kc + 1) * 128, :], reads=[C.dd(xT_name)], writes=[dxs[kc]])
    wi, dwi = load_fm_bf16(C, "mwi", wi_d, 8, 672)
    wsw, dwsw = load_fm_bf16(C, "mwsw", wsw_d, 8, 96)
    dc = Dep()
    gcol = P.sb("mgc", [128, 5], F32)
    P.dma("sp", gcol[:], gc_d, writes=[dc])
    cs = P.sb("mcs", [96, 2, 2048], F32)
    P.dma("sp", cs[64:96, :, :], cs_d, writes=[dc], merge=True)
    ones = P.sb("mones", [128, 128], BF16)
    P.op("pool", lambda e: e.memset(ones[:], 1.0), writes=[dc], merge=True)
    hT = P.sb("mhT", [128, 5, 2048], F32)
    nT = P.sb("mnT", [128, 5, 2048], BF16)
    dhT = Dep()
    dnT = [Dep() for _ in range(5)]
    sq_r = Ring(P, "msq", 3, [128, 512], BF16)
    r_r = Ring(P, "mr", 2, [128, 512], F32)
    for (chunks, n) in (((0, 1, 2), 384.0), ((3, 4), 256.0)):
        for nt in range(4):
            ns = slice(nt * 512, (nt + 1) * 512)
            sbk, dsbk = C.bank(hold=True)
            for ci, c in enumerate(chunks):
                bk, dbk = C.bank()
                mm_acc(P, bk[:], [(wi[:, kc, c * 128:(c + 1) * 128], xs[:, kc, ns]) for kc in range(8)], reads=dxs + dwi, dwrite=dbk)
                P.op("act", lambda e, c=c, ns=ns, bk=bk: e.copy(out=hT[:, c, ns], in_=bk[:]), reads=[dbk], writes=[dhT], merge=True)
                sq, dsq = sq_r.next()
                P.op("act", lambda e, sq=sq, bk=bk: e.activation(out=sq[:], in_=bk[:], func=AF.Square), reads=[dbk], writes=[dsq])
                P.op("pe", lambda e, sbk=sbk, sq=sq, ci=ci, chunks=chunks: e.matmul(sbk[:], lhsT=ones[:], rhs=sq[:], start=(ci == 0), stop=(ci == len(chunks) - 1)),
                     reads=[dsq, dc], writes=[dsbk], merge=(ci > 0))
            r, dr = r_r.next()
            P.op("dve", lambda e, r=r, sbk=sbk, n=n: e.tensor_scalar(out=r[:], in0=sbk[:], scalar1=1.0 / n, scalar2=EPS, op0=ALU.mult, op1=ALU.add), reads=[dsbk], writes=[dr])
            C.release_all()
            P.op("act", lambda e, r=r: e.activation(out=r[:], in_=r[:], func=AF.Sqrt), reads=[dr], writes=[dr])
            P.op("dve", lambda e, r=r: e.reciprocal(out=r[:], in_=r[:]), reads=[dr], writes=[dr])
            for c in chunks:
                P.op("dve", lambda e, c=c, ns=ns, r=r: e.scalar_tensor_tensor(out=nT[:, c, ns], in0=hT[:, c, ns], scalar=gcol[:, c:c + 1], in1=r[:], op0=ALU.mult, op1=ALU.mult),
                     reads=[dhT, dr, dc], writes=[dnT[c]], merge=True)
    for c in range(5):
        P.dma("sp", nT_d[c * 128:(c + 1) * 128, :], nT[:, c, :], reads=[dnT[c]], writes=[C.dd("mla_nT")], merge=True)
    krT = P.sb("mkr", [96, 2048], BF16)
    dkr = Dep()
    ta_r = Ring(P, "mta", 2, [96, 512], F32)
    tb_r = Ring(P, "mtb", 2, [96, 512], F32)
    for nt in range(4):
        ns = slice(nt * 512, (nt + 1) * 512)
        bk, dbk = C.bank()
        bs, dbs = C.bank()
        mm_acc(P, bk[0:96, :], [(wi[:, kc, 576:672], xs[:, kc, ns]) for kc in range(8)], reads=dxs + dwi, dwrite=dbk)
        mm_acc(P, bs[0:96, :], [(wsw[:, kc, :], xs[:, kc, ns]) for kc in range(8)], reads=dxs + dwsw, dwrite=dbs)
        ta, dta = ta_r.next()
        tb, dtb = tb_r.next()
        P.op("dve", lambda e, ta=ta, bk=bk, ns=ns: e.tensor_tensor(out=ta[64:96, :], in0=bk[64:96, :], in1=cs[64:96, 0, ns], op=ALU.mult), reads=[dbk, dc], writes=[dta])
        P.op("dve", lambda e, tb=tb, bs=bs, ns=ns: e.tensor_tensor(out=tb[64:96, :], in0=bs[64:96, :], in1=cs[64:96, 1, ns], op=ALU.mult), reads=[dbs, dc], writes=[dtb])
        P.op("pool", lambda e, ta=ta, tb=tb, ns=ns: e.tensor_tensor(out=krT[64:96, ns], in0=ta[64:96, :], in1=tb[64:96, :], op=ALU.add), reads=[dta, dtb], writes=[dkr], merge=(nt > 0))
    P.dma("sp", kr_d, krT[64:96, :], reads=[dkr], writes=[C.dd("mla_krT")])


def stage_mla2(C):
    P = C.P
    P.begin_stage()
    nT_d = C.D("mla_nT", [640, 2048], BF16)
    kr_d = C.D("mla_krT", [32, 2048], BF16)
    cs_d = C.D("mla_cs2", [32, 2, 2048], F32)
    wq_d = C.D("mla_w_q_up", [384, 1536], F32)
    wqs_d = C.D("mla_w_q_sw", [384, 1536], F32)
    wk_d = C.D("mla_w_kv_k", [256, 1024], F32)
    wv_d = C.D("mla_w_kv_v", [256, 1024], F32)
    mixT = C.D("mixT", [1024, 2048], BF16)
    nT = P.sb("anT", [128, 5, 2048], BF16)
    dnT = [Dep() for _ in range(5)]
    for c in range(5):
        P.dma("sp", nT[:, c, :], nT_d[c * 128:(c + 1) * 128, :], reads=[C.dd("mla_nT")], writes=[dnT[c]])
    dq = dnT[0:3]
    dkv = dnT[3:5]
    dc = Dep()
    KRT = P.sb("aKRT", [96, 2048], BF16)
    P.dma("sp", KRT[64:96, :], kr_d, reads=[C.dd("mla_krT")], writes=[dc])
    cs = P.sb("acs", [96, 2, 2048], F32)
    P.dma("sp", cs[64:96, :, :], cs_d, writes=[dc], merge=True)
    wq, dwq = load_fm_bf16(C, "awq", wq_d, 3, 1536)
    wqs, dwqs = load_fm_bf16(C, "awqs", wqs_d, 3, 1536)
    wk, dwk = load_fm_bf16(C, "awk", wk_d, 2, 1024)
    wv, dwv = load_fm_bf16(C, "awv", wv_d, 2, 1024)
    onesf = P.sb("aones", [128, 64], F32)
    P.op("pool", lambda e: e.memset(onesf[:], 1.0), writes=[dc], merge=True)
    Vx = P.sb("aVx", [128, 16, 16, 65], BF16)
    dV = Dep()
    P.op("pool", lambda e: e.memset(Vx[:], 1.0), writes=[dV])
    for tt in range(16):
        for half in range(2):
            bk, dbk = C.bank()
            mm_acc(P, bk[:], [(nT[:, 3 + kc, tt * 128:(tt + 1) * 128], wv[:, kc, half * 512:(half + 1) * 512]) for kc in range(2)], reads=dkv + dwv, dwrite=dbk)
            P.op("act", lambda e, tt=tt, half=half, bk=bk: e.copy(out=Vx[:, tt, half * 8:(half + 1) * 8, 0:64], in_=bk[:].rearrange("p (h d) -> p h d", h=8)),
                 reads=[dbk], writes=[dV], merge=True)
    QT_r = Ring(P, "aQT", 2, [96, 2048], BF16)
    KT_r = Ring(P, "aKT", 2, [96, 2048], BF16)
    ta_r = Ring(P, "ata", 2, [96, 512], F32)
    tb_r = Ring(P, "atb", 2, [96, 512], F32)
    p_r = Ring(P, "apT", 4, [128, 512], BF16)
    rd_r = Ring(P, "ard", 2, [65, 512], F32)
    bs_r = Ring(P, "absb", 2, [64, 512], F32)
    yo_r = Ring(P, "ayo", 3, [64, 512], BF16)
    for h in range(16):
        QT, dQT = QT_r.next()
        KT, dKT = KT_r.next()
        for nt in range(4):
            ns = slice(nt * 512, (nt + 1) * 512)
            bq, dbq = C.bank()
            bs, dbs = C.bank()
            mm_acc(P, bq[0:96, :], [(wq[:, kc, h * 96:(h + 1) * 96], nT[:, kc, ns]) for kc in range(3)], reads=dq + dwq, dwrite=dbq)
            mm_acc(P, bs[0:96, :], [(wqs[:, kc, h * 96:(h + 1) * 96], nT[:, kc, ns]) for kc in range(3)], reads=dq + dwqs, dwrite=dbs)
            P.op("act", lambda e, QT=QT, ns=ns, bq=bq: e.copy(out=QT[0:64, ns], in_=bq[0:64, :]), reads=[dbq], writes=[dQT], merge=(nt > 0))
            ta, dta = ta_r.next()
            tb, dtb = tb_r.next()
            P.op("dve", lambda e, ta=ta, bq=bq, ns=ns: e.tensor_tensor(out=ta[64:96, :], in0=bq[64:96, :], in1=cs[64:96, 0, ns], op=ALU.mult), reads=[dbq, dc], writes=[dta])
            P.op("dve", lambda e, tb=tb, bs=bs, ns=ns: e.tensor_tensor(out=tb[64:96, :], in0=bs[64:96, :], in1=cs[64:96, 1, ns], op=ALU.mult), reads=[dbs, dc], writes=[dtb])
            P.op("pool", lambda e, QT=QT, ta=ta, tb=tb, ns=ns: e.tensor_tensor(out=QT[64:96, ns], in0=ta[64:96, :], in1=tb[64:96, :], op=ALU.add), reads=[dta, dtb], writes=[dQT], merge=True)
            bkk, dbkk = C.bank()
            mm_acc(P, bkk[0:64, :], [(wk[:, kc, h * 64:(h + 1) * 64], nT[:, 3 + kc, ns]) for kc in range(2)], reads=dkv + dwk, dwrite=dbkk)
            P.op("act", lambda e, KT=KT, ns=ns, bkk=bkk: e.copy(out=KT[0:64, ns], in_=bkk[0:64, :]), reads=[dbkk], writes=[dKT], merge=(nt > 0))
        P.op("pool", lambda e, KT=KT: e.tensor_copy(out=KT[64:96, :], in_=KRT[64:96, :]), reads=[dc], writes=[dKT], merge=True)
        for qc in range(4):
            qs = slice(qc * 512, (qc + 1) * 512)
            acc, dacc = C.bank(hold=True)

            def pv(kt, pT, dpT, acc=acc, dacc=dacc, h=h):
                P.op("pe", lambda e, acc=acc, kt=kt, h=h, pT=pT: e.matmul(acc[0:65, :], lhsT=Vx[:, kt, h, :], rhs=pT[:], start=(kt == 0), stop=(kt == 15)),
                     reads=[dV, dpT], writes=[dacc], merge=(kt > 0))
            pend = None
            for kt in range(16):
                sb_, dsb_ = C.bank()
                P.op("pe", lambda e, sb_=sb_, KT=KT, QT=QT, kt=kt, qs=qs: e.matmul(sb_[:], lhsT=KT[0:96, kt * 128:(kt + 1) * 128], rhs=QT[0:96, qs], start=True, stop=True),
                     reads=[dKT, dQT], writes=[dsb_])
                pT, dpT = p_r.next()
                P.op("act", lambda e, pT=pT, sb_=sb_: e.activation(out=pT[:], in_=sb_[:], func=AF.Exp, scale=MLA_SCALE), reads=[dsb_], writes=[dpT])
                if pend is not None:
                    pv(*pend)
                pend = (kt, pT, dpT)
            pv(*pend)
            rd, drd = rd_r.next()
            P.op("dve", lambda e, rd=rd, acc=acc: e.reciprocal(out=rd[64:65, :], in_=acc[64:65, :]), reads=[dacc], writes=[drd])
            bb, dbb = C.bank()
            P.op("pe", lambda e, bb=bb, rd=rd: e.matmul(bb[0:64, :], lhsT=onesf[64:65, 0:64], rhs=rd[64:65, :], start=True, stop=True), reads=[drd, dc], writes=[dbb])
            bsb, dbsb = bs_r.next()
            P.op("act", lambda e, bsb=bsb, bb=bb: e.copy(out=bsb[:], in_=bb[0:64, :]), reads=[dbb], writes=[dbsb])
            yo, dyo = yo_r.next()
            P.op("dve", lambda e, yo=yo, acc=acc, bsb=bsb: e.tensor_tensor(out=yo[:], in0=acc[0:64, :], in1=bsb[:], op=ALU.mult), reads=[dacc, dbsb], writes=[dyo])
            C.release_all()
            P.dma("sp", mixT[h * 64:(h + 1) * 64, qs], yo[:], reads=[dyo], writes=[C.dd("mixT")], merge=True)


def _rep128(v):
    v = np.asarray(v, np.float32)
    return np.ascontiguousarray(np.broadcast_to(v[None, :], (128, v.shape[0])))


def shared_inputs(inp):
    f32 = lambda a: np.ascontiguousarray(np.asarray(a, np.float32))
    s = {}
    s["ab_w_in"] = f32(inp["ab_w_in"][0])
    s["na_tab"] = na_tables(np.asarray(inp["na_rpb"][0], np.float32))
    s.update(hyena_consts())
    s["hy_f_w1"] = f32(inp["hy_f_w1"][0])
    s["hy_f_w2"] = f32(inp["hy_f_w2"][0])
    s["hy_f_w3"] = f32(inp["hy_f_w3"][0])
    s["hy_cols"] = f32(np.stack([inp["hy_f_b1"][0], inp["hy_f_freq"][0], inp["hy_f_b2"][0]], axis=1))
    s["hy_skipb"] = np.stack([_rep128(inp["hy_skip"][0][0]), _rep128(inp["hy_skip"][0][1])])
    s["hy_cw"] = f32(np.asarray(inp["hy_conv_w"][0]).reshape(3, 12, 128).transpose(2, 1, 0))
    s["hy_cb"] = f32(np.asarray(inp["hy_conv_b"][0]).reshape(12, 128).T)
    s["ident"] = np.eye(128, dtype=np.float32)
    esel = np.zeros((16, 16, 128), np.float32)
    for e in range(16):
        esel[e, e, :] = 1.0
    s["esel"] = esel
    s["iota_col"] = (np.arange(16)[None, :] * 128 + np.arange(128)[:, None]).astype(np.float32)
    s["iota_row"] = _rep128(np.arange(2048, dtype=np.float32))
    s["ab_w_out"] = f32(inp["ab_w_out"][0])
    w_in = np.asarray(inp["mla_w_in"][0], np.float32)
    perm = np.concatenate([np.arange(16, 32), np.arange(0, 16)])
    s["mla_w_in"] = f32(w_in)
    s["mla_w_in_sw"] = f32(np.concatenate([w_in[:, 576:640], w_in[:, 640 + perm]], axis=1))
    wq = np.asarray(inp["mla_w_q_up"][0], np.float32)
    wqs = wq.reshape(384, 16, 96).copy()
    wqs[:, :, 64:] = wqs[:, :, 64 + perm]
    s["mla_w_q_up"] = f32(wq)
    s["mla_w_q_sw"] = f32(wqs.reshape(384, 1536))
    wkv = np.asarray(inp["mla_w_kv_up"][0], np.float32).reshape(256, 16, 128)
    s["mla_w_kv_k"] = f32(wkv[:, :, :64].reshape(256, 1024))
    s["mla_w_kv_v"] = f32(wkv[:, :, 64:].reshape(256, 1024))
    s["mla_gcols"] = f32(np.concatenate([np.asarray(inp["mla_q_norm"][0]).reshape(3, 128).T,
                                         np.asarray(inp["mla_kv_norm"][0]).reshape(2, 128).T], axis=1))
    s.update(mla_consts())
    s["mla_w_out"] = f32(inp["mla_w_out"][0])
    for li in range(2):
        s[f"ln1_g{li}"] = _rep128(inp["ln1_g"][li])
        s[f"ln1_b{li}"] = _rep128(inp["ln1_b"][li])
        s[f"ln2_g{li}"] = _rep128(inp["ln2_g"][li])
        s[f"ln2_b{li}"] = _rep128(inp["ln2_b"][li])
        s[f"moe_router{li}"] = f32(inp["moe_router"][li])
        s[f"moe_w_gate{li}"] = f32(inp["moe_w_gate"][li])
        s[f"moe_w_up{li}"] = f32(inp["moe_w_up"][li])
        s[f"moe_w_down{li}"] = f32(inp["moe_w_down"][li])
        s[f"ple_gate{li}"] = f32(inp["ple_gate"][li])
        s[f"ple_proj{li}"] = f32(inp["ple_proj"][li])
    return s


PER_CORE = ("x_tm", "xT", "pT0", "pT1")


def build_full(shared_names):
    C = Ctx(ext_in=set(shared_names) | set(PER_CORE), ext_out={"out"})
    stage_a1(C)
    stage_a2(C)
    stage_hy_filter(C)
    stage_hy_prep(C)
    stage_hy_conv(C)
    stage_proj_ln(C, "ab_w_out", "x_tm", "ln1", 0, "x1_0")
    stage_moe1(C, 0, "x1_0")
    stage_moe2(C, 0, "x1_0", "x2_0")
    stage_ple(C, 0, "x2_0", "x3_0", True)
    stage_mla1(C, "x3_0T")
    stage_mla2(C)
    stage_proj_ln(C, "mla_w_out", "x3_0", "ln1", 1, "x1_1")
    stage_moe1(C, 1, "x1_1")
    stage_moe2(C, 1, "x1_1", "x2_1")
    stage_ple(C, 1, "x2_1", "out", False)
    C.P.finish()
    return C


def kernel(**inputs):
    inp = {k: np.asarray(v) for k, v in inputs.items()}
    shared = shared_inputs(inp)
    x = np.asarray(inp["x"], np.float32)
    p = np.asarray(inp["p"], np.float32)
    C = build_full(shared.keys())
    used = set(C.dram.keys())
    in_maps = []
    for b in range(8):
        m = {k: v for k, v in shared.items() if k in used}
        m["x_tm"] = np.ascontiguousarray(x[b])
        m["xT"] = np.ascontiguousarray(x[b].T)
        m["pT0"] = np.ascontiguousarray(p[0, b].T)
        m["pT1"] = np.ascontiguousarray(p[1, b].T)
        in_maps.append(m)
    res = run_bass_kernel_spmd(C.nc, in_maps, core_ids=list(range(8)))
    return np.stack([np.asarray(r["out"], np.float32) for r in res.results], axis=0)
```

```python
from contextlib import ExitStack
import math
import numpy as np
import ml_dtypes
import concourse.bass as bass
import concourse.mybir as mybir
from concourse.bass_utils import run_bass_kernel_spmd

F32 = mybir.dt.float32
BF16 = mybir.dt.bfloat16
I32 = mybir.dt.int32
U32 = mybir.dt.uint32
AF = mybir.ActivationFunctionType
ALU = mybir.AluOpType
AX = mybir.AxisListType
NPBF = ml_dtypes.bfloat16

D_MODEL = 1024
SEQ = 2048
NT = SEQ // 128
ALPHA = 4.0 ** 0.25
EPS = 1e-5

ENGS = ("pe", "act", "dve", "pool", "sp")
N_DMA_SEMS = 16


class Dep:
    __slots__ = ("w", "r", "name")

    def __init__(self, name=""):
        self.w = {}
        self.r = {}
        self.name = name


class Prog:
    def __init__(self, nc, strict=True):
        self.nc = nc
        self.es = ExitStack()
        self.q = {e: [] for e in ENGS}
        self.cnt = {e: 0 for e in ENGS}
        self.seen = {e: {} for e in ENGS}
        self.sem = {}
        self.strict = strict
        for e in ENGS:
            self.sem[e] = self.es.enter_context(nc.semaphore("s_" + e))
        self.dma_sems = {}
        self.dma_tot = {}
        self.dma_rr = {}
        for e in ("sp", "pool", "act"):
            self.dma_sems[e] = [self.es.enter_context(nc.semaphore(f"d_{e}{i}")) for i in range(N_DMA_SEMS)]
            self.dma_tot[e] = [0] * N_DMA_SEMS
            self.dma_rr[e] = 0
        self.all_events = {}
        self.n_ops = 0
        self.stage_es = None
        self.uid = 0

    def begin_stage(self):
        self.barrier()
        if self.stage_es is not None:
            self.stage_es.close()
        self.stage_es = ExitStack()

    def sb(self, name, shape, dt):
        self.uid += 1
        t = self.stage_es.enter_context(self.nc.sbuf_tensor(f"{name}_{self.uid}", list(shape), dt))
        return t

    def ps(self, name, shape, dt=F32):
        return self.es.enter_context(self.nc.psum_tensor(name, list(shape), dt))

    def _semobj(self, key):
        if isinstance(key, str):
            return self.sem[key]
        e, i = key
        return self.dma_sems[e][i]

    def _need(self, eng, reads, writes, merge):
        need = {}

        def add(k, v):
            if k == eng and not self.strict:
                return
            if self.seen[eng].get(k, 0) >= v:
                return
            if need.get(k, 0) < v:
                need[k] = v
        for d in reads:
            for k, v in d.w.items():
                add(k, v)
        for d in writes:
            if not merge:
                for k, v in d.w.items():
                    add(k, v)
            for k, v in d.r.items():
                add(k, v)
        for k, v in need.items():
            self.seen[eng][k] = v
        return list(need.items())

    def _commit(self, ev, reads, writes, merge):
        k, v = ev
        for d in reads:
            if d.r.get(k, 0) < v:
                d.r[k] = v
        for d in writes:
            if merge:
                d.w[k] = v
            else:
                d.w = {k: v}
                d.r = {}
        self.all_events[k] = v

    def op(self, eng, fn, reads=(), writes=(), merge=False):
        waits = self._need(eng, reads, writes, merge)
        self.cnt[eng] += 1
        ev = (eng, self.cnt[eng])
        sem = self.sem[eng]
        waitobjs = [(self._semobj(k), v) for k, v in waits]

        def emit(e, fn=fn, waitobjs=waitobjs, sem=sem):
            for s, v in waitobjs:
                e.wait_ge(s, v)
            fn(e).then_inc(sem, 1)
        self.q[eng].append(emit)
        self._commit(ev, reads, writes, merge)
        self.n_ops += 1
        return ev

    def dma(self, eng, out, in_, reads=(), writes=(), merge=False, **kw):
        i = self.dma_rr[eng]
        self.dma_rr[eng] = (i + 1) % N_DMA_SEMS
        key = (eng, i)
        prev = self.dma_tot[eng][i]
        waits = self._need(eng, reads, writes, merge)
        if prev > 0 and self.seen[eng].get(key, 0) < prev:
            waits.append((key, prev))
            self.seen[eng][key] = prev
        self.dma_tot[eng][i] = prev + 16
        ev = (key, prev + 16)
        sem = self.dma_sems[eng][i]
        waitobjs = [(self._semobj(k), v) for k, v in waits]

        def emit(e, waitobjs=waitobjs, sem=sem, out=out, in_=in_, kw=kw):
            for s, v in waitobjs:
                e.wait_ge(s, v)
            e.dma_start(out=out, in_=in_, **kw).then_inc(sem, 16)
        self.q[eng].append(emit)
        self._commit(ev, reads, writes, merge)
        self.n_ops += 1
        return ev

    def coll(self, kind, out, in_, reads=(), writes=()):
        eng = "pool"
        i = self.dma_rr[eng]
        self.dma_rr[eng] = (i + 1) % N_DMA_SEMS
        key = (eng, i)
        prev = self.dma_tot[eng][i]
        waits = self._need(eng, reads, writes, False)
        if prev > 0 and self.seen[eng].get(key, 0) < prev:
            waits.append((key, prev))
            self.seen[eng][key] = prev
        self.dma_tot[eng][i] = prev + 16
        ev = (key, prev + 16)
        sem = self.dma_sems[eng][i]
        waitobjs = [(self._semobj(k), v) for k, v in waits]

        def emit(e, waitobjs=waitobjs, sem=sem, out=out, in_=in_, kind=kind):
            for s_, v in waitobjs:
                e.wait_ge(s_, v)
            e.collective_compute(kind, ALU.bypass, replica_groups=[list(range(8))], ins=[in_], outs=[out]).then_inc(sem, 16)
        self.q[eng].append(emit)
        self._commit(ev, reads, writes, False)
        self.n_ops += 1
        return ev

    def barrier(self):
        snap = dict(self.all_events)
        for eng in ENGS:
            waits = []
            for k, v in snap.items():
                if k == eng:
                    continue
                if self.seen[eng].get(k, 0) >= v:
                    continue
                waits.append((self._semobj(k), v))
                self.seen[eng][k] = v
            if waits:
                def emit(e, waits=waits):
                    for s, v in waits:
                        e.wait_ge(s, v)
                self.q[eng].append(emit)

    def finish(self):
        self.barrier()
        nc = self.nc
        q = self.q
        with nc.Block() as block:
            @block.tensor
            def _(e):
                for f in q["pe"]:
                    f(e)

            @block.scalar
            def _(e):
                for f in q["act"]:
                    f(e)

            @block.vector
            def _(e):
                for f in q["dve"]:
                    f(e)

            @block.gpsimd
            def _(e):
                for f in q["pool"]:
                    f(e)

            @block.sync
            def _(e):
                for f in q["sp"]:
                    f(e)
        if self.stage_es is not None:
            self.stage_es.close()
        self.es.close()


class Ring:
    def __init__(self, P, name, n, shape, dt):
        self.bufs = [(P.sb(f"{name}{i}", shape, dt), Dep(f"{name}{i}")) for i in range(n)]
        self.i = 0

    def next(self):
        b = self.bufs[self.i]
        self.i = (self.i + 1) % len(self.bufs)
        return b


class Ctx:
    def __init__(self, ext_in, ext_out):
        self.nc = bass.Bass("TRN2", target_bir_lowering=False)
        self.P = Prog(self.nc)
        self.ext_in = set(ext_in)
        self.ext_out = set(ext_out)
        self.dram = {}
        self.ddep = {}
        P = self.P
        self.banks = [(P.ps(f"bank{i}", [128, 512], F32), Dep(f"bank{i}")) for i in range(8)]
        self.bank_i = 0
        self.held = set()

    def D(self, name, shape=None, dt=F32):
        if name in self.dram:
            return self.dram[name]
        kind = "Internal"
        if name in self.ext_in:
            kind = "ExternalInput"
        elif name in self.ext_out:
            kind = "ExternalOutput"
        t = self.nc.dram_tensor(name, list(shape), dt, kind=kind).ap()
        self.dram[name] = t
        self.ddep[name] = Dep(name)
        return t

    def dd(self, name):
        return self.ddep[name]

    def bank(self, hold=False):
        for _ in range(8):
            i = self.bank_i
            self.bank_i = (self.bank_i + 1) % 8
            if i not in self.held:
                if hold:
                    self.held.add(i)
                return self.banks[i]
        raise RuntimeError("no free PSUM bank")

    def release_all(self):
        self.held = set()

    def release(self, b):
        for i, bb in enumerate(self.banks):
            if bb[0] is b[0]:
                self.held.discard(i)


def mm_acc(P, out, pairs, reads, dwrite):
    n = len(pairs)
    for i, (l, r) in enumerate(pairs):
        P.op("pe", lambda e, l=l, r=r, i=i: e.matmul(out, lhsT=l, rhs=r, start=(i == 0), stop=(i == n - 1)),
             reads=reads, writes=[dwrite], merge=(i > 0))


def load_fm_bf16(C, name, src, kc_n, width, eng="pool"):
    P = C.P
    t = P.sb(name, [128, kc_n, width], BF16)
    deps = [Dep(f"{name}{k}") for k in range(kc_n)]
    for k in range(kc_n):
        P.dma(eng, t[:, k, :], src[k * 128:(k + 1) * 128, :], writes=[deps[k]])
    return t, deps


def stage_a1(C):
    P = C.P
    P.begin_stage()
    xT = C.D("xT", [1024, 2048], F32)
    w_in = C.D("ab_w_in", [1024, 3072], F32)
    qkT = C.D("qkT", [1024, 2048], BF16)
    v_tm = C.D("v_tm", [2048, 512], BF16)
    hbT = C.D("hbT", [1536, 2048], F32)
    xs, dxs = load_fm_bf16(C, "xTb", xT, 8, 2048)
    ws, dws = load_fm_bf16(C, "winb", w_in, 8, 3072)
    st_b = Ring(P, "a1sb", 3, [128, 2048], BF16)
    st_f = Ring(P, "a1sf", 3, [128, 2048], F32)
    ev_i = 0
    for mc in list(range(8)) + list(range(12, 24)):
        is_hb = mc >= 12
        stg, dstg = (st_f if is_hb else st_b).next()
        for nt in range(4):
            bk, dbk = C.bank()
            mm_acc(P, bk[:], [(ws[:, kc, mc * 128:(mc + 1) * 128], xs[:, kc, nt * 512:(nt + 1) * 512]) for kc in range(8)],
                   reads=dxs + dws, dwrite=dbk)
            eng = "act" if ev_i % 2 == 0 else "dve"
            ev_i += 1
            o = stg[:, nt * 512:(nt + 1) * 512]
            if eng == "act":
                P.op("act", lambda e, o=o, bk=bk: e.copy(out=o, in_=bk[:]), reads=[dbk], writes=[dstg], merge=(nt > 0))
            else:
                P.op("dve", lambda e, o=o, bk=bk: e.tensor_copy(out=o, in_=bk[:]), reads=[dbk], writes=[dstg], merge=(nt > 0))
        if is_hb:
            P.dma("act", hbT[(mc - 12) * 128:(mc - 11) * 128, :], stg[:], reads=[dstg], writes=[C.dd("hbT")], merge=True)
        else:
            P.dma("act", qkT[mc * 128:(mc + 1) * 128, :], stg[:], reads=[dstg], writes=[C.dd("qkT")], merge=True)
    st_v = Ring(P, "a1sv", 3, [128, 512], BF16)
    for tt in range(NT):
        bk, dbk = C.bank()
        mm_acc(P, bk[:], [(xs[:, kc, tt * 128:(tt + 1) * 128], ws[:, kc, 1024:1536]) for kc in range(8)],
               reads=dxs + dws, dwrite=dbk)
        stg, dstg = st_v.next()
        if tt % 2 == 0:
            P.op("act", lambda e, stg=stg, bk=bk: e.copy(out=stg[:], in_=bk[:]), reads=[dbk], writes=[dstg])
        else:
            P.op("dve", lambda e, stg=stg, bk=bk: e.tensor_copy(out=stg[:], in_=bk[:]), reads=[dbk], writes=[dstg])
        P.dma("act", v_tm[tt * 128:(tt + 1) * 128, :], stg[:], reads=[dstg], writes=[C.dd("v_tm")], merge=True)


def na_plan():
    rows, wr = 32, 8
    r0 = np.clip(np.arange(rows) - wr // 2, 0, rows - wr)
    plan = []
    keys = {}
    for i in range(16):
        lo = r0[2 * i] // 2
        hi = (r0[2 * i + 1] + 7) // 2
        lst = []
        for j in range(lo, hi + 1):
            val = []
            for ak in range(2):
                for aq in range(2):
                    r = 2 * i + aq
                    kr = 2 * j + ak
                    val.append(bool(r0[r] <= kr <= r0[r] + 7))
            key = (j - i, tuple(val))
            if key not in keys:
                keys[key] = len(keys)
            lst.append((j, keys[key]))
        plan.append(lst)
    return plan, keys


def na_tables(rpb):
    plan, keys = na_plan()
    c = np.arange(64)
    c0 = np.clip(c - 8, 0, 48)
    col_ok = (c[None, :] >= c0[:, None]) & (c[None, :] < c0[:, None] + 16)
    dc_idx = np.clip(c[None, :] - c[:, None], -15, 15) + 15
    tab = np.full((len(keys), 2, 64, 8, 2, 64), -1e30, np.float32)
    for (delta, val), tid in keys.items():
        vi = 0
        for ak in range(2):
            for aq in range(2):
                ok = val[vi]
                vi += 1
                if not ok:
                    continue
                dr = 2 * delta + ak - aq
                b = rpb[:, dr + 7, :][:, dc_idx]
                b = np.where(col_ok[None], b, np.float32(-1e30))
                tab[tid, ak, :, :, aq, :] = b.transpose(2, 0, 1)
    return tab.reshape(len(keys), 128, 8, 128)


def stage_a2(C):
    P = C.P
    P.begin_stage()
    plan, keys = na_plan()
    ntab = len(keys)
    qkT = C.D("qkT", [1024, 2048], BF16)
    v_tm = C.D("v_tm", [2048, 512], BF16)
    tab_d = C.D("na_tab", [ntab, 128, 8, 128], F32)
    mixT = C.D("mixT", [1024, 2048], BF16)
    QT = P.sb("QT", [128, 4, 2048], BF16)
    KT = P.sb("KT", [128, 4, 2048], BF16)
    dQ = [Dep() for _ in range(4)]
    dK = [Dep() for _ in range(4)]
    for hp in range(4):
        P.dma("sp", QT[:, hp, :], qkT[hp * 128:(hp + 1) * 128, :], reads=[C.dd("qkT")], writes=[dQ[hp]])
        P.dma("sp", KT[:, hp, :], qkT[512 + hp * 128:512 + (hp + 1) * 128, :], reads=[C.dd("qkT")], writes=[dK[hp]])
    tab = P.sb("natab", [128, ntab, 8, 128], F32)
    dtab = Dep()
    for t in range(ntab):
        P.dma("sp", tab[:, t, :, :], tab_d[t], writes=[dtab], merge=True)
    Vx = P.sb("Vx", [128, 8, NT, 128], BF16)
    dV = Dep()
    P.op("pool", lambda e: e.memset(Vx[:], 0.0), writes=[dV])
    vv = v_tm.rearrange("(t p) c -> p t c", p=128)
    for h in range(8):
        a = h % 2
        P.dma("sp", Vx[:, h, :, a * 64:(a + 1) * 64], vv[:, :, h * 64:(h + 1) * 64], reads=[C.dd("v_tm")], writes=[dV], merge=(h > 0))
    ones2 = P.sb("ones2", [128, 2, 128], BF16)
    dones = Dep()
    P.op("pool", lambda e: e.memset(ones2[:], 0.0), writes=[dones])
    P.op("pool", lambda e: e.memset(ones2[:, 0, 0:64], 1.0), writes=[dones])
    P.op("pool", lambda e: e.memset(ones2[:, 1, 64:128], 1.0), writes=[dones])
    yaT = P.sb("yaT", [128, 4, 2048], BF16)
    dya = [Dep() for _ in range(4)]
    s_ring = Ring(P, "na_s", 3, [128, 640], F32)
    p_ring = Ring(P, "na_p", 4, [128, 640], BF16)
    rd_ring = Ring(P, "na_rd", 2, [128, 128], F32)
    units = [(i, hp, a) for i in range(16) for hp in range(4) for a in range(2)]
    pair = {}

    def phase1(u):
        i, hp, a = u
        q0 = i * 128
        h = hp * 2 + a
        pa = slice(a * 64, (a + 1) * 64)
        lst = plan[i]
        nkb = len(lst)
        bA, dbA = C.bank()
        bB, dbB = (C.bank() if nkb > 4 else (None, None))
        ssb, dss = s_ring.next()
        for jj, (j, tid) in enumerate(lst):
            bk, dbk = (bA, dbA) if jj < 4 else (bB, dbB)
            o = bk[:, (jj % 4) * 128:(jj % 4 + 1) * 128]
            P.op("pe", lambda e, o=o, j=j, pa=pa, hp=hp, q0=q0: e.matmul(
                o, lhsT=KT[pa, hp, j * 128:(j + 1) * 128], rhs=QT[pa, hp, q0:q0 + 128], start=True, stop=True),
                reads=[dK[hp], dQ[hp]], writes=[dbk], merge=(jj % 4 > 0))
        for jj, (j, tid) in enumerate(lst):
            bk, dbk = (bA, dbA) if jj < 4 else (bB, dbB)
            o = bk[:, (jj % 4) * 128:(jj % 4 + 1) * 128]
            P.op("dve", lambda e, o=o, jj=jj, tid=tid, h=h, ssb=ssb: e.scalar_tensor_tensor(
                out=ssb[:, jj * 128:(jj + 1) * 128], in0=o, scalar=0.125, in1=tab[:, tid, h, :],
                op0=ALU.mult, op1=ALU.add), reads=[dbk, dtab], writes=[dss], merge=(jj > 0))
        pT, dpT = p_ring.next()
        P.op("act", lambda e, pT=pT, ssb=ssb, nkb=nkb: e.activation(
            out=pT[:, 0:nkb * 128], in_=ssb[:, 0:nkb * 128], func=AF.Exp), reads=[dss], writes=[dpT])
        return (pT, dpT)

    def phase2(u, pp):
        i, hp, a = u
        q0 = i * 128
        h = hp * 2 + a
        pT, dpT = pp
        lst = plan[i]
        nkb = len(lst)
        if a == 0:
            pair[(i, hp)] = (C.bank(hold=True), C.bank(hold=True))
        (bo, dbo), (bd, dbd) = pair[(i, hp)]
        for jj, (j, tid) in enumerate(lst):
            first = (a == 0 and jj == 0)
            last = (a == 1 and jj == nkb - 1)
            P.op("pe", lambda e, bo=bo, h=h, j=j, pT=pT, jj=jj, first=first, last=last: e.matmul(
                bo[:, 0:128], lhsT=Vx[:, h, j, :], rhs=pT[:, jj * 128:(jj + 1) * 128], start=first, stop=last),
                reads=[dV, dpT], writes=[dbo], merge=(not first))
            P.op("pe", lambda e, bd=bd, a=a, pT=pT, jj=jj, first=first, last=last: e.matmul(
                bd[:, 0:128], lhsT=ones2[:, a, :], rhs=pT[:, jj * 128:(jj + 1) * 128], start=first, stop=last),
                reads=[dones, dpT], writes=[dbd], merge=(not first))
        if a == 1:
            rd, drd = rd_ring.next()
            P.op("dve", lambda e, rd=rd, bd=bd: e.reciprocal(out=rd[:], in_=bd[:, 0:128]), reads=[dbd], writes=[drd])
            P.op("dve", lambda e, rd=rd, bo=bo, hp=hp, q0=q0: e.tensor_tensor(
                out=yaT[:, hp, q0:q0 + 128], in0=bo[:, 0:128], in1=rd[:], op=ALU.mult),
                reads=[dbo, drd], writes=[dya[hp]], merge=True)
            C.release(pair[(i, hp)][0])
            C.release(pair[(i, hp)][1])
            del pair[(i, hp)]

    pend = None
    for u in units:
        pp = phase1(u)
        if pend is not None:
            phase2(*pend)
        pend = (u, pp)
    phase2(*pend)
    for hp in range(4):
        P.dma("sp", mixT[hp * 128:(hp + 1) * 128, :], yaT[:, hp, :], reads=[dya[hp]], writes=[C.dd("mixT")], merge=True)


class LNBufs:
    def __init__(self, P, name):
        self.stats = Ring(P, name + "st", 2, [128, 2, 6], F32)
        self.mv = Ring(P, name + "mv", 2, [128, 2], F32)
        self.rstd = Ring(P, name + "rs", 2, [128, 1], F32)
        self.nmr = Ring(P, name + "nm", 2, [128, 1], F32)


def layer_norm_tile(P, lb, r, dr, gb, bb, dgb, y, dy):
    st, dst = lb.stats.next()
    mv, dmv = lb.mv.next()
    rs, drs = lb.rstd.next()
    nm, dnm = lb.nmr.next()
    P.op("dve", lambda e: e.bn_stats(out=st[:, 0, :], in_=r[:, 0:512]), reads=[dr], writes=[dst])
    P.op("dve", lambda e: e.bn_stats(out=st[:, 1, :], in_=r[:, 512:1024]), reads=[dr], writes=[dst], merge=True)
    P.op("dve", lambda e: e.bn_aggr(out=mv[:], in_=st[:]), reads=[dst], writes=[dmv])
    P.op("dve", lambda e: e.tensor_scalar(out=rs[:], in0=mv[:, 1:2], scalar1=EPS, scalar2=None, op0=ALU.add), reads=[dmv], writes=[drs])
    P.op("act", lambda e: e.activation(out=rs[:], in_=rs[:], func=AF.Sqrt), reads=[drs], writes=[drs])
    P.op("dve", lambda e: e.reciprocal(out=rs[:], in_=rs[:]), reads=[drs], writes=[drs])
    P.op("dve", lambda e: e.scalar_tensor_tensor(out=nm[:], in0=mv[:, 0:1], scalar=-1.0, in1=rs[:], op0=ALU.mult, op1=ALU.mult),
         reads=[dmv, drs], writes=[dnm])
    P.op("act", lambda e: e.activation(out=y[:], in_=r[:], func=AF.Identity, bias=nm[:], scale=rs[:]),
         reads=[dr, drs, dnm], writes=[dy])
    P.op("pool", lambda e: e.tensor_tensor(out=y[:], in0=y[:], in1=gb[:], op=ALU.mult), reads=[dy, dgb], writes=[dy])
    P.op("pool", lambda e: e.tensor_tensor(out=y[:], in0=y[:], in1=bb[:], op=ALU.add), reads=[dy, dgb], writes=[dy])


def stage_proj_ln(C, w_name, x_name, lnname, li, out_name):
    P = C.P
    P.begin_stage()
    mixT = C.D("mixT", [1024, 2048], BF16)
    w = C.D(w_name, [1024, 1024], F32)
    x = C.D(x_name, [2048, 1024], F32)
    g_d = C.D(f"{lnname}_g{li}", [128, 1024], F32)
    b_d = C.D(f"{lnname}_b{li}", [128, 1024], F32)
    out = C.D(out_name, [2048, 1024], F32)
    ms = P.sb("ms", [128, 8, 2048], BF16)
    dms = [Dep() for _ in range(8)]
    for kc in range(8):
        P.dma("sp", ms[:, kc, :], mixT[kc * 128:(kc + 1) * 128, :], reads=[C.dd("mixT")], writes=[dms[kc]])
    ws, dws = load_fm_bf16(C, "wout", w, 8, 1024)
    gb = P.sb("gb", [128, 1024], F32)
    bb = P.sb("bb", [128, 1024], F32)
    dgb = Dep()
    P.dma("sp", gb[:], g_d, writes=[dgb])
    P.dma("sp", bb[:], b_d, writes=[dgb], merge=True)
    lb = LNBufs(P, "ln")
    xr = Ring(P, "xr", 3, [128, 1024], F32)
    rr = Ring(P, "rr", 2, [128, 1024], F32)
    yr = Ring(P, "yr", 2, [128, 1024], F32)
    for tt in range(NT):
        xt, dxt = xr.next()
        P.dma("sp", xt[:], x[tt * 128:(tt + 1) * 128, :], reads=[C.dd(x_name)], writes=[dxt])
        r, dr = rr.next()
        for half in range(2):
            bk, dbk = C.bank()
            hs = slice(half * 512, (half + 1) * 512)
            mm_acc(P, bk[:], [(ms[:, kc, tt * 128:(tt + 1) * 128], ws[:, kc, hs]) for kc in range(8)], reads=dms + dws, dwrite=dbk)
            P.op("dve", lambda e, r=r, xt=xt, bk=bk, hs=hs: e.scalar_tensor_tensor(
                out=r[:, hs], in0=xt[:, hs], scalar=ALPHA, in1=bk[:], op0=ALU.mult, op1=ALU.add),
                reads=[dxt, dbk], writes=[dr], merge=(half > 0))
        y, dy = yr.next()
        layer_norm_tile(P, lb, r, dr, gb, bb, dgb, y, dy)
        P.dma("pool", out[tt * 128:(tt + 1) * 128, :], y[:], reads=[dy], writes=[C.dd(out_name)], merge=True)


def stage_moe1(C, li, x_name):
    P = C.P
    P.begin_stage()
    x1 = C.D(x_name, [2048, 1024], F32)
    wr_d = C.D(f"moe_router{li}", [1024, 16], F32)
    ident_d = C.D("ident", [128, 128], F32)
    esel_d = C.D("esel", [16, 16, 128], F32)
    iotac_d = C.D("iota_col", [128, 16], F32)
    xe_all = C.D("xeT_all", [16, 128, 8, 256], BF16)
    idxc_d = C.D("moe_idxc", [128, 2, 16], F32)
    gc_d = C.D("moe_gc", [128, 2, 16], F32)
    wr = P.sb("wr", [128, 8, 16], F32)
    dcst = Dep()
    P.dma("sp", wr[:], wr_d.rearrange("(kc p) e -> p kc e", p=128), writes=[dcst])
    ident = P.sb("ident", [128, 128], F32)
    P.dma("sp", ident[:], ident_d, writes=[dcst], merge=True)
    esel = P.sb("esel", [16, 16, 128], F32)
    P.dma("sp", esel[:], esel_d, writes=[dcst], merge=True)
    iotac = P.sb("iotac", [128, 16], F32)
    P.dma("sp", iotac[:], iotac_d, writes=[dcst], merge=True)
    x1b = P.sb("x1b", [128, NT, 1024], BF16)
    dx1b = [Dep() for _ in range(NT)]
    affT = P.sb("affT", [16, 2048], F32)
    daffT = Dep()
    xr = Ring(P, "m1x", 2, [128, 1024], F32)
    xTr = Ring(P, "m1xT", 2, [128, 8, 128], F32)
    sm_r = Ring(P, "m1sm", 2, [128, 4], F32)
    ex_r = Ring(P, "m1ex", 2, [128, 16], F32)
    af_r = Ring(P, "m1af", 2, [128, 16], F32)
    for tt in range(NT):
        xt, dxt = xr.next()
        P.dma("sp", xt[:], x1[tt * 128:(tt + 1) * 128, :], reads=[C.dd(x_name)], writes=[dxt])
        P.op("act", lambda e, xt=xt, tt=tt: e.copy(out=x1b[:, tt, :], in_=xt[:]), reads=[dxt], writes=[dx1b[tt]])
        xT, dxT = xTr.next()
        for hb in range(2):
            bk, dbk = C.bank()
            for k4 in range(4):
                kc = hb * 4 + k4
                P.op("pe", lambda e, bk=bk, k4=k4, kc=kc, xt=xt: e.transpose(bk[:, k4 * 128:(k4 + 1) * 128], xt[:, kc * 128:(kc + 1) * 128], ident[:]),
                     reads=[dxt, dcst], writes=[dbk], merge=(k4 > 0))
            P.op("dve", lambda e, xT=xT, hb=hb, bk=bk: e.tensor_copy(out=xT[:, hb * 4:(hb + 1) * 4, :], in_=bk[:].rearrange("p (k t) -> p k t", k=4)),
                 reads=[dbk], writes=[dxT], merge=(hb > 0))
        bk, dbk = C.bank()
        mm_acc(P, bk[:, 0:16], [(xT[:, kc, :], wr[:, kc, :]) for kc in range(8)], reads=[dxT, dcst], dwrite=dbk)
        sm, dsm = sm_r.next()
        ex, dex = ex_r.next()
        af, daf = af_r.next()
        P.op("dve", lambda e, sm=sm, bk=bk: e.reduce_max(out=sm[:, 0:1], in_=bk[:, 0:16], axis=AX.X), reads=[dbk], writes=[dsm])
        P.op("dve", lambda e, sm=sm: e.tensor_scalar(out=sm[:, 1:2], in0=sm[:, 0:1], scalar1=-1.0, scalar2=None, op0=ALU.mult),
             reads=[dsm], writes=[dsm])
        P.op("act", lambda e, ex=ex, bk=bk, sm=sm: e.activation(out=ex[:], in_=bk[:, 0:16], func=AF.Exp, bias=sm[:, 1:2], accum_out=sm[:, 2:3]),
             reads=[dbk, dsm], writes=[dex, dsm])
        P.op("dve", lambda e, sm=sm: e.reciprocal(out=sm[:, 3:4], in_=sm[:, 2:3]), reads=[dsm], writes=[dsm])
        P.op("dve", lambda e, af=af, ex=ex, sm=sm: e.tensor_scalar(out=af[:], in0=ex[:], scalar1=sm[:, 3:4], scalar2=None, op0=ALU.mult),
             reads=[dex, dsm], writes=[daf])
        bk2, dbk2 = C.bank()
        P.op("pe", lambda e, bk2=bk2, af=af: e.transpose(bk2[0:16, 0:128], af[:], ident[:]), reads=[daf, dcst], writes=[dbk2])
        P.op("act", lambda e, bk2=bk2, tt=tt: e.copy(out=affT[:, tt * 128:(tt + 1) * 128], in_=bk2[0:16, 0:128]),
             reads=[dbk2], writes=[daffT], merge=(tt > 0))
    work = P.sb("work", [16, 2048], F32)
    dwork = Dep()
    g_all = P.sb("g_all", [16, 256], F32)
    idx_all = P.sb("idx_all", [16, 256], U32)
    dg = Dep()
    di = Dep()
    for r in range(32):
        src, dsrc = (affT, daffT) if r == 0 else (work, dwork)
        sl = slice(r * 8, (r + 1) * 8)
        P.op("dve", lambda e, src=src, sl=sl: e.max(out=g_all[:, sl], in_=src[:]), reads=[dsrc], writes=[dg], merge=(r > 0))
        P.op("dve", lambda e, src=src, sl=sl: e.max_index(out=idx_all[:, sl], in_max=g_all[:, sl], in_values=src[:]),
             reads=[dsrc, dg], writes=[di], merge=(r > 0))
        if r < 31:
            P.op("dve", lambda e, src=src, sl=sl: e.match_replace(out=work[:], in_to_replace=g_all[:, sl], in_values=src[:], imm_value=-1.0),
                 reads=[dsrc, dg], writes=[dwork])
    idxf = P.sb("idxf", [16, 256], F32)
    didxf = Dep()
    P.op("dve", lambda e: e.tensor_copy(out=idxf[:], in_=idx_all[:]), reads=[di], writes=[didxf])
    colt = P.sb("colt", [128, 2, 2, 16], F32)
    dcol = Dep()
    for which, (src, dsrc) in enumerate(((idxf, didxf), (g_all, dg))):
        for cc in range(2):
            bk, dbk = C.bank()
            P.op("pe", lambda e, bk=bk, src=src, cc=cc: e.transpose(bk[:, 0:16], src[:, cc * 128:(cc + 1) * 128], ident[0:16, 0:16]),
                 reads=[dsrc, dcst], writes=[dbk])
            P.op("act", lambda e, bk=bk, which=which, cc=cc: e.copy(out=colt[:, which, cc, :], in_=bk[:, 0:16]),
                 reads=[dbk], writes=[dcol], merge=True)
    P.dma("act", idxc_d, colt[:, 0, :, :], reads=[dcol], writes=[C.dd("moe_idxc")])
    P.dma("act", gc_d, colt[:, 1, :, :], reads=[dcol], writes=[C.dd("moe_gc")])
    sel_r = Ring(P, "sel", 2, [128, NT, 256], BF16)
    xe_r = Ring(P, "xe", 2, [128, 8, 256], BF16)
    for ex_i in range(16):
        bk, dbk = C.bank()
        P.op("pe", lambda e, bk=bk, ex_i=ex_i: e.matmul(bk[:, 0:256], lhsT=esel[:, ex_i, :], rhs=idxf[:], start=True, stop=True),
             reads=[dcst, didxf], writes=[dbk])
        sel, dsel = sel_r.next()
        for tt in range(NT):
            P.op("dve", lambda e, sel=sel, bk=bk, tt=tt: e.tensor_scalar(out=sel[:, tt, :], in0=bk[:, 0:256], scalar1=iotac[:, tt:tt + 1], scalar2=None, op0=ALU.is_equal),
                 reads=[dbk, dcst], writes=[dsel], merge=(tt > 0))
        xe, dxe = xe_r.next()
        for dc in range(8):
            bk2, dbk2 = C.bank()
            mm_acc(P, bk2[:, 0:256], [(x1b[:, tt, dc * 128:(dc + 1) * 128], sel[:, tt, :]) for tt in range(NT)],
                   reads=dx1b + [dsel], dwrite=dbk2)
            if dc % 2 == 0:
                P.op("act", lambda e, xe=xe, dc=dc, bk2=bk2: e.copy(out=xe[:, dc, :], in_=bk2[:, 0:256]), reads=[dbk2], writes=[dxe], merge=(dc > 0))
            else:
                P.op("dve", lambda e, xe=xe, dc=dc, bk2=bk2: e.tensor_copy(out=xe[:, dc, :], in_=bk2[:, 0:256]), reads=[dbk2], writes=[dxe], merge=True)
        P.dma("act", xe_all[ex_i], xe[:], reads=[dxe], writes=[C.dd("xeT_all")], merge=True)


def stage_moe2(C, li, x_name, out_name):
    P = C.P
    P.begin_stage()
    x1 = C.D(x_name, [2048, 1024], F32)
    wg_d = C.D(f"moe_w_gate{li}", [16, 1024, 2048], F32)
    wu_d = C.D(f"moe_w_up{li}", [16, 1024, 2048], F32)
    wd_d = C.D(f"moe_w_down{li}", [16, 2048, 1024], F32)
    xe_all = C.D("xeT_all", [16, 128, 8, 256], BF16)
    idxc_d = C.D("moe_idxc", [128, 2, 16], F32)
    gc_d = C.D("moe_gc", [128, 2, 16], F32)
    iotar_d = C.D("iota_row", [128, 2048], F32)
    ident_d = C.D("ident", [128, 128], F32)
    g_d = C.D(f"ln2_g{li}", [128, 1024], F32)
    b_d = C.D(f"ln2_b{li}", [128, 1024], F32)
    out = C.D(out_name, [2048, 1024], F32)
    outT = C.D(out_name + "T", [1024, 2048], BF16)
    dcst = Dep()
    idxc = P.sb("idxc", [128, 2, 16], F32)
    gc = P.sb("gc", [128, 2, 16], F32)
    iotar = P.sb("iotar", [128, 2048], F32)
    P.dma("sp", idxc[:], idxc_d, reads=[C.dd("moe_idxc")], writes=[dcst])
    P.dma("sp", gc[:], gc_d, reads=[C.dd("moe_gc")], writes=[dcst], merge=True)
    P.dma("sp", iotar[:], iotar_d, writes=[dcst], merge=True)
    f_acc = P.sb("f_acc", [128, NT, 1024], F32)
    dfa = [Dep() for _ in range(NT)]
    xe_r = Ring(P, "m2xe", 2, [128, 8, 256], BF16)
    selT_r = Ring(P, "selT", 2, [128, 2, 2048], BF16)
    wg_r = Ring(P, "wg", 2, [128, 8, 512], BF16)
    wu_r = Ring(P, "wu", 2, [128, 8, 512], BF16)
    wd_r = Ring(P, "wd", 2, [128, 4, 1024], BF16)
    sg_r = Ring(P, "sg", 2, [128, 256], F32)
    hT_r = Ring(P, "hT", 2, [128, 16, 256], BF16)
    ye_r = Ring(P, "ye", 2, [128, 2, 1024], BF16)
    ev = 0
    for e_i in range(16):
        xe, dxe = xe_r.next()
        P.dma("sp", xe[:], xe_all[e_i], reads=[C.dd("xeT_all")], writes=[dxe])
        selT, dselT = selT_r.next()
        for cc in range(2):
            P.op("pool", lambda e, selT=selT, cc=cc, e_i=e_i: e.tensor_scalar(
                out=selT[:, cc, :], in0=iotar[:], scalar1=idxc[:, cc, e_i:e_i + 1], scalar2=gc[:, cc, e_i:e_i + 1],
                op0=ALU.is_equal, op1=ALU.mult), reads=[dcst], writes=[dselT], merge=(cc > 0))
        hT, dhT = hT_r.next()
        wgv = wg_d[e_i].rearrange("(kc p) f -> p kc f", p=128)
        wuv = wu_d[e_i].rearrange("(kc p) f -> p kc f", p=128)
        wdv = wd_d[e_i].rearrange("(fc p) d -> p fc d", p=128)
        for q in range(4):
            wg, dwg = wg_r.next()
            wu, dwu = wu_r.next()
            P.dma("pool", wg[:], wgv[:, :, q * 512:(q + 1) * 512], writes=[dwg])
            P.dma("pool", wu[:], wuv[:, :, q * 512:(q + 1) * 512], writes=[dwu])
            for fcl in range(4):
                fc = q * 4 + fcl
                fs = slice(fcl * 128, (fcl + 1) * 128)
                bg, dbg = C.bank()
                bu, dbu = C.bank()
                mm_acc(P, bg[:, 0:256], [(wg[:, kc, fs], xe[:, kc, :]) for kc in range(8)], reads=[dwg, dxe], dwrite=dbg)
                mm_acc(P, bu[:, 0:256], [(wu[:, kc, fs], xe[:, kc, :]) for kc in range(8)], reads=[dwu, dxe], dwrite=dbu)
                sg, dsg = sg_r.next()
                P.op("act", lambda e, sg=sg, bg=bg: e.activation(out=sg[:], in_=bg[:, 0:256], func=AF.Silu), reads=[dbg], writes=[dsg])
                P.op("dve", lambda e, hT=hT, fc=fc, sg=sg, bu=bu: e.tensor_tensor(out=hT[:, fc, :], in0=sg[:], in1=bu[:, 0:256], op=ALU.mult),
                     reads=[dsg, dbu], writes=[dhT], merge=(fc > 0))
        ye, dye = ye_r.next()
        dbanks = [C.bank() for _ in range(4)]
        for r in range(4):
            wd, dwd = wd_r.next()
            P.dma("pool", wd[:], wdv[:, r * 4:(r + 1) * 4, :], writes=[dwd])
            for ct in range(2):
                for dh in range(2):
                    bk, dbk = dbanks[ct * 2 + dh]
                    for f4 in range(4):
                        fc = r * 4 + f4
                        first = (fc == 0)
                        last = (fc == 15)
                        P.op("pe", lambda e, bk=bk, hT=hT, fc=fc, ct=ct, wd=wd, f4=f4, dh=dh, first=first, last=last: e.matmul(
                            bk[:], lhsT=hT[:, fc, ct * 128:(ct + 1) * 128], rhs=wd[:, f4, dh * 512:(dh + 1) * 512], start=first, stop=last),
                            reads=[dhT, dwd], writes=[dbk], merge=(not first))
        for ct in range(2):
            for dh in range(2):
                bk, dbk = dbanks[ct * 2 + dh]
                P.op("act", lambda e, ye=ye, ct=ct, dh=dh, bk=bk: e.copy(out=ye[:, ct, dh * 512:(dh + 1) * 512], in_=bk[:]),
                     reads=[dbk], writes=[dye], merge=(ct + dh > 0))
        for tt in range(NT):
            for dh in range(2):
                bk, dbk = C.bank()
                ds = slice(dh * 512, (dh + 1) * 512)
                mm_acc(P, bk[:], [(selT[:, ct, tt * 128:(tt + 1) * 128], ye[:, ct, ds]) for ct in range(2)], reads=[dselT, dye], dwrite=dbk)
                if e_i == 0:
                    P.op("dve", lambda e, tt=tt, ds=ds, bk=bk: e.tensor_copy(out=f_acc[:, tt, ds], in_=bk[:]), reads=[dbk], writes=[dfa[tt]], merge=(dh > 0))
                else:
                    P.op("dve", lambda e, tt=tt, ds=ds, bk=bk: e.tensor_tensor(out=f_acc[:, tt, ds], in0=f_acc[:, tt, ds], in1=bk[:], op=ALU.add),
                         reads=[dbk, dfa[tt]], writes=[dfa[tt]])
    gb = P.sb("gb2", [128, 1024], F32)
    bb = P.sb("bb2", [128, 1024], F32)
    ident = P.sb("ident2", [128, 128], F32)
    dgb = Dep()
    P.dma("sp", gb[:], g_d, writes=[dgb])
    P.dma("sp", bb[:], b_d, writes=[dgb], merge=True)
    P.dma("sp", ident[:], ident_d, writes=[dgb], merge=True)
    lb = LNBufs(P, "ln2")
    xr = Ring(P, "m2x", 2, [128, 1024], F32)
    yr = Ring(P, "m2y", 2, [128, 1024], F32)
    yT_r = Ring(P, "m2yT", 2, [128, 8, 128], BF16)
    for tt in range(NT):
        xt, dxt = xr.next()
        P.dma("sp", xt[:], x1[tt * 128:(tt + 1) * 128, :], reads=[C.dd(x_name)], writes=[dxt])
        P.op("dve", lambda e, xt=xt, tt=tt: e.scalar_tensor_tensor(out=xt[:], in0=xt[:], scalar=ALPHA, in1=f_acc[:, tt, :], op0=ALU.mult, op1=ALU.add),
             reads=[dxt, dfa[tt]], writes=[dxt])
        y, dy = yr.next()
        layer_norm_tile(P, lb, xt, dxt, gb, bb, dgb, y, dy)
        P.dma("pool", out[tt * 128:(tt + 1) * 128, :], y[:], reads=[dy], writes=[C.dd(out_name)], merge=True)
        transpose_tile_to_dram(C, y, dy, ident, dgb, yT_r, outT, out_name + "T", tt)


def transpose_tile_to_dram(C, y, dy, ident, dident, yT_r, outT, outT_name, tt):
    P = C.P
    yT, dyT = yT_r.next()
    for hb in range(2):
        bk, dbk = C.bank()
        for k4 in range(4):
            kc = hb * 4 + k4
            P.op("pe", lambda e, bk=bk, k4=k4, kc=kc: e.transpose(bk[:, k4 * 128:(k4 + 1) * 128], y[:, kc * 128:(kc + 1) * 128], ident[:]),
                 reads=[dy, dident], writes=[dbk], merge=(k4 > 0))
        P.op("act", lambda e, hb=hb, bk=bk: e.copy(out=yT[:, hb * 4:(hb + 1) * 4, :], in_=bk[:].rearrange("p (k t) -> p k t", k=4)),
             reads=[dbk], writes=[dyT], merge=(hb > 0))
    P.dma("act", outT.rearrange("(kc p) t -> p kc t", p=128)[:, :, tt * 128:(tt + 1) * 128], yT[:], reads=[dyT], writes=[C.dd(outT_name)], merge=True)


def stage_ple(C, li, x_name, out_name, want_T):
    P = C.P
    P.begin_stage()
    x2 = C.D(x_name, [2048, 1024], F32)
    x2T = C.D(x_name + "T", [1024, 2048], BF16)
    pT_d = C.D(f"pT{li}", [256, 2048], F32)
    wg_d = C.D(f"ple_gate{li}", [1024, 1024], F32)
    wp_d = C.D(f"ple_proj{li}", [256, 1024], F32)
    ident_d = C.D("ident", [128, 128], F32)
    out = C.D(out_name, [2048, 1024], F32)
    xs = P.sb("plx", [128, 8, 2048], BF16)
    dxs = [Dep() for _ in range(8)]
    for kc in range(8):
        P.dma("sp", xs[:, kc, :], x2T[kc * 128:(kc + 1) * 128, :], reads=[C.dd(x_name + "T")], writes=[dxs[kc]])
    ps_, dps_ = load_fm_bf16(C, "plp", pT_d, 2, 2048)
    wg, dwg = load_fm_bf16(C, "plwg", wg_d, 8, 1024)
    wp, dwp = load_fm_bf16(C, "plwp", wp_d, 2, 1024)
    ident = P.sb("ident3", [128, 128], F32)
    dident = Dep()
    P.dma("sp", ident[:], ident_d, writes=[dident])
    if want_T:
        outT = C.D(out_name + "T", [1024, 2048], BF16)
        yT_r = Ring(P, "plyT", 2, [128, 8, 128], BF16)
    xr = Ring(P, "plxr", 2, [128, 1024], F32)
    gr = Ring(P, "plg", 2, [128, 1024], F32)
    yr = Ring(P, "ply", 2, [128, 1024], F32)
    for tt in range(NT):
        ts_ = slice(tt * 128, (tt + 1) * 128)
        xt, dxt = xr.next()
        P.dma("sp", xt[:], x2[ts_, :], reads=[C.dd(x_name)], writes=[dxt])
        gt, dgt = gr.next()
        y, dy = yr.next()
        for half in range(2):
            hs = slice(half * 512, (half + 1) * 512)
            bk, dbk = C.bank()
            mm_acc(P, bk[:], [(xs[:, kc, ts_], wg[:, kc, hs]) for kc in range(8)], reads=dxs + dwg, dwrite=dbk)
            P.op("act", lambda e, gt=gt, hs=hs, bk=bk: e.activation(out=gt[:, hs], in_=bk[:], func=AF.Sigmoid), reads=[dbk], writes=[dgt], merge=(half > 0))
            bk2, dbk2 = C.bank()
            mm_acc(P, bk2[:], [(ps_[:, kc, ts_], wp[:, kc, hs]) for kc in range(2)], reads=dps_ + dwp, dwrite=dbk2)
            P.op("dve", lambda e, gt=gt, hs=hs, bk2=bk2: e.tensor_tensor(out=gt[:, hs], in0=gt[:, hs], in1=bk2[:], op=ALU.mult),
                 reads=[dgt, dbk2], writes=[dgt])
        P.op("pool", lambda e, y=y, xt=xt, gt=gt: e.tensor_tensor(out=y[:], in0=xt[:], in1=gt[:], op=ALU.add), reads=[dxt, dgt], writes=[dy])
        P.dma("pool", out[ts_, :], y[:], reads=[dy], writes=[C.dd(out_name)], merge=True)
        if want_T:
            transpose_tile_to_dram(C, y, dy, ident, dident, yT_r, outT, out_name + "T", tt)


def hyena_consts():
    L, N = 2048, 4096
    R = np.arange(N)
    f = np.where(R <= 2048, R, R - 2048).astype(np.int64)
    is_im = R > 2048
    t = np.arange(L, dtype=np.int64)
    k = (t[:, None] * f[None, :]) % N
    ang = 2.0 * np.pi * k.astype(np.float64) / N
    Wf = np.where(is_im[None, :], -np.sin(ang), np.cos(ang))
    cR = np.full(N, 2.0 / N)
    cR[0] = 1.0 / N
    cR[2048] = 1.0 / N
    WfT = np.ascontiguousarray(Wf.T)
    Wf_d = Wf.reshape(16, 128, 32, 128).transpose(2, 1, 0, 3)
    WA_d = WfT.reshape(32, 128, 16, 128).transpose(2, 1, 0, 3)
    WB_d = WfT.reshape(2, 16, 128, 4, 512).transpose(3, 0, 2, 1, 4)
    tl = np.linspace(0.0, 1.0, L, dtype=np.float32)[:, None]
    w = (2.0 * np.float32(math.pi) * np.arange(L, dtype=np.float32)[:, None] / np.float32(L)).astype(np.float32)
    fb = np.linspace(1e-4, 15, 16, dtype=np.float32)[None, :]
    z = np.concatenate([tl, np.cos(fb * w), -np.sin(fb * w)], axis=-1).astype(np.float32)
    min_decay = math.log(1e-2) / 1.5
    max_decay = math.log(1e-2) / 0.3
    deltas = np.abs(np.linspace(min_decay, max_decay, 512, dtype=np.float32))
    decay = np.exp(-tl * deltas[None, :]).astype(np.float32)
    return {
        "hy_Wf": np.ascontiguousarray(Wf_d).astype(NPBF),
        "hy_WA": np.ascontiguousarray(WA_d).astype(NPBF), "hy_WB": np.ascontiguousarray(WB_d).astype(NPBF),
        "hy_cR": np.ascontiguousarray(cR.reshape(32, 128).T).astype(np.float32),
        "hy_zT": np.ascontiguousarray(z.T), "hy_decay": np.ascontiguousarray(decay.reshape(16, 128, 512)),
    }


TWO_PI = 2.0 * math.pi


def stage_hy_filter(C):
    P = C.P
    P.begin_stage()
    zT_d = C.D("hy_zT", [33, 2048], F32)
    w1_d = C.D("hy_f_w1", [33, 64], F32)
    w2_d = C.D("hy_f_w2", [64, 64], F32)
    w3_d = C.D("hy_f_w3", [64, 2048], F32)
    cols_d = C.D("hy_cols", [64, 3], F32)
    dec_d = C.D("hy_decay", [16, 128, 512], F32)
    Wf_d = C.D("hy_Wf", [32, 128, 16, 128], BF16)
    cR_d = C.D("hy_cR", [128, 32], F32)
    skip_d = C.D("hy_skipb", [2, 128, 512], F32)
    Kf_d = C.D("hy_Kf", [32, 2, 128, 512], F32)
    dc = Dep()
    zT = P.sb("zT", [33, 2048], F32)
    w1 = P.sb("w1", [33, 64], F32)
    w2 = P.sb("w2", [64, 64], F32)
    w3 = P.sb("w3", [64, 2048], BF16)
    cols = P.sb("cols", [64, 8], F32)
    cR = P.sb("cR", [128, 32], F32)
    skipb = P.sb("skipb", [128, 2, 512], F32)
    P.dma("sp", zT[:], zT_d, writes=[dc])
    P.dma("sp", w1[:], w1_d, writes=[dc], merge=True)
    P.dma("sp", w2[:], w2_d, writes=[dc], merge=True)
    P.dma("pool", w3[:], w3_d, writes=[dc], merge=True)
    P.dma("sp", cols[:, 0:3], cols_d, writes=[dc], merge=True)
    P.dma("sp", cR[:], cR_d, writes=[dc], merge=True)
    for o in range(2):
        P.dma("sp", skipb[:, o, :], skip_d[o], writes=[dc], merge=True)
    dcol = Dep()
    P.op("dve", lambda e: e.tensor_tensor(out=cols[:, 3:4], in0=cols[:, 0:1], in1=cols[:, 1:2], op=ALU.mult), reads=[dc], writes=[dcol])
    P.op("dve", lambda e: e.tensor_tensor(out=cols[:, 4:5], in0=cols[:, 2:3], in1=cols[:, 1:2], op=ALU.mult), reads=[dc], writes=[dcol], merge=True)
    P.op("pool", lambda e: e.memset(cols[:, 5:6], -math.pi), reads=[dc], writes=[dcol], merge=True)
    h1T = P.sb("h1T", [64, 2048], F32)
    h2T = P.sb("h2T", [64, 2048], BF16)
    dh1 = Dep()
    dh2 = Dep()
    u_r = Ring(P, "hyu", 2, [64, 512], F32)
    s_r = Ring(P, "hys", 4, [64, 512], F32)
    for layer in range(2):
        for nt in range(4):
            ns = slice(nt * 512, (nt + 1) * 512)
            bk, dbk = C.bank()
            if layer == 0:
                P.op("pe", lambda e, bk=bk, ns=ns: e.matmul(bk[0:64, :], lhsT=w1[:], rhs=zT[:, ns], start=True, stop=True), reads=[dc], writes=[dbk])
            else:
                P.op("pe", lambda e, bk=bk, ns=ns: e.matmul(bk[0:64, :], lhsT=w2[:], rhs=h1T[:, ns], start=True, stop=True), reads=[dc, dh1], writes=[dbk])
            u, du = u_r.next()
            fbc = 3 + layer
            P.op("dve", lambda e, u=u, bk=bk, fbc=fbc: e.tensor_scalar(out=u[:], in0=bk[0:64, :], scalar1=cols[:, 1:2], scalar2=cols[:, fbc:fbc + 1], op0=ALU.mult, op1=ALU.add),
                 reads=[dbk, dcol, dc], writes=[du])
            s2, ds2 = s_r.next()
            s4, ds4 = s_r.next()
            P.op("act", lambda e, u=u, s2=s2: e.activation(out=s2[:], in_=u[:], func=AF.Sin, scale=0.5), reads=[du], writes=[ds2])
            P.op("act", lambda e, u=u, s4=s4: e.activation(out=s4[:], in_=u[:], func=AF.Sin, scale=0.25), reads=[du], writes=[ds4])
            P.op("dve", lambda e, s4=s4: e.tensor_tensor(out=s4[:], in0=s4[:], in1=s4[:], op=ALU.mult), reads=[ds4], writes=[ds4])
            P.op("dve", lambda e, s4=s4: e.tensor_scalar(out=s4[:], in0=s4[:], scalar1=-2.0, scalar2=1.0, op0=ALU.mult, op1=ALU.add), reads=[ds4], writes=[ds4])
            if layer == 0:
                P.op("dve", lambda e, s2=s2, s4=s4, ns=ns: e.scalar_tensor_tensor(out=h1T[:, ns], in0=s2[:], scalar=2.0, in1=s4[:], op0=ALU.mult, op1=ALU.mult),
                     reads=[ds2, ds4], writes=[dh1], merge=(nt > 0))
            else:
                P.op("dve", lambda e, s2=s2, s4=s4, ns=ns: e.scalar_tensor_tensor(out=h2T[:, ns], in0=s2[:], scalar=2.0, in1=s4[:], op0=ALU.mult, op1=ALU.mult),
                     reads=[ds2, ds4], writes=[dh2], merge=(nt > 0))
    Kt = P.sb("Ksd", [128, 16, 4, 512], BF16)
    dKt = [Dep() for _ in range(16)]
    ones = P.sb("onesb", [128, 128], BF16)
    dones = Dep()
    P.op("pool", lambda e: e.memset(ones[:], 1.0), writes=[dones])
    dec_r = Ring(P, "dec", 2, [128, 512], F32)
    sq_r = Ring(P, "sq", 3, [128, 512], BF16)
    kf32_r = Ring(P, "kf32", 2, [128, 512], F32)
    kb32_r = Ring(P, "kb32", 2, [128, 512], F32)
    ssq = [C.bank(hold=True), C.bank(hold=True)]
    for tt in range(16):
        dec, ddec = dec_r.next()
        P.dma("sp", dec[:], dec_d[tt], writes=[ddec])
        for o in range(2):
            bf_, dbf_ = C.bank()
            bb_, dbb_ = C.bank()
            for (bk, dbk, q) in ((bf_, dbf_, 2 * o), (bb_, dbb_, 2 * o + 1)):
                P.op("pe", lambda e, bk=bk, tt=tt, q=q: e.matmul(bk[:], lhsT=h2T[:, tt * 128:(tt + 1) * 128], rhs=w3[:, q * 512:(q + 1) * 512], start=True, stop=True),
                     reads=[dh2, dc], writes=[dbk])
            kf32, dkf32 = kf32_r.next()
            kb32, dkb32 = kb32_r.next()
            P.op("dve", lambda e, kf32=kf32, bf_=bf_, dec=dec: e.tensor_tensor(out=kf32[:], in0=bf_[:], in1=dec[:], op=ALU.mult), reads=[dbf_, ddec], writes=[dkf32])
            P.op("dve", lambda e, kb32=kb32, bb_=bb_, dec=dec: e.tensor_tensor(out=kb32[:], in0=bb_[:], in1=dec[:], op=ALU.mult), reads=[dbb_, ddec], writes=[dkb32])
            if tt == 0:
                P.op("pool", lambda e, kb32=kb32: e.memset(kb32[0:1, :], 0.0), reads=[dkb32], writes=[dkb32])
            P.op("pool", lambda e, tt=tt, o=o, kf32=kf32, kb32=kb32: e.tensor_tensor(out=Kt[:, tt, 2 * o, :], in0=kf32[:], in1=kb32[:], op=ALU.add),
                 reads=[dkf32, dkb32], writes=[dKt[tt]], merge=True)
            P.op("pool", lambda e, tt=tt, o=o, kf32=kf32, kb32=kb32: e.tensor_tensor(out=Kt[:, tt, 2 * o + 1, :], in0=kf32[:], in1=kb32[:], op=ALU.subtract),
                 reads=[dkf32, dkb32], writes=[dKt[tt]], merge=True)
        for q in range(4):
            sq, dsq = sq_r.next()
            P.op("act", lambda e, sq=sq, tt=tt, q=q: e.activation(out=sq[:], in_=Kt[:, tt, q, :], func=AF.Square), reads=[dKt[tt]], writes=[dsq])
            sb_, dsb_ = ssq[q // 2]
            first = (tt == 0 and q % 2 == 0)
            last = (tt == 15 and q % 2 == 1)
            P.op("pe", lambda e, sb_=sb_, sq=sq, first=first, last=last: e.matmul(sb_[:], lhsT=ones[:], rhs=sq[:], start=first, stop=last),
                 reads=[dsq, dones], writes=[dsb_], merge=(not first))
    rs = P.sb("hyrs", [128, 2, 512], F32)
    drs = Dep()
    for o in range(2):
        sb_, dsb_ = ssq[o]
        P.op("dve", lambda e, o=o, sb_=sb_: e.tensor_scalar(out=rs[:, o, :], in0=sb_[:], scalar1=0.5, scalar2=1e-12, op0=ALU.mult, op1=ALU.add), reads=[dsb_], writes=[drs], merge=(o > 0))
    C.release_all()
    P.op("act", lambda e: e.activation(out=rs[:], in_=rs[:], func=AF.Sqrt), reads=[drs], writes=[drs])
    P.op("dve", lambda e: e.reciprocal(out=rs[:], in_=rs[:]), reads=[drs], writes=[drs])
    wf_r = Ring(P, "wf", 2, [128, 16, 128], BF16)
    kf_r = Ring(P, "kf", 3, [128, 512], F32)
    for ft in range(32):
        wf, dwf = wf_r.next()
        P.dma("sp", wf[:], Wf_d[ft], writes=[dwf])
        for o in range(2):
            bk, dbk = C.bank()
            sel = 2 * o if ft < 16 else 2 * o + 1
            mm_acc(P, bk[:], [(wf[:, tt, :], Kt[:, tt, sel, :]) for tt in range(16)], reads=[dwf] + dKt, dwrite=dbk)
            kf, dkf = kf_r.next()
            P.op("dve", lambda e, kf=kf, bk=bk, o=o: e.tensor_tensor(out=kf[:], in0=bk[:], in1=rs[:, o, :], op=ALU.mult), reads=[dbk, drs], writes=[dkf])
            if ft < 16:
                P.op("pool", lambda e, kf=kf, o=o: e.tensor_tensor(out=kf[:], in0=kf[:], in1=skipb[:, o, :], op=ALU.add), reads=[dkf, dc], writes=[dkf])
            elif ft == 16:
                bn, dbn = C.bank()
                mm_acc(P, bn[0:32, :], [(wf[:, tt, 0:32], Kt[:, tt, 2 * o, :]) for tt in range(16)], reads=[dwf] + dKt, dwrite=dbn)
                P.op("dve", lambda e, kf=kf, bn=bn, o=o: e.tensor_tensor(out=kf[0:1, :], in0=bn[0:1, :], in1=rs[0:1, o, :], op=ALU.mult), reads=[dbn, drs, dkf], writes=[dkf])
                P.op("pool", lambda e, kf=kf, o=o: e.tensor_tensor(out=kf[0:1, :], in0=kf[0:1, :], in1=skipb[0:1, o, :], op=ALU.add), reads=[dkf, dc], writes=[dkf])
            P.op("pool", lambda e, kf=kf, ft=ft: e.tensor_scalar(out=kf[:], in0=kf[:], scalar1=cR[:, ft:ft + 1], scalar2=None, op0=ALU.mult), reads=[dkf, dc], writes=[dkf])
            P.dma("pool", Kf_d[ft, o], kf[:], reads=[dkf], writes=[C.dd("hy_Kf")], merge=True)


def stage_hy_prep(C):
    P = C.P
    P.begin_stage()
    hbT = C.D("hbT", [1536, 2048], F32)
    cw_d = C.D("hy_cw", [128, 12, 3], F32)
    cb_d = C.D("hy_cb", [128, 12], F32)
    ident_d = C.D("ident", [128, 128], F32)
    hv = C.D("hv_tm", [2048, 512], BF16)
    hx1 = C.D("hx1_tm", [2048, 512], F32)
    hx2T = C.D("hx2T", [512, 2048], F32)
    dc = Dep()
    cw = P.sb("cw", [128, 12, 3], F32)
    cb = P.sb("cb", [128, 12], F32)
    ident = P.sb("identh", [128, 128], F32)
    P.dma("sp", cw[:], cw_d, writes=[dc])
    P.dma("sp", cb[:], cb_d, writes=[dc], merge=True)
    P.dma("sp", ident[:], ident_d, writes=[dc], merge=True)
    xin_r = Ring(P, "hxin", 2, [128, 2048], F32)
    y_r = Ring(P, "hy", 2, [128, 2048], F32)
    sv_r = Ring(P, "hsv", 2, [128, 16, 128], BF16)
    sx_r = Ring(P, "hsx", 2, [128, 16, 128], F32)
    for ch in range(12):
        xin, dxin = xin_r.next()
        P.dma("sp", xin[:], hbT[ch * 128:(ch + 1) * 128, :], reads=[C.dd("hbT")], writes=[dxin])
        y, dy = y_r.next()
        P.op("act", lambda e, y=y, xin=xin, ch=ch: e.activation(out=y[:], in_=xin[:], func=AF.Identity, bias=cb[:, ch:ch + 1], scale=cw[:, ch, 1:2]),
             reads=[dxin, dc], writes=[dy])
        P.op("dve", lambda e, y=y, xin=xin, ch=ch: e.scalar_tensor_tensor(out=y[:, 1:2048], in0=xin[:, 0:2047], scalar=cw[:, ch, 0:1], in1=y[:, 1:2048], op0=ALU.mult, op1=ALU.add),
             reads=[dxin, dc, dy], writes=[dy])
        P.op("dve", lambda e, y=y, xin=xin, ch=ch: e.scalar_tensor_tensor(out=y[:, 0:2047], in0=xin[:, 1:2048], scalar=cw[:, ch, 2:3], in1=y[:, 0:2047], op0=ALU.mult, op1=ALU.add),
             reads=[dxin, dc, dy], writes=[dy])
        if ch >= 8:
            P.dma("act", hx2T[(ch - 8) * 128:(ch - 7) * 128, :], y[:], reads=[dy], writes=[C.dd("hx2T")], merge=True)
            continue
        stg, dstg = (sv_r if ch < 4 else sx_r).next()
        for g in range(4):
            bk, dbk = C.bank()
            for k4 in range(4):
                tt = g * 4 + k4
                P.op("pe", lambda e, bk=bk, k4=k4, tt=tt, y=y: e.transpose(bk[:, k4 * 128:(k4 + 1) * 128], y[:, tt * 128:(tt + 1) * 128], ident[:]),
                     reads=[dy, dc], writes=[dbk], merge=(k4 > 0))
            P.op("act", lambda e, stg=stg, g=g, bk=bk: e.copy(out=stg[:, g * 4:(g + 1) * 4, :], in_=bk[:].rearrange("p (k t) -> p k t", k=4)),
                 reads=[dbk], writes=[dstg], merge=(g > 0))
        if ch < 4:
            P.dma("act", hv.rearrange("(tt p) c -> p tt c", p=128)[:, :, ch * 128:(ch + 1) * 128], stg[:], reads=[dstg], writes=[C.dd("hv_tm")], merge=True)
        else:
            P.dma("act", hx1.rearrange("(tt p) c -> p tt c", p=128)[:, :, (ch - 4) * 128:(ch - 3) * 128], stg[:], reads=[dstg], writes=[C.dd("hx1_tm")], merge=True)


def stage_hy_conv(C):
    P = C.P
    P.begin_stage()
    hv = C.D("hv_tm", [2048, 512], BF16)
    hx1 = C.D("hx1_tm", [2048, 512], F32)
    hx2T = C.D("hx2T", [512, 2048], F32)
    Kf_d = C.D("hy_Kf", [32, 2, 128, 512], F32)
    Wf_d = C.D("hy_Wf", [32, 128, 16, 128], BF16)
    WA_d = C.D("hy_WA", [16, 128, 32, 128], BF16)
    WB_d = C.D("hy_WB", [4, 2, 128, 16, 512], BF16)
    mixT = C.D("mixT", [1024, 2048], BF16)
    ztm = P.sb("ztm", [128, 16, 512], BF16)
    dz = [Dep() for _ in range(16)]
    hvv = hv.rearrange("(tt p) c -> p tt c", p=128)
    for tt in range(16):
        P.dma("sp", ztm[:, tt, :], hvv[:, tt, :], reads=[C.dd("hv_tm")], writes=[dz[tt]])
    Yt = P.sb("Yt", [128, 32, 512], BF16)
    dY = [Dep() for _ in range(32)]
    wf_r = Ring(P, "cwf", 3, [128, 16, 128], BF16)
    kf_r = Ring(P, "ckf", 4, [128, 512], F32)
    t_r = Ring(P, "ct", 4, [128, 512], F32)
    wa_r = Ring(P, "cwa", 2, [128, 32, 128], BF16)
    wb_r = Ring(P, "cwb", 2, [128, 16, 512], BF16)
    x_r = Ring(P, "cx", 3, [128, 512], F32)
    zo_r = Ring(P, "czo", 3, [128, 512], BF16)
    for o in range(2):
        for j in range(16):
            ub = []
            for part in range(2):
                ft = part * 16 + j
                wf, dwf = wf_r.next()
                P.dma("sp", wf[:], Wf_d[ft], writes=[dwf])
                bk, dbk = C.bank()
                mm_acc(P, bk[:], [(wf[:, tt, :], ztm[:, tt, :]) for tt in range(16)], reads=[dwf] + dz, dwrite=dbk)
                ub.append((bk, dbk))
            kre, dkre = kf_r.next()
            kim, dkim = kf_r.next()
            P.dma("sp", kre[:], Kf_d[j, o], reads=[C.dd("hy_Kf")], writes=[dkre])
            P.dma("sp", kim[:], Kf_d[16 + j, o], reads=[C.dd("hy_Kf")], writes=[dkim])
            (ure, dure), (uim, duim) = ub
            t1, dt1 = t_r.next()
            t2, dt2 = t_r.next()
            P.op("dve", lambda e, t1=t1, ure=ure, kre=kre: e.tensor_tensor(out=t1[:], in0=ure[:], in1=kre[:], op=ALU.mult), reads=[dure, dkre], writes=[dt1])
            P.op("dve", lambda e, t2=t2, uim=uim, kim=kim: e.tensor_tensor(out=t2[:], in0=uim[:], in1=kim[:], op=ALU.mult), reads=[duim, dkim], writes=[dt2])
            P.op("pool", lambda e, j=j, t1=t1, t2=t2: e.tensor_tensor(out=Yt[:, j, :], in0=t1[:], in1=t2[:], op=ALU.subtract), reads=[dt1, dt2], writes=[dY[j]])
            if j == 0:
                P.op("pool", lambda e, t1=t1: e.tensor_copy(out=Yt[0:1, 0, :], in_=t1[0:1, :]), reads=[dt1, dY[0]], writes=[dY[0]])
            t3, dt3 = t_r.next()
            t4, dt4 = t_r.next()
            P.op("dve", lambda e, t3=t3, ure=ure, kim=kim: e.tensor_tensor(out=t3[:], in0=ure[:], in1=kim[:], op=ALU.mult), reads=[dure, dkim], writes=[dt3])
            P.op("dve", lambda e, t4=t4, uim=uim, kre=kre: e.tensor_tensor(out=t4[:], in0=uim[:], in1=kre[:], op=ALU.mult), reads=[duim, dkre], writes=[dt4])
            P.op("pool", lambda e, j=j, t3=t3, t4=t4: e.tensor_tensor(out=Yt[:, 16 + j, :], in0=t3[:], in1=t4[:], op=ALU.add), reads=[dt3, dt4], writes=[dY[16 + j]])
            if j == 0:
                P.op("pool", lambda e, t2=t2: e.tensor_copy(out=Yt[0:1, 16, :], in_=t2[0:1, :]), reads=[dt2, dY[16]], writes=[dY[16]])
        if o == 0:
            for tt in range(16):
                wa, dwa = wa_r.next()
                P.dma("sp", wa[:], WA_d[tt], writes=[dwa])
                bk, dbk = C.bank()
                mm_acc(P, bk[:], [(wa[:, kt, :], Yt[:, kt, :]) for kt in range(32)], reads=[dwa] + dY, dwrite=dbk)
                xt, dxt = x_r.next()
                P.dma("sp", xt[:], hx1[tt * 128:(tt + 1) * 128, :], reads=[C.dd("hx1_tm")], writes=[dxt])
                P.op("dve", lambda e, tt=tt, bk=bk, xt=xt: e.tensor_tensor(out=ztm[:, tt, :], in0=bk[:], in1=xt[:], op=ALU.mult), reads=[dbk, dxt], writes=[dz[tt]])
        else:
            for nt in range(4):
                banks = [C.bank() for _ in range(4)]
                for hf in range(2):
                    wb, dwb = wb_r.next()
                    P.dma("sp", wb[:], WB_d[nt, hf], writes=[dwb])
                    for cc in range(4):
                        bk, dbk = banks[cc]
                        for k in range(16):
                            first = (hf == 0 and k == 0)
                            last = (hf == 1 and k == 15)
                            kt = hf * 16 + k
                            P.op("pe", lambda e, bk=bk, kt=kt, cc=cc, wb=wb, k=k, first=first, last=last: e.matmul(
                                bk[:], lhsT=Yt[:, kt, cc * 128:(cc + 1) * 128], rhs=wb[:, k, :], start=first, stop=last),
                                reads=[dY[kt], dwb], writes=[dbk], merge=(not first))
                for cc in range(4):
                    bk, dbk = banks[cc]
                    xt, dxt = x_r.next()
                    P.dma("sp", xt[:], hx2T[cc * 128:(cc + 1) * 128, nt * 512:(nt + 1) * 512], reads=[C.dd("hx2T")], writes=[dxt])
                    zo, dzo = zo_r.next()
                    P.op("dve", lambda e, zo=zo, bk=bk, xt=xt: e.tensor_tensor(out=zo[:], in0=bk[:], in1=xt[:], op=ALU.mult), reads=[dbk, dxt], writes=[dzo])
                    P.dma("act", mixT[512 + cc * 128:512 + (cc + 1) * 128, nt * 512:(nt + 1) * 512], zo[:], reads=[dzo], writes=[C.dd("mixT")], merge=True)


MLA_SCALE = 96.0 ** -0.5


def mla_consts():
    inv = 1.0 / (10000.0 ** (np.arange(0, 32, 2, dtype=np.float32) / 32.0))
    ang = np.arange(2048, dtype=np.float32)[:, None] * inv[None, :].astype(np.float32)
    cos = np.cos(ang).astype(np.float32).T
    sin = np.sin(ang).astype(np.float32).T
    cos2 = np.concatenate([cos, cos], axis=0)
    sin2 = np.concatenate([-sin, sin], axis=0)
    return {"mla_cs2": np.ascontiguousarray(np.stack([cos2, sin2], axis=1)).astype(np.float32)}


def stage_mla1(C, xT_name):
    P = C.P
    P.begin_stage()
    xT = C.D(xT_name, [1024, 2048], BF16)
    wi_d = C.D("mla_w_in", [1024, 672], F32)
    wsw_d = C.D("mla_w_in_sw", [1024, 96], F32)
    gc_d = C.D("mla_gcols", [128, 5], F32)
    cs_d = C.D("mla_cs2", [32, 2, 2048], F32)
    nT_d = C.D("mla_nT", [640, 2048], BF16)
    kr_d = C.D("mla_krT", [32, 2048], BF16)
    xs = P.sb("mxs", [128, 8, 2048], BF16)
    dxs = [Dep() for _ in range(8)]
    for kc in range(8):
        P.dma("sp", xs[:, kc, :], xT[kc * 128:(kc + 1) * 128, :], reads=[C.dd(xT_name)], writes=[dxs[kc]])
    wi, dwi = load_fm_bf16(C, "mwi", wi_d, 8, 672)
    wsw, dwsw = load_fm_bf16(C, "mwsw", wsw_d, 8, 96)
    dc = Dep()
    gcol = P.sb("mgc", [128, 5], F32)
    P.dma("sp", gcol[:], gc_d, writes=[dc])
    cs = P.sb("mcs", [96, 2, 2048], F32)
    P.dma("sp", cs[64:96, :, :], cs_d, writes=[dc], merge=True)
    ones = P.sb("mones", [128, 128], BF16)
    P.op("pool", lambda e: e.memset(ones[:], 1.0), writes=[dc], merge=True)
    hT = P.sb("mhT", [128, 5, 2048], F32)
    nT = P.sb("mnT", [128, 5, 2048], BF16)
    dhT = Dep()
    dnT = [Dep() for _ in range(5)]
    sq_r = Ring(P, "msq", 3, [128, 512], BF16)
    r_r = Ring(P, "mr", 2, [128, 512], F32)
    for (chunks, n) in (((0, 1, 2), 384.0), ((3, 4), 256.0)):
        for nt in range(4):
            ns = slice(nt * 512, (nt + 1) * 512)
            sbk, dsbk = C.bank(hold=True)
            for ci, c in enumerate(chunks):
                bk, dbk = C.bank()
                mm_acc(P, bk[:], [(wi[:, kc, c * 128:(c + 1) * 128], xs[:, kc, ns]) for kc in range(8)], reads=dxs + dwi, dwrite=dbk)
                P.op("act", lambda e, c=c, ns=ns, bk=bk: e.copy(out=hT[:, c, ns], in_=bk[:]), reads=[dbk], writes=[dhT], merge=True)
                sq, dsq = sq_r.next()
                P.op("act", lambda e, sq=sq, bk=bk: e.activation(out=sq[:], in_=bk[:], func=AF.Square), reads=[dbk], writes=[dsq])
                P.op("pe", lambda e, sbk=sbk, sq=sq, ci=ci, chunks=chunks: e.matmul(sbk[:], lhsT=ones[:], rhs=sq[:], start=(ci == 0), stop=(ci == len(chunks) - 1)),
                     reads=[dsq, dc], writes=[dsbk], merge=(ci > 0))
            r, dr = r_r.next()
            P.op("dve", lambda e, r=r, sbk=sbk, n=n: e.tensor_scalar(out=r[:], in0=sbk[:], scalar1=1.0 / n, scalar2=EPS, op0=ALU.mult, op1=ALU.add), reads=[dsbk], writes=[dr])
            C.release_all()
            P.op("act", lambda e, r=r: e.activation(out=r[:], in_=r[:], func=AF.Sqrt), reads=[dr], writes=[dr])
            P.op("dve", lambda e, r=r: e.reciprocal(out=r[:], in_=r[:]), reads=[dr], writes=[dr])
            for c in chunks:
                P.op("dve", lambda e, c=c, ns=ns, r=r: e.scalar_tensor_tensor(out=nT[:, c, ns], in0=hT[:, c, ns], scalar=gcol[:, c:c + 1], in1=r[:], op0=ALU.mult, op1=ALU.mult),
                     reads=[dhT, dr, dc], writes=[dnT[c]], merge=True)
    for c in range(5):
        P.dma("act", nT_d[c * 128:(c + 1) * 128, :], nT[:, c, :], reads=[dnT[c]], writes=[C.dd("mla_nT")], merge=True)
    krT = P.sb("mkr", [96, 2048], BF16)
    dkr = Dep()
    ta_r = Ring(P, "mta", 2, [96, 512], F32)
    tb_r = Ring(P, "mtb", 2, [96, 512], F32)
    for nt in range(4):
        ns = slice(nt * 512, (nt + 1) * 512)
        bk, dbk = C.bank()
        bs, dbs = C.bank()
        mm_acc(P, bk[0:96, :], [(wi[:, kc, 576:672], xs[:, kc, ns]) for kc in range(8)], reads=dxs + dwi, dwrite=dbk)
        mm_acc(P, bs[0:96, :], [(wsw[:, kc, :], xs[:, kc, ns]) for kc in range(8)], reads=dxs + dwsw, dwrite=dbs)
        ta, dta = ta_r.next()
        tb, dtb = tb_r.next()
        P.op("dve", lambda e, ta=ta, bk=bk, ns=ns: e.tensor_tensor(out=ta[64:96, :], in0=bk[64:96, :], in1=cs[64:96, 0, ns], op=ALU.mult), reads=[dbk, dc], writes=[dta])
        P.op("dve", lambda e, tb=tb, bs=bs, ns=ns: e.tensor_tensor(out=tb[64:96, :], in0=bs[64:96, :], in1=cs[64:96, 1, ns], op=ALU.mult), reads=[dbs, dc], writes=[dtb])
        P.op("pool", lambda e, ta=ta, tb=tb, ns=ns: e.tensor_tensor(out=krT[64:96, ns], in0=ta[64:96, :], in1=tb[64:96, :], op=ALU.add), reads=[dta, dtb], writes=[dkr], merge=(nt > 0))
    P.dma("pool", kr_d, krT[64:96, :], reads=[dkr], writes=[C.dd("mla_krT")])


def stage_mla2(C):
    P = C.P
    P.begin_stage()
    nT_d = C.D("mla_nT", [640, 2048], BF16)
    kr_d = C.D("mla_krT", [32, 2048], BF16)
    cs_d = C.D("mla_cs2", [32, 2, 2048], F32)
    wq_d = C.D("mla_w_q_up", [384, 1536], F32)
    wqs_d = C.D("mla_w_q_sw", [384, 1536], F32)
    wk_d = C.D("mla_w_kv_k", [256, 1024], F32)
    wv_d = C.D("mla_w_kv_v", [256, 1024], F32)
    mixT = C.D("mixT", [1024, 2048], BF16)
    nT = P.sb("anT", [128, 5, 2048], BF16)
    dnT = [Dep() for _ in range(5)]
    for c in range(5):
        P.dma("sp", nT[:, c, :], nT_d[c * 128:(c + 1) * 128, :], reads=[C.dd("mla_nT")], writes=[dnT[c]])
    dq = dnT[0:3]
    dkv = dnT[3:5]
    dc = Dep()
    KRT = P.sb("aKRT", [96, 2048], BF16)
    P.dma("sp", KRT[64:96, :], kr_d, reads=[C.dd("mla_krT")], writes=[dc])
    cs = P.sb("acs", [96, 2, 2048], F32)
    P.dma("sp", cs[64:96, :, :], cs_d, writes=[dc], merge=True)
    wq, dwq = load_fm_bf16(C, "awq", wq_d, 3, 1536)
    wqs, dwqs = load_fm_bf16(C, "awqs", wqs_d, 3, 1536)
    wk, dwk = load_fm_bf16(C, "awk", wk_d, 2, 1024)
    wv, dwv = load_fm_bf16(C, "awv", wv_d, 2, 1024)
    onesf = P.sb("aones", [128, 64], F32)
    P.op("pool", lambda e: e.memset(onesf[:], 1.0), writes=[dc], merge=True)
    Vx = P.sb("aVx", [128, 16, 16, 65], BF16)
    dV = Dep()
    P.op("pool", lambda e: e.memset(Vx[:], 1.0), writes=[dV])
    for tt in range(16):
        for half in range(2):
            bk, dbk = C.bank()
            mm_acc(P, bk[:], [(nT[:, 3 + kc, tt * 128:(tt + 1) * 128], wv[:, kc, half * 512:(half + 1) * 512]) for kc in range(2)], reads=dkv + dwv, dwrite=dbk)
            P.op("act", lambda e, tt=tt, half=half, bk=bk: e.copy(out=Vx[:, tt, half * 8:(half + 1) * 8, 0:64], in_=bk[:].rearrange("p (h d) -> p h d", h=8)),
                 reads=[dbk], writes=[dV], merge=True)
    QT_r = Ring(P, "aQT", 2, [96, 2048], BF16)
    KT_r = Ring(P, "aKT", 2, [96, 2048], BF16)
    ta_r = Ring(P, "ata", 2, [96, 512], F32)
    tb_r = Ring(P, "atb", 2, [96, 512], F32)
    p_r = Ring(P, "apT", 4, [128, 512], BF16)
    rd_r = Ring(P, "ard", 2, [65, 512], F32)
    bs_r = Ring(P, "absb", 2, [64, 512], F32)
    yo_r = Ring(P, "ayo", 3, [64, 512], BF16)
    for h in range(16):
        QT, dQT = QT_r.next()
        KT, dKT = KT_r.next()
        for nt in range(4):
            ns = slice(nt * 512, (nt + 1) * 512)
            bq, dbq = C.bank()
            bs, dbs = C.bank()
            mm_acc(P, bq[0:96, :], [(wq[:, kc, h * 96:(h + 1) * 96], nT[:, kc, ns]) for kc in range(3)], reads=dq + dwq, dwrite=dbq)
            mm_acc(P, bs[0:96, :], [(wqs[:, kc, h * 96:(h + 1) * 96], nT[:, kc, ns]) for kc in range(3)], reads=dq + dwqs, dwrite=dbs)
            P.op("act", lambda e, QT=QT, ns=ns, bq=bq: e.copy(out=QT[0:64, ns], in_=bq[0:64, :]), reads=[dbq], writes=[dQT], merge=(nt > 0))
            ta, dta = ta_r.next()
            tb, dtb = tb_r.next()
            P.op("dve", lambda e, ta=ta, bq=bq, ns=ns: e.tensor_tensor(out=ta[64:96, :], in0=bq[64:96, :], in1=cs[64:96, 0, ns], op=ALU.mult), reads=[dbq, dc], writes=[dta])
            P.op("dve", lambda e, tb=tb, bs=bs, ns=ns: e.tensor_tensor(out=tb[64:96, :], in0=bs[64:96, :], in1=cs[64:96, 1, ns], op=ALU.mult), reads=[dbs, dc], writes=[dtb])
            P.op("pool", lambda e, QT=QT, ta=ta, tb=tb, ns=ns: e.tensor_tensor(out=QT[64:96, ns], in0=ta[64:96, :], in1=tb[64:96, :], op=ALU.add), reads=[dta, dtb], writes=[dQT], merge=True)
            bkk, dbkk = C.bank()
            mm_acc(P, bkk[0:64, :], [(wk[:, kc, h * 64:(h + 1) * 64], nT[:, 3 + kc, ns]) for kc in range(2)], reads=dkv + dwk, dwrite=dbkk)
            P.op("act", lambda e, KT=KT, ns=ns, bkk=bkk: e.copy(out=KT[0:64, ns], in_=bkk[0:64, :]), reads=[dbkk], writes=[dKT], merge=(nt > 0))
        P.op("pool", lambda e, KT=KT: e.tensor_copy(out=KT[64:96, :], in_=KRT[64:96, :]), reads=[dc], writes=[dKT], merge=True)
        for qc in range(4):
            qs = slice(qc * 512, (qc + 1) * 512)
            acc, dacc = C.bank(hold=True)

            def pv(kt, pT, dpT, acc=acc, dacc=dacc, h=h):
                P.op("pe", lambda e, acc=acc, kt=kt, h=h, pT=pT: e.matmul(acc[0:65, :], lhsT=Vx[:, kt, h, :], rhs=pT[:], start=(kt == 0), stop=(kt == 15)),
                     reads=[dV, dpT], writes=[dacc], merge=(kt > 0))
            pend = None
            for kt in range(16):
                sb_, dsb_ = C.bank()
                P.op("pe", lambda e, sb_=sb_, KT=KT, QT=QT, kt=kt, qs=qs: e.matmul(sb_[:], lhsT=KT[0:96, kt * 128:(kt + 1) * 128], rhs=QT[0:96, qs], start=True, stop=True),
                     reads=[dKT, dQT], writes=[dsb_])
                pT, dpT = p_r.next()
                P.op("act", lambda e, pT=pT, sb_=sb_: e.activation(out=pT[:], in_=sb_[:], func=AF.Exp, scale=MLA_SCALE), reads=[dsb_], writes=[dpT])
                if pend is not None:
                    pv(*pend)
                pend = (kt, pT, dpT)
            pv(*pend)
            rd, drd = rd_r.next()
            P.op("dve", lambda e, rd=rd, acc=acc: e.reciprocal(out=rd[64:65, :], in_=acc[64:65, :]), reads=[dacc], writes=[drd])
            bb, dbb = C.bank()
            P.op("pe", lambda e, bb=bb, rd=rd: e.matmul(bb[0:64, :], lhsT=onesf[64:65, 0:64], rhs=rd[64:65, :], start=True, stop=True), reads=[drd, dc], writes=[dbb])
            bsb, dbsb = bs_r.next()
            P.op("act", lambda e, bsb=bsb, bb=bb: e.copy(out=bsb[:], in_=bb[0:64, :]), reads=[dbb], writes=[dbsb])
            yo, dyo = yo_r.next()
            P.op("dve", lambda e, yo=yo, acc=acc, bsb=bsb: e.tensor_tensor(out=yo[:], in0=acc[0:64, :], in1=bsb[:], op=ALU.mult), reads=[dacc, dbsb], writes=[dyo])
            C.release_all()
            P.dma("pool", mixT[h * 64:(h + 1) * 64, qs], yo[:], reads=[dyo], writes=[C.dd("mixT")], merge=True)


def _rep128(v):
    v = np.asarray(v, np.float32)
    return np.ascontiguousarray(np.broadcast_to(v[None, :], (128, v.shape[0])))


def shared_inputs(inp):
    f32 = lambda a: np.ascontiguousarray(np.asarray(a, np.float32))
    s = {}
    s["ab_w_in"] = f32(inp["ab_w_in"][0])
    s["na_tab"] = na_tables(np.asarray(inp["na_rpb"][0], np.float32))
    s.update(hyena_consts())
    s["hy_f_w1"] = f32(inp["hy_f_w1"][0])
    s["hy_f_w2"] = f32(inp["hy_f_w2"][0])
    s["hy_f_w3"] = f32(inp["hy_f_w3"][0])
    s["hy_cols"] = f32(np.stack([inp["hy_f_b1"][0], inp["hy_f_freq"][0], inp["hy_f_b2"][0]], axis=1))
    s["hy_skipb"] = np.stack([_rep128(inp["hy_skip"][0][0]), _rep128(inp["hy_skip"][0][1])])
    s["hy_cw"] = f32(np.asarray(inp["hy_conv_w"][0]).reshape(3, 12, 128).transpose(2, 1, 0))
    s["hy_cb"] = f32(np.asarray(inp["hy_conv_b"][0]).reshape(12, 128).T)
    s["ident"] = np.eye(128, dtype=np.float32)
    esel = np.zeros((16, 16, 128), np.float32)
    for e in range(16):
        esel[e, e, :] = 1.0
    s["esel"] = esel
    s["iota_col"] = (np.arange(16)[None, :] * 128 + np.arange(128)[:, None]).astype(np.float32)
    s["iota_row"] = _rep128(np.arange(2048, dtype=np.float32))
    s["ab_w_out"] = f32(inp["ab_w_out"][0])
    w_in = np.asarray(inp["mla_w_in"][0], np.float32)
    perm = np.concatenate([np.arange(16, 32), np.arange(0, 16)])
    s["mla_w_in"] = f32(w_in)
    s["mla_w_in_sw"] = f32(np.concatenate([w_in[:, 576:640], w_in[:, 640 + perm]], axis=1))
    wq = np.asarray(inp["mla_w_q_up"][0], np.float32)
    wqs = wq.reshape(384, 16, 96).copy()
    wqs[:, :, 64:] = wqs[:, :, 64 + perm]
    s["mla_w_q_up"] = f32(wq)
    s["mla_w_q_sw"] = f32(wqs.reshape(384, 1536))
    wkv = np.asarray(inp["mla_w_kv_up"][0], np.float32).reshape(256, 16, 128)
    s["mla_w_kv_k"] = f32(wkv[:, :, :64].reshape(256, 1024))
    s["mla_w_kv_v"] = f32(wkv[:, :, 64:].reshape(256, 1024))
    s["mla_gcols"] = f32(np.concatenate([np.asarray(inp["mla_q_norm"][0]).reshape(3, 128).T,
                                         np.asarray(inp["mla_kv_norm"][0]).reshape(2, 128).T], axis=1))
    s.update(mla_consts())
    s["mla_w_out"] = f32(inp["mla_w_out"][0])
    for li in range(2):
        s[f"ln1_g{li}"] = _rep128(inp["ln1_g"][li])
        s[f"ln1_b{li}"] = _rep128(inp["ln1_b"][li])
        s[f"ln2_g{li}"] = _rep128(inp["ln2_g"][li])
        s[f"ln2_b{li}"] = _rep128(inp["ln2_b"][li])
        s[f"moe_router{li}"] = f32(inp["moe_router"][li])
        s[f"moe_w_gate{li}"] = f32(inp["moe_w_gate"][li])
        s[f"moe_w_up{li}"] = f32(inp["moe_w_up"][li])
        s[f"moe_w_down{li}"] = f32(inp["moe_w_down"][li])
        s[f"ple_gate{li}"] = f32(inp["ple_gate"][li])
        s[f"ple_proj{li}"] = f32(inp["ple_proj"][li])
    return s


PER_CORE = ("x_tm", "xT", "pT0", "pT1")


def build_full(shared_names):
    C = Ctx(ext_in=set(shared_names) | set(PER_CORE), ext_out={"out"})
    stage_a1(C)
    stage_a2(C)
    stage_hy_filter(C)
    stage_hy_prep(C)
    stage_hy_conv(C)
    stage_proj_ln(C, "ab_w_out", "x_tm", "ln1", 0, "x1_0")
    stage_moe1(C, 0, "x1_0")
    stage_moe2(C, 0, "x1_0", "x2_0")
    stage_ple(C, 0, "x2_0", "x3_0", True)
    stage_mla1(C, "x3_0T")
    stage_mla2(C)
    stage_proj_ln(C, "mla_w_out", "x3_0", "ln1", 1, "x1_1")
    stage_moe1(C, 1, "x1_1")
    stage_moe2(C, 1, "x1_1", "x2_1")
    stage_ple(C, 1, "x2_1", "out", False)
    C.P.finish()
    return C


def kernel(**inputs):
    inp = {k: np.asarray(v) for k, v in inputs.items()}
    shared = shared_inputs(inp)
    x = np.asarray(inp["x"], np.float32)
    p = np.asarray(inp["p"], np.float32)
    C = build_full(shared.keys())
    used = set(C.dram.keys())
    in_maps = []
    for b in range(8):
        m = {k: v for k, v in shared.items() if k in used}
        m["x_tm"] = np.ascontiguousarray(x[b])
        m["xT"] = np.ascontiguousarray(x[b].T)
        m["pT0"] = np.ascontiguousarray(p[0, b].T)
        m["pT1"] = np.ascontiguousarray(p[1, b].T)
        in_maps.append(m)
    res = run_bass_kernel_spmd(C.nc, in_maps, core_ids=list(range(8)))
    return np.stack([np.asarray(r["out"], np.float32) for r in res.results], axis=0)
```

```python
from contextlib import ExitStack
import math
import numpy as np
import ml_dtypes
import concourse.bass as bass
import concourse.mybir as mybir
from concourse.bass_utils import run_bass_kernel_spmd

F32 = mybir.dt.float32
BF16 = mybir.dt.bfloat16
I32 = mybir.dt.int32
U32 = mybir.dt.uint32
AF = mybir.ActivationFunctionType
ALU = mybir.AluOpType
AX = mybir.AxisListType
NPBF = ml_dtypes.bfloat16

D_MODEL = 1024
SEQ = 2048
NT = SEQ // 128
ALPHA = 4.0 ** 0.25
EPS = 1e-5

ENGS = ("pe", "act", "dve", "pool", "sp")
N_DMA_SEMS = 16


class Dep:
    __slots__ = ("w", "r", "name")

    def __init__(self, name=""):
        self.w = {}
        self.r = {}
        self.name = name


class Prog:
    def __init__(self, nc, strict=True):
        self.nc = nc
        self.es = ExitStack()
        self.q = {e: [] for e in ENGS}
        self.cnt = {e: 0 for e in ENGS}
        self.seen = {e: {} for e in ENGS}
        self.sem = {}
        self.strict = strict
        for e in ENGS:
            self.sem[e] = self.es.enter_context(nc.semaphore("s_" + e))
        self.dma_sems = {}
        self.dma_tot = {}
        self.dma_rr = {}
        for e in ("sp", "pool", "act"):
            self.dma_sems[e] = [self.es.enter_context(nc.semaphore(f"d_{e}{i}")) for i in range(N_DMA_SEMS)]
            self.dma_tot[e] = [0] * N_DMA_SEMS
            self.dma_rr[e] = 0
        self.all_events = {}
        self.n_ops = 0
        self.stage_es = None
        self.uid = 0

    def begin_stage(self):
        self.barrier()
        if self.stage_es is not None:
            self.stage_es.close()
        self.stage_es = ExitStack()

    def sb(self, name, shape, dt):
        self.uid += 1
        t = self.stage_es.enter_context(self.nc.sbuf_tensor(f"{name}_{self.uid}", list(shape), dt))
        return t

    def ps(self, name, shape, dt=F32):
        return self.es.enter_context(self.nc.psum_tensor(name, list(shape), dt))

    def _semobj(self, key):
        if isinstance(key, str):
            return self.sem[key]
        e, i = key
        return self.dma_sems[e][i]

    def _need(self, eng, reads, writes, merge):
        need = {}

        def add(k, v):
            if k == eng and (not self.strict or eng == "pe"):
                return
            if self.seen[eng].get(k, 0) >= v:
                return
            if need.get(k, 0) < v:
                need[k] = v
        for d in reads:
            for k, v in d.w.items():
                add(k, v)
        for d in writes:
            if not merge:
                for k, v in d.w.items():
                    add(k, v)
            for k, v in d.r.items():
                add(k, v)
        for k, v in need.items():
            self.seen[eng][k] = v
        return list(need.items())

    def _commit(self, ev, reads, writes, merge):
        k, v = ev
        for d in reads:
            if d.r.get(k, 0) < v:
                d.r[k] = v
        for d in writes:
            if merge:
                d.w[k] = v
            else:
                d.w = {k: v}
                d.r = {}
        self.all_events[k] = v

    def op(self, eng, fn, reads=(), writes=(), merge=False):
        waits = self._need(eng, reads, writes, merge)
        self.cnt[eng] += 1
        ev = (eng, self.cnt[eng])
        sem = self.sem[eng]
        waitobjs = [(self._semobj(k), v) for k, v in waits]

        def emit(e, fn=fn, waitobjs=waitobjs, sem=sem):
            for s, v in waitobjs:
                e.wait_ge(s, v)
            fn(e).then_inc(sem, 1)
        self.q[eng].append(emit)
        self._commit(ev, reads, writes, merge)
        self.n_ops += 1
        return ev

    def dma(self, eng, out, in_, reads=(), writes=(), merge=False, **kw):
        i = self.dma_rr[eng]
        self.dma_rr[eng] = (i + 1) % N_DMA_SEMS
        key = (eng, i)
        prev = self.dma_tot[eng][i]
        waits = self._need(eng, reads, writes, merge)
        if prev > 0 and self.seen[eng].get(key, 0) < prev:
            waits.append((key, prev))
            self.seen[eng][key] = prev
        self.dma_tot[eng][i] = prev + 16
        ev = (key, prev + 16)
        sem = self.dma_sems[eng][i]
        waitobjs = [(self._semobj(k), v) for k, v in waits]

        def emit(e, waitobjs=waitobjs, sem=sem, out=out, in_=in_, kw=kw):
            for s, v in waitobjs:
                e.wait_ge(s, v)
            e.dma_start(out=out, in_=in_, **kw).then_inc(sem, 16)
        self.q[eng].append(emit)
        self._commit(ev, reads, writes, merge)
        self.n_ops += 1
        return ev

    def coll(self, kind, out, in_, reads=(), writes=()):
        eng = "pool"
        i = self.dma_rr[eng]
        self.dma_rr[eng] = (i + 1) % N_DMA_SEMS
        key = (eng, i)
        prev = self.dma_tot[eng][i]
        waits = self._need(eng, reads, writes, False)
        if prev > 0 and self.seen[eng].get(key, 0) < prev:
            waits.append((key, prev))
            self.seen[eng][key] = prev
        self.dma_tot[eng][i] = prev + 16
        ev = (key, prev + 16)
        sem = self.dma_sems[eng][i]
        waitobjs = [(self._semobj(k), v) for k, v in waits]

        def emit(e, waitobjs=waitobjs, sem=sem, out=out, in_=in_, kind=kind):
            for s_, v in waitobjs:
                e.wait_ge(s_, v)
            e.collective_compute(kind, ALU.bypass, replica_groups=[list(range(8))], ins=[in_], outs=[out]).then_inc(sem, 16)
        self.q[eng].append(emit)
        self._commit(ev, reads, writes, False)
        self.n_ops += 1
        return ev

    def barrier(self):
        snap = dict(self.all_events)
        for eng in ENGS:
            waits = []
            for k, v in snap.items():
                if k == eng:
                    continue
                if self.seen[eng].get(k, 0) >= v:
                    continue
                waits.append((self._semobj(k), v))
                self.seen[eng][k] = v
            if waits:
                def emit(e, waits=waits):
                    for s, v in waits:
                        e.wait_ge(s, v)
                self.q[eng].append(emit)

    def finish(self):
        self.barrier()
        nc = self.nc
        q = self.q
        with nc.Block() as block:
            @block.tensor
            def _(e):
                for f in q["pe"]:
                    f(e)

            @block.scalar
            def _(e):
                for f in q["act"]:
                    f(e)

            @block.vector
            def _(e):
                for f in q["dve"]:
                    f(e)

            @block.gpsimd
            def _(e):
                for f in q["pool"]:
                    f(e)

            @block.sync
            def _(e):
                for f in q["sp"]:
                    f(e)
        if self.stage_es is not None:
            self.stage_es.close()
        self.es.close()


class Ring:
    def __init__(self, P, name, n, shape, dt):
        self.bufs = [(P.sb(f"{name}{i}", shape, dt), Dep(f"{name}{i}")) for i in range(n)]
        self.i = 0

    def next(self):
        b = self.bufs[self.i]
        self.i = (self.i + 1) % len(self.bufs)
        return b


class Ctx:
    def __init__(self, ext_in, ext_out):
        self.nc = bass.Bass("TRN2", target_bir_lowering=False)
        self.P = Prog(self.nc)
        self.ext_in = set(ext_in)
        self.ext_out = set(ext_out)
        self.dram = {}
        self.ddep = {}
        P = self.P
        self.banks = [(P.ps(f"bank{i}", [128, 512], F32), Dep(f"bank{i}")) for i in range(8)]
        self.bank_i = 0
        self.held = set()

    def D(self, name, shape=None, dt=F32):
        if name in self.dram:
            return self.dram[name]
        kind = "Internal"
        if name in self.ext_in:
            kind = "ExternalInput"
        elif name in self.ext_out:
            kind = "ExternalOutput"
        t = self.nc.dram_tensor(name, list(shape), dt, kind=kind).ap()
        self.dram[name] = t
        self.ddep[name] = Dep(name)
        return t

    def dd(self, name):
        return self.ddep[name]

    def bank(self, hold=False):
        for _ in range(8):
            i = self.bank_i
            self.bank_i = (self.bank_i + 1) % 8
            if i not in self.held:
                if hold:
                    self.held.add(i)
                return self.banks[i]
        raise RuntimeError("no free PSUM bank")

    def release_all(self):
        self.held = set()

    def release(self, b):
        for i, bb in enumerate(self.banks):
            if bb[0] is b[0]:
                self.held.discard(i)


def mm_acc(P, out, pairs, reads, dwrite):
    n = len(pairs)
    for i, (l, r) in enumerate(pairs):
        P.op("pe", lambda e, l=l, r=r, i=i: e.matmul(out, lhsT=l, rhs=r, start=(i == 0), stop=(i == n - 1)),
             reads=reads, writes=[dwrite], merge=(i > 0))


def load_fm_bf16(C, name, src, kc_n, width, eng="pool"):
    P = C.P
    t = P.sb(name, [128, kc_n, width], BF16)
    deps = [Dep(f"{name}{k}") for k in range(kc_n)]
    for k in range(kc_n):
        P.dma(eng, t[:, k, :], src[k * 128:(k + 1) * 128, :], writes=[deps[k]])
    return t, deps


def stage_a1(C):
    P = C.P
    P.begin_stage()
    xT = C.D("xT", [1024, 2048], F32)
    w_in = C.D("ab_w_in", [1024, 3072], F32)
    qkT = C.D("qkT", [1024, 2048], BF16)
    v_tm = C.D("v_tm", [2048, 512], BF16)
    hbT = C.D("hbT", [1536, 2048], F32)
    xs, dxs = load_fm_bf16(C, "xTb", xT, 8, 2048)
    ws, dws = load_fm_bf16(C, "winb", w_in, 8, 3072)
    st_b = Ring(P, "a1sb", 3, [128, 2048], BF16)
    st_f = Ring(P, "a1sf", 3, [128, 2048], F32)
    ev_i = 0
    for mc in list(range(8)) + list(range(12, 24)):
        is_hb = mc >= 12
        stg, dstg = (st_f if is_hb else st_b).next()
        for nt in range(4):
            bk, dbk = C.bank()
            mm_acc(P, bk[:], [(ws[:, kc, mc * 128:(mc + 1) * 128], xs[:, kc, nt * 512:(nt + 1) * 512]) for kc in range(8)],
                   reads=dxs + dws, dwrite=dbk)
            eng = "act" if ev_i % 2 == 0 else "dve"
            ev_i += 1
            o = stg[:, nt * 512:(nt + 1) * 512]
            if eng == "act":
                P.op("act", lambda e, o=o, bk=bk: e.copy(out=o, in_=bk[:]), reads=[dbk], writes=[dstg], merge=(nt > 0))
            else:
                P.op("dve", lambda e, o=o, bk=bk: e.tensor_copy(out=o, in_=bk[:]), reads=[dbk], writes=[dstg], merge=(nt > 0))
        if is_hb:
            P.dma("act", hbT[(mc - 12) * 128:(mc - 11) * 128, :], stg[:], reads=[dstg], writes=[C.dd("hbT")], merge=True)
        else:
            P.dma("act", qkT[mc * 128:(mc + 1) * 128, :], stg[:], reads=[dstg], writes=[C.dd("qkT")], merge=True)
    st_v = Ring(P, "a1sv", 3, [128, 512], BF16)
    for tt in range(NT):
        bk, dbk = C.bank()
        mm_acc(P, bk[:], [(xs[:, kc, tt * 128:(tt + 1) * 128], ws[:, kc, 1024:1536]) for kc in range(8)],
               reads=dxs + dws, dwrite=dbk)
        stg, dstg = st_v.next()
        if tt % 2 == 0:
            P.op("act", lambda e, stg=stg, bk=bk: e.copy(out=stg[:], in_=bk[:]), reads=[dbk], writes=[dstg])
        else:
            P.op("dve", lambda e, stg=stg, bk=bk: e.tensor_copy(out=stg[:], in_=bk[:]), reads=[dbk], writes=[dstg])
        P.dma("act", v_tm[tt * 128:(tt + 1) * 128, :], stg[:], reads=[dstg], writes=[C.dd("v_tm")], merge=True)


def na_plan():
    rows, wr = 32, 8
    r0 = np.clip(np.arange(rows) - wr // 2, 0, rows - wr)
    plan = []
    keys = {}
    for i in range(16):
        lo = r0[2 * i] // 2
        hi = (r0[2 * i + 1] + 7) // 2
        lst = []
        for j in range(lo, hi + 1):
            val = []
            for ak in range(2):
                for aq in range(2):
                    r = 2 * i + aq
                    kr = 2 * j + ak
                    val.append(bool(r0[r] <= kr <= r0[r] + 7))
            key = (j - i, tuple(val))
            if key not in keys:
                keys[key] = len(keys)
            lst.append((j, keys[key]))
        plan.append(lst)
    return plan, keys


def na_tables(rpb):
    plan, keys = na_plan()
    c = np.arange(64)
    c0 = np.clip(c - 8, 0, 48)
    col_ok = (c[None, :] >= c0[:, None]) & (c[None, :] < c0[:, None] + 16)
    dc_idx = np.clip(c[None, :] - c[:, None], -15, 15) + 15
    tab = np.full((len(keys), 2, 64, 8, 2, 64), -1e30, np.float32)
    for (delta, val), tid in keys.items():
        vi = 0
        for ak in range(2):
            for aq in range(2):
                ok = val[vi]
                vi += 1
                if not ok:
                    continue
                dr = 2 * delta + ak - aq
                b = rpb[:, dr + 7, :][:, dc_idx]
                b = np.where(col_ok[None], b, np.float32(-1e30))
                tab[tid, ak, :, :, aq, :] = b.transpose(2, 0, 1)
    return tab.reshape(len(keys), 128, 8, 128)


def stage_a2(C):
    P = C.P
    P.begin_stage()
    plan, keys = na_plan()
    ntab = len(keys)
    qkT = C.D("qkT", [1024, 2048], BF16)
    v_tm = C.D("v_tm", [2048, 512], BF16)
    tab_d = C.D("na_tab", [ntab, 128, 8, 128], F32)
    mixT = C.D("mixT", [1024, 2048], BF16)
    QT = P.sb("QT", [128, 4, 2048], BF16)
    KT = P.sb("KT", [128, 4, 2048], BF16)
    dQ = [Dep() for _ in range(4)]
    dK = [Dep() for _ in range(4)]
    for hp in range(4):
        P.dma("sp", QT[:, hp, :], qkT[hp * 128:(hp + 1) * 128, :], reads=[C.dd("qkT")], writes=[dQ[hp]])
        P.dma("sp", KT[:, hp, :], qkT[512 + hp * 128:512 + (hp + 1) * 128, :], reads=[C.dd("qkT")], writes=[dK[hp]])
    tab = P.sb("natab", [128, ntab, 8, 128], F32)
    dtab = Dep()
    for t in range(ntab):
        P.dma("sp", tab[:, t, :, :], tab_d[t], writes=[dtab], merge=True)
    Vx = P.sb("Vx", [128, 8, NT, 128], BF16)
    dV = Dep()
    P.op("pool", lambda e: e.memset(Vx[:], 0.0), writes=[dV])
    vv = v_tm.rearrange("(t p) c -> p t c", p=128)
    for h in range(8):
        a = h % 2
        P.dma("sp", Vx[:, h, :, a * 64:(a + 1) * 64], vv[:, :, h * 64:(h + 1) * 64], reads=[C.dd("v_tm")], writes=[dV], merge=(h > 0))
    ones2 = P.sb("ones2", [128, 2, 128], BF16)
    dones = Dep()
    P.op("pool", lambda e: e.memset(ones2[:], 0.0), writes=[dones])
    P.op("pool", lambda e: e.memset(ones2[:, 0, 0:64], 1.0), writes=[dones])
    P.op("pool", lambda e: e.memset(ones2[:, 1, 64:128], 1.0), writes=[dones])
    yaT = P.sb("yaT", [128, 4, 2048], BF16)
    dya = [Dep() for _ in range(4)]
    s_ring = Ring(P, "na_s", 3, [128, 640], F32)
    p_ring = Ring(P, "na_p", 4, [128, 640], BF16)
    rd_ring = Ring(P, "na_rd", 2, [128, 128], F32)
    units = [(i, hp, a) for i in range(16) for hp in range(4) for a in range(2)]
    pair = {}

    def phase1(u):
        i, hp, a = u
        q0 = i * 128
        h = hp * 2 + a
        pa = slice(a * 64, (a + 1) * 64)
        lst = plan[i]
        nkb = len(lst)
        bA, dbA = C.bank()
        bB, dbB = (C.bank() if nkb > 4 else (None, None))
        ssb, dss = s_ring.next()
        for jj, (j, tid) in enumerate(lst):
            bk, dbk = (bA, dbA) if jj < 4 else (bB, dbB)
            o = bk[:, (jj % 4) * 128:(jj % 4 + 1) * 128]
            P.op("pe", lambda e, o=o, j=j, pa=pa, hp=hp, q0=q0: e.matmul(
                o, lhsT=KT[pa, hp, j * 128:(j + 1) * 128], rhs=QT[pa, hp, q0:q0 + 128], start=True, stop=True),
                reads=[dK[hp], dQ[hp]], writes=[dbk], merge=(jj % 4 > 0))
        for jj, (j, tid) in enumerate(lst):
            bk, dbk = (bA, dbA) if jj < 4 else (bB, dbB)
            o = bk[:, (jj % 4) * 128:(jj % 4 + 1) * 128]
            P.op("dve", lambda e, o=o, jj=jj, tid=tid, h=h, ssb=ssb: e.scalar_tensor_tensor(
                out=ssb[:, jj * 128:(jj + 1) * 128], in0=o, scalar=0.125, in1=tab[:, tid, h, :],
                op0=ALU.mult, op1=ALU.add), reads=[dbk, dtab], writes=[dss], merge=(jj > 0))
        pT, dpT = p_ring.next()
        P.op("act", lambda e, pT=pT, ssb=ssb, nkb=nkb: e.activation(
            out=pT[:, 0:nkb * 128], in_=ssb[:, 0:nkb * 128], func=AF.Exp), reads=[dss], writes=[dpT])
        return (pT, dpT)

    def phase2(u, pp):
        i, hp, a = u
        q0 = i * 128
        h = hp * 2 + a
        pT, dpT = pp
        lst = plan[i]
        nkb = len(lst)
        if a == 0:
            pair[(i, hp)] = (C.bank(hold=True), C.bank(hold=True))
        (bo, dbo), (bd, dbd) = pair[(i, hp)]
        for jj, (j, tid) in enumerate(lst):
            first = (a == 0 and jj == 0)
            last = (a == 1 and jj == nkb - 1)
            P.op("pe", lambda e, bo=bo, h=h, j=j, pT=pT, jj=jj, first=first, last=last: e.matmul(
                bo[:, 0:128], lhsT=Vx[:, h, j, :], rhs=pT[:, jj * 128:(jj + 1) * 128], start=first, stop=last),
                reads=[dV, dpT], writes=[dbo], merge=(not first))
            P.op("pe", lambda e, bd=bd, a=a, pT=pT, jj=jj, first=first, last=last: e.matmul(
                bd[:, 0:128], lhsT=ones2[:, a, :], rhs=pT[:, jj * 128:(jj + 1) * 128], start=first, stop=last),
                reads=[dones, dpT], writes=[dbd], merge=(not first))
        if a == 1:
            rd, drd = rd_ring.next()
            P.op("dve", lambda e, rd=rd, bd=bd: e.reciprocal(out=rd[:], in_=bd[:, 0:128]), reads=[dbd], writes=[drd])
            P.op("dve", lambda e, rd=rd, bo=bo, hp=hp, q0=q0: e.tensor_tensor(
                out=yaT[:, hp, q0:q0 + 128], in0=bo[:, 0:128], in1=rd[:], op=ALU.mult),
                reads=[dbo, drd], writes=[dya[hp]], merge=True)
            C.release(pair[(i, hp)][0])
            C.release(pair[(i, hp)][1])
            del pair[(i, hp)]

    pend = None
    for u in units:
        pp = phase1(u)
        if pend is not None:
            phase2(*pend)
        pend = (u, pp)
    phase2(*pend)
    for hp in range(4):
        P.dma("sp", mixT[hp * 128:(hp + 1) * 128, :], yaT[:, hp, :], reads=[dya[hp]], writes=[C.dd("mixT")], merge=True)


class LNBufs:
    def __init__(self, P, name):
        self.stats = Ring(P, name + "st", 2, [128, 2, 6], F32)
        self.mv = Ring(P, name + "mv", 2, [128, 2], F32)
        self.rstd = Ring(P, name + "rs", 2, [128, 1], F32)
        self.nmr = Ring(P, name + "nm", 2, [128, 1], F32)


def layer_norm_tile(P, lb, r, dr, gb, bb, dgb, y, dy):
    st, dst = lb.stats.next()
    mv, dmv = lb.mv.next()
    rs, drs = lb.rstd.next()
    nm, dnm = lb.nmr.next()
    P.op("dve", lambda e: e.bn_stats(out=st[:, 0, :], in_=r[:, 0:512]), reads=[dr], writes=[dst])
    P.op("dve", lambda e: e.bn_stats(out=st[:, 1, :], in_=r[:, 512:1024]), reads=[dr], writes=[dst], merge=True)
    P.op("dve", lambda e: e.bn_aggr(out=mv[:], in_=st[:]), reads=[dst], writes=[dmv])
    P.op("dve", lambda e: e.tensor_scalar(out=rs[:], in0=mv[:, 1:2], scalar1=EPS, scalar2=None, op0=ALU.add), reads=[dmv], writes=[drs])
    P.op("act", lambda e: e.activation(out=rs[:], in_=rs[:], func=AF.Sqrt), reads=[drs], writes=[drs])
    P.op("dve", lambda e: e.reciprocal(out=rs[:], in_=rs[:]), reads=[drs], writes=[drs])
    P.op("dve", lambda e: e.scalar_tensor_tensor(out=nm[:], in0=mv[:, 0:1], scalar=-1.0, in1=rs[:], op0=ALU.mult, op1=ALU.mult),
         reads=[dmv, drs], writes=[dnm])
    P.op("act", lambda e: e.activation(out=y[:], in_=r[:], func=AF.Identity, bias=nm[:], scale=rs[:]),
         reads=[dr, drs, dnm], writes=[dy])
    P.op("pool", lambda e: e.tensor_tensor(out=y[:], in0=y[:], in1=gb[:], op=ALU.mult), reads=[dy, dgb], writes=[dy])
    P.op("pool", lambda e: e.tensor_tensor(out=y[:], in0=y[:], in1=bb[:], op=ALU.add), reads=[dy, dgb], writes=[dy])


def stage_proj_ln(C, w_name, x_name, lnname, li, out_name):
    P = C.P
    P.begin_stage()
    mixT = C.D("mixT", [1024, 2048], BF16)
    w = C.D(w_name, [1024, 1024], F32)
    x = C.D(x_name, [2048, 1024], F32)
    g_d = C.D(f"{lnname}_g{li}", [128, 1024], F32)
    b_d = C.D(f"{lnname}_b{li}", [128, 1024], F32)
    out = C.D(out_name, [2048, 1024], F32)
    ms = P.sb("ms", [128, 8, 2048], BF16)
    dms = [Dep() for _ in range(8)]
    for kc in range(8):
        P.dma("sp", ms[:, kc, :], mixT[kc * 128:(kc + 1) * 128, :], reads=[C.dd("mixT")], writes=[dms[kc]])
    ws, dws = load_fm_bf16(C, "wout", w, 8, 1024)
    gb = P.sb("gb", [128, 1024], F32)
    bb = P.sb("bb", [128, 1024], F32)
    dgb = Dep()
    P.dma("sp", gb[:], g_d, writes=[dgb])
    P.dma("sp", bb[:], b_d, writes=[dgb], merge=True)
    lb = LNBufs(P, "ln")
    xr = Ring(P, "xr", 3, [128, 1024], F32)
    rr = Ring(P, "rr", 2, [128, 1024], F32)
    yr = Ring(P, "yr", 2, [128, 1024], F32)
    for tt in range(NT):
        xt, dxt = xr.next()
        P.dma("sp", xt[:], x[tt * 128:(tt + 1) * 128, :], reads=[C.dd(x_name)], writes=[dxt])
        r, dr = rr.next()
        for half in range(2):
            bk, dbk = C.bank()
            hs = slice(half * 512, (half + 1) * 512)
            mm_acc(P, bk[:], [(ms[:, kc, tt * 128:(tt + 1) * 128], ws[:, kc, hs]) for kc in range(8)], reads=dms + dws, dwrite=dbk)
            P.op("dve", lambda e, r=r, xt=xt, bk=bk, hs=hs: e.scalar_tensor_tensor(
                out=r[:, hs], in0=xt[:, hs], scalar=ALPHA, in1=bk[:], op0=ALU.mult, op1=ALU.add),
                reads=[dxt, dbk], writes=[dr], merge=(half > 0))
        y, dy = yr.next()
        layer_norm_tile(P, lb, r, dr, gb, bb, dgb, y, dy)
        P.dma("pool", out[tt * 128:(tt + 1) * 128, :], y[:], reads=[dy], writes=[C.dd(out_name)], merge=True)


def stage_moe1(C, li, x_name):
    P = C.P
    P.begin_stage()
    x1 = C.D(x_name, [2048, 1024], F32)
    wr_d = C.D(f"moe_router{li}", [1024, 16], F32)
    ident_d = C.D("ident", [128, 128], F32)
    esel_d = C.D("esel", [16, 16, 128], F32)
    iotac_d = C.D("iota_col", [128, 16], F32)
    xe_all = C.D("xeT_all", [16, 128, 8, 256], BF16)
    idxc_d = C.D("moe_idxc", [128, 2, 16], F32)
    gc_d = C.D("moe_gc", [128, 2, 16], F32)
    wr = P.sb("wr", [128, 8, 16], F32)
    dcst = Dep()
    P.dma("sp", wr[:], wr_d.rearrange("(kc p) e -> p kc e", p=128), writes=[dcst])
    ident = P.sb("ident", [128, 128], F32)
    P.dma("sp", ident[:], ident_d, writes=[dcst], merge=True)
    esel = P.sb("esel", [16, 16, 128], F32)
    P.dma("sp", esel[:], esel_d, writes=[dcst], merge=True)
    iotac = P.sb("iotac", [128, 16], F32)
    P.dma("sp", iotac[:], iotac_d, writes=[dcst], merge=True)
    x1b = P.sb("x1b", [128, NT, 1024], BF16)
    dx1b = [Dep() for _ in range(NT)]
    affT = P.sb("affT", [16, 2048], F32)
    daffT = Dep()
    xr = Ring(P, "m1x", 2, [128, 1024], F32)
    xTr = Ring(P, "m1xT", 2, [128, 8, 128], F32)
    sm_r = Ring(P, "m1sm", 2, [128, 4], F32)
    ex_r = Ring(P, "m1ex", 2, [128, 16], F32)
    af_r = Ring(P, "m1af", 2, [128, 16], F32)
    for tt in range(NT):
        xt, dxt = xr.next()
        P.dma("sp", xt[:], x1[tt * 128:(tt + 1) * 128, :], reads=[C.dd(x_name)], writes=[dxt])
        P.op("act", lambda e, xt=xt, tt=tt: e.copy(out=x1b[:, tt, :], in_=xt[:]), reads=[dxt], writes=[dx1b[tt]])
        xT, dxT = xTr.next()
        for hb in range(2):
            bk, dbk = C.bank()
            for k4 in range(4):
                kc = hb * 4 + k4
                P.op("pe", lambda e, bk=bk, k4=k4, kc=kc, xt=xt: e.transpose(bk[:, k4 * 128:(k4 + 1) * 128], xt[:, kc * 128:(kc + 1) * 128], ident[:]),
                     reads=[dxt, dcst], writes=[dbk], merge=(k4 > 0))
            P.op("dve", lambda e, xT=xT, hb=hb, bk=bk: e.tensor_copy(out=xT[:, hb * 4:(hb + 1) * 4, :], in_=bk[:].rearrange("p (k t) -> p k t", k=4)),
                 reads=[dbk], writes=[dxT], merge=(hb > 0))
        bk, dbk = C.bank()
        mm_acc(P, bk[:, 0:16], [(xT[:, kc, :], wr[:, kc, :]) for kc in range(8)], reads=[dxT, dcst], dwrite=dbk)
        sm, dsm = sm_r.next()
        ex, dex = ex_r.next()
        af, daf = af_r.next()
        P.op("dve", lambda e, sm=sm, bk=bk: e.reduce_max(out=sm[:, 0:1], in_=bk[:, 0:16], axis=AX.X), reads=[dbk], writes=[dsm])
        P.op("dve", lambda e, sm=sm: e.tensor_scalar(out=sm[:, 1:2], in0=sm[:, 0:1], scalar1=-1.0, scalar2=None, op0=ALU.mult),
             reads=[dsm], writes=[dsm])
        P.op("act", lambda e, ex=ex, bk=bk, sm=sm: e.activation(out=ex[:], in_=bk[:, 0:16], func=AF.Exp, bias=sm[:, 1:2], accum_out=sm[:, 2:3]),
             reads=[dbk, dsm], writes=[dex, dsm])
        P.op("dve", lambda e, sm=sm: e.reciprocal(out=sm[:, 3:4], in_=sm[:, 2:3]), reads=[dsm], writes=[dsm])
        P.op("dve", lambda e, af=af, ex=ex, sm=sm: e.tensor_scalar(out=af[:], in0=ex[:], scalar1=sm[:, 3:4], scalar2=None, op0=ALU.mult),
             reads=[dex, dsm], writes=[daf])
        bk2, dbk2 = C.bank()
        P.op("pe", lambda e, bk2=bk2, af=af: e.transpose(bk2[0:16, 0:128], af[:], ident[:]), reads=[daf, dcst], writes=[dbk2])
        P.op("act", lambda e, bk2=bk2, tt=tt: e.copy(out=affT[:, tt * 128:(tt + 1) * 128], in_=bk2[0:16, 0:128]),
             reads=[dbk2], writes=[daffT], merge=(tt > 0))
    work = P.sb("work", [16, 2048], F32)
    dwork = Dep()
    g_all = P.sb("g_all", [16, 256], F32)
    idx_all = P.sb("idx_all", [16, 256], U32)
    dg = Dep()
    di = Dep()
    for r in range(32):
        src, dsrc = (affT, daffT) if r == 0 else (work, dwork)
        sl = slice(r * 8, (r + 1) * 8)
        P.op("dve", lambda e, src=src, sl=sl: e.max(out=g_all[:, sl], in_=src[:]), reads=[dsrc], writes=[dg], merge=(r > 0))
        P.op("dve", lambda e, src=src, sl=sl: e.max_index(out=idx_all[:, sl], in_max=g_all[:, sl], in_values=src[:]),
             reads=[dsrc, dg], writes=[di], merge=(r > 0))
        if r < 31:
            P.op("dve", lambda e, src=src, sl=sl: e.match_replace(out=work[:], in_to_replace=g_all[:, sl], in_values=src[:], imm_value=-1.0),
                 reads=[dsrc, dg], writes=[dwork])
    idxf = P.sb("idxf", [16, 256], F32)
    didxf = Dep()
    P.op("dve", lambda e: e.tensor_copy(out=idxf[:], in_=idx_all[:]), reads=[di], writes=[didxf])
    colt = P.sb("colt", [128, 2, 2, 16], F32)
    dcol = Dep()
    for which, (src, dsrc) in enumerate(((idxf, didxf), (g_all, dg))):
        for cc in range(2):
            bk, dbk = C.bank()
            P.op("pe", lambda e, bk=bk, src=src, cc=cc: e.transpose(bk[:, 0:16], src[:, cc * 128:(cc + 1) * 128], ident[0:16, 0:16]),
                 reads=[dsrc, dcst], writes=[dbk])
            P.op("act", lambda e, bk=bk, which=which, cc=cc: e.copy(out=colt[:, which, cc, :], in_=bk[:, 0:16]),
                 reads=[dbk], writes=[dcol], merge=True)
    P.dma("act", idxc_d, colt[:, 0, :, :], reads=[dcol], writes=[C.dd("moe_idxc")])
    P.dma("act", gc_d, colt[:, 1, :, :], reads=[dcol], writes=[C.dd("moe_gc")])
    sel_r = Ring(P, "sel", 2, [128, NT, 256], BF16)
    xe_r = Ring(P, "xe", 2, [128, 8, 256], BF16)
    for ex_i in range(16):
        bk, dbk = C.bank()
        P.op("pe", lambda e, bk=bk, ex_i=ex_i: e.matmul(bk[:, 0:256], lhsT=esel[:, ex_i, :], rhs=idxf[:], start=True, stop=True),
             reads=[dcst, didxf], writes=[dbk])
        sel, dsel = sel_r.next()
        for tt in range(NT):
            P.op("dve", lambda e, sel=sel, bk=bk, tt=tt: e.tensor_scalar(out=sel[:, tt, :], in0=bk[:, 0:256], scalar1=iotac[:, tt:tt + 1], scalar2=None, op0=ALU.is_equal),
                 reads=[dbk, dcst], writes=[dsel], merge=(tt > 0))
        xe, dxe = xe_r.next()
        for dc in range(8):
            bk2, dbk2 = C.bank()
            mm_acc(P, bk2[:, 0:256], [(x1b[:, tt, dc * 128:(dc + 1) * 128], sel[:, tt, :]) for tt in range(NT)],
                   reads=dx1b + [dsel], dwrite=dbk2)
            if dc % 2 == 0:
                P.op("act", lambda e, xe=xe, dc=dc, bk2=bk2: e.copy(out=xe[:, dc, :], in_=bk2[:, 0:256]), reads=[dbk2], writes=[dxe], merge=(dc > 0))
            else:
                P.op("dve", lambda e, xe=xe, dc=dc, bk2=bk2: e.tensor_copy(out=xe[:, dc, :], in_=bk2[:, 0:256]), reads=[dbk2], writes=[dxe], merge=True)
        P.dma("act", xe_all[ex_i], xe[:], reads=[dxe], writes=[C.dd("xeT_all")], merge=True)


def stage_moe2(C, li, x_name, out_name):
    P = C.P
    P.begin_stage()
    x1 = C.D(x_name, [2048, 1024], F32)
    wg_d = C.D(f"moe_w_gate{li}", [16, 1024, 2048], F32)
    wu_d = C.D(f"moe_w_up{li}", [16, 1024, 2048], F32)
    wd_d = C.D(f"moe_w_down{li}", [16, 2048, 1024], F32)
    xe_all = C.D("xeT_all", [16, 128, 8, 256], BF16)
    idxc_d = C.D("moe_idxc", [128, 2, 16], F32)
    gc_d = C.D("moe_gc", [128, 2, 16], F32)
    iotar_d = C.D("iota_row", [128, 2048], F32)
    ident_d = C.D("ident", [128, 128], F32)
    g_d = C.D(f"ln2_g{li}", [128, 1024], F32)
    b_d = C.D(f"ln2_b{li}", [128, 1024], F32)
    out = C.D(out_name, [2048, 1024], F32)
    outT = C.D(out_name + "T", [1024, 2048], BF16)
    dcst = Dep()
    idxc = P.sb("idxc", [128, 2, 16], F32)
    gc = P.sb("gc", [128, 2, 16], F32)
    iotar = P.sb("iotar", [128, 2048], F32)
    P.dma("sp", idxc[:], idxc_d, reads=[C.dd("moe_idxc")], writes=[dcst])
    P.dma("sp", gc[:], gc_d, reads=[C.dd("moe_gc")], writes=[dcst], merge=True)
    P.dma("sp", iotar[:], iotar_d, writes=[dcst], merge=True)
    f_acc = P.sb("f_acc", [128, NT, 1024], F32)
    dfa = [Dep() for _ in range(NT)]
    xe_r = Ring(P, "m2xe", 2, [128, 8, 256], BF16)
    selT_r = Ring(P, "selT", 2, [128, 2, 2048], BF16)
    wg_r = Ring(P, "wg", 2, [128, 8, 512], BF16)
    wu_r = Ring(P, "wu", 2, [128, 8, 512], BF16)
    wd_r = Ring(P, "wd", 2, [128, 4, 1024], BF16)
    sg_r = Ring(P, "sg", 2, [128, 256], F32)
    hT_r = Ring(P, "hT", 2, [128, 16, 256], BF16)
    ye_r = Ring(P, "ye", 2, [128, 2, 1024], BF16)
    ev = 0
    for e_i in range(16):
        xe, dxe = xe_r.next()
        P.dma("sp", xe[:], xe_all[e_i], reads=[C.dd("xeT_all")], writes=[dxe])
        selT, dselT = selT_r.next()
        for cc in range(2):
            P.op("pool", lambda e, selT=selT, cc=cc, e_i=e_i: e.tensor_scalar(
                out=selT[:, cc, :], in0=iotar[:], scalar1=idxc[:, cc, e_i:e_i + 1], scalar2=gc[:, cc, e_i:e_i + 1],
                op0=ALU.is_equal, op1=ALU.mult), reads=[dcst], writes=[dselT], merge=(cc > 0))
        hT, dhT = hT_r.next()
        wgv = wg_d[e_i].rearrange("(kc p) f -> p kc f", p=128)
        wuv = wu_d[e_i].rearrange("(kc p) f -> p kc f", p=128)
        wdv = wd_d[e_i].rearrange("(fc p) d -> p fc d", p=128)
        for q in range(4):
            wg, dwg = wg_r.next()
            wu, dwu = wu_r.next()
            P.dma("pool", wg[:], wgv[:, :, q * 512:(q + 1) * 512], writes=[dwg])
            P.dma("pool", wu[:], wuv[:, :, q * 512:(q + 1) * 512], writes=[dwu])
            for fcl in range(4):
                fc = q * 4 + fcl
                fs = slice(fcl * 128, (fcl + 1) * 128)
                bg, dbg = C.bank()
                bu, dbu = C.bank()
                mm_acc(P, bg[:, 0:256], [(wg[:, kc, fs], xe[:, kc, :]) for kc in range(8)], reads=[dwg, dxe], dwrite=dbg)
                mm_acc(P, bu[:, 0:256], [(wu[:, kc, fs], xe[:, kc, :]) for kc in range(8)], reads=[dwu, dxe], dwrite=dbu)
                sg, dsg = sg_r.next()
                P.op("act", lambda e, sg=sg, bg=bg: e.activation(out=sg[:], in_=bg[:, 0:256], func=AF.Silu), reads=[dbg], writes=[dsg])
                P.op("dve", lambda e, hT=hT, fc=fc, sg=sg, bu=bu: e.tensor_tensor(out=hT[:, fc, :], in0=sg[:], in1=bu[:, 0:256], op=ALU.mult),
                     reads=[dsg, dbu], writes=[dhT], merge=(fc > 0))
        ye, dye = ye_r.next()
        dbanks = [C.bank() for _ in range(4)]
        for r in range(4):
            wd, dwd = wd_r.next()
            P.dma("pool", wd[:], wdv[:, r * 4:(r + 1) * 4, :], writes=[dwd])
            for ct in range(2):
                for dh in range(2):
                    bk, dbk = dbanks[ct * 2 + dh]
                    for f4 in range(4):
                        fc = r * 4 + f4
                        first = (fc == 0)
                        last = (fc == 15)
                        P.op("pe", lambda e, bk=bk, hT=hT, fc=fc, ct=ct, wd=wd, f4=f4, dh=dh, first=first, last=last: e.matmul(
                            bk[:], lhsT=hT[:, fc, ct * 128:(ct + 1) * 128], rhs=wd[:, f4, dh * 512:(dh + 1) * 512], start=first, stop=last),
                            reads=[dhT, dwd], writes=[dbk], merge=(not first))
        for ct in range(2):
            for dh in range(2):
                bk, dbk = dbanks[ct * 2 + dh]
                P.op("act", lambda e, ye=ye, ct=ct, dh=dh, bk=bk: e.copy(out=ye[:, ct, dh * 512:(dh + 1) * 512], in_=bk[:]),
                     reads=[dbk], writes=[dye], merge=(ct + dh > 0))
        for tt in range(NT):
            for dh in range(2):
                bk, dbk = C.bank()
                ds = slice(dh * 512, (dh + 1) * 512)
                mm_acc(P, bk[:], [(selT[:, ct, tt * 128:(tt + 1) * 128], ye[:, ct, ds]) for ct in range(2)], reads=[dselT, dye], dwrite=dbk)
                if e_i == 0:
                    P.op("dve", lambda e, tt=tt, ds=ds, bk=bk: e.tensor_copy(out=f_acc[:, tt, ds], in_=bk[:]), reads=[dbk], writes=[dfa[tt]], merge=(dh > 0))
                else:
                    P.op("dve", lambda e, tt=tt, ds=ds, bk=bk: e.tensor_tensor(out=f_acc[:, tt, ds], in0=f_acc[:, tt, ds], in1=bk[:], op=ALU.add),
                         reads=[dbk, dfa[tt]], writes=[dfa[tt]])
    gb = P.sb("gb2", [128, 1024], F32)
    bb = P.sb("bb2", [128, 1024], F32)
    ident = P.sb("ident2", [128, 128], F32)
    dgb = Dep()
    P.dma("sp", gb[:], g_d, writes=[dgb])
    P.dma("sp", bb[:], b_d, writes=[dgb], merge=True)
    P.dma("sp", ident[:], ident_d, writes=[dgb], merge=True)
    lb = LNBufs(P, "ln2")
    xr = Ring(P, "m2x", 2, [128, 1024], F32)
    yr = Ring(P, "m2y", 2, [128, 1024], F32)
    yT_r = Ring(P, "m2yT", 2, [128, 8, 128], BF16)
    for tt in range(NT):
        xt, dxt = xr.next()
        P.dma("sp", xt[:], x1[tt * 128:(tt + 1) * 128, :], reads=[C.dd(x_name)], writes=[dxt])
        P.op("dve", lambda e, xt=xt, tt=tt: e.scalar_tensor_tensor(out=xt[:], in0=xt[:], scalar=ALPHA, in1=f_acc[:, tt, :], op0=ALU.mult, op1=ALU.add),
             reads=[dxt, dfa[tt]], writes=[dxt])
        y, dy = yr.next()
        layer_norm_tile(P, lb, xt, dxt, gb, bb, dgb, y, dy)
        P.dma("pool", out[tt * 128:(tt + 1) * 128, :], y[:], reads=[dy], writes=[C.dd(out_name)], merge=True)
        transpose_tile_to_dram(C, y, dy, ident, dgb, yT_r, outT, out_name + "T", tt)


def transpose_tile_to_dram(C, y, dy, ident, dident, yT_r, outT, outT_name, tt):
    P = C.P
    yT, dyT = yT_r.next()
    for hb in range(2):
        bk, dbk = C.bank()
        for k4 in range(4):
            kc = hb * 4 + k4
            P.op("pe", lambda e, bk=bk, k4=k4, kc=kc: e.transpose(bk[:, k4 * 128:(k4 + 1) * 128], y[:, kc * 128:(kc + 1) * 128], ident[:]),
                 reads=[dy, dident], writes=[dbk], merge=(k4 > 0))
        P.op("act", lambda e, hb=hb, bk=bk: e.copy(out=yT[:, hb * 4:(hb + 1) * 4, :], in_=bk[:].rearrange("p (k t) -> p k t", k=4)),
             reads=[dbk], writes=[dyT], merge=(hb > 0))
    P.dma("act", outT.rearrange("(kc p) t -> p kc t", p=128)[:, :, tt * 128:(tt + 1) * 128], yT[:], reads=[dyT], writes=[C.dd(outT_name)], merge=True)


def stage_ple(C, li, x_name, out_name, want_T):
    P = C.P
    P.begin_stage()
    x2 = C.D(x_name, [2048, 1024], F32)
    x2T = C.D(x_name + "T", [1024, 2048], BF16)
    pT_d = C.D(f"pT{li}", [256, 2048], F32)
    wg_d = C.D(f"ple_gate{li}", [1024, 1024], F32)
    wp_d = C.D(f"ple_proj{li}", [256, 1024], F32)
    ident_d = C.D("ident", [128, 128], F32)
    out = C.D(out_name, [2048, 1024], F32)
    xs = P.sb("plx", [128, 8, 2048], BF16)
    dxs = [Dep() for _ in range(8)]
    for kc in range(8):
        P.dma("sp", xs[:, kc, :], x2T[kc * 128:(kc + 1) * 128, :], reads=[C.dd(x_name + "T")], writes=[dxs[kc]])
    ps_, dps_ = load_fm_bf16(C, "plp", pT_d, 2, 2048)
    wg, dwg = load_fm_bf16(C, "plwg", wg_d, 8, 1024)
    wp, dwp = load_fm_bf16(C, "plwp", wp_d, 2, 1024)
    ident = P.sb("ident3", [128, 128], F32)
    dident = Dep()
    P.dma("sp", ident[:], ident_d, writes=[dident])
    if want_T:
        outT = C.D(out_name + "T", [1024, 2048], BF16)
        yT_r = Ring(P, "plyT", 2, [128, 8, 128], BF16)
    xr = Ring(P, "plxr", 2, [128, 1024], F32)
    gr = Ring(P, "plg", 2, [128, 1024], F32)
    yr = Ring(P, "ply", 2, [128, 1024], F32)
    for tt in range(NT):
        ts_ = slice(tt * 128, (tt + 1) * 128)
        xt, dxt = xr.next()
        P.dma("sp", xt[:], x2[ts_, :], reads=[C.dd(x_name)], writes=[dxt])
        gt, dgt = gr.next()
        y, dy = yr.next()
        for half in range(2):
            hs = slice(half * 512, (half + 1) * 512)
            bk, dbk = C.bank()
            mm_acc(P, bk[:], [(xs[:, kc, ts_], wg[:, kc, hs]) for kc in range(8)], reads=dxs + dwg, dwrite=dbk)
            P.op("act", lambda e, gt=gt, hs=hs, bk=bk: e.activation(out=gt[:, hs], in_=bk[:], func=AF.Sigmoid), reads=[dbk], writes=[dgt], merge=(half > 0))
            bk2, dbk2 = C.bank()
            mm_acc(P, bk2[:], [(ps_[:, kc, ts_], wp[:, kc, hs]) for kc in range(2)], reads=dps_ + dwp, dwrite=dbk2)
            P.op("dve", lambda e, gt=gt, hs=hs, bk2=bk2: e.tensor_tensor(out=gt[:, hs], in0=gt[:, hs], in1=bk2[:], op=ALU.mult),
                 reads=[dgt, dbk2], writes=[dgt])
        P.op("pool", lambda e, y=y, xt=xt, gt=gt: e.tensor_tensor(out=y[:], in0=xt[:], in1=gt[:], op=ALU.add), reads=[dxt, dgt], writes=[dy])
        P.dma("pool", out[ts_, :], y[:], reads=[dy], writes=[C.dd(out_name)], merge=True)
        if want_T:
            transpose_tile_to_dram(C, y, dy, ident, dident, yT_r, outT, out_name + "T", tt)


def hyena_consts():
    L, N = 2048, 4096
    R = np.arange(N)
    f = np.where(R <= 2048, R, R - 2048).astype(np.int64)
    is_im = R > 2048
    t = np.arange(L, dtype=np.int64)
    k = (t[:, None] * f[None, :]) % N
    ang = 2.0 * np.pi * k.astype(np.float64) / N
    Wf = np.where(is_im[None, :], -np.sin(ang), np.cos(ang))
    cR = np.full(N, 2.0 / N)
    cR[0] = 1.0 / N
    cR[2048] = 1.0 / N
    WfT = np.ascontiguousarray(Wf.T)
    Wf_d = Wf.reshape(16, 128, 32, 128).transpose(2, 1, 0, 3)
    WA_d = WfT.reshape(32, 128, 16, 128).transpose(2, 1, 0, 3)
    WB_d = WfT.reshape(2, 16, 128, 4, 512).transpose(3, 0, 2, 1, 4)
    tl = np.linspace(0.0, 1.0, L, dtype=np.float32)[:, None]
    w = (2.0 * np.float32(math.pi) * np.arange(L, dtype=np.float32)[:, None] / np.float32(L)).astype(np.float32)
    fb = np.linspace(1e-4, 15, 16, dtype=np.float32)[None, :]
    z = np.concatenate([tl, np.cos(fb * w), -np.sin(fb * w)], axis=-1).astype(np.float32)
    min_decay = math.log(1e-2) / 1.5
    max_decay = math.log(1e-2) / 0.3
    deltas = np.abs(np.linspace(min_decay, max_decay, 512, dtype=np.float32))
    decay = np.exp(-tl * deltas[None, :]).astype(np.float32)
    return {
        "hy_Wf": np.ascontiguousarray(Wf_d).astype(NPBF),
        "hy_WA": np.ascontiguousarray(WA_d).astype(NPBF), "hy_WB": np.ascontiguousarray(WB_d).astype(NPBF),
        "hy_cR": np.ascontiguousarray(cR.reshape(32, 128).T).astype(np.float32),
        "hy_zT": np.ascontiguousarray(z.T), "hy_decay": np.ascontiguousarray(decay.reshape(16, 128, 512)),
    }


TWO_PI = 2.0 * math.pi


def stage_hy_filter(C):
    P = C.P
    P.begin_stage()
    zT_d = C.D("hy_zT", [33, 2048], F32)
    w1_d = C.D("hy_f_w1", [33, 64], F32)
    w2_d = C.D("hy_f_w2", [64, 64], F32)
    w3_d = C.D("hy_f_w3", [64, 2048], F32)
    cols_d = C.D("hy_cols", [64, 3], F32)
    dec_d = C.D("hy_decay", [16, 128, 512], F32)
    Wf_d = C.D("hy_Wf", [32, 128, 16, 128], BF16)
    cR_d = C.D("hy_cR", [128, 32], F32)
    skip_d = C.D("hy_skipb", [2, 128, 512], F32)
    Kf_d = C.D("hy_Kf", [32, 2, 128, 512], F32)
    dc = Dep()
    zT = P.sb("zT", [33, 2048], F32)
    w1 = P.sb("w1", [33, 64], F32)
    w2 = P.sb("w2", [64, 64], F32)
    w3 = P.sb("w3", [64, 2048], BF16)
    cols = P.sb("cols", [64, 8], F32)
    cR = P.sb("cR", [128, 32], F32)
    skipb = P.sb("skipb", [128, 2, 512], F32)
    P.dma("sp", zT[:], zT_d, writes=[dc])
    P.dma("sp", w1[:], w1_d, writes=[dc], merge=True)
    P.dma("sp", w2[:], w2_d, writes=[dc], merge=True)
    P.dma("pool", w3[:], w3_d, writes=[dc], merge=True)
    P.dma("sp", cols[:, 0:3], cols_d, writes=[dc], merge=True)
    P.dma("sp", cR[:], cR_d, writes=[dc], merge=True)
    for o in range(2):
        P.dma("sp", skipb[:, o, :], skip_d[o], writes=[dc], merge=True)
    dcol = Dep()
    P.op("dve", lambda e: e.tensor_tensor(out=cols[:, 3:4], in0=cols[:, 0:1], in1=cols[:, 1:2], op=ALU.mult), reads=[dc], writes=[dcol])
    P.op("dve", lambda e: e.tensor_tensor(out=cols[:, 4:5], in0=cols[:, 2:3], in1=cols[:, 1:2], op=ALU.mult), reads=[dc], writes=[dcol], merge=True)
    P.op("pool", lambda e: e.memset(cols[:, 5:6], -math.pi), reads=[dc], writes=[dcol], merge=True)
    h1T = P.sb("h1T", [64, 2048], F32)
    h2T = P.sb("h2T", [64, 2048], BF16)
    dh1 = Dep()
    dh2 = Dep()
    u_r = Ring(P, "hyu", 2, [64, 512], F32)
    s_r = Ring(P, "hys", 4, [64, 512], F32)
    for layer in range(2):
        for nt in range(4):
            ns = slice(nt * 512, (nt + 1) * 512)
            bk, dbk = C.bank()
            if layer == 0:
                P.op("pe", lambda e, bk=bk, ns=ns: e.matmul(bk[0:64, :], lhsT=w1[:], rhs=zT[:, ns], start=True, stop=True), reads=[dc], writes=[dbk])
            else:
                P.op("pe", lambda e, bk=bk, ns=ns: e.matmul(bk[0:64, :], lhsT=w2[:], rhs=h1T[:, ns], start=True, stop=True), reads=[dc, dh1], writes=[dbk])
            u, du = u_r.next()
            fbc = 3 + layer
            P.op("dve", lambda e, u=u, bk=bk, fbc=fbc: e.tensor_scalar(out=u[:], in0=bk[0:64, :], scalar1=cols[:, 1:2], scalar2=cols[:, fbc:fbc + 1], op0=ALU.mult, op1=ALU.add),
                 reads=[dbk, dcol, dc], writes=[du])
            s2, ds2 = s_r.next()
            s4, ds4 = s_r.next()
            P.op("act", lambda e, u=u, s2=s2: e.activation(out=s2[:], in_=u[:], func=AF.Sin, scale=0.5), reads=[du], writes=[ds2])
            P.op("act", lambda e, u=u, s4=s4: e.activation(out=s4[:], in_=u[:], func=AF.Sin, scale=0.25), reads=[du], writes=[ds4])
            P.op("dve", lambda e, s4=s4: e.tensor_tensor(out=s4[:], in0=s4[:], in1=s4[:], op=ALU.mult), reads=[ds4], writes=[ds4])
            P.op("dve", lambda e, s4=s4: e.tensor_scalar(out=s4[:], in0=s4[:], scalar1=-2.0, scalar2=1.0, op0=ALU.mult, op1=ALU.add), reads=[ds4], writes=[ds4])
            if layer == 0:
                P.op("dve", lambda e, s2=s2, s4=s4, ns=ns: e.scalar_tensor_tensor(out=h1T[:, ns], in0=s2[:], scalar=2.0, in1=s4[:], op0=ALU.mult, op1=ALU.mult),
                     reads=[ds2, ds4], writes=[dh1], merge=(nt > 0))
            else:
                P.op("dve", lambda e, s2=s2, s4=s4, ns=ns: e.scalar_tensor_tensor(out=h2T[:, ns], in0=s2[:], scalar=2.0, in1=s4[:], op0=ALU.mult, op1=ALU.mult),
                     reads=[ds2, ds4], writes=[dh2], merge=(nt > 0))
    Kt = P.sb("Ksd", [128, 16, 4, 512], BF16)
    dKt = [Dep() for _ in range(16)]
    ones = P.sb("onesb", [128, 128], BF16)
    dones = Dep()
    P.op("pool", lambda e: e.memset(ones[:], 1.0), writes=[dones])
    dec_r = Ring(P, "dec", 2, [128, 512], F32)
    sq_r = Ring(P, "sq", 3, [128, 512], BF16)
    kf32_r = Ring(P, "kf32", 2, [128, 512], F32)
    kb32_r = Ring(P, "kb32", 2, [128, 512], F32)
    ssq = [C.bank(hold=True), C.bank(hold=True)]
    for tt in range(16):
        dec, ddec = dec_r.next()
        P.dma("sp", dec[:], dec_d[tt], writes=[ddec])
        for o in range(2):
            bf_, dbf_ = C.bank()
            bb_, dbb_ = C.bank()
            for (bk, dbk, q) in ((bf_, dbf_, 2 * o), (bb_, dbb_, 2 * o + 1)):
                P.op("pe", lambda e, bk=bk, tt=tt, q=q: e.matmul(bk[:], lhsT=h2T[:, tt * 128:(tt + 1) * 128], rhs=w3[:, q * 512:(q + 1) * 512], start=True, stop=True),
                     reads=[dh2, dc], writes=[dbk])
            kf32, dkf32 = kf32_r.next()
            kb32, dkb32 = kb32_r.next()
            P.op("dve", lambda e, kf32=kf32, bf_=bf_, dec=dec: e.tensor_tensor(out=kf32[:], in0=bf_[:], in1=dec[:], op=ALU.mult), reads=[dbf_, ddec], writes=[dkf32])
            P.op("dve", lambda e, kb32=kb32, bb_=bb_, dec=dec: e.tensor_tensor(out=kb32[:], in0=bb_[:], in1=dec[:], op=ALU.mult), reads=[dbb_, ddec], writes=[dkb32])
            if tt == 0:
                P.op("pool", lambda e, kb32=kb32: e.memset(kb32[0:1, :], 0.0), reads=[dkb32], writes=[dkb32])
            P.op("pool", lambda e, tt=tt, o=o, kf32=kf32, kb32=kb32: e.tensor_tensor(out=Kt[:, tt, 2 * o, :], in0=kf32[:], in1=kb32[:], op=ALU.add),
                 reads=[dkf32, dkb32], writes=[dKt[tt]], merge=True)
            P.op("dve", lambda e, tt=tt, o=o, kf32=kf32, kb32=kb32: e.tensor_tensor(out=Kt[:, tt, 2 * o + 1, :], in0=kf32[:], in1=kb32[:], op=ALU.subtract),
                 reads=[dkf32, dkb32], writes=[dKt[tt]], merge=True)
        for q in range(4):
            sq, dsq = sq_r.next()
            P.op("act", lambda e, sq=sq, tt=tt, q=q: e.activation(out=sq[:], in_=Kt[:, tt, q, :], func=AF.Square), reads=[dKt[tt]], writes=[dsq])
            sb_, dsb_ = ssq[q // 2]
            first = (tt == 0 and q % 2 == 0)
            last = (tt == 15 and q % 2 == 1)
            P.op("pe", lambda e, sb_=sb_, sq=sq, first=first, last=last: e.matmul(sb_[:], lhsT=ones[:], rhs=sq[:], start=first, stop=last),
                 reads=[dsq, dones], writes=[dsb_], merge=(not first))
    rs = P.sb("hyrs", [128, 2, 512], F32)
    drs = Dep()
    for o in range(2):
        sb_, dsb_ = ssq[o]
        P.op("dve", lambda e, o=o, sb_=sb_: e.tensor_scalar(out=rs[:, o, :], in0=sb_[:], scalar1=0.5, scalar2=1e-12, op0=ALU.mult, op1=ALU.add), reads=[dsb_], writes=[drs], merge=(o > 0))
    C.release_all()
    P.op("act", lambda e: e.activation(out=rs[:], in_=rs[:], func=AF.Sqrt), reads=[drs], writes=[drs])
    P.op("dve", lambda e: e.reciprocal(out=rs[:], in_=rs[:]), reads=[drs], writes=[drs])
    wf_r = Ring(P, "wf", 2, [128, 16, 128], BF16)
    kf_r = Ring(P, "kf", 3, [128, 512], F32)
    for ft in range(32):
        wf, dwf = wf_r.next()
        P.dma("sp", wf[:], Wf_d[ft], writes=[dwf])
        for o in range(2):
            bk, dbk = C.bank()
            sel = 2 * o if ft < 16 else 2 * o + 1
            mm_acc(P, bk[:], [(wf[:, tt, :], Kt[:, tt, sel, :]) for tt in range(16)], reads=[dwf] + dKt, dwrite=dbk)
            kf, dkf = kf_r.next()
            P.op("dve", lambda e, kf=kf, bk=bk, o=o: e.tensor_tensor(out=kf[:], in0=bk[:], in1=rs[:, o, :], op=ALU.mult), reads=[dbk, drs], writes=[dkf])
            if ft < 16:
                P.op("dve", lambda e, kf=kf, o=o: e.tensor_tensor(out=kf[:], in0=kf[:], in1=skipb[:, o, :], op=ALU.add), reads=[dkf, dc], writes=[dkf])
            elif ft == 16:
                bn, dbn = C.bank()
                mm_acc(P, bn[0:32, :], [(wf[:, tt, 0:32], Kt[:, tt, 2 * o, :]) for tt in range(16)], reads=[dwf] + dKt, dwrite=dbn)
                P.op("dve", lambda e, kf=kf, bn=bn, o=o: e.tensor_tensor(out=kf[0:1, :], in0=bn[0:1, :], in1=rs[0:1, o, :], op=ALU.mult), reads=[dbn, drs, dkf], writes=[dkf])
                P.op("pool", lambda e, kf=kf, o=o: e.tensor_tensor(out=kf[0:1, :], in0=kf[0:1, :], in1=skipb[0:1, o, :], op=ALU.add), reads=[dkf, dc], writes=[dkf])
            P.op("act", lambda e, kf=kf, ft=ft: e.activation(out=kf[:], in_=kf[:], func=AF.Copy, scale=cR[:, ft:ft + 1]), reads=[dkf, dc], writes=[dkf])
            P.dma("act", Kf_d[ft, o], kf[:], reads=[dkf], writes=[C.dd("hy_Kf")], merge=True)


def stage_hy_prep(C):
    P = C.P
    P.begin_stage()
    hbT = C.D("hbT", [1536, 2048], F32)
    cw_d = C.D("hy_cw", [128, 12, 3], F32)
    cb_d = C.D("hy_cb", [128, 12], F32)
    ident_d = C.D("ident", [128, 128], F32)
    hv = C.D("hv_tm", [2048, 512], BF16)
    hx1 = C.D("hx1_tm", [2048, 512], F32)
    hx2T = C.D("hx2T", [512, 2048], F32)
    dc = Dep()
    cw = P.sb("cw", [128, 12, 3], F32)
    cb = P.sb("cb", [128, 12], F32)
    ident = P.sb("identh", [128, 128], F32)
    P.dma("sp", cw[:], cw_d, writes=[dc])
    P.dma("sp", cb[:], cb_d, writes=[dc], merge=True)
    P.dma("sp", ident[:], ident_d, writes=[dc], merge=True)
    xin_r = Ring(P, "hxin", 2, [128, 2048], F32)
    y_r = Ring(P, "hy", 2, [128, 2048], F32)
    sv_r = Ring(P, "hsv", 2, [128, 16, 128], BF16)
    sx_r = Ring(P, "hsx", 2, [128, 16, 128], F32)
    for ch in range(12):
        xin, dxin = xin_r.next()
        P.dma("sp", xin[:], hbT[ch * 128:(ch + 1) * 128, :], reads=[C.dd("hbT")], writes=[dxin])
        y, dy = y_r.next()
        P.op("act", lambda e, y=y, xin=xin, ch=ch: e.activation(out=y[:], in_=xin[:], func=AF.Identity, bias=cb[:, ch:ch + 1], scale=cw[:, ch, 1:2]),
             reads=[dxin, dc], writes=[dy])
        P.op("dve", lambda e, y=y, xin=xin, ch=ch: e.scalar_tensor_tensor(out=y[:, 1:2048], in0=xin[:, 0:2047], scalar=cw[:, ch, 0:1], in1=y[:, 1:2048], op0=ALU.mult, op1=ALU.add),
             reads=[dxin, dc, dy], writes=[dy])
        P.op("dve", lambda e, y=y, xin=xin, ch=ch: e.scalar_tensor_tensor(out=y[:, 0:2047], in0=xin[:, 1:2048], scalar=cw[:, ch, 2:3], in1=y[:, 0:2047], op0=ALU.mult, op1=ALU.add),
             reads=[dxin, dc, dy], writes=[dy])
        if ch >= 8:
            P.dma("act", hx2T[(ch - 8) * 128:(ch - 7) * 128, :], y[:], reads=[dy], writes=[C.dd("hx2T")], merge=True)
            continue
        stg, dstg = (sv_r if ch < 4 else sx_r).next()
        for g in range(4):
            bk, dbk = C.bank()
            for k4 in range(4):
                tt = g * 4 + k4
                P.op("pe", lambda e, bk=bk, k4=k4, tt=tt, y=y: e.transpose(bk[:, k4 * 128:(k4 + 1) * 128], y[:, tt * 128:(tt + 1) * 128], ident[:]),
                     reads=[dy, dc], writes=[dbk], merge=(k4 > 0))
            P.op("act", lambda e, stg=stg, g=g, bk=bk: e.copy(out=stg[:, g * 4:(g + 1) * 4, :], in_=bk[:].rearrange("p (k t) -> p k t", k=4)),
                 reads=[dbk], writes=[dstg], merge=(g > 0))
        if ch < 4:
            P.dma("act", hv.rearrange("(tt p) c -> p tt c", p=128)[:, :, ch * 128:(ch + 1) * 128], stg[:], reads=[dstg], writes=[C.dd("hv_tm")], merge=True)
        else:
            P.dma("act", hx1.rearrange("(tt p) c -> p tt c", p=128)[:, :, (ch - 4) * 128:(ch - 3) * 128], stg[:], reads=[dstg], writes=[C.dd("hx1_tm")], merge=True)


def stage_hy_conv(C):
    P = C.P
    P.begin_stage()
    hv = C.D("hv_tm", [2048, 512], BF16)
    hx1 = C.D("hx1_tm", [2048, 512], F32)
    hx2T = C.D("hx2T", [512, 2048], F32)
    Kf_d = C.D("hy_Kf", [32, 2, 128, 512], F32)
    Wf_d = C.D("hy_Wf", [32, 128, 16, 128], BF16)
    WA_d = C.D("hy_WA", [16, 128, 32, 128], BF16)
    WB_d = C.D("hy_WB", [4, 2, 128, 16, 512], BF16)
    mixT = C.D("mixT", [1024, 2048], BF16)
    ztm = P.sb("ztm", [128, 16, 512], BF16)
    dz = [Dep() for _ in range(16)]
    hvv = hv.rearrange("(tt p) c -> p tt c", p=128)
    for tt in range(16):
        P.dma("sp", ztm[:, tt, :], hvv[:, tt, :], reads=[C.dd("hv_tm")], writes=[dz[tt]])
    Yt = P.sb("Yt", [128, 32, 512], BF16)
    dY = [Dep() for _ in range(32)]
    wf_r = Ring(P, "cwf", 3, [128, 16, 128], BF16)
    kf_r = Ring(P, "ckf", 4, [128, 512], F32)
    t_r = Ring(P, "ct", 4, [128, 512], F32)
    wa_r = Ring(P, "cwa", 2, [128, 32, 128], BF16)
    wb_r = Ring(P, "cwb", 2, [128, 16, 512], BF16)
    x_r = Ring(P, "cx", 3, [128, 512], F32)
    zo_r = Ring(P, "czo", 3, [128, 512], BF16)
    for o in range(2):
        for j in range(16):
            ub = []
            for part in range(2):
                ft = part * 16 + j
                wf, dwf = wf_r.next()
                P.dma("sp", wf[:], Wf_d[ft], writes=[dwf])
                bk, dbk = C.bank()
                mm_acc(P, bk[:], [(wf[:, tt, :], ztm[:, tt, :]) for tt in range(16)], reads=[dwf] + dz, dwrite=dbk)
                ub.append((bk, dbk))
            kre, dkre = kf_r.next()
            kim, dkim = kf_r.next()
            P.dma("sp", kre[:], Kf_d[j, o], reads=[C.dd("hy_Kf")], writes=[dkre])
            P.dma("sp", kim[:], Kf_d[16 + j, o], reads=[C.dd("hy_Kf")], writes=[dkim])
            (ure, dure), (uim, duim) = ub
            t1, dt1 = t_r.next()
            t2, dt2 = t_r.next()
            P.op("dve", lambda e, t1=t1, ure=ure, kre=kre: e.tensor_tensor(out=t1[:], in0=ure[:], in1=kre[:], op=ALU.mult), reads=[dure, dkre], writes=[dt1])
            P.op("dve", lambda e, t2=t2, uim=uim, kim=kim: e.tensor_tensor(out=t2[:], in0=uim[:], in1=kim[:], op=ALU.mult), reads=[duim, dkim], writes=[dt2])
            P.op("pool", lambda e, j=j, t1=t1, t2=t2: e.tensor_tensor(out=Yt[:, j, :], in0=t1[:], in1=t2[:], op=ALU.subtract), reads=[dt1, dt2], writes=[dY[j]])
            if j == 0:
                P.op("pool", lambda e, t1=t1: e.tensor_copy(out=Yt[0:1, 0, :], in_=t1[0:1, :]), reads=[dt1, dY[0]], writes=[dY[0]])
            t3, dt3 = t_r.next()
            t4, dt4 = t_r.next()
            P.op("dve", lambda e, t3=t3, ure=ure, kim=kim: e.tensor_tensor(out=t3[:], in0=ure[:], in1=kim[:], op=ALU.mult), reads=[dure, dkim], writes=[dt3])
            P.op("dve", lambda e, t4=t4, uim=uim, kre=kre: e.tensor_tensor(out=t4[:], in0=uim[:], in1=kre[:], op=ALU.mult), reads=[duim, dkre], writes=[dt4])
            P.op("pool", lambda e, j=j, t3=t3, t4=t4: e.tensor_tensor(out=Yt[:, 16 + j, :], in0=t3[:], in1=t4[:], op=ALU.add), reads=[dt3, dt4], writes=[dY[16 + j]])
            if j == 0:
                P.op("pool", lambda e, t2=t2: e.tensor_copy(out=Yt[0:1, 16, :], in_=t2[0:1, :]), reads=[dt2, dY[16]], writes=[dY[16]])
        if o == 0:
            for tt in range(16):
                wa, dwa = wa_r.next()
                P.dma("sp", wa[:], WA_d[tt], writes=[dwa])
                bk, dbk = C.bank()
                mm_acc(P, bk[:], [(wa[:, kt, :], Yt[:, kt, :]) for kt in range(32)], reads=[dwa] + dY, dwrite=dbk)
                xt, dxt = x_r.next()
                P.dma("sp", xt[:], hx1[tt * 128:(tt + 1) * 128, :], reads=[C.dd("hx1_tm")], writes=[dxt])
                P.op("dve", lambda e, tt=tt, bk=bk, xt=xt: e.tensor_tensor(out=ztm[:, tt, :], in0=bk[:], in1=xt[:], op=ALU.mult), reads=[dbk, dxt], writes=[dz[tt]])
        else:
            for nt in range(4):
                banks = [C.bank() for _ in range(4)]
                for hf in range(2):
                    wb, dwb = wb_r.next()
                    P.dma("sp", wb[:], WB_d[nt, hf], writes=[dwb])
                    for cc in range(4):
                        bk, dbk = banks[cc]
                        for k in range(16):
                            first = (hf == 0 and k == 0)
                            last = (hf == 1 and k == 15)
                            kt = hf * 16 + k
                            P.op("pe", lambda e, bk=bk, kt=kt, cc=cc, wb=wb, k=k, first=first, last=last: e.matmul(
                                bk[:], lhsT=Yt[:, kt, cc * 128:(cc + 1) * 128], rhs=wb[:, k, :], start=first, stop=last),
                                reads=[dY[kt], dwb], writes=[dbk], merge=(not first))
                for cc in range(4):
                    bk, dbk = banks[cc]
                    xt, dxt = x_r.next()
                    P.dma("sp", xt[:], hx2T[cc * 128:(cc + 1) * 128, nt * 512:(nt + 1) * 512], reads=[C.dd("hx2T")], writes=[dxt])
                    zo, dzo = zo_r.next()
                    P.op("dve", lambda e, zo=zo, bk=bk, xt=xt: e.tensor_tensor(out=zo[:], in0=bk[:], in1=xt[:], op=ALU.mult), reads=[dbk, dxt], writes=[dzo])
                    P.dma("act", mixT[512 + cc * 128:512 + (cc + 1) * 128, nt * 512:(nt + 1) * 512], zo[:], reads=[dzo], writes=[C.dd("mixT")], merge=True)


MLA_SCALE = 96.0 ** -0.5


def mla_consts():
    inv = 1.0 / (10000.0 ** (np.arange(0, 32, 2, dtype=np.float32) / 32.0))
    ang = np.arange(2048, dtype=np.float32)[:, None] * inv[None, :].astype(np.float32)
    cos = np.cos(ang).astype(np.float32).T
    sin = np.sin(ang).astype(np.float32).T
    cos2 = np.concatenate([cos, cos], axis=0)
    sin2 = np.concatenate([-sin, sin], axis=0)
    return {"mla_cs2": np.ascontiguousarray(np.stack([cos2, sin2], axis=1)).astype(np.float32)}


def stage_mla1(C, xT_name):
    P = C.P
    P.begin_stage()
    xT = C.D(xT_name, [1024, 2048], BF16)
    wi_d = C.D("mla_w_in", [1024, 672], F32)
    wsw_d = C.D("mla_w_in_sw", [1024, 96], F32)
    gc_d = C.D("mla_gcols", [128, 5], F32)
    cs_d = C.D("mla_cs2", [32, 2, 2048], F32)
    nT_d = C.D("mla_nT", [640, 2048], BF16)
    kr_d = C.D("mla_krT", [32, 2048], BF16)
    xs = P.sb("mxs", [128, 8, 2048], BF16)
    dxs = [Dep() for _ in range(8)]
    for kc in range(8):
        P.dma("sp", xs[:, kc, :], xT[kc * 128:(kc + 1) * 128, :], reads=[C.dd(xT_name)], writes=[dxs[kc]])
    wi, dwi = load_fm_bf16(C, "mwi", wi_d, 8, 672)
    wsw, dwsw = load_fm_bf16(C, "mwsw", wsw_d, 8, 96)
    dc = Dep()
    gcol = P.sb("mgc", [128, 5], F32)
    P.dma("sp", gcol[:], gc_d, writes=[dc])
    cs = P.sb("mcs", [96, 2, 2048], F32)
    P.dma("sp", cs[64:96, :, :], cs_d, writes=[dc], merge=True)
    ones = P.sb("mones", [128, 128], BF16)
    P.op("pool", lambda e: e.memset(ones[:], 1.0), writes=[dc], merge=True)
    hT = P.sb("mhT", [128, 5, 2048], F32)
    nT = P.sb("mnT", [128, 5, 2048], BF16)
    dhT = Dep()
    dnT = [Dep() for _ in range(5)]
    sq_r = Ring(P, "msq", 3, [128, 512], BF16)
    r_r = Ring(P, "mr", 2, [128, 512], F32)
    for (chunks, n) in (((0, 1, 2), 384.0), ((3, 4), 256.0)):
        for nt in range(4):
            ns = slice(nt * 512, (nt + 1) * 512)
            sbk, dsbk = C.bank(hold=True)
            for ci, c in enumerate(chunks):
                bk, dbk = C.bank()
                mm_acc(P, bk[:], [(wi[:, kc, c * 128:(c + 1) * 128], xs[:, kc, ns]) for kc in range(8)], reads=dxs + dwi, dwrite=dbk)
                P.op("act", lambda e, c=c, ns=ns, bk=bk: e.copy(out=hT[:, c, ns], in_=bk[:]), reads=[dbk], writes=[dhT], merge=True)
                sq, dsq = sq_r.next()
                P.op("act", lambda e, sq=sq, bk=bk: e.activation(out=sq[:], in_=bk[:], func=AF.Square), reads=[dbk], writes=[dsq])
                P.op("pe", lambda e, sbk=sbk, sq=sq, ci=ci, chunks=chunks: e.matmul(sbk[:], lhsT=ones[:], rhs=sq[:], start=(ci == 0), stop=(ci == len(chunks) - 1)),
                     reads=[dsq, dc], writes=[dsbk], merge=(ci > 0))
            r, dr = r_r.next()
            P.op("dve", lambda e, r=r, sbk=sbk, n=n: e.tensor_scalar(out=r[:], in0=sbk[:], scalar1=1.0 / n, scalar2=EPS, op0=ALU.mult, op1=ALU.add), reads=[dsbk], writes=[dr])
            C.release_all()
            P.op("act", lambda e, r=r: e.activation(out=r[:], in_=r[:], func=AF.Sqrt), reads=[dr], writes=[dr])
            P.op("dve", lambda e, r=r: e.reciprocal(out=r[:], in_=r[:]), reads=[dr], writes=[dr])
            for c in chunks:
                P.op("dve", lambda e, c=c, ns=ns, r=r: e.scalar_tensor_tensor(out=nT[:, c, ns], in0=hT[:, c, ns], scalar=gcol[:, c:c + 1], in1=r[:], op0=ALU.mult, op1=ALU.mult),
                     reads=[dhT, dr, dc], writes=[dnT[c]], merge=True)
    for c in range(5):
        P.dma("act", nT_d[c * 128:(c + 1) * 128, :], nT[:, c, :], reads=[dnT[c]], writes=[C.dd("mla_nT")], merge=True)
    krT = P.sb("mkr", [96, 2048], BF16)
    dkr = Dep()
    ta_r = Ring(P, "mta", 2, [96, 512], F32)
    tb_r = Ring(P, "mtb", 2, [96, 512], F32)
    for nt in range(4):
        ns = slice(nt * 512, (nt + 1) * 512)
        bk, dbk = C.bank()
        bs, dbs = C.bank()
        mm_acc(P, bk[0:96, :], [(wi[:, kc, 576:672], xs[:, kc, ns]) for kc in range(8)], reads=dxs + dwi, dwrite=dbk)
        mm_acc(P, bs[0:96, :], [(wsw[:, kc, :], xs[:, kc, ns]) for kc in range(8)], reads=dxs + dwsw, dwrite=dbs)
        ta, dta = ta_r.next()
        tb, dtb = tb_r.next()
        P.op("dve", lambda e, ta=ta, bk=bk, ns=ns: e.tensor_tensor(out=ta[64:96, :], in0=bk[64:96, :], in1=cs[64:96, 0, ns], op=ALU.mult), reads=[dbk, dc], writes=[dta])
        P.op("dve", lambda e, tb=tb, bs=bs, ns=ns: e.tensor_tensor(out=tb[64:96, :], in0=bs[64:96, :], in1=cs[64:96, 1, ns], op=ALU.mult), reads=[dbs, dc], writes=[dtb])
        P.op("pool", lambda e, ta=ta, tb=tb, ns=ns: e.tensor_tensor(out=krT[64:96, ns], in0=ta[64:96, :], in1=tb[64:96, :], op=ALU.add), reads=[dta, dtb], writes=[dkr], merge=(nt > 0))
    P.dma("pool", kr_d, krT[64:96, :], reads=[dkr], writes=[C.dd("mla_krT")])


def stage_mla2(C):
    P = C.P
    P.begin_stage()
    nT_d = C.D("mla_nT", [640, 2048], BF16)
    kr_d = C.D("mla_krT", [32, 2048], BF16)
    cs_d = C.D("mla_cs2", [32, 2, 2048], F32)
    wq_d = C.D("mla_w_q_up", [384, 1536], F32)
    wqs_d = C.D("mla_w_q_sw", [384, 1536], F32)
    wk_d = C.D("mla_w_kv_k", [256, 1024], F32)
    wv_d = C.D("mla_w_kv_v", [256, 1024], F32)
    mixT = C.D("mixT", [1024, 2048], BF16)
    nT = P.sb("anT", [128, 5, 2048], BF16)
    dnT = [Dep() for _ in range(5)]
    for c in range(5):
        P.dma("sp", nT[:, c, :], nT_d[c * 128:(c + 1) * 128, :], reads=[C.dd("mla_nT")], writes=[dnT[c]])
    dq = dnT[0:3]
    dkv = dnT[3:5]
    dc = Dep()
    KRT = P.sb("aKRT", [96, 2048], BF16)
    P.dma("sp", KRT[64:96, :], kr_d, reads=[C.dd("mla_krT")], writes=[dc])
    cs = P.sb("acs", [96, 2, 2048], F32)
    P.dma("sp", cs[64:96, :, :], cs_d, writes=[dc], merge=True)
    wq, dwq = load_fm_bf16(C, "awq", wq_d, 3, 1536)
    wqs, dwqs = load_fm_bf16(C, "awqs", wqs_d, 3, 1536)
    wk, dwk = load_fm_bf16(C, "awk", wk_d, 2, 1024)
    wv, dwv = load_fm_bf16(C, "awv", wv_d, 2, 1024)
    onesf = P.sb("aones", [128, 64], F32)
    P.op("pool", lambda e: e.memset(onesf[:], 1.0), writes=[dc], merge=True)
    Vx = P.sb("aVx", [128, 16, 16, 65], BF16)
    dV = Dep()
    P.op("pool", lambda e: e.memset(Vx[:], 1.0), writes=[dV])
    for tt in range(16):
        for half in range(2):
            bk, dbk = C.bank()
            mm_acc(P, bk[:], [(nT[:, 3 + kc, tt * 128:(tt + 1) * 128], wv[:, kc, half * 512:(half + 1) * 512]) for kc in range(2)], reads=dkv + dwv, dwrite=dbk)
            P.op("act", lambda e, tt=tt, half=half, bk=bk: e.copy(out=Vx[:, tt, half * 8:(half + 1) * 8, 0:64], in_=bk[:].rearrange("p (h d) -> p h d", h=8)),
                 reads=[dbk], writes=[dV], merge=True)
    QT_r = Ring(P, "aQT", 2, [96, 2048], BF16)
    KT_r = Ring(P, "aKT", 2, [96, 2048], BF16)
    ta_r = Ring(P, "ata", 2, [96, 512], F32)
    tb_r = Ring(P, "atb", 2, [96, 512], F32)
    p_r = Ring(P, "apT", 4, [128, 512], BF16)
    rd_r = Ring(P, "ard", 2, [65, 512], F32)
    bs_r = Ring(P, "absb", 2, [64, 512], F32)
    yo_r = Ring(P, "ayo", 3, [64, 512], BF16)
    for h in range(16):
        QT, dQT = QT_r.next()
        KT, dKT = KT_r.next()
        for nt in range(4):
            ns = slice(nt * 512, (nt + 1) * 512)
            bq, dbq = C.bank()
            bs, dbs = C.bank()
            mm_acc(P, bq[0:96, :], [(wq[:, kc, h * 96:(h + 1) * 96], nT[:, kc, ns]) for kc in range(3)], reads=dq + dwq, dwrite=dbq)
            mm_acc(P, bs[0:96, :], [(wqs[:, kc, h * 96:(h + 1) * 96], nT[:, kc, ns]) for kc in range(3)], reads=dq + dwqs, dwrite=dbs)
            P.op("act", lambda e, QT=QT, ns=ns, bq=bq: e.copy(out=QT[0:64, ns], in_=bq[0:64, :]), reads=[dbq], writes=[dQT], merge=(nt > 0))
            ta, dta = ta_r.next()
            tb, dtb = tb_r.next()
            P.op("dve", lambda e, ta=ta, bq=bq, ns=ns: e.tensor_tensor(out=ta[64:96, :], in0=bq[64:96, :], in1=cs[64:96, 0, ns], op=ALU.mult), reads=[dbq, dc], writes=[dta])
            P.op("dve", lambda e, tb=tb, bs=bs, ns=ns: e.tensor_tensor(out=tb[64:96, :], in0=bs[64:96, :], in1=cs[64:96, 1, ns], op=ALU.mult), reads=[dbs, dc], writes=[dtb])
            P.op("pool", lambda e, QT=QT, ta=ta, tb=tb, ns=ns: e.tensor_tensor(out=QT[64:96, ns], in0=ta[64:96, :], in1=tb[64:96, :], op=ALU.add), reads=[dta, dtb], writes=[dQT], merge=True)
            bkk, dbkk = C.bank()
            mm_acc(P, bkk[0:64, :], [(wk[:, kc, h * 64:(h + 1) * 64], nT[:, 3 + kc, ns]) for kc in range(2)], reads=dkv + dwk, dwrite=dbkk)
            P.op("act", lambda e, KT=KT, ns=ns, bkk=bkk: e.copy(out=KT[0:64, ns], in_=bkk[0:64, :]), reads=[dbkk], writes=[dKT], merge=(nt > 0))
        P.op("pool", lambda e, KT=KT: e.tensor_copy(out=KT[64:96, :], in_=KRT[64:96, :]), reads=[dc], writes=[dKT], merge=True)
        for qc in range(4):
            qs = slice(qc * 512, (qc + 1) * 512)
            acc, dacc = C.bank(hold=True)

            def pv(kt, pT, dpT, acc=acc, dacc=dacc, h=h):
                P.op("pe", lambda e, acc=acc, kt=kt, h=h, pT=pT: e.matmul(acc[0:65, :], lhsT=Vx[:, kt, h, :], rhs=pT[:], start=(kt == 0), stop=(kt == 15)),
                     reads=[dV, dpT], writes=[dacc], merge=(kt > 0))
            pend = None
            for kt in range(16):
                sb_, dsb_ = C.bank()
                P.op("pe", lambda e, sb_=sb_, KT=KT, QT=QT, kt=kt, qs=qs: e.matmul(sb_[:], lhsT=KT[0:96, kt * 128:(kt + 1) * 128], rhs=QT[0:96, qs], start=True, stop=True),
                     reads=[dKT, dQT], writes=[dsb_])
                pT, dpT = p_r.next()
                P.op("act", lambda e, pT=pT, sb_=sb_: e.activation(out=pT[:], in_=sb_[:], func=AF.Exp, scale=MLA_SCALE), reads=[dsb_], writes=[dpT])
                if pend is not None:
                    pv(*pend)
                pend = (kt, pT, dpT)
            pv(*pend)
            rd, drd = rd_r.next()
            P.op("dve", lambda e, rd=rd, acc=acc: e.reciprocal(out=rd[64:65, :], in_=acc[64:65, :]), reads=[dacc], writes=[drd])
            bb, dbb = C.bank()
            P.op("pe", lambda e, bb=bb, rd=rd: e.matmul(bb[0:64, :], lhsT=onesf[64:65, 0:64], rhs=rd[64:65, :], start=True, stop=True), reads=[drd, dc], writes=[dbb])
            bsb, dbsb = bs_r.next()
            P.op("act", lambda e, bsb=bsb, bb=bb: e.copy(out=bsb[:], in_=bb[0:64, :]), reads=[dbb], writes=[dbsb])
            yo, dyo = yo_r.next()
            P.op("dve", lambda e, yo=yo, acc=acc, bsb=bsb: e.tensor_tensor(out=yo[:], in0=acc[0:64, :], in1=bsb[:], op=ALU.mult), reads=[dacc, dbsb], writes=[dyo])
            C.release_all()
            P.dma("pool", mixT[h * 64:(h + 1) * 64, qs], yo[:], reads=[dyo], writes=[C.dd("mixT")], merge=True)


def _rep128(v):
    v = np.asarray(v, np.float32)
    return np.ascontiguousarray(np.broadcast_to(v[None, :], (128, v.shape[0])))


def shared_inputs(inp):
    f32 = lambda a: np.ascontiguousarray(np.asarray(a, np.float32))
    s = {}
    s["ab_w_in"] = f32(inp["ab_w_in"][0])
    s["na_tab"] = na_tables(np.asarray(inp["na_rpb"][0], np.float32))
    s.update(hyena_consts())
    s["hy_f_w1"] = f32(inp["hy_f_w1"][0])
    s["hy_f_w2"] = f32(inp["hy_f_w2"][0])
    s["hy_f_w3"] = f32(inp["hy_f_w3"][0])
    s["hy_cols"] = f32(np.stack([inp["hy_f_b1"][0], inp["hy_f_freq"][0], inp["hy_f_b2"][0]], axis=1))
    s["hy_skipb"] = np.stack([_rep128(inp["hy_skip"][0][0]), _rep128(inp["hy_skip"][0][1])])
    s["hy_cw"] = f32(np.asarray(inp["hy_conv_w"][0]).reshape(3, 12, 128).transpose(2, 1, 0))
    s["hy_cb"] = f32(np.asarray(inp["hy_conv_b"][0]).reshape(12, 128).T)
    s["ident"] = np.eye(128, dtype=np.float32)
    esel = np.zeros((16, 16, 128), np.float32)
    for e in range(16):
        esel[e, e, :] = 1.0
    s["esel"] = esel
    s["iota_col"] = (np.arange(16)[None, :] * 128 + np.arange(128)[:, None]).astype(np.float32)
    s["iota_row"] = _rep128(np.arange(2048, dtype=np.float32))
    s["ab_w_out"] = f32(inp["ab_w_out"][0])
    w_in = np.asarray(inp["mla_w_in"][0], np.float32)
    perm = np.concatenate([np.arange(16, 32), np.arange(0, 16)])
    s["mla_w_in"] = f32(w_in)
    s["mla_w_in_sw"] = f32(np.concatenate([w_in[:, 576:640], w_in[:, 640 + perm]], axis=1))
    wq = np.asarray(inp["mla_w_q_up"][0], np.float32)
    wqs = wq.reshape(384, 16, 96).copy()
    wqs[:, :, 64:] = wqs[:, :, 64 + perm]
    s["mla_w_q_up"] = f32(wq)
    s["mla_w_q_sw"] = f32(wqs.reshape(384, 1536))
    wkv = np.asarray(inp["mla_w_kv_up"][0], np.float32).reshape(256, 16, 128)
    s["mla_w_kv_k"] = f32(wkv[:, :, :64].reshape(256, 1024))
    s["mla_w_kv_v"] = f32(wkv[:, :, 64:].reshape(256, 1024))
    s["mla_gcols"] = f32(np.concatenate([np.asarray(inp["mla_q_norm"][0]).reshape(3, 128).T,
                                         np.asarray(inp["mla_kv_norm"][0]).reshape(2, 128).T], axis=1))
    s.update(mla_consts())
    s["mla_w_out"] = f32(inp["mla_w_out"][0])
    for li in range(2):
        s[f"ln1_g{li}"] = _rep128(inp["ln1_g"][li])
        s[f"ln1_b{li}"] = _rep128(inp["ln1_b"][li])
        s[f"ln2_g{li}"] = _rep128(inp["ln2_g"][li])
        s[f"ln2_b{li}"] = _rep128(inp["ln2_b"][li])
        s[f"moe_router{li}"] = f32(inp["moe_router"][li])
        s[f"moe_w_gate{li}"] = f32(inp["moe_w_gate"][li])
        s[f"moe_w_up{li}"] = f32(inp["moe_w_up"][li])
        s[f"moe_w_down{li}"] = f32(inp["moe_w_down"][li])
        s[f"ple_gate{li}"] = f32(inp["ple_gate"][li])
        s[f"ple_proj{li}"] = f32(inp["ple_proj"][li])
    return s


PER_CORE = ("x_tm", "xT", "pT0", "pT1")


def build_full(shared_names):
    C = Ctx(ext_in=set(shared_names) | set(PER_CORE), ext_out={"out"})
    stage_a1(C)
    stage_a2(C)
    stage_hy_filter(C)
    stage_hy_prep(C)
    stage_hy_conv(C)
    stage_proj_ln(C, "ab_w_out", "x_tm", "ln1", 0, "x1_0")
    stage_moe1(C, 0, "x1_0")
    stage_moe2(C, 0, "x1_0", "x2_0")
    stage_ple(C, 0, "x2_0", "x3_0", True)
    stage_mla1(C, "x3_0T")
    stage_mla2(C)
    stage_proj_ln(C, "mla_w_out", "x3_0", "ln1", 1, "x1_1")
    stage_moe1(C, 1, "x1_1")
    stage_moe2(C, 1, "x1_1", "x2_1")
    stage_ple(C, 1, "x2_1", "out", False)
    C.P.finish()
    return C


def kernel(**inputs):
    inp = {k: np.asarray(v) for k, v in inputs.items()}
    shared = shared_inputs(inp)
    x = np.asarray(inp["x"], np.float32)
    p = np.asarray(inp["p"], np.float32)
    C = build_full(shared.keys())
    used = set(C.dram.keys())
    in_maps = []
    for b in range(8):
        m = {k: v for k, v in shared.items() if k in used}
        m["x_tm"] = np.ascontiguousarray(x[b])
        m["xT"] = np.ascontiguousarray(x[b].T)
        m["pT0"] = np.ascontiguousarray(p[0, b].T)
        m["pT1"] = np.ascontiguousarray(p[1, b].T)
        in_maps.append(m)
    res = run_bass_kernel_spmd(C.nc, in_maps, core_ids=list(range(8)))
    return np.stack([np.asarray(r["out"], np.float32) for r in res.results], axis=0)
```

```python
from contextlib import ExitStack
import math
import numpy as np
import ml_dtypes
import concourse.bass as bass
import concourse.mybir as mybir
from concourse.bass_utils import run_bass_kernel_spmd

F32 = mybir.dt.float32
BF16 = mybir.dt.bfloat16
I32 = mybir.dt.int32
U32 = mybir.dt.uint32
AF = mybir.ActivationFunctionType
ALU = mybir.AluOpType
AX = mybir.AxisListType
NPBF = ml_dtypes.bfloat16

D_MODEL = 1024
SEQ = 2048
NT = SEQ // 128
ALPHA = 4.0 ** 0.25
EPS = 1e-5

ENGS = ("pe", "act", "dve", "pool", "sp")
N_DMA_SEMS = 16


class Dep:
    __slots__ = ("w", "r", "name")

    def __init__(self, name=""):
        self.w = {}
        self.r = {}
        self.name = name


class Prog:
    def __init__(self, nc, strict=True):
        self.nc = nc
        self.es = ExitStack()
        self.q = {e: [] for e in ENGS}
        self.cnt = {e: 0 for e in ENGS}
        self.seen = {e: {} for e in ENGS}
        self.sem = {}
        self.strict = strict
        for e in ENGS:
            self.sem[e] = self.es.enter_context(nc.semaphore("s_" + e))
        self.dma_sems = {}
        self.dma_tot = {}
        self.dma_rr = {}
        for e in ("sp", "pool", "act"):
            self.dma_sems[e] = [self.es.enter_context(nc.semaphore(f"d_{e}{i}")) for i in range(N_DMA_SEMS)]
            self.dma_tot[e] = [0] * N_DMA_SEMS
            self.dma_rr[e] = 0
        self.all_events = {}
        self.n_ops = 0
        self.stage_es = None
        self.uid = 0

    def begin_stage(self):
        self.barrier()
        if self.stage_es is not None:
            self.stage_es.close()
        self.stage_es = ExitStack()

    def sb(self, name, shape, dt):
        self.uid += 1
        t = self.stage_es.enter_context(self.nc.sbuf_tensor(f"{name}_{self.uid}", list(shape), dt))
        return t

    def ps(self, name, shape, dt=F32):
        return self.es.enter_context(self.nc.psum_tensor(name, list(shape), dt))

    def _semobj(self, key):
        if isinstance(key, str):
            return self.sem[key]
        e, i = key
        return self.dma_sems[e][i]

    def _need(self, eng, reads, writes, merge):
        need = {}

        def add(k, v):
            if k == eng and (not self.strict or eng == "pe"):
                return
            if self.seen[eng].get(k, 0) >= v:
                return
            if need.get(k, 0) < v:
                need[k] = v
        for d in reads:
            for k, v in d.w.items():
                add(k, v)
        for d in writes:
            if not merge:
                for k, v in d.w.items():
                    add(k, v)
            for k, v in d.r.items():
                add(k, v)
        for k, v in need.items():
            self.seen[eng][k] = v
        return list(need.items())

    def _commit(self, ev, reads, writes, merge):
        k, v = ev
        for d in reads:
            if d.r.get(k, 0) < v:
                d.r[k] = v
        for d in writes:
            if merge:
                d.w[k] = v
            else:
                d.w = {k: v}
                d.r = {}
        self.all_events[k] = v

    def op(self, eng, fn, reads=(), writes=(), merge=False):
        waits = self._need(eng, reads, writes, merge)
        self.cnt[eng] += 1
        ev = (eng, self.cnt[eng])
        sem = self.sem[eng]
        waitobjs = [(self._semobj(k), v) for k, v in waits]

        def emit(e, fn=fn, waitobjs=waitobjs, sem=sem):
            for s, v in waitobjs:
                e.wait_ge(s, v)
            fn(e).then_inc(sem, 1)
        self.q[eng].append(emit)
        self._commit(ev, reads, writes, merge)
        self.n_ops += 1
        return ev

    def dma(self, eng, out, in_, reads=(), writes=(), merge=False, **kw):
        i = self.dma_rr[eng]
        self.dma_rr[eng] = (i + 1) % N_DMA_SEMS
        key = (eng, i)
        prev = self.dma_tot[eng][i]
        waits = self._need(eng, reads, writes, merge)
        if prev > 0 and self.seen[eng].get(key, 0) < prev:
            waits.append((key, prev))
            self.seen[eng][key] = prev
        self.dma_tot[eng][i] = prev + 16
        ev = (key, prev + 16)
        sem = self.dma_sems[eng][i]
        waitobjs = [(self._semobj(k), v) for k, v in waits]

        def emit(e, waitobjs=waitobjs, sem=sem, out=out, in_=in_, kw=kw):
            for s, v in waitobjs:
                e.wait_ge(s, v)
            e.dma_start(out=out, in_=in_, **kw).then_inc(sem, 16)
        self.q[eng].append(emit)
        self._commit(ev, reads, writes, merge)
        self.n_ops += 1
        return ev

    def coll(self, kind, out, in_, reads=(), writes=()):
        eng = "pool"
        i = self.dma_rr[eng]
        self.dma_rr[eng] = (i + 1) % N_DMA_SEMS
        key = (eng, i)
        prev = self.dma_tot[eng][i]
        waits = self._need(eng, reads, writes, False)
        if prev > 0 and self.seen[eng].get(key, 0) < prev:
            waits.append((key, prev))
            self.seen[eng][key] = prev
        self.dma_tot[eng][i] = prev + 16
        ev = (key, prev + 16)
        sem = self.dma_sems[eng][i]
        waitobjs = [(self._semobj(k), v) for k, v in waits]

        def emit(e, waitobjs=waitobjs, sem=sem, out=out, in_=in_, kind=kind):
            for s_, v in waitobjs:
                e.wait_ge(s_, v)
            e.collective_compute(kind, ALU.bypass, replica_groups=[list(range(8))], ins=[in_], outs=[out]).then_inc(sem, 16)
        self.q[eng].append(emit)
        self._commit(ev, reads, writes, False)
        self.n_ops += 1
        return ev

    def barrier(self):
        snap = dict(self.all_events)
        for eng in ENGS:
            waits = []
            for k, v in snap.items():
                if k == eng:
                    continue
                if self.seen[eng].get(k, 0) >= v:
                    continue
                waits.append((self._semobj(k), v))
                self.seen[eng][k] = v
            if waits:
                def emit(e, waits=waits):
                    for s, v in waits:
                        e.wait_ge(s, v)
                self.q[eng].append(emit)

    def finish(self):
        self.barrier()
        nc = self.nc
        q = self.q
        with nc.Block() as block:
            @block.tensor
            def _(e):
                for f in q["pe"]:
                    f(e)

            @block.scalar
            def _(e):
                for f in q["act"]:
                    f(e)

            @block.vector
            def _(e):
                for f in q["dve"]:
                    f(e)

            @block.gpsimd
            def _(e):
                for f in q["pool"]:
                    f(e)

            @block.sync
            def _(e):
                for f in q["sp"]:
                    f(e)
        if self.stage_es is not None:
            self.stage_es.close()
        self.es.close()


class Ring:
    def __init__(self, P, name, n, shape, dt):
        self.bufs = [(P.sb(f"{name}{i}", shape, dt), Dep(f"{name}{i}")) for i in range(n)]
        self.i = 0

    def next(self):
        b = self.bufs[self.i]
        self.i = (self.i + 1) % len(self.bufs)
        return b


class Ctx:
    def __init__(self, ext_in, ext_out):
        self.nc = bass.Bass("TRN2", target_bir_lowering=False)
        self.P = Prog(self.nc)
        self.ext_in = set(ext_in)
        self.ext_out = set(ext_out)
        self.dram = {}
        self.ddep = {}
        P = self.P
        self.banks = [(P.ps(f"bank{i}", [128, 512], F32), Dep(f"bank{i}")) for i in range(8)]
        self.bank_i = 0
        self.held = set()

    def D(self, name, shape=None, dt=F32):
        if name in self.dram:
            return self.dram[name]
        kind = "Internal"
        if name in self.ext_in:
            kind = "ExternalInput"
        elif name in self.ext_out:
            kind = "ExternalOutput"
        t = self.nc.dram_tensor(name, list(shape), dt, kind=kind).ap()
        self.dram[name] = t
        self.ddep[name] = Dep(name)
        return t

    def dd(self, name):
        return self.ddep[name]

    def bank(self, hold=False):
        for _ in range(8):
            i = self.bank_i
            self.bank_i = (self.bank_i + 1) % 8
            if i not in self.held:
                if hold:
                    self.held.add(i)
                return self.banks[i]
        raise RuntimeError("no free PSUM bank")

    def release_all(self):
        self.held = set()

    def release(self, b):
        for i, bb in enumerate(self.banks):
            if bb[0] is b[0]:
                self.held.discard(i)


def mm_acc(P, out, pairs, reads, dwrite):
    n = len(pairs)
    for i, (l, r) in enumerate(pairs):
        P.op("pe", lambda e, l=l, r=r, i=i: e.matmul(out, lhsT=l, rhs=r, start=(i == 0), stop=(i == n - 1)),
             reads=reads, writes=[dwrite], merge=(i > 0))


def load_fm_bf16(C, name, src, kc_n, width, eng="pool"):
    P = C.P
    t = P.sb(name, [128, kc_n, width], BF16)
    deps = [Dep(f"{name}{k}") for k in range(kc_n)]
    for k in range(kc_n):
        P.dma(eng, t[:, k, :], src[k * 128:(k + 1) * 128, :], writes=[deps[k]])
    return t, deps


def stage_a1(C):
    P = C.P
    P.begin_stage()
    xT = C.D("xT", [1024, 2048], F32)
    w_in = C.D("ab_w_in", [1024, 3072], F32)
    qkT = C.D("qkT", [1024, 2048], BF16)
    v_tm = C.D("v_tm", [2048, 512], BF16)
    hbT = C.D("hbT", [1536, 2048], F32)
    xs, dxs = load_fm_bf16(C, "xTb", xT, 8, 2048)
    ws, dws = load_fm_bf16(C, "winb", w_in, 8, 3072)
    st_b = Ring(P, "a1sb", 3, [128, 2048], BF16)
    st_f = Ring(P, "a1sf", 3, [128, 2048], F32)
    ev_i = 0
    for mc in list(range(8)) + list(range(12, 24)):
        is_hb = mc >= 12
        stg, dstg = (st_f if is_hb else st_b).next()
        for nt in range(4):
            bk, dbk = C.bank()
            mm_acc(P, bk[:], [(ws[:, kc, mc * 128:(mc + 1) * 128], xs[:, kc, nt * 512:(nt + 1) * 512]) for kc in range(8)],
                   reads=dxs + dws, dwrite=dbk)
            eng = "act" if ev_i % 2 == 0 else "dve"
            ev_i += 1
            o = stg[:, nt * 512:(nt + 1) * 512]
            if eng == "act":
                P.op("act", lambda e, o=o, bk=bk: e.copy(out=o, in_=bk[:]), reads=[dbk], writes=[dstg], merge=(nt > 0))
            else:
                P.op("dve", lambda e, o=o, bk=bk: e.tensor_copy(out=o, in_=bk[:]), reads=[dbk], writes=[dstg], merge=(nt > 0))
        if is_hb:
            P.dma("act", hbT[(mc - 12) * 128:(mc - 11) * 128, :], stg[:], reads=[dstg], writes=[C.dd("hbT")], merge=True)
        else:
            P.dma("act", qkT[mc * 128:(mc + 1) * 128, :], stg[:], reads=[dstg], writes=[C.dd("qkT")], merge=True)
    st_v = Ring(P, "a1sv", 3, [128, 512], BF16)
    for tt in range(NT):
        bk, dbk = C.bank()
        mm_acc(P, bk[:], [(xs[:, kc, tt * 128:(tt + 1) * 128], ws[:, kc, 1024:1536]) for kc in range(8)],
               reads=dxs + dws, dwrite=dbk)
        stg, dstg = st_v.next()
        if tt % 2 == 0:
            P.op("act", lambda e, stg=stg, bk=bk: e.copy(out=stg[:], in_=bk[:]), reads=[dbk], writes=[dstg])
        else:
            P.op("dve", lambda e, stg=stg, bk=bk: e.tensor_copy(out=stg[:], in_=bk[:]), reads=[dbk], writes=[dstg])
        P.dma("act", v_tm[tt * 128:(tt + 1) * 128, :], stg[:], reads=[dstg], writes=[C.dd("v_tm")], merge=True)


def na_plan():
    rows, wr = 32, 8
    r0 = np.clip(np.arange(rows) - wr // 2, 0, rows - wr)
    plan = []
    keys = {}
    for i in range(16):
        lo = r0[2 * i] // 2
        hi = (r0[2 * i + 1] + 7) // 2
        lst = []
        for j in range(lo, hi + 1):
            val = []
            for ak in range(2):
                for aq in range(2):
                    r = 2 * i + aq
                    kr = 2 * j + ak
                    val.append(bool(r0[r] <= kr <= r0[r] + 7))
            key = (j - i, tuple(val))
            if key not in keys:
                keys[key] = len(keys)
            lst.append((j, keys[key]))
        plan.append(lst)
    return plan, keys


def na_tables(rpb):
    plan, keys = na_plan()
    c = np.arange(64)
    c0 = np.clip(c - 8, 0, 48)
    col_ok = (c[None, :] >= c0[:, None]) & (c[None, :] < c0[:, None] + 16)
    dc_idx = np.clip(c[None, :] - c[:, None], -15, 15) + 15
    tab = np.full((len(keys), 2, 64, 8, 2, 64), -1e30, np.float32)
    for (delta, val), tid in keys.items():
        vi = 0
        for ak in range(2):
            for aq in range(2):
                ok = val[vi]
                vi += 1
                if not ok:
                    continue
                dr = 2 * delta + ak - aq
                b = rpb[:, dr + 7, :][:, dc_idx]
                b = np.where(col_ok[None], b, np.float32(-1e30))
                tab[tid, ak, :, :, aq, :] = b.transpose(2, 0, 1)
    return tab.reshape(len(keys), 128, 8, 128)


def stage_a2(C):
    P = C.P
    P.begin_stage()
    plan, keys = na_plan()
    ntab = len(keys)
    qkT = C.D("qkT", [1024, 2048], BF16)
    v_tm = C.D("v_tm", [2048, 512], BF16)
    tab_d = C.D("na_tab", [ntab, 128, 8, 128], F32)
    mixT = C.D("mixT", [1024, 2048], BF16)
    QT = P.sb("QT", [128, 4, 2048], BF16)
    KT = P.sb("KT", [128, 4, 2048], BF16)
    dQ = [Dep() for _ in range(4)]
    dK = [Dep() for _ in range(4)]
    for hp in range(4):
        P.dma("sp", QT[:, hp, :], qkT[hp * 128:(hp + 1) * 128, :], reads=[C.dd("qkT")], writes=[dQ[hp]])
        P.dma("sp", KT[:, hp, :], qkT[512 + hp * 128:512 + (hp + 1) * 128, :], reads=[C.dd("qkT")], writes=[dK[hp]])
    tab = P.sb("natab", [128, ntab, 8, 128], F32)
    dtab = Dep()
    for t in range(ntab):
        P.dma("sp", tab[:, t, :, :], tab_d[t], writes=[dtab], merge=True)
    Vx = P.sb("Vx", [128, 8, NT, 128], BF16)
    dV = Dep()
    P.op("pool", lambda e: e.memset(Vx[:], 0.0), writes=[dV])
    vv = v_tm.rearrange("(t p) c -> p t c", p=128)
    for h in range(8):
        a = h % 2
        P.dma("sp", Vx[:, h, :, a * 64:(a + 1) * 64], vv[:, :, h * 64:(h + 1) * 64], reads=[C.dd("v_tm")], writes=[dV], merge=(h > 0))
    ones2 = P.sb("ones2", [128, 2, 128], BF16)
    dones = Dep()
    P.op("pool", lambda e: e.memset(ones2[:], 0.0), writes=[dones])
    P.op("pool", lambda e: e.memset(ones2[:, 0, 0:64], 1.0), writes=[dones])
    P.op("pool", lambda e: e.memset(ones2[:, 1, 64:128], 1.0), writes=[dones])
    yaT = P.sb("yaT", [128, 4, 2048], BF16)
    dya = [Dep() for _ in range(4)]
    s_ring = Ring(P, "na_s", 3, [128, 640], F32)
    p_ring = Ring(P, "na_p", 4, [128, 640], BF16)
    rd_ring = Ring(P, "na_rd", 2, [128, 128], F32)
    units = [(i, hp, a) for i in range(16) for hp in range(4) for a in range(2)]
    pair = {}

    def phase1(u):
        i, hp, a = u
        q0 = i * 128
        h = hp * 2 + a
        pa = slice(a * 64, (a + 1) * 64)
        lst = plan[i]
        nkb = len(lst)
        bA, dbA = C.bank()
        bB, dbB = (C.bank() if nkb > 4 else (None, None))
        ssb, dss = s_ring.next()
        for jj, (j, tid) in enumerate(lst):
            bk, dbk = (bA, dbA) if jj < 4 else (bB, dbB)
            o = bk[:, (jj % 4) * 128:(jj % 4 + 1) * 128]
            P.op("pe", lambda e, o=o, j=j, pa=pa, hp=hp, q0=q0: e.matmul(
                o, lhsT=KT[pa, hp, j * 128:(j + 1) * 128], rhs=QT[pa, hp, q0:q0 + 128], start=True, stop=True),
                reads=[dK[hp], dQ[hp]], writes=[dbk], merge=(jj % 4 > 0))
        for jj, (j, tid) in enumerate(lst):
            bk, dbk = (bA, dbA) if jj < 4 else (bB, dbB)
            o = bk[:, (jj % 4) * 128:(jj % 4 + 1) * 128]
            P.op("dve", lambda e, o=o, jj=jj, tid=tid, h=h, ssb=ssb: e.scalar_tensor_tensor(
                out=ssb[:, jj * 128:(jj + 1) * 128], in0=o, scalar=0.125, in1=tab[:, tid, h, :],
                op0=ALU.mult, op1=ALU.add), reads=[dbk, dtab], writes=[dss], merge=(jj > 0))
        pT, dpT = p_ring.next()
        P.op("act", lambda e, pT=pT, ssb=ssb, nkb=nkb: e.activation(
            out=pT[:, 0:nkb * 128], in_=ssb[:, 0:nkb * 128], func=AF.Exp), reads=[dss], writes=[dpT])
        return (pT, dpT)

    def phase2(u, pp):
        i, hp, a = u
        q0 = i * 128
        h = hp * 2 + a
        pT, dpT = pp
        lst = plan[i]
        nkb = len(lst)
        if a == 0:
            pair[(i, hp)] = (C.bank(hold=True), C.bank(hold=True))
        (bo, dbo), (bd, dbd) = pair[(i, hp)]
        for jj, (j, tid) in enumerate(lst):
            first = (a == 0 and jj == 0)
            last = (a == 1 and jj == nkb - 1)
            P.op("pe", lambda e, bo=bo, h=h, j=j, pT=pT, jj=jj, first=first, last=last: e.matmul(
                bo[:, 0:128], lhsT=Vx[:, h, j, :], rhs=pT[:, jj * 128:(jj + 1) * 128], start=first, stop=last),
                reads=[dV, dpT], writes=[dbo], merge=(not first))
            P.op("pe", lambda e, bd=bd, a=a, pT=pT, jj=jj, first=first, last=last: e.matmul(
                bd[:, 0:128], lhsT=ones2[:, a, :], rhs=pT[:, jj * 128:(jj + 1) * 128], start=first, stop=last),
                reads=[dones, dpT], writes=[dbd], merge=(not first))
        if a == 1:
            rd, drd = rd_ring.next()
            P.op("dve", lambda e, rd=rd, bd=bd: e.reciprocal(out=rd[:], in_=bd[:, 0:128]), reads=[dbd], writes=[drd])
            P.op("dve", lambda e, rd=rd, bo=bo, hp=hp, q0=q0: e.tensor_tensor(
                out=yaT[:, hp, q0:q0 + 128], in0=bo[:, 0:128], in1=rd[:], op=ALU.mult),
                reads=[dbo, drd], writes=[dya[hp]], merge=True)
            C.release(pair[(i, hp)][0])
            C.release(pair[(i, hp)][1])
            del pair[(i, hp)]

    pend = None
    for u in units:
        pp = phase1(u)
        if pend is not None:
            phase2(*pend)
        pend = (u, pp)
    phase2(*pend)
    for hp in range(4):
        P.dma("sp", mixT[hp * 128:(hp + 1) * 128, :], yaT[:, hp, :], reads=[dya[hp]], writes=[C.dd("mixT")], merge=True)


class LNBufs:
    def __init__(self, P, name):
        self.stats = Ring(P, name + "st", 2, [128, 2, 6], F32)
        self.mv = Ring(P, name + "mv", 2, [128, 2], F32)
        self.rstd = Ring(P, name + "rs", 2, [128, 1], F32)
        self.nmr = Ring(P, name + "nm", 2, [128, 1], F32)


def layer_norm_tile(P, lb, r, dr, gb, bb, dgb, y, dy):
    st, dst = lb.stats.next()
    mv, dmv = lb.mv.next()
    rs, drs = lb.rstd.next()
    nm, dnm = lb.nmr.next()
    P.op("dve", lambda e: e.bn_stats(out=st[:, 0, :], in_=r[:, 0:512]), reads=[dr], writes=[dst])
    P.op("dve", lambda e: e.bn_stats(out=st[:, 1, :], in_=r[:, 512:1024]), reads=[dr], writes=[dst], merge=True)
    P.op("dve", lambda e: e.bn_aggr(out=mv[:], in_=st[:]), reads=[dst], writes=[dmv])
    P.op("dve", lambda e: e.tensor_scalar(out=rs[:], in0=mv[:, 1:2], scalar1=EPS, scalar2=None, op0=ALU.add), reads=[dmv], writes=[drs])
    P.op("act", lambda e: e.activation(out=rs[:], in_=rs[:], func=AF.Sqrt), reads=[drs], writes=[drs])
    P.op("dve", lambda e: e.reciprocal(out=rs[:], in_=rs[:]), reads=[drs], writes=[drs])
    P.op("dve", lambda e: e.scalar_tensor_tensor(out=nm[:], in0=mv[:, 0:1], scalar=-1.0, in1=rs[:], op0=ALU.mult, op1=ALU.mult),
         reads=[dmv, drs], writes=[dnm])
    P.op("act", lambda e: e.activation(out=y[:], in_=r[:], func=AF.Identity, bias=nm[:], scale=rs[:]),
         reads=[dr, drs, dnm], writes=[dy])
    P.op("pool", lambda e: e.tensor_tensor(out=y[:], in0=y[:], in1=gb[:], op=ALU.mult), reads=[dy, dgb], writes=[dy])
    P.op("pool", lambda e: e.tensor_tensor(out=y[:], in0=y[:], in1=bb[:], op=ALU.add), reads=[dy, dgb], writes=[dy])


def stage_proj_ln(C, w_name, x_name, lnname, li, out_name):
    P = C.P
    P.begin_stage()
    mixT = C.D("mixT", [1024, 2048], BF16)
    w = C.D(w_name, [1024, 1024], F32)
    x = C.D(x_name, [2048, 1024], F32)
    g_d = C.D(f"{lnname}_g{li}", [128, 1024], F32)
    b_d = C.D(f"{lnname}_b{li}", [128, 1024], F32)
    out = C.D(out_name, [2048, 1024], F32)
    ms = P.sb("ms", [128, 8, 2048], BF16)
    dms = [Dep() for _ in range(8)]
    for kc in range(8):
        P.dma("sp", ms[:, kc, :], mixT[kc * 128:(kc + 1) * 128, :], reads=[C.dd("mixT")], writes=[dms[kc]])
    ws, dws = load_fm_bf16(C, "wout", w, 8, 1024)
    gb = P.sb("gb", [128, 1024], F32)
    bb = P.sb("bb", [128, 1024], F32)
    dgb = Dep()
    P.dma("sp", gb[:], g_d, writes=[dgb])
    P.dma("sp", bb[:], b_d, writes=[dgb], merge=True)
    lb = LNBufs(P, "ln")
    xr = Ring(P, "xr", 3, [128, 1024], F32)
    rr = Ring(P, "rr", 2, [128, 1024], F32)
    yr = Ring(P, "yr", 2, [128, 1024], F32)
    for tt in range(NT):
        xt, dxt = xr.next()
        P.dma("sp", xt[:], x[tt * 128:(tt + 1) * 128, :], reads=[C.dd(x_name)], writes=[dxt])
        r, dr = rr.next()
        for half in range(2):
            bk, dbk = C.bank()
            hs = slice(half * 512, (half + 1) * 512)
            mm_acc(P, bk[:], [(ms[:, kc, tt * 128:(tt + 1) * 128], ws[:, kc, hs]) for kc in range(8)], reads=dms + dws, dwrite=dbk)
            P.op("dve", lambda e, r=r, xt=xt, bk=bk, hs=hs: e.scalar_tensor_tensor(
                out=r[:, hs], in0=xt[:, hs], scalar=ALPHA, in1=bk[:], op0=ALU.mult, op1=ALU.add),
                reads=[dxt, dbk], writes=[dr], merge=(half > 0))
        y, dy = yr.next()
        layer_norm_tile(P, lb, r, dr, gb, bb, dgb, y, dy)
        P.dma("pool", out[tt * 128:(tt + 1) * 128, :], y[:], reads=[dy], writes=[C.dd(out_name)], merge=True)


def stage_moe1(C, li, x_name):
    P = C.P
    P.begin_stage()
    x1 = C.D(x_name, [2048, 1024], F32)
    wr_d = C.D(f"moe_router{li}", [1024, 16], F32)
    ident_d = C.D("ident", [128, 128], F32)
    esel_d = C.D("esel", [16, 16, 128], F32)
    iotac_d = C.D("iota_col", [128, 16], F32)
    xe_all = C.D("xeT_all", [16, 128, 8, 256], BF16)
    idxc_d = C.D("moe_idxc", [128, 2, 16], F32)
    gc_d = C.D("moe_gc", [128, 2, 16], F32)
    wr = P.sb("wr", [128, 8, 16], F32)
    dcst = Dep()
    P.dma("sp", wr[:], wr_d.rearrange("(kc p) e -> p kc e", p=128), writes=[dcst])
    ident = P.sb("ident", [128, 128], F32)
    P.dma("sp", ident[:], ident_d, writes=[dcst], merge=True)
    esel = P.sb("esel", [16, 16, 128], F32)
    P.dma("sp", esel[:], esel_d, writes=[dcst], merge=True)
    iotac = P.sb("iotac", [128, 16], F32)
    P.dma("sp", iotac[:], iotac_d, writes=[dcst], merge=True)
    x1b = P.sb("x1b", [128, NT, 1024], BF16)
    dx1b = [Dep() for _ in range(NT)]
    affT = P.sb("affT", [16, 2048], F32)
    daffT = Dep()
    xr = Ring(P, "m1x", 2, [128, 1024], F32)
    xTr = Ring(P, "m1xT", 2, [128, 8, 128], F32)
    sm_r = Ring(P, "m1sm", 2, [128, 4], F32)
    ex_r = Ring(P, "m1ex", 2, [128, 16], F32)
    af_r = Ring(P, "m1af", 2, [128, 16], F32)
    for tt in range(NT):
        xt, dxt = xr.next()
        P.dma("sp", xt[:], x1[tt * 128:(tt + 1) * 128, :], reads=[C.dd(x_name)], writes=[dxt])
        P.op("act", lambda e, xt=xt, tt=tt: e.copy(out=x1b[:, tt, :], in_=xt[:]), reads=[dxt], writes=[dx1b[tt]])
        xT, dxT = xTr.next()
        for hb in range(2):
            bk, dbk = C.bank()
            for k4 in range(4):
                kc = hb * 4 + k4
                P.op("pe", lambda e, bk=bk, k4=k4, kc=kc, xt=xt: e.transpose(bk[:, k4 * 128:(k4 + 1) * 128], xt[:, kc * 128:(kc + 1) * 128], ident[:]),
                     reads=[dxt, dcst], writes=[dbk], merge=(k4 > 0))
            P.op("dve", lambda e, xT=xT, hb=hb, bk=bk: e.tensor_copy(out=xT[:, hb * 4:(hb + 1) * 4, :], in_=bk[:].rearrange("p (k t) -> p k t", k=4)),
                 reads=[dbk], writes=[dxT], merge=(hb > 0))
        bk, dbk = C.bank()
        mm_acc(P, bk[:, 0:16], [(xT[:, kc, :], wr[:, kc, :]) for kc in range(8)], reads=[dxT, dcst], dwrite=dbk)
        sm, dsm = sm_r.next()
        ex, dex = ex_r.next()
        af, daf = af_r.next()
        P.op("dve", lambda e, sm=sm, bk=bk: e.reduce_max(out=sm[:, 0:1], in_=bk[:, 0:16], axis=AX.X), reads=[dbk], writes=[dsm])
        P.op("dve", lambda e, sm=sm: e.tensor_scalar(out=sm[:, 1:2], in0=sm[:, 0:1], scalar1=-1.0, scalar2=None, op0=ALU.mult),
             reads=[dsm], writes=[dsm])
        P.op("act", lambda e, ex=ex, bk=bk, sm=sm: e.activation(out=ex[:], in_=bk[:, 0:16], func=AF.Exp, bias=sm[:, 1:2], accum_out=sm[:, 2:3]),
             reads=[dbk, dsm], writes=[dex, dsm])
        P.op("dve", lambda e, sm=sm: e.reciprocal(out=sm[:, 3:4], in_=sm[:, 2:3]), reads=[dsm], writes=[dsm])
        P.op("dve", lambda e, af=af, ex=ex, sm=sm: e.tensor_scalar(out=af[:], in0=ex[:], scalar1=sm[:, 3:4], scalar2=None, op0=ALU.mult),
             reads=[dex, dsm], writes=[daf])
        bk2, dbk2 = C.bank()
        P.op("pe", lambda e, bk2=bk2, af=af: e.transpose(bk2[0:16, 0:128], af[:], ident[:]), reads=[daf, dcst], writes=[dbk2])
        P.op("act", lambda e, bk2=bk2, tt=tt: e.copy(out=affT[:, tt * 128:(tt + 1) * 128], in_=bk2[0:16, 0:128]),
             reads=[dbk2], writes=[daffT], merge=(tt > 0))
    work = P.sb("work", [16, 2048], F32)
    dwork = Dep()
    g_all = P.sb("g_all", [16, 256], F32)
    idx_all = P.sb("idx_all", [16, 256], U32)
    dg = Dep()
    di = Dep()
    for r in range(32):
        src, dsrc = (affT, daffT) if r == 0 else (work, dwork)
        sl = slice(r * 8, (r + 1) * 8)
        P.op("dve", lambda e, src=src, sl=sl: e.max(out=g_all[:, sl], in_=src[:]), reads=[dsrc], writes=[dg], merge=(r > 0))
        P.op("dve", lambda e, src=src, sl=sl: e.max_index(out=idx_all[:, sl], in_max=g_all[:, sl], in_values=src[:]),
             reads=[dsrc, dg], writes=[di], merge=(r > 0))
        if r < 31:
            P.op("dve", lambda e, src=src, sl=sl: e.match_replace(out=work[:], in_to_replace=g_all[:, sl], in_values=src[:], imm_value=-1.0),
                 reads=[dsrc, dg], writes=[dwork])
    idxf = P.sb("idxf", [16, 256], F32)
    didxf = Dep()
    P.op("dve", lambda e: e.tensor_copy(out=idxf[:], in_=idx_all[:]), reads=[di], writes=[didxf])
    colt = P.sb("colt", [128, 2, 2, 16], F32)
    dcol = Dep()
    for which, (src, dsrc) in enumerate(((idxf, didxf), (g_all, dg))):
        for cc in range(2):
            bk, dbk = C.bank()
            P.op("pe", lambda e, bk=bk, src=src, cc=cc: e.transpose(bk[:, 0:16], src[:, cc * 128:(cc + 1) * 128], ident[0:16, 0:16]),
                 reads=[dsrc, dcst], writes=[dbk])
            P.op("act", lambda e, bk=bk, which=which, cc=cc: e.copy(out=colt[:, which, cc, :], in_=bk[:, 0:16]),
                 reads=[dbk], writes=[dcol], merge=True)
    P.dma("act", idxc_d, colt[:, 0, :, :], reads=[dcol], writes=[C.dd("moe_idxc")])
    P.dma("act", gc_d, colt[:, 1, :, :], reads=[dcol], writes=[C.dd("moe_gc")])
    sel_r = Ring(P, "sel", 2, [128, NT, 256], BF16)
    xe_r = Ring(P, "xe", 2, [128, 8, 256], BF16)
    for ex_i in range(16):
        bk, dbk = C.bank()
        P.op("pe", lambda e, bk=bk, ex_i=ex_i: e.matmul(bk[:, 0:256], lhsT=esel[:, ex_i, :], rhs=idxf[:], start=True, stop=True),
             reads=[dcst, didxf], writes=[dbk])
        sel, dsel = sel_r.next()
        for tt in range(NT):
            P.op("dve", lambda e, sel=sel, bk=bk, tt=tt: e.tensor_scalar(out=sel[:, tt, :], in0=bk[:, 0:256], scalar1=iotac[:, tt:tt + 1], scalar2=None, op0=ALU.is_equal),
                 reads=[dbk, dcst], writes=[dsel], merge=(tt > 0))
        xe, dxe = xe_r.next()
        for dc in range(8):
            bk2, dbk2 = C.bank()
            mm_acc(P, bk2[:, 0:256], [(x1b[:, tt, dc * 128:(dc + 1) * 128], sel[:, tt, :]) for tt in range(NT)],
                   reads=dx1b + [dsel], dwrite=dbk2)
            if dc % 2 == 0:
                P.op("act", lambda e, xe=xe, dc=dc, bk2=bk2: e.copy(out=xe[:, dc, :], in_=bk2[:, 0:256]), reads=[dbk2], writes=[dxe], merge=(dc > 0))
            else:
                P.op("dve", lambda e, xe=xe, dc=dc, bk2=bk2: e.tensor_copy(out=xe[:, dc, :], in_=bk2[:, 0:256]), reads=[dbk2], writes=[dxe], merge=True)
        P.dma("act", xe_all[ex_i], xe[:], reads=[dxe], writes=[C.dd("xeT_all")], merge=True)


def stage_moe2(C, li, x_name, out_name):
    P = C.P
    P.begin_stage()
    x1 = C.D(x_name, [2048, 1024], F32)
    wg_d = C.D(f"moe_w_gate{li}", [16, 1024, 2048], F32)
    wu_d = C.D(f"moe_w_up{li}", [16, 1024, 2048], F32)
    wd_d = C.D(f"moe_w_down{li}", [16, 2048, 1024], F32)
    xe_all = C.D("xeT_all", [16, 128, 8, 256], BF16)
    idxc_d = C.D("moe_idxc", [128, 2, 16], F32)
    gc_d = C.D("moe_gc", [128, 2, 16], F32)
    iotar_d = C.D("iota_row", [128, 2048], F32)
    ident_d = C.D("ident", [128, 128], F32)
    g_d = C.D(f"ln2_g{li}", [128, 1024], F32)
    b_d = C.D(f"ln2_b{li}", [128, 1024], F32)
    out = C.D(out_name, [2048, 1024], F32)
    outT = C.D(out_name + "T", [1024, 2048], BF16)
    dcst = Dep()
    idxc = P.sb("idxc", [128, 2, 16], F32)
    gc = P.sb("gc", [128, 2, 16], F32)
    iotar = P.sb("iotar", [128, 2048], F32)
    P.dma("sp", idxc[:], idxc_d, reads=[C.dd("moe_idxc")], writes=[dcst])
    P.dma("sp", gc[:], gc_d, reads=[C.dd("moe_gc")], writes=[dcst], merge=True)
    P.dma("sp", iotar[:], iotar_d, writes=[dcst], merge=True)
    f_acc = P.sb("f_acc", [128, NT, 1024], F32)
    dfa = [Dep() for _ in range(NT)]
    xe_r = Ring(P, "m2xe", 2, [128, 8, 256], BF16)
    selT_r = Ring(P, "selT", 2, [128, 2, 2048], BF16)
    wg_r = Ring(P, "wg", 2, [128, 8, 512], BF16)
    wu_r = Ring(P, "wu", 2, [128, 8, 512], BF16)
    wd_r = Ring(P, "wd", 2, [128, 4, 1024], BF16)
    sg_r = Ring(P, "sg", 2, [128, 256], F32)
    hT_r = Ring(P, "hT", 2, [128, 16, 256], BF16)
    ye_r = Ring(P, "ye", 2, [128, 2, 1024], BF16)
    ev = 0
    for e_i in range(16):
        xe, dxe = xe_r.next()
        P.dma("sp", xe[:], xe_all[e_i], reads=[C.dd("xeT_all")], writes=[dxe])
        selT, dselT = selT_r.next()
        for cc in range(2):
            P.op("pool", lambda e, selT=selT, cc=cc, e_i=e_i: e.tensor_scalar(
                out=selT[:, cc, :], in0=iotar[:], scalar1=idxc[:, cc, e_i:e_i + 1], scalar2=gc[:, cc, e_i:e_i + 1],
                op0=ALU.is_equal, op1=ALU.mult), reads=[dcst], writes=[dselT], merge=(cc > 0))
        hT, dhT = hT_r.next()
        wgv = wg_d[e_i].rearrange("(kc p) f -> p kc f", p=128)
        wuv = wu_d[e_i].rearrange("(kc p) f -> p kc f", p=128)
        wdv = wd_d[e_i].rearrange("(fc p) d -> p fc d", p=128)
        for q in range(4):
            wg, dwg = wg_r.next()
            wu, dwu = wu_r.next()
            P.dma("pool", wg[:], wgv[:, :, q * 512:(q + 1) * 512], writes=[dwg])
            P.dma("pool", wu[:], wuv[:, :, q * 512:(q + 1) * 512], writes=[dwu])
            for fcl in range(4):
                fc = q * 4 + fcl
                fs = slice(fcl * 128, (fcl + 1) * 128)
                bg, dbg = C.bank()
                bu, dbu = C.bank()
                mm_acc(P, bg[:, 0:256], [(wg[:, kc, fs], xe[:, kc, :]) for kc in range(8)], reads=[dwg, dxe], dwrite=dbg)
                mm_acc(P, bu[:, 0:256], [(wu[:, kc, fs], xe[:, kc, :]) for kc in range(8)], reads=[dwu, dxe], dwrite=dbu)
                sg, dsg = sg_r.next()
                P.op("act", lambda e, sg=sg, bg=bg: e.activation(out=sg[:], in_=bg[:, 0:256], func=AF.Silu), reads=[dbg], writes=[dsg])
                P.op("dve", lambda e, hT=hT, fc=fc, sg=sg, bu=bu: e.tensor_tensor(out=hT[:, fc, :], in0=sg[:], in1=bu[:, 0:256], op=ALU.mult),
                     reads=[dsg, dbu], writes=[dhT], merge=(fc > 0))
        ye, dye = ye_r.next()
        dbanks = [C.bank() for _ in range(4)]
        for r in range(4):
            wd, dwd = wd_r.next()
            P.dma("pool", wd[:], wdv[:, r * 4:(r + 1) * 4, :], writes=[dwd])
            for ct in range(2):
                for dh in range(2):
                    bk, dbk = dbanks[ct * 2 + dh]
                    for f4 in range(4):
                        fc = r * 4 + f4
                        first = (fc == 0)
                        last = (fc == 15)
                        P.op("pe", lambda e, bk=bk, hT=hT, fc=fc, ct=ct, wd=wd, f4=f4, dh=dh, first=first, last=last: e.matmul(
                            bk[:], lhsT=hT[:, fc, ct * 128:(ct + 1) * 128], rhs=wd[:, f4, dh * 512:(dh + 1) * 512], start=first, stop=last),
                            reads=[dhT, dwd], writes=[dbk], merge=(not first))
        for ct in range(2):
            for dh in range(2):
                bk, dbk = dbanks[ct * 2 + dh]
                P.op("act", lambda e, ye=ye, ct=ct, dh=dh, bk=bk: e.copy(out=ye[:, ct, dh * 512:(dh + 1) * 512], in_=bk[:]),
                     reads=[dbk], writes=[dye], merge=(ct + dh > 0))
        for tt in range(NT):
            for dh in range(2):
                bk, dbk = C.bank()
                ds = slice(dh * 512, (dh + 1) * 512)
                mm_acc(P, bk[:], [(selT[:, ct, tt * 128:(tt + 1) * 128], ye[:, ct, ds]) for ct in range(2)], reads=[dselT, dye], dwrite=dbk)
                if e_i == 0:
                    P.op("dve", lambda e, tt=tt, ds=ds, bk=bk: e.tensor_copy(out=f_acc[:, tt, ds], in_=bk[:]), reads=[dbk], writes=[dfa[tt]], merge=(dh > 0))
                else:
                    P.op("dve", lambda e, tt=tt, ds=ds, bk=bk: e.tensor_tensor(out=f_acc[:, tt, ds], in0=f_acc[:, tt, ds], in1=bk[:], op=ALU.add),
                         reads=[dbk, dfa[tt]], writes=[dfa[tt]])
    gb = P.sb("gb2", [128, 1024], F32)
    bb = P.sb("bb2", [128, 1024], F32)
    ident = P.sb("ident2", [128, 128], F32)
    dgb = Dep()
    P.dma("sp", gb[:], g_d, writes=[dgb])
    P.dma("sp", bb[:], b_d, writes=[dgb], merge=True)
    P.dma("sp", ident[:], ident_d, writes=[dgb], merge=True)
    lb = LNBufs(P, "ln2")
    xr = Ring(P, "m2x", 2, [128, 1024], F32)
    yr = Ring(P, "m2y", 2, [128, 1024], F32)
    yT_r = Ring(P, "m2yT", 2, [128, 8, 128], BF16)
    for tt in range(NT):
        xt, dxt = xr.next()
        P.dma("sp", xt[:], x1[tt * 128:(tt + 1) * 128, :], reads=[C.dd(x_name)], writes=[dxt])
        P.op("dve", lambda e, xt=xt, tt=tt: e.scalar_tensor_tensor(out=xt[:], in0=xt[:], scalar=ALPHA, in1=f_acc[:, tt, :], op0=ALU.mult, op1=ALU.add),
             reads=[dxt, dfa[tt]], writes=[dxt])
        y, dy = yr.next()
        layer_norm_tile(P, lb, xt, dxt, gb, bb, dgb, y, dy)
        P.dma("pool", out[tt * 128:(tt + 1) * 128, :], y[:], reads=[dy], writes=[C.dd(out_name)], merge=True)
        transpose_tile_to_dram(C, y, dy, ident, dgb, yT_r, outT, out_name + "T", tt)


def transpose_tile_to_dram(C, y, dy, ident, dident, yT_r, outT, outT_name, tt):
    P = C.P
    yT, dyT = yT_r.next()
    for hb in range(2):
        bk, dbk = C.bank()
        for k4 in range(4):
            kc = hb * 4 + k4
            P.op("pe", lambda e, bk=bk, k4=k4, kc=kc: e.transpose(bk[:, k4 * 128:(k4 + 1) * 128], y[:, kc * 128:(kc + 1) * 128], ident[:]),
                 reads=[dy, dident], writes=[dbk], merge=(k4 > 0))
        P.op("act", lambda e, hb=hb, bk=bk: e.copy(out=yT[:, hb * 4:(hb + 1) * 4, :], in_=bk[:].rearrange("p (k t) -> p k t", k=4)),
             reads=[dbk], writes=[dyT], merge=(hb > 0))
    P.dma("act", outT.rearrange("(kc p) t -> p kc t", p=128)[:, :, tt * 128:(tt + 1) * 128], yT[:], reads=[dyT], writes=[C.dd(outT_name)], merge=True)


def stage_ple(C, li, x_name, out_name, want_T):
    P = C.P
    P.begin_stage()
    x2 = C.D(x_name, [2048, 1024], F32)
    x2T = C.D(x_name + "T", [1024, 2048], BF16)
    pT_d = C.D(f"pT{li}", [256, 2048], F32)
    wg_d = C.D(f"ple_gate{li}", [1024, 1024], F32)
    wp_d = C.D(f"ple_proj{li}", [256, 1024], F32)
    ident_d = C.D("ident", [128, 128], F32)
    out = C.D(out_name, [2048, 1024], F32)
    xs = P.sb("plx", [128, 8, 2048], BF16)
    dxs = [Dep() for _ in range(8)]
    for kc in range(8):
        P.dma("sp", xs[:, kc, :], x2T[kc * 128:(kc + 1) * 128, :], reads=[C.dd(x_name + "T")], writes=[dxs[kc]])
    ps_, dps_ = load_fm_bf16(C, "plp", pT_d, 2, 2048)
    wg, dwg = load_fm_bf16(C, "plwg", wg_d, 8, 1024)
    wp, dwp = load_fm_bf16(C, "plwp", wp_d, 2, 1024)
    ident = P.sb("ident3", [128, 128], F32)
    dident = Dep()
    P.dma("sp", ident[:], ident_d, writes=[dident])
    if want_T:
        outT = C.D(out_name + "T", [1024, 2048], BF16)
        yT_r = Ring(P, "plyT", 2, [128, 8, 128], BF16)
    xr = Ring(P, "plxr", 2, [128, 1024], F32)
    gr = Ring(P, "plg", 2, [128, 1024], F32)
    yr = Ring(P, "ply", 2, [128, 1024], F32)
    for tt in range(NT):
        ts_ = slice(tt * 128, (tt + 1) * 128)
        xt, dxt = xr.next()
        P.dma("sp", xt[:], x2[ts_, :], reads=[C.dd(x_name)], writes=[dxt])
        gt, dgt = gr.next()
        y, dy = yr.next()
        for half in range(2):
            hs = slice(half * 512, (half + 1) * 512)
            bk, dbk = C.bank()
            mm_acc(P, bk[:], [(xs[:, kc, ts_], wg[:, kc, hs]) for kc in range(8)], reads=dxs + dwg, dwrite=dbk)
            P.op("act", lambda e, gt=gt, hs=hs, bk=bk: e.activation(out=gt[:, hs], in_=bk[:], func=AF.Sigmoid), reads=[dbk], writes=[dgt], merge=(half > 0))
            bk2, dbk2 = C.bank()
            mm_acc(P, bk2[:], [(ps_[:, kc, ts_], wp[:, kc, hs]) for kc in range(2)], reads=dps_ + dwp, dwrite=dbk2)
            P.op("dve", lambda e, gt=gt, hs=hs, bk2=bk2: e.tensor_tensor(out=gt[:, hs], in0=gt[:, hs], in1=bk2[:], op=ALU.mult),
                 reads=[dgt, dbk2], writes=[dgt])
        P.op("pool", lambda e, y=y, xt=xt, gt=gt: e.tensor_tensor(out=y[:], in0=xt[:], in1=gt[:], op=ALU.add), reads=[dxt, dgt], writes=[dy])
        P.dma("pool", out[ts_, :], y[:], reads=[dy], writes=[C.dd(out_name)], merge=True)
        if want_T:
            transpose_tile_to_dram(C, y, dy, ident, dident, yT_r, outT, out_name + "T", tt)


def hyena_consts():
    L, N = 2048, 4096
    R = np.arange(N)
    f = np.where(R <= 2048, R, R - 2048).astype(np.int64)
    is_im = R > 2048
    t = np.arange(L, dtype=np.int64)
    k = (t[:, None] * f[None, :]) % N
    ang = 2.0 * np.pi * k.astype(np.float64) / N
    Wf = np.where(is_im[None, :], -np.sin(ang), np.cos(ang))
    cR = np.full(N, 2.0 / N)
    cR[0] = 1.0 / N
    cR[2048] = 1.0 / N
    WfT = np.ascontiguousarray(Wf.T)
    Wf_d = Wf.reshape(16, 128, 32, 128).transpose(2, 1, 0, 3)
    WA_d = WfT.reshape(32, 128, 16, 128).transpose(2, 1, 0, 3)
    WB_d = WfT.reshape(2, 16, 128, 4, 512).transpose(3, 0, 2, 1, 4)
    tl = np.linspace(0.0, 1.0, L, dtype=np.float32)[:, None]
    w = (2.0 * np.float32(math.pi) * np.arange(L, dtype=np.float32)[:, None] / np.float32(L)).astype(np.float32)
    fb = np.linspace(1e-4, 15, 16, dtype=np.float32)[None, :]
    z = np.concatenate([tl, np.cos(fb * w), -np.sin(fb * w)], axis=-1).astype(np.float32)
    min_decay = math.log(1e-2) / 1.5
    max_decay = math.log(1e-2) / 0.3
    deltas = np.abs(np.linspace(min_decay, max_decay, 512, dtype=np.float32))
    decay = np.exp(-tl * deltas[None, :]).astype(np.float32)
    return {
        "hy_Wf": np.ascontiguousarray(Wf_d).astype(NPBF),
        "hy_WA": np.ascontiguousarray(WA_d).astype(NPBF), "hy_WB": np.ascontiguousarray(WB_d).astype(NPBF),
        "hy_cR": np.ascontiguousarray(cR.reshape(32, 128).T).astype(np.float32),
        "hy_zT": np.ascontiguousarray(z.T), "hy_decay": np.ascontiguousarray(decay.reshape(16, 128, 512)),
    }


TWO_PI = 2.0 * math.pi


def stage_hy_filter(C):
    P = C.P
    P.begin_stage()
    zT_d = C.D("hy_zT", [33, 2048], F32)
    w1_d = C.D("hy_f_w1", [33, 64], F32)
    w2_d = C.D("hy_f_w2", [64, 64], F32)
    w3_d = C.D("hy_f_w3", [64, 2048], F32)
    cols_d = C.D("hy_cols", [64, 3], F32)
    dec_d = C.D("hy_decay", [16, 128, 512], F32)
    Wf_d = C.D("hy_Wf", [32, 128, 16, 128], BF16)
    cR_d = C.D("hy_cR", [128, 32], F32)
    skip_d = C.D("hy_skipb", [2, 128, 512], F32)
    Kf_d = C.D("hy_Kf", [32, 2, 128, 512], F32)
    dc = Dep()
    zT = P.sb("zT", [33, 2048], F32)
    w1 = P.sb("w1", [33, 64], F32)
    w2 = P.sb("w2", [64, 64], F32)
    w3 = P.sb("w3", [64, 2048], BF16)
    cols = P.sb("cols", [64, 8], F32)
    cR = P.sb("cR", [128, 32], F32)
    skipb = P.sb("skipb", [128, 2, 512], F32)
    P.dma("sp", zT[:], zT_d, writes=[dc])
    P.dma("sp", w1[:], w1_d, writes=[dc], merge=True)
    P.dma("sp", w2[:], w2_d, writes=[dc], merge=True)
    P.dma("pool", w3[:], w3_d, writes=[dc], merge=True)
    P.dma("sp", cols[:, 0:3], cols_d, writes=[dc], merge=True)
    P.dma("sp", cR[:], cR_d, writes=[dc], merge=True)
    for o in range(2):
        P.dma("sp", skipb[:, o, :], skip_d[o], writes=[dc], merge=True)
    dcol = Dep()
    P.op("dve", lambda e: e.tensor_tensor(out=cols[:, 3:4], in0=cols[:, 0:1], in1=cols[:, 1:2], op=ALU.mult), reads=[dc], writes=[dcol])
    P.op("dve", lambda e: e.tensor_tensor(out=cols[:, 4:5], in0=cols[:, 2:3], in1=cols[:, 1:2], op=ALU.mult), reads=[dc], writes=[dcol], merge=True)
    P.op("pool", lambda e: e.memset(cols[:, 5:6], -math.pi), reads=[dc], writes=[dcol], merge=True)
    h1T = P.sb("h1T", [64, 2048], F32)
    h2T = P.sb("h2T", [64, 2048], BF16)
    dh1 = Dep()
    dh2 = Dep()
    u_r = Ring(P, "hyu", 2, [64, 512], F32)
    s_r = Ring(P, "hys", 4, [64, 512], F32)
    for layer in range(2):
        for nt in range(4):
            ns = slice(nt * 512, (nt + 1) * 512)
            bk, dbk = C.bank()
            if layer == 0:
                P.op("pe", lambda e, bk=bk, ns=ns: e.matmul(bk[0:64, :], lhsT=w1[:], rhs=zT[:, ns], start=True, stop=True), reads=[dc], writes=[dbk])
            else:
                P.op("pe", lambda e, bk=bk, ns=ns: e.matmul(bk[0:64, :], lhsT=w2[:], rhs=h1T[:, ns], start=True, stop=True), reads=[dc, dh1], writes=[dbk])
            u, du = u_r.next()
            fbc = 3 + layer
            P.op("dve", lambda e, u=u, bk=bk, fbc=fbc: e.tensor_scalar(out=u[:], in0=bk[0:64, :], scalar1=cols[:, 1:2], scalar2=cols[:, fbc:fbc + 1], op0=ALU.mult, op1=ALU.add),
                 reads=[dbk, dcol, dc], writes=[du])
            s2, ds2 = s_r.next()
            s4, ds4 = s_r.next()
            P.op("act", lambda e, u=u, s2=s2: e.activation(out=s2[:], in_=u[:], func=AF.Sin, scale=0.5), reads=[du], writes=[ds2])
            P.op("act", lambda e, u=u, s4=s4: e.activation(out=s4[:], in_=u[:], func=AF.Sin, scale=0.25), reads=[du], writes=[ds4])
            P.op("dve", lambda e, s4=s4: e.tensor_tensor(out=s4[:], in0=s4[:], in1=s4[:], op=ALU.mult), reads=[ds4], writes=[ds4])
            P.op("dve", lambda e, s4=s4: e.tensor_scalar(out=s4[:], in0=s4[:], scalar1=-2.0, scalar2=1.0, op0=ALU.mult, op1=ALU.add), reads=[ds4], writes=[ds4])
            if layer == 0:
                P.op("dve", lambda e, s2=s2, s4=s4, ns=ns: e.scalar_tensor_tensor(out=h1T[:, ns], in0=s2[:], scalar=2.0, in1=s4[:], op0=ALU.mult, op1=ALU.mult),
                     reads=[ds2, ds4], writes=[dh1], merge=(nt > 0))
            else:
                P.op("dve", lambda e, s2=s2, s4=s4, ns=ns: e.scalar_tensor_tensor(out=h2T[:, ns], in0=s2[:], scalar=2.0, in1=s4[:], op0=ALU.mult, op1=ALU.mult),
                     reads=[ds2, ds4], writes=[dh2], merge=(nt > 0))
    Kt = P.sb("Ksd", [128, 16, 4, 512], BF16)
    dKt = [Dep() for _ in range(16)]
    ones = P.sb("onesb", [128, 128], BF16)
    dones = Dep()
    P.op("pool", lambda e: e.memset(ones[:], 1.0), writes=[dones])
    dec_r = Ring(P, "dec", 2, [128, 512], F32)
    sq_r = Ring(P, "sq", 3, [128, 512], BF16)
    kf32_r = Ring(P, "kf32", 2, [128, 512], F32)
    kb32_r = Ring(P, "kb32", 2, [128, 512], F32)
    ssq = [C.bank(hold=True), C.bank(hold=True)]
    for tt in range(16):
        dec, ddec = dec_r.next()
        P.dma("sp", dec[:], dec_d[tt], writes=[ddec])
        for o in range(2):
            bf_, dbf_ = C.bank()
            bb_, dbb_ = C.bank()
            for (bk, dbk, q) in ((bf_, dbf_, 2 * o), (bb_, dbb_, 2 * o + 1)):
                P.op("pe", lambda e, bk=bk, tt=tt, q=q: e.matmul(bk[:], lhsT=h2T[:, tt * 128:(tt + 1) * 128], rhs=w3[:, q * 512:(q + 1) * 512], start=True, stop=True),
                     reads=[dh2, dc], writes=[dbk])
            kf32, dkf32 = kf32_r.next()
            kb32, dkb32 = kb32_r.next()
            P.op("dve", lambda e, kf32=kf32, bf_=bf_, dec=dec: e.tensor_tensor(out=kf32[:], in0=bf_[:], in1=dec[:], op=ALU.mult), reads=[dbf_, ddec], writes=[dkf32])
            P.op("dve", lambda e, kb32=kb32, bb_=bb_, dec=dec: e.tensor_tensor(out=kb32[:], in0=bb_[:], in1=dec[:], op=ALU.mult), reads=[dbb_, ddec], writes=[dkb32])
            if tt == 0:
                P.op("pool", lambda e, kb32=kb32: e.memset(kb32[0:1, :], 0.0), reads=[dkb32], writes=[dkb32])
            P.op("pool", lambda e, tt=tt, o=o, kf32=kf32, kb32=kb32: e.tensor_tensor(out=Kt[:, tt, 2 * o, :], in0=kf32[:], in1=kb32[:], op=ALU.add),
                 reads=[dkf32, dkb32], writes=[dKt[tt]], merge=True)
            P.op("dve", lambda e, tt=tt, o=o, kf32=kf32, kb32=kb32: e.tensor_tensor(out=Kt[:, tt, 2 * o + 1, :], in0=kf32[:], in1=kb32[:], op=ALU.subtract),
                 reads=[dkf32, dkb32], writes=[dKt[tt]], merge=True)
        for q in range(4):
            sq, dsq = sq_r.next()
            P.op("act", lambda e, sq=sq, tt=tt, q=q: e.activation(out=sq[:], in_=Kt[:, tt, q, :], func=AF.Square), reads=[dKt[tt]], writes=[dsq])
            sb_, dsb_ = ssq[q // 2]
            first = (tt == 0 and q % 2 == 0)
            last = (tt == 15 and q % 2 == 1)
            P.op("pe", lambda e, sb_=sb_, sq=sq, first=first, last=last: e.matmul(sb_[:], lhsT=ones[:], rhs=sq[:], start=first, stop=last),
                 reads=[dsq, dones], writes=[dsb_], merge=(not first))
    rs = P.sb("hyrs", [128, 2, 512], F32)
    drs = Dep()
    for o in range(2):
        sb_, dsb_ = ssq[o]
        P.op("dve", lambda e, o=o, sb_=sb_: e.tensor_scalar(out=rs[:, o, :], in0=sb_[:], scalar1=0.5, scalar2=1e-12, op0=ALU.mult, op1=ALU.add), reads=[dsb_], writes=[drs], merge=(o > 0))
    C.release_all()
    P.op("act", lambda e: e.activation(out=rs[:], in_=rs[:], func=AF.Sqrt), reads=[drs], writes=[drs])
    P.op("dve", lambda e: e.reciprocal(out=rs[:], in_=rs[:]), reads=[drs], writes=[drs])
    wf_r = Ring(P, "wf", 2, [128, 16, 128], BF16)
    kf_r = Ring(P, "kf", 3, [128, 512], F32)
    for ft in range(32):
        wf, dwf = wf_r.next()
        P.dma("sp", wf[:], Wf_d[ft], writes=[dwf])
        for o in range(2):
            bk, dbk = C.bank()
            sel = 2 * o if ft < 16 else 2 * o + 1
            mm_acc(P, bk[:], [(wf[:, tt, :], Kt[:, tt, sel, :]) for tt in range(16)], reads=[dwf] + dKt, dwrite=dbk)
            kf, dkf = kf_r.next()
            P.op("dve", lambda e, kf=kf, bk=bk, o=o: e.tensor_tensor(out=kf[:], in0=bk[:], in1=rs[:, o, :], op=ALU.mult), reads=[dbk, drs], writes=[dkf])
            if ft < 16:
                P.op("dve", lambda e, kf=kf, o=o: e.tensor_tensor(out=kf[:], in0=kf[:], in1=skipb[:, o, :], op=ALU.add), reads=[dkf, dc], writes=[dkf])
            elif ft == 16:
                bn, dbn = C.bank()
                mm_acc(P, bn[0:32, :], [(wf[:, tt, 0:32], Kt[:, tt, 2 * o, :]) for tt in range(16)], reads=[dwf] + dKt, dwrite=dbn)
                P.op("dve", lambda e, kf=kf, bn=bn, o=o: e.tensor_tensor(out=kf[0:1, :], in0=bn[0:1, :], in1=rs[0:1, o, :], op=ALU.mult), reads=[dbn, drs, dkf], writes=[dkf])
                P.op("pool", lambda e, kf=kf, o=o: e.tensor_tensor(out=kf[0:1, :], in0=kf[0:1, :], in1=skipb[0:1, o, :], op=ALU.add), reads=[dkf, dc], writes=[dkf])
            P.op("act", lambda e, kf=kf, ft=ft: e.activation(out=kf[:], in_=kf[:], func=AF.Copy, scale=cR[:, ft:ft + 1]), reads=[dkf, dc], writes=[dkf])
            P.dma("act", Kf_d[ft, o], kf[:], reads=[dkf], writes=[C.dd("hy_Kf")], merge=True)


def stage_hy_prep(C):
    P = C.P
    P.begin_stage()
    hbT = C.D("hbT", [1536, 2048], F32)
    cw_d = C.D("hy_cw", [128, 12, 3], F32)
    cb_d = C.D("hy_cb", [128, 12], F32)
    ident_d = C.D("ident", [128, 128], F32)
    hv = C.D("hv_tm", [2048, 512], BF16)
    hx1 = C.D("hx1_tm", [2048, 512], F32)
    hx2T = C.D("hx2T", [512, 2048], F32)
    dc = Dep()
    cw = P.sb("cw", [128, 12, 3], F32)
    cb = P.sb("cb", [128, 12], F32)
    ident = P.sb("identh", [128, 128], F32)
    P.dma("sp", cw[:], cw_d, writes=[dc])
    P.dma("sp", cb[:], cb_d, writes=[dc], merge=True)
    P.dma("sp", ident[:], ident_d, writes=[dc], merge=True)
    xin_r = Ring(P, "hxin", 2, [128, 2048], F32)
    y_r = Ring(P, "hy", 2, [128, 2048], F32)
    sv_r = Ring(P, "hsv", 2, [128, 16, 128], BF16)
    sx_r = Ring(P, "hsx", 2, [128, 16, 128], F32)
    for ch in range(12):
        xin, dxin = xin_r.next()
        P.dma("sp", xin[:], hbT[ch * 128:(ch + 1) * 128, :], reads=[C.dd("hbT")], writes=[dxin])
        y, dy = y_r.next()
        P.op("act", lambda e, y=y, xin=xin, ch=ch: e.activation(out=y[:], in_=xin[:], func=AF.Identity, bias=cb[:, ch:ch + 1], scale=cw[:, ch, 1:2]),
             reads=[dxin, dc], writes=[dy])
        P.op("dve", lambda e, y=y, xin=xin, ch=ch: e.scalar_tensor_tensor(out=y[:, 1:2048], in0=xin[:, 0:2047], scalar=cw[:, ch, 0:1], in1=y[:, 1:2048], op0=ALU.mult, op1=ALU.add),
             reads=[dxin, dc, dy], writes=[dy])
        P.op("dve", lambda e, y=y, xin=xin, ch=ch: e.scalar_tensor_tensor(out=y[:, 0:2047], in0=xin[:, 1:2048], scalar=cw[:, ch, 2:3], in1=y[:, 0:2047], op0=ALU.mult, op1=ALU.add),
             reads=[dxin, dc, dy], writes=[dy])
        if ch >= 8:
            P.dma("act", hx2T[(ch - 8) * 128:(ch - 7) * 128, :], y[:], reads=[dy], writes=[C.dd("hx2T")], merge=True)
            continue
        stg, dstg = (sv_r if ch < 4 else sx_r).next()
        for g in range(4):
            bk, dbk = C.bank()
            for k4 in range(4):
                tt = g * 4 + k4
                P.op("pe", lambda e, bk=bk, k4=k4, tt=tt, y=y: e.transpose(bk[:, k4 * 128:(k4 + 1) * 128], y[:, tt * 128:(tt + 1) * 128], ident[:]),
                     reads=[dy, dc], writes=[dbk], merge=(k4 > 0))
            P.op("act", lambda e, stg=stg, g=g, bk=bk: e.copy(out=stg[:, g * 4:(g + 1) * 4, :], in_=bk[:].rearrange("p (k t) -> p k t", k=4)),
                 reads=[dbk], writes=[dstg], merge=(g > 0))
        if ch < 4:
            P.dma("act", hv.rearrange("(tt p) c -> p tt c", p=128)[:, :, ch * 128:(ch + 1) * 128], stg[:], reads=[dstg], writes=[C.dd("hv_tm")], merge=True)
        else:
            P.dma("act", hx1.rearrange("(tt p) c -> p tt c", p=128)[:, :, (ch - 4) * 128:(ch - 3) * 128], stg[:], reads=[dstg], writes=[C.dd("hx1_tm")], merge=True)


def stage_hy_conv(C):
    P = C.P
    P.begin_stage()
    hv = C.D("hv_tm", [2048, 512], BF16)
    hx1 = C.D("hx1_tm", [2048, 512], F32)
    hx2T = C.D("hx2T", [512, 2048], F32)
    Kf_d = C.D("hy_Kf", [32, 2, 128, 512], F32)
    Wf_d = C.D("hy_Wf", [32, 128, 16, 128], BF16)
    WA_d = C.D("hy_WA", [16, 128, 32, 128], BF16)
    WB_d = C.D("hy_WB", [4, 2, 128, 16, 512], BF16)
    mixT = C.D("mixT", [1024, 2048], BF16)
    ztm = P.sb("ztm", [128, 16, 512], BF16)
    dz = [Dep() for _ in range(16)]
    hvv = hv.rearrange("(tt p) c -> p tt c", p=128)
    for tt in range(16):
        P.dma("sp", ztm[:, tt, :], hvv[:, tt, :], reads=[C.dd("hv_tm")], writes=[dz[tt]])
    Yt = P.sb("Yt", [128, 32, 512], BF16)
    dY = [Dep() for _ in range(32)]
    wf_r = Ring(P, "cwf", 3, [128, 16, 128], BF16)
    kf_r = Ring(P, "ckf", 4, [128, 512], F32)
    t_r = Ring(P, "ct", 4, [128, 512], F32)
    wa_r = Ring(P, "cwa", 2, [128, 32, 128], BF16)
    wb_r = Ring(P, "cwb", 2, [128, 16, 512], BF16)
    x_r = Ring(P, "cx", 3, [128, 512], F32)
    zo_r = Ring(P, "czo", 3, [128, 512], BF16)
    for o in range(2):
        for j in range(16):
            ub = []
            for part in range(2):
                ft = part * 16 + j
                wf, dwf = wf_r.next()
                P.dma("sp", wf[:], Wf_d[ft], writes=[dwf])
                bk, dbk = C.bank()
                mm_acc(P, bk[:], [(wf[:, tt, :], ztm[:, tt, :]) for tt in range(16)], reads=[dwf] + dz, dwrite=dbk)
                ub.append((bk, dbk))
            kre, dkre = kf_r.next()
            kim, dkim = kf_r.next()
            P.dma("sp", kre[:], Kf_d[j, o], reads=[C.dd("hy_Kf")], writes=[dkre])
            P.dma("sp", kim[:], Kf_d[16 + j, o], reads=[C.dd("hy_Kf")], writes=[dkim])
            (ure, dure), (uim, duim) = ub
            t1, dt1 = t_r.next()
            t2, dt2 = t_r.next()
            P.op("dve", lambda e, t1=t1, ure=ure, kre=kre: e.tensor_tensor(out=t1[:], in0=ure[:], in1=kre[:], op=ALU.mult), reads=[dure, dkre], writes=[dt1])
            P.op("dve", lambda e, t2=t2, uim=uim, kim=kim: e.tensor_tensor(out=t2[:], in0=uim[:], in1=kim[:], op=ALU.mult), reads=[duim, dkim], writes=[dt2])
            P.op("pool", lambda e, j=j, t1=t1, t2=t2: e.tensor_tensor(out=Yt[:, j, :], in0=t1[:], in1=t2[:], op=ALU.subtract), reads=[dt1, dt2], writes=[dY[j]])
            if j == 0:
                P.op("pool", lambda e, t1=t1: e.tensor_copy(out=Yt[0:1, 0, :], in_=t1[0:1, :]), reads=[dt1, dY[0]], writes=[dY[0]])
            t3, dt3 = t_r.next()
            t4, dt4 = t_r.next()
            P.op("dve", lambda e, t3=t3, ure=ure, kim=kim: e.tensor_tensor(out=t3[:], in0=ure[:], in1=kim[:], op=ALU.mult), reads=[dure, dkim], writes=[dt3])
            P.op("dve", lambda e, t4=t4, uim=uim, kre=kre: e.tensor_tensor(out=t4[:], in0=uim[:], in1=kre[:], op=ALU.mult), reads=[duim, dkre], writes=[dt4])
            P.op("pool", lambda e, j=j, t3=t3, t4=t4: e.tensor_tensor(out=Yt[:, 16 + j, :], in0=t3[:], in1=t4[:], op=ALU.add), reads=[dt3, dt4], writes=[dY[16 + j]])
            if j == 0:
                P.op("pool", lambda e, t2=t2: e.tensor_copy(out=Yt[0:1, 16, :], in_=t2[0:1, :]), reads=[dt2, dY[16]], writes=[dY[16]])
        if o == 0:
            for tt in range(16):
                wa, dwa = wa_r.next()
                P.dma("sp", wa[:], WA_d[tt], writes=[dwa])
                bk, dbk = C.bank()
                mm_acc(P, bk[:], [(wa[:, kt, :], Yt[:, kt, :]) for kt in range(32)], reads=[dwa] + dY, dwrite=dbk)
                xt, dxt = x_r.next()
                P.dma("sp", xt[:], hx1[tt * 128:(tt + 1) * 128, :], reads=[C.dd("hx1_tm")], writes=[dxt])
                P.op("dve", lambda e, tt=tt, bk=bk, xt=xt: e.tensor_tensor(out=ztm[:, tt, :], in0=bk[:], in1=xt[:], op=ALU.mult), reads=[dbk, dxt], writes=[dz[tt]])
        else:
            for nt in range(4):
                banks = [C.bank() for _ in range(4)]
                for hf in range(2):
                    wb, dwb = wb_r.next()
                    P.dma("sp", wb[:], WB_d[nt, hf], writes=[dwb])
                    for cc in range(4):
                        bk, dbk = banks[cc]
                        for k in range(16):
                            first = (hf == 0 and k == 0)
                            last = (hf == 1 and k == 15)
                            kt = hf * 16 + k
                            P.op("pe", lambda e, bk=bk, kt=kt, cc=cc, wb=wb, k=k, first=first, last=last: e.matmul(
                                bk[:], lhsT=Yt[:, kt, cc * 128:(cc + 1) * 128], rhs=wb[:, k, :], start=first, stop=last),
                                reads=[dY[kt], dwb], writes=[dbk], merge=(not first))
                for cc in range(4):
                    bk, dbk = banks[cc]
                    xt, dxt = x_r.next()
                    P.dma("sp", xt[:], hx2T[cc * 128:(cc + 1) * 128, nt * 512:(nt + 1) * 512], reads=[C.dd("hx2T")], writes=[dxt])
                    zo, dzo = zo_r.next()
                    P.op("dve", lambda e, zo=zo, bk=bk, xt=xt: e.tensor_tensor(out=zo[:], in0=bk[:], in1=xt[:], op=ALU.mult), reads=[dbk, dxt], writes=[dzo])
                    P.dma("act", mixT[512 + cc * 128:512 + (cc + 1) * 128, nt * 512:(nt + 1) * 512], zo[:], reads=[dzo], writes=[C.dd("mixT")], merge=True)


MLA_SCALE = 96.0 ** -0.5


def mla_consts():
    inv = 1.0 / (10000.0 ** (np.arange(0, 32, 2, dtype=np.float32) / 32.0))
    ang = np.arange(2048, dtype=np.float32)[:, None] * inv[None, :].astype(np.float32)
    cos = np.cos(ang).astype(np.float32).T
    sin = np.sin(ang).astype(np.float32).T
    cos2 = np.concatenate([cos, cos], axis=0)
    sin2 = np.concatenate([-sin, sin], axis=0)
    return {"mla_cs2": np.ascontiguousarray(np.stack([cos2, sin2], axis=1)).astype(np.float32)}


def stage_mla1(C, xT_name):
    P = C.P
    P.begin_stage()
    xT = C.D(xT_name, [1024, 2048], BF16)
    wi_d = C.D("mla_w_in", [1024, 672], F32)
    wsw_d = C.D("mla_w_in_sw", [1024, 96], F32)
    gc_d = C.D("mla_gcols", [128, 5], F32)
    cs_d = C.D("mla_cs2", [32, 2, 2048], F32)
    nT_d = C.D("mla_nT", [640, 2048], BF16)
    kr_d = C.D("mla_krT", [32, 2048], BF16)
    xs = P.sb("mxs", [128, 8, 2048], BF16)
    dxs = [Dep() for _ in range(8)]
    for kc in range(8):
        P.dma("sp", xs[:, kc, :], xT[kc * 128:(kc + 1) * 128, :], reads=[C.dd(xT_name)], writes=[dxs[kc]])
    wi, dwi = load_fm_bf16(C, "mwi", wi_d, 8, 672)
    wsw, dwsw = load_fm_bf16(C, "mwsw", wsw_d, 8, 96)
    dc = Dep()
    gcol = P.sb("mgc", [128, 5], F32)
    P.dma("sp", gcol[:], gc_d, writes=[dc])
    cs = P.sb("mcs", [96, 2, 2048], F32)
    P.dma("sp", cs[64:96, :, :], cs_d, writes=[dc], merge=True)
    ones = P.sb("mones", [128, 128], BF16)
    P.op("pool", lambda e: e.memset(ones[:], 1.0), writes=[dc], merge=True)
    hT = P.sb("mhT", [128, 5, 2048], F32)
    nT = P.sb("mnT", [128, 5, 2048], BF16)
    dhT = Dep()
    dnT = [Dep() for _ in range(5)]
    sq_r = Ring(P, "msq", 3, [128, 512], BF16)
    r_r = Ring(P, "mr", 2, [128, 512], F32)
    for (chunks, n) in (((0, 1, 2), 384.0), ((3, 4), 256.0)):
        for nt in range(4):
            ns = slice(nt * 512, (nt + 1) * 512)
            sbk, dsbk = C.bank(hold=True)
            for ci, c in enumerate(chunks):
                bk, dbk = C.bank()
                mm_acc(P, bk[:], [(wi[:, kc, c * 128:(c + 1) * 128], xs[:, kc, ns]) for kc in range(8)], reads=dxs + dwi, dwrite=dbk)
                P.op("act", lambda e, c=c, ns=ns, bk=bk: e.copy(out=hT[:, c, ns], in_=bk[:]), reads=[dbk], writes=[dhT], merge=True)
                sq, dsq = sq_r.next()
                P.op("act", lambda e, sq=sq, bk=bk: e.activation(out=sq[:], in_=bk[:], func=AF.Square), reads=[dbk], writes=[dsq])
                P.op("pe", lambda e, sbk=sbk, sq=sq, ci=ci, chunks=chunks: e.matmul(sbk[:], lhsT=ones[:], rhs=sq[:], start=(ci == 0), stop=(ci == len(chunks) - 1)),
                     reads=[dsq, dc], writes=[dsbk], merge=(ci > 0))
            r, dr = r_r.next()
            P.op("dve", lambda e, r=r, sbk=sbk, n=n: e.tensor_scalar(out=r[:], in0=sbk[:], scalar1=1.0 / n, scalar2=EPS, op0=ALU.mult, op1=ALU.add), reads=[dsbk], writes=[dr])
            C.release_all()
            P.op("act", lambda e, r=r: e.activation(out=r[:], in_=r[:], func=AF.Sqrt), reads=[dr], writes=[dr])
            P.op("dve", lambda e, r=r: e.reciprocal(out=r[:], in_=r[:]), reads=[dr], writes=[dr])
            for c in chunks:
                P.op("dve", lambda e, c=c, ns=ns, r=r: e.scalar_tensor_tensor(out=nT[:, c, ns], in0=hT[:, c, ns], scalar=gcol[:, c:c + 1], in1=r[:], op0=ALU.mult, op1=ALU.mult),
                     reads=[dhT, dr, dc], writes=[dnT[c]], merge=True)
    for c in range(5):
        P.dma("act", nT_d[c * 128:(c + 1) * 128, :], nT[:, c, :], reads=[dnT[c]], writes=[C.dd("mla_nT")], merge=True)
    krT = P.sb("mkr", [96, 2048], BF16)
    dkr = Dep()
    ta_r = Ring(P, "mta", 2, [96, 512], F32)
    tb_r = Ring(P, "mtb", 2, [96, 512], F32)
    for nt in range(4):
        ns = slice(nt * 512, (nt + 1) * 512)
        bk, dbk = C.bank()
        bs, dbs = C.bank()
        mm_acc(P, bk[0:96, :], [(wi[:, kc, 576:672], xs[:, kc, ns]) for kc in range(8)], reads=dxs + dwi, dwrite=dbk)
        mm_acc(P, bs[0:96, :], [(wsw[:, kc, :], xs[:, kc, ns]) for kc in range(8)], reads=dxs + dwsw, dwrite=dbs)
        ta, dta = ta_r.next()
        tb, dtb = tb_r.next()
        P.op("dve", lambda e, ta=ta, bk=bk, ns=ns: e.tensor_tensor(out=ta[64:96, :], in0=bk[64:96, :], in1=cs[64:96, 0, ns], op=ALU.mult), reads=[dbk, dc], writes=[dta])
        P.op("dve", lambda e, tb=tb, bs=bs, ns=ns: e.tensor_tensor(out=tb[64:96, :], in0=bs[64:96, :], in1=cs[64:96, 1, ns], op=ALU.mult), reads=[dbs, dc], writes=[dtb])
        P.op("pool", lambda e, ta=ta, tb=tb, ns=ns: e.tensor_tensor(out=krT[64:96, ns], in0=ta[64:96, :], in1=tb[64:96, :], op=ALU.add), reads=[dta, dtb], writes=[dkr], merge=(nt > 0))
    P.dma("pool", kr_d, krT[64:96, :], reads=[dkr], writes=[C.dd("mla_krT")])


def stage_mla2(C):
    P = C.P
    P.begin_stage()
    nT_d = C.D("mla_nT", [640, 2048], BF16)
    kr_d = C.D("mla_krT", [32, 2048], BF16)
    cs_d = C.D("mla_cs2", [32, 2, 2048], F32)
    wq_d = C.D("mla_w_q_up", [384, 1536], F32)
    wqs_d = C.D("mla_w_q_sw", [384, 1536], F32)
    wk_d = C.D("mla_w_kv_k", [256, 1024], F32)
    wv_d = C.D("mla_w_kv_v", [256, 1024], F32)
    mixT = C.D("mixT", [1024, 2048], BF16)
    nT = P.sb("anT", [128, 5, 2048], BF16)
    dnT = [Dep() for _ in range(5)]
    for c in range(5):
        P.dma("sp", nT[:, c, :], nT_d[c * 128:(c + 1) * 128, :], reads=[C.dd("mla_nT")], writes=[dnT[c]])
    dq = dnT[0:3]
    dkv = dnT[3:5]
    dc = Dep()
    KRT = P.sb("aKRT", [96, 2048], BF16)
    P.dma("sp", KRT[64:96, :], kr_d, reads=[C.dd("mla_krT")], writes=[dc])
    cs = P.sb("acs", [96, 2, 2048], F32)
    P.dma("sp", cs[64:96, :, :], cs_d, writes=[dc], merge=True)
    wq, dwq = load_fm_bf16(C, "awq", wq_d, 3, 1536)
    wqs, dwqs = load_fm_bf16(C, "awqs", wqs_d, 3, 1536)
    wk, dwk = load_fm_bf16(C, "awk", wk_d, 2, 1024)
    wv, dwv = load_fm_bf16(C, "awv", wv_d, 2, 1024)
    onesf = P.sb("aones", [128, 64], F32)
    P.op("pool", lambda e: e.memset(onesf[:], 1.0), writes=[dc], merge=True)
    Vx = P.sb("aVx", [128, 16, 16, 65], BF16)
    dV = Dep()
    P.op("pool", lambda e: e.memset(Vx[:], 1.0), writes=[dV])
    for tt in range(16):
        for half in range(2):
            bk, dbk = C.bank()
            mm_acc(P, bk[:], [(nT[:, 3 + kc, tt * 128:(tt + 1) * 128], wv[:, kc, half * 512:(half + 1) * 512]) for kc in range(2)], reads=dkv + dwv, dwrite=dbk)
            P.op("act", lambda e, tt=tt, half=half, bk=bk: e.copy(out=Vx[:, tt, half * 8:(half + 1) * 8, 0:64], in_=bk[:].rearrange("p (h d) -> p h d", h=8)),
                 reads=[dbk], writes=[dV], merge=True)
    QT_r = Ring(P, "aQT", 2, [96, 2048], BF16)
    KT_r = Ring(P, "aKT", 2, [96, 2048], BF16)
    ta_r = Ring(P, "ata", 2, [96, 512], F32)
    tb_r = Ring(P, "atb", 2, [96, 512], F32)
    p_r = Ring(P, "apT", 4, [128, 512], BF16)
    rd_r = Ring(P, "ard", 2, [65, 512], F32)
    bs_r = Ring(P, "absb", 2, [64, 512], F32)
    yo_r = Ring(P, "ayo", 3, [64, 512], BF16)
    def proj(h):
        QT, dQT = QT_r.next()
        KT, dKT = KT_r.next()
        for nt in range(4):
            ns = slice(nt * 512, (nt + 1) * 512)
            bq, dbq = C.bank()
            bs, dbs = C.bank()
            mm_acc(P, bq[0:96, :], [(wq[:, kc, h * 96:(h + 1) * 96], nT[:, kc, ns]) for kc in range(3)], reads=dq + dwq, dwrite=dbq)
            mm_acc(P, bs[0:96, :], [(wqs[:, kc, h * 96:(h + 1) * 96], nT[:, kc, ns]) for kc in range(3)], reads=dq + dwqs, dwrite=dbs)
            P.op("dve", lambda e, QT=QT, ns=ns, bq=bq: e.tensor_copy(out=QT[0:64, ns], in_=bq[0:64, :]), reads=[dbq], writes=[dQT], merge=(nt > 0))
            ta, dta = ta_r.next()
            tb, dtb = tb_r.next()
            P.op("dve", lambda e, ta=ta, bq=bq, ns=ns: e.tensor_tensor(out=ta[64:96, :], in0=bq[64:96, :], in1=cs[64:96, 0, ns], op=ALU.mult), reads=[dbq, dc], writes=[dta])
            P.op("dve", lambda e, tb=tb, bs=bs, ns=ns: e.tensor_tensor(out=tb[64:96, :], in0=bs[64:96, :], in1=cs[64:96, 1, ns], op=ALU.mult), reads=[dbs, dc], writes=[dtb])
            P.op("pool", lambda e, QT=QT, ta=ta, tb=tb, ns=ns: e.tensor_tensor(out=QT[64:96, ns], in0=ta[64:96, :], in1=tb[64:96, :], op=ALU.add), reads=[dta, dtb], writes=[dQT], merge=True)
            bkk, dbkk = C.bank()
            mm_acc(P, bkk[0:64, :], [(wk[:, kc, h * 64:(h + 1) * 64], nT[:, 3 + kc, ns]) for kc in range(2)], reads=dkv + dwk, dwrite=dbkk)
            P.op("dve", lambda e, KT=KT, ns=ns, bkk=bkk: e.tensor_copy(out=KT[0:64, ns], in_=bkk[0:64, :]), reads=[dbkk], writes=[dKT], merge=(nt > 0))
        P.op("pool", lambda e, KT=KT: e.tensor_copy(out=KT[64:96, :], in_=KRT[64:96, :]), reads=[dc], writes=[dKT], merge=True)
        return QT, dQT, KT, dKT

    pending = []

    def make_epilogue(accb, h, qs):
        acc, dacc = accb

        def epi():
            rd, drd = rd_r.next()
            P.op("dve", lambda e, rd=rd, acc=acc: e.reciprocal(out=rd[64:65, :], in_=acc[64:65, :]), reads=[dacc], writes=[drd])
            bb, dbb = C.bank()
            P.op("pe", lambda e, bb=bb, rd=rd: e.matmul(bb[0:64, :], lhsT=onesf[64:65, 0:64], rhs=rd[64:65, :], start=True, stop=True), reads=[drd, dc], writes=[dbb])
            bsb, dbsb = bs_r.next()
            P.op("dve", lambda e, bsb=bsb, bb=bb: e.tensor_copy(out=bsb[:], in_=bb[0:64, :]), reads=[dbb], writes=[dbsb])
            yo, dyo = yo_r.next()
            P.op("dve", lambda e, yo=yo, acc=acc, bsb=bsb: e.tensor_tensor(out=yo[:], in0=acc[0:64, :], in1=bsb[:], op=ALU.mult), reads=[dacc, dbsb], writes=[dyo])
            C.release(accb)
            P.dma("pool", mixT[h * 64:(h + 1) * 64, qs], yo[:], reads=[dyo], writes=[C.dd("mixT")], merge=True)
        return epi

    def attention(h, QT, dQT, KT, dKT):
        for qc in range(4):
            qs = slice(qc * 512, (qc + 1) * 512)
            accb = C.bank(hold=True)
            acc, dacc = accb

            def pv(kt, pT, dpT, acc=acc, dacc=dacc, h=h):
                P.op("pe", lambda e, acc=acc, kt=kt, h=h, pT=pT: e.matmul(acc[0:65, :], lhsT=Vx[:, kt, h, :], rhs=pT[:], start=(kt == 0), stop=(kt == 15)),
                     reads=[dV, dpT], writes=[dacc], merge=(kt > 0))
            pend = None
            for kt in range(16):
                sb_, dsb_ = C.bank()
                P.op("pe", lambda e, sb_=sb_, KT=KT, QT=QT, kt=kt, qs=qs: e.matmul(sb_[:], lhsT=KT[0:96, kt * 128:(kt + 1) * 128], rhs=QT[0:96, qs], start=True, stop=True),
                     reads=[dKT, dQT], writes=[dsb_])
                pT, dpT = p_r.next()
                P.op("act", lambda e, pT=pT, sb_=sb_: e.activation(out=pT[:], in_=sb_[:], func=AF.Exp, scale=MLA_SCALE), reads=[dsb_], writes=[dpT])
                if pend is not None:
                    pv(*pend)
                pend = (kt, pT, dpT)
                if kt == 3 and pending:
                    pending.pop(0)()
            pv(*pend)
            pending.append(make_epilogue(accb, h, qs))

    nxt = proj(0)
    for h in range(16):
        cur = nxt
        if h + 1 < 16:
            nxt = proj(h + 1)
        attention(h, *cur)
    while pending:
        pending.pop(0)()


def _rep128(v):
    v = np.asarray(v, np.float32)
    return np.ascontiguousarray(np.broadcast_to(v[None, :], (128, v.shape[0])))


def shared_inputs(inp):
    f32 = lambda a: np.ascontiguousarray(np.asarray(a, np.float32))
    s = {}
    s["ab_w_in"] = f32(inp["ab_w_in"][0])
    s["na_tab"] = na_tables(np.asarray(inp["na_rpb"][0], np.float32))
    s.update(hyena_consts())
    s["hy_f_w1"] = f32(inp["hy_f_w1"][0])
    s["hy_f_w2"] = f32(inp["hy_f_w2"][0])
    s["hy_f_w3"] = f32(inp["hy_f_w3"][0])
    s["hy_cols"] = f32(np.stack([inp["hy_f_b1"][0], inp["hy_f_freq"][0], inp["hy_f_b2"][0]], axis=1))
    s["hy_skipb"] = np.stack([_rep128(inp["hy_skip"][0][0]), _rep128(inp["hy_skip"][0][1])])
    s["hy_cw"] = f32(np.asarray(inp["hy_conv_w"][0]).reshape(3, 12, 128).transpose(2, 1, 0))
    s["hy_cb"] = f32(np.asarray(inp["hy_conv_b"][0]).reshape(12, 128).T)
    s["ident"] = np.eye(128, dtype=np.float32)
    esel = np.zeros((16, 16, 128), np.float32)
    for e in range(16):
        esel[e, e, :] = 1.0
    s["esel"] = esel
    s["iota_col"] = (np.arange(16)[None, :] * 128 + np.arange(128)[:, None]).astype(np.float32)
    s["iota_row"] = _rep128(np.arange(2048, dtype=np.float32))
    s["ab_w_out"] = f32(inp["ab_w_out"][0])
    w_in = np.asarray(inp["mla_w_in"][0], np.float32)
    perm = np.concatenate([np.arange(16, 32), np.arange(0, 16)])
    s["mla_w_in"] = f32(w_in)
    s["mla_w_in_sw"] = f32(np.concatenate([w_in[:, 576:640], w_in[:, 640 + perm]], axis=1))
    wq = np.asarray(inp["mla_w_q_up"][0], np.float32)
    wqs = wq.reshape(384, 16, 96).copy()
    wqs[:, :, 64:] = wqs[:, :, 64 + perm]
    s["mla_w_q_up"] = f32(wq)
    s["mla_w_q_sw"] = f32(wqs.reshape(384, 1536))
    wkv = np.asarray(inp["mla_w_kv_up"][0], np.float32).reshape(256, 16, 128)
    s["mla_w_kv_k"] = f32(wkv[:, :, :64].reshape(256, 1024))
    s["mla_w_kv_v"] = f32(wkv[:, :, 64:].reshape(256, 1024))
    s["mla_gcols"] = f32(np.concatenate([np.asarray(inp["mla_q_norm"][0]).reshape(3, 128).T,
                                         np.asarray(inp["mla_kv_norm"][0]).reshape(2, 128).T], axis=1))
    s.update(mla_consts())
    s["mla_w_out"] = f32(inp["mla_w_out"][0])
    for li in range(2):
        s[f"ln1_g{li}"] = _rep128(inp["ln1_g"][li])
        s[f"ln1_b{li}"] = _rep128(inp["ln1_b"][li])
        s[f"ln2_g{li}"] = _rep128(inp["ln2_g"][li])
        s[f"ln2_b{li}"] = _rep128(inp["ln2_b"][li])
        s[f"moe_router{li}"] = f32(inp["moe_router"][li])
        s[f"moe_w_gate{li}"] = f32(inp["moe_w_gate"][li])
        s[f"moe_w_up{li}"] = f32(inp["moe_w_up"][li])
        s[f"moe_w_down{li}"] = f32(inp["moe_w_down"][li])
        s[f"ple_gate{li}"] = f32(inp["ple_gate"][li])
        s[f"ple_proj{li}"] = f32(inp["ple_proj"][li])
    return s


PER_CORE = ("x_tm", "xT", "pT0", "pT1")


def build_full(shared_names):
    C = Ctx(ext_in=set(shared_names) | set(PER_CORE), ext_out={"out"})
    stage_a1(C)
    stage_a2(C)
    stage_hy_filter(C)
    stage_hy_prep(C)
    stage_hy_conv(C)
    stage_proj_ln(C, "ab_w_out", "x_tm", "ln1", 0, "x1_0")
    stage_moe1(C, 0, "x1_0")
    stage_moe2(C, 0, "x1_0", "x2_0")
    stage_ple(C, 0, "x2_0", "x3_0", True)
    stage_mla1(C, "x3_0T")
    stage_mla2(C)
    stage_proj_ln(C, "mla_w_out", "x3_0", "ln1", 1, "x1_1")
    stage_moe1(C, 1, "x1_1")
    stage_moe2(C, 1, "x1_1", "x2_1")
    stage_ple(C, 1, "x2_1", "out", False)
    C.P.finish()
    return C


def kernel(**inputs):
    inp = {k: np.asarray(v) for k, v in inputs.items()}
    shared = shared_inputs(inp)
    x = np.asarray(inp["x"], np.float32)
    p = np.asarray(inp["p"], np.float32)
    C = build_full(shared.keys())
    used = set(C.dram.keys())
    in_maps = []
    for b in range(8):
        m = {k: v for k, v in shared.items() if k in used}
        m["x_tm"] = np.ascontiguousarray(x[b])
        m["xT"] = np.ascontiguousarray(x[b].T)
        m["pT0"] = np.ascontiguousarray(p[0, b].T)
        m["pT1"] = np.ascontiguousarray(p[1, b].T)
        in_maps.append(m)
    res = run_bass_kernel_spmd(C.nc, in_maps, core_ids=list(range(8)))
    return np.stack([np.asarray(r["out"], np.float32) for r in res.results], axis=0)
```

```python
from contextlib import ExitStack
import math
import numpy as np
import ml_dtypes
import concourse.bass as bass
import concourse.mybir as mybir
from concourse.bass_utils import run_bass_kernel_spmd

F32 = mybir.dt.float32
BF16 = mybir.dt.bfloat16
I32 = mybir.dt.int32
U32 = mybir.dt.uint32
AF = mybir.ActivationFunctionType
ALU = mybir.AluOpType
AX = mybir.AxisListType
NPBF = ml_dtypes.bfloat16

D_MODEL = 1024
SEQ = 2048
NT = SEQ // 128
ALPHA = 4.0 ** 0.25
EPS = 1e-5

ENGS = ("pe", "act", "dve", "pool", "sp")
N_DMA_SEMS = 16


class Dep:
    __slots__ = ("w", "r", "name")

    def __init__(self, name=""):
        self.w = {}
        self.r = {}
        self.name = name


class Prog:
    def __init__(self, nc, strict=True):
        self.nc = nc
        self.es = ExitStack()
        self.q = {e: [] for e in ENGS}
        self.cnt = {e: 0 for e in ENGS}
        self.seen = {e: {} for e in ENGS}
        self.sem = {}
        self.strict = strict
        for e in ENGS:
            self.sem[e] = self.es.enter_context(nc.semaphore("s_" + e))
        self.dma_sems = {}
        self.dma_tot = {}
        self.dma_rr = {}
        for e in ("sp", "pool", "act"):
            self.dma_sems[e] = [self.es.enter_context(nc.semaphore(f"d_{e}{i}")) for i in range(N_DMA_SEMS)]
            self.dma_tot[e] = [0] * N_DMA_SEMS
            self.dma_rr[e] = 0
        self.all_events = {}
        self.n_ops = 0
        self.stage_es = None
        self.uid = 0

    def begin_stage(self):
        self.barrier()
        if self.stage_es is not None:
            self.stage_es.close()
        self.stage_es = ExitStack()

    def sb(self, name, shape, dt):
        self.uid += 1
        t = self.stage_es.enter_context(self.nc.sbuf_tensor(f"{name}_{self.uid}", list(shape), dt))
        return t

    def ps(self, name, shape, dt=F32):
        return self.es.enter_context(self.nc.psum_tensor(name, list(shape), dt))

    def _semobj(self, key):
        if isinstance(key, str):
            return self.sem[key]
        e, i = key
        return self.dma_sems[e][i]

    def _need(self, eng, reads, writes, merge):
        need = {}

        def add(k, v):
            if k == eng and (not self.strict or eng == "pe"):
                return
            if self.seen[eng].get(k, 0) >= v:
                return
            if need.get(k, 0) < v:
                need[k] = v
        for d in reads:
            for k, v in d.w.items():
                add(k, v)
        for d in writes:
            if not merge:
                for k, v in d.w.items():
                    add(k, v)
            for k, v in d.r.items():
                add(k, v)
        for k, v in need.items():
            self.seen[eng][k] = v
        return list(need.items())

    def _commit(self, ev, reads, writes, merge):
        k, v = ev
        for d in reads:
            if d.r.get(k, 0) < v:
                d.r[k] = v
        for d in writes:
            if merge:
                d.w[k] = v
            else:
                d.w = {k: v}
                d.r = {}
        self.all_events[k] = v

    def op(self, eng, fn, reads=(), writes=(), merge=False):
        waits = self._need(eng, reads, writes, merge)
        self.cnt[eng] += 1
        ev = (eng, self.cnt[eng])
        sem = self.sem[eng]
        waitobjs = [(self._semobj(k), v) for k, v in waits]

        def emit(e, fn=fn, waitobjs=waitobjs, sem=sem):
            for s, v in waitobjs:
                e.wait_ge(s, v)
            fn(e).then_inc(sem, 1)
        self.q[eng].append(emit)
        self._commit(ev, reads, writes, merge)
        self.n_ops += 1
        return ev

    def dma(self, eng, out, in_, reads=(), writes=(), merge=False, **kw):
        i = self.dma_rr[eng]
        self.dma_rr[eng] = (i + 1) % N_DMA_SEMS
        key = (eng, i)
        prev = self.dma_tot[eng][i]
        waits = self._need(eng, reads, writes, merge)
        if prev > 0 and self.seen[eng].get(key, 0) < prev:
            waits.append((key, prev))
            self.seen[eng][key] = prev
        self.dma_tot[eng][i] = prev + 16
        ev = (key, prev + 16)
        sem = self.dma_sems[eng][i]
        waitobjs = [(self._semobj(k), v) for k, v in waits]

        def emit(e, waitobjs=waitobjs, sem=sem, out=out, in_=in_, kw=kw):
            for s, v in waitobjs:
                e.wait_ge(s, v)
            e.dma_start(out=out, in_=in_, **kw).then_inc(sem, 16)
        self.q[eng].append(emit)
        self._commit(ev, reads, writes, merge)
        self.n_ops += 1
        return ev

    def coll(self, kind, out, in_, reads=(), writes=()):
        eng = "pool"
        i = self.dma_rr[eng]
        self.dma_rr[eng] = (i + 1) % N_DMA_SEMS
        key = (eng, i)
        prev = self.dma_tot[eng][i]
        waits = self._need(eng, reads, writes, False)
        if prev > 0 and self.seen[eng].get(key, 0) < prev:
            waits.append((key, prev))
            self.seen[eng][key] = prev
        self.dma_tot[eng][i] = prev + 16
        ev = (key, prev + 16)
        sem = self.dma_sems[eng][i]
        waitobjs = [(self._semobj(k), v) for k, v in waits]

        def emit(e, waitobjs=waitobjs, sem=sem, out=out, in_=in_, kind=kind):
            for s_, v in waitobjs:
                e.wait_ge(s_, v)
            e.collective_compute(kind, ALU.bypass, replica_groups=[list(range(8))], ins=[in_], outs=[out]).then_inc(sem, 16)
        self.q[eng].append(emit)
        self._commit(ev, reads, writes, False)
        self.n_ops += 1
        return ev

    def barrier(self):
        snap = dict(self.all_events)
        for eng in ENGS:
            waits = []
            for k, v in snap.items():
                if k == eng:
                    continue
                if self.seen[eng].get(k, 0) >= v:
                    continue
                waits.append((self._semobj(k), v))
                self.seen[eng][k] = v
            if waits:
                def emit(e, waits=waits):
                    for s, v in waits:
                        e.wait_ge(s, v)
                self.q[eng].append(emit)

    def finish(self):
        self.barrier()
        nc = self.nc
        q = self.q
        with nc.Block() as block:
            @block.tensor
            def _(e):
                for f in q["pe"]:
                    f(e)

            @block.scalar
            def _(e):
                for f in q["act"]:
                    f(e)

            @block.vector
            def _(e):
                for f in q["dve"]:
                    f(e)

            @block.gpsimd
            def _(e):
                for f in q["pool"]:
                    f(e)

            @block.sync
            def _(e):
                for f in q["sp"]:
                    f(e)
        if self.stage_es is not None:
            self.stage_es.close()
        self.es.close()


class Ring:
    def __init__(self, P, name, n, shape, dt):
        self.bufs = [(P.sb(f"{name}{i}", shape, dt), Dep(f"{name}{i}")) for i in range(n)]
        self.i = 0

    def next(self):
        b = self.bufs[self.i]
        self.i = (self.i + 1) % len(self.bufs)
        return b


class Ctx:
    def __init__(self, ext_in, ext_out):
        self.nc = bass.Bass("TRN2", target_bir_lowering=False)
        self.P = Prog(self.nc)
        self.ext_in = set(ext_in)
        self.ext_out = set(ext_out)
        self.dram = {}
        self.ddep = {}
        P = self.P
        self.banks = [(P.ps(f"bank{i}", [128, 512], F32), Dep(f"bank{i}")) for i in range(8)]
        self.bank_i = 0
        self.held = set()

    def D(self, name, shape=None, dt=F32):
        if name in self.dram:
            return self.dram[name]
        kind = "Internal"
        if name in self.ext_in:
            kind = "ExternalInput"
        elif name in self.ext_out:
            kind = "ExternalOutput"
        t = self.nc.dram_tensor(name, list(shape), dt, kind=kind).ap()
        self.dram[name] = t
        self.ddep[name] = Dep(name)
        return t

    def dd(self, name):
        return self.ddep[name]

    def bank(self, hold=False):
        for _ in range(8):
            i = self.bank_i
            self.bank_i = (self.bank_i + 1) % 8
            if i not in self.held:
                if hold:
                    self.held.add(i)
                return self.banks[i]
        raise RuntimeError("no free PSUM bank")

    def release_all(self):
        self.held = set()

    def release(self, b):
        for i, bb in enumerate(self.banks):
            if bb[0] is b[0]:
                self.held.discard(i)


def mm_acc(P, out, pairs, reads, dwrite):
    n = len(pairs)
    for i, (l, r) in enumerate(pairs):
        P.op("pe", lambda e, l=l, r=r, i=i: e.matmul(out, lhsT=l, rhs=r, start=(i == 0), stop=(i == n - 1)),
             reads=reads, writes=[dwrite], merge=(i > 0))


def load_fm_bf16(C, name, src, kc_n, width, eng="pool"):
    P = C.P
    t = P.sb(name, [128, kc_n, width], BF16)
    deps = [Dep(f"{name}{k}") for k in range(kc_n)]
    for k in range(kc_n):
        P.dma(eng, t[:, k, :], src[k * 128:(k + 1) * 128, :], writes=[deps[k]])
    return t, deps


def stage_a1(C):
    P = C.P
    P.begin_stage()
    xT = C.D("xT", [1024, 2048], F32)
    w_in = C.D("ab_w_in", [1024, 3072], F32)
    qkT = C.D("qkT", [1024, 2048], BF16)
    v_tm = C.D("v_tm", [2048, 512], BF16)
    hbT = C.D("hbT", [1536, 2048], F32)
    xs, dxs = load_fm_bf16(C, "xTb", xT, 8, 2048)
    ws, dws = load_fm_bf16(C, "winb", w_in, 8, 3072)
    st_b = Ring(P, "a1sb", 3, [128, 2048], BF16)
    st_f = Ring(P, "a1sf", 3, [128, 2048], F32)
    ev_i = 0
    for mc in list(range(8)) + list(range(12, 24)):
        is_hb = mc >= 12
        stg, dstg = (st_f if is_hb else st_b).next()
        for nt in range(4):
            bk, dbk = C.bank()
            mm_acc(P, bk[:], [(ws[:, kc, mc * 128:(mc + 1) * 128], xs[:, kc, nt * 512:(nt + 1) * 512]) for kc in range(8)],
                   reads=dxs + dws, dwrite=dbk)
            eng = "act" if ev_i % 2 == 0 else "dve"
            ev_i += 1
            o = stg[:, nt * 512:(nt + 1) * 512]
            if eng == "act":
                P.op("act", lambda e, o=o, bk=bk: e.copy(out=o, in_=bk[:]), reads=[dbk], writes=[dstg], merge=(nt > 0))
            else:
                P.op("dve", lambda e, o=o, bk=bk: e.tensor_copy(out=o, in_=bk[:]), reads=[dbk], writes=[dstg], merge=(nt > 0))
        if is_hb:
            P.dma("act", hbT[(mc - 12) * 128:(mc - 11) * 128, :], stg[:], reads=[dstg], writes=[C.dd("hbT")], merge=True)
        else:
            P.dma("act", qkT[mc * 128:(mc + 1) * 128, :], stg[:], reads=[dstg], writes=[C.dd("qkT")], merge=True)
    st_v = Ring(P, "a1sv", 3, [128, 512], BF16)
    for tt in range(NT):
        bk, dbk = C.bank()
        mm_acc(P, bk[:], [(xs[:, kc, tt * 128:(tt + 1) * 128], ws[:, kc, 1024:1536]) for kc in range(8)],
               reads=dxs + dws, dwrite=dbk)
        stg, dstg = st_v.next()
        if tt % 2 == 0:
            P.op("act", lambda e, stg=stg, bk=bk: e.copy(out=stg[:], in_=bk[:]), reads=[dbk], writes=[dstg])
        else:
            P.op("dve", lambda e, stg=stg, bk=bk: e.tensor_copy(out=stg[:], in_=bk[:]), reads=[dbk], writes=[dstg])
        P.dma("act", v_tm[tt * 128:(tt + 1) * 128, :], stg[:], reads=[dstg], writes=[C.dd("v_tm")], merge=True)


def na_plan():
    rows, wr = 32, 8
    r0 = np.clip(np.arange(rows) - wr // 2, 0, rows - wr)
    plan = []
    keys = {}
    for i in range(16):
        lo = r0[2 * i] // 2
        hi = (r0[2 * i + 1] + 7) // 2
        lst = []
        for j in range(lo, hi + 1):
            val = []
            for ak in range(2):
                for aq in range(2):
                    r = 2 * i + aq
                    kr = 2 * j + ak
                    val.append(bool(r0[r] <= kr <= r0[r] + 7))
            key = (j - i, tuple(val))
            if key not in keys:
                keys[key] = len(keys)
            lst.append((j, keys[key]))
        plan.append(lst)
    return plan, keys


def na_tables(rpb):
    plan, keys = na_plan()
    c = np.arange(64)
    c0 = np.clip(c - 8, 0, 48)
    col_ok = (c[None, :] >= c0[:, None]) & (c[None, :] < c0[:, None] + 16)
    dc_idx = np.clip(c[None, :] - c[:, None], -15, 15) + 15
    tab = np.full((len(keys), 2, 64, 8, 2, 64), -1e30, np.float32)
    for (delta, val), tid in keys.items():
        vi = 0
        for ak in range(2):
            for aq in range(2):
                ok = val[vi]
                vi += 1
                if not ok:
                    continue
                dr = 2 * delta + ak - aq
                b = rpb[:, dr + 7, :][:, dc_idx]
                b = np.where(col_ok[None], b, np.float32(-1e30))
                tab[tid, ak, :, :, aq, :] = b.transpose(2, 0, 1)
    return tab.reshape(len(keys), 128, 8, 128)


def stage_a2(C):
    P = C.P
    P.begin_stage()
    plan, keys = na_plan()
    ntab = len(keys)
    qkT = C.D("qkT", [1024, 2048], BF16)
    v_tm = C.D("v_tm", [2048, 512], BF16)
    tab_d = C.D("na_tab", [ntab, 128, 8, 128], F32)
    mixT = C.D("mixT", [1024, 2048], BF16)
    QT = P.sb("QT", [128, 4, 2048], BF16)
    KT = P.sb("KT", [128, 4, 2048], BF16)
    dQ = [Dep() for _ in range(4)]
    dK = [Dep() for _ in range(4)]
    for hp in range(4):
        P.dma("sp", QT[:, hp, :], qkT[hp * 128:(hp + 1) * 128, :], reads=[C.dd("qkT")], writes=[dQ[hp]])
        P.dma("sp", KT[:, hp, :], qkT[512 + hp * 128:512 + (hp + 1) * 128, :], reads=[C.dd("qkT")], writes=[dK[hp]])
    tab = P.sb("natab", [128, ntab, 8, 128], F32)
    dtab = Dep()
    for t in range(ntab):
        P.dma("sp", tab[:, t, :, :], tab_d[t], writes=[dtab], merge=True)
    Vx = P.sb("Vx", [128, 8, NT, 128], BF16)
    dV = Dep()
    P.op("pool", lambda e: e.memset(Vx[:], 0.0), writes=[dV])
    vv = v_tm.rearrange("(t p) c -> p t c", p=128)
    for h in range(8):
        a = h % 2
        P.dma("sp", Vx[:, h, :, a * 64:(a + 1) * 64], vv[:, :, h * 64:(h + 1) * 64], reads=[C.dd("v_tm")], writes=[dV], merge=(h > 0))
    ones2 = P.sb("ones2", [128, 2, 128], BF16)
    dones = Dep()
    P.op("pool", lambda e: e.memset(ones2[:], 0.0), writes=[dones])
    P.op("pool", lambda e: e.memset(ones2[:, 0, 0:64], 1.0), writes=[dones])
    P.op("pool", lambda e: e.memset(ones2[:, 1, 64:128], 1.0), writes=[dones])
    yaT = P.sb("yaT", [128, 4, 2048], BF16)
    dya = [Dep() for _ in range(4)]
    s_ring = Ring(P, "na_s", 3, [128, 640], F32)
    p_ring = Ring(P, "na_p", 4, [128, 640], BF16)
    rd_ring = Ring(P, "na_rd", 2, [128, 128], F32)
    units = [(i, hp, a) for i in range(16) for hp in range(4) for a in range(2)]
    pair = {}

    def phase1(u):
        i, hp, a = u
        q0 = i * 128
        h = hp * 2 + a
        pa = slice(a * 64, (a + 1) * 64)
        lst = plan[i]
        nkb = len(lst)
        bA, dbA = C.bank()
        bB, dbB = (C.bank() if nkb > 4 else (None, None))
        ssb, dss = s_ring.next()
        for jj, (j, tid) in enumerate(lst):
            bk, dbk = (bA, dbA) if jj < 4 else (bB, dbB)
            o = bk[:, (jj % 4) * 128:(jj % 4 + 1) * 128]
            P.op("pe", lambda e, o=o, j=j, pa=pa, hp=hp, q0=q0: e.matmul(
                o, lhsT=KT[pa, hp, j * 128:(j + 1) * 128], rhs=QT[pa, hp, q0:q0 + 128], start=True, stop=True),
                reads=[dK[hp], dQ[hp]], writes=[dbk], merge=(jj % 4 > 0))
        for jj, (j, tid) in enumerate(lst):
            bk, dbk = (bA, dbA) if jj < 4 else (bB, dbB)
            o = bk[:, (jj % 4) * 128:(jj % 4 + 1) * 128]
            P.op("dve", lambda e, o=o, jj=jj, tid=tid, h=h, ssb=ssb: e.scalar_tensor_tensor(
                out=ssb[:, jj * 128:(jj + 1) * 128], in0=o, scalar=0.125, in1=tab[:, tid, h, :],
                op0=ALU.mult, op1=ALU.add), reads=[dbk, dtab], writes=[dss], merge=(jj > 0))
        pT, dpT = p_ring.next()
        P.op("act", lambda e, pT=pT, ssb=ssb, nkb=nkb: e.activation(
            out=pT[:, 0:nkb * 128], in_=ssb[:, 0:nkb * 128], func=AF.Exp), reads=[dss], writes=[dpT])
        return (pT, dpT)

    def phase2(u, pp):
        i, hp, a = u
        q0 = i * 128
        h = hp * 2 + a
        pT, dpT = pp
        lst = plan[i]
        nkb = len(lst)
        if a == 0:
            pair[(i, hp)] = (C.bank(hold=True), C.bank(hold=True))
        (bo, dbo), (bd, dbd) = pair[(i, hp)]
        for jj, (j, tid) in enumerate(lst):
            first = (a == 0 and jj == 0)
            last = (a == 1 and jj == nkb - 1)
            P.op("pe", lambda e, bo=bo, h=h, j=j, pT=pT, jj=jj, first=first, last=last: e.matmul(
                bo[:, 0:128], lhsT=Vx[:, h, j, :], rhs=pT[:, jj * 128:(jj + 1) * 128], start=first, stop=last),
                reads=[dV, dpT], writes=[dbo], merge=(not first))
            P.op("pe", lambda e, bd=bd, a=a, pT=pT, jj=jj, first=first, last=last: e.matmul(
                bd[:, 0:128], lhsT=ones2[:, a, :], rhs=pT[:, jj * 128:(jj + 1) * 128], start=first, stop=last),
                reads=[dones, dpT], writes=[dbd], merge=(not first))
        if a == 1:
            rd, drd = rd_ring.next()
            P.op("dve", lambda e, rd=rd, bd=bd: e.reciprocal(out=rd[:], in_=bd[:, 0:128]), reads=[dbd], writes=[drd])
            P.op("dve", lambda e, rd=rd, bo=bo, hp=hp, q0=q0: e.tensor_tensor(
                out=yaT[:, hp, q0:q0 + 128], in0=bo[:, 0:128], in1=rd[:], op=ALU.mult),
                reads=[dbo, drd], writes=[dya[hp]], merge=True)
            C.release(pair[(i, hp)][0])
            C.release(pair[(i, hp)][1])
            del pair[(i, hp)]

    pend = None
    for u in units:
        pp = phase1(u)
        if pend is not None:
            phase2(*pend)
        pend = (u, pp)
    phase2(*pend)
    for hp in range(4):
        P.dma("sp", mixT[hp * 128:(hp + 1) * 128, :], yaT[:, hp, :], reads=[dya[hp]], writes=[C.dd("mixT")], merge=True)


class LNBufs:
    def __init__(self, P, name):
        self.stats = Ring(P, name + "st", 2, [128, 2, 6], F32)
        self.mv = Ring(P, name + "mv", 2, [128, 2], F32)
        self.rstd = Ring(P, name + "rs", 2, [128, 1], F32)
        self.nmr = Ring(P, name + "nm", 2, [128, 1], F32)


def layer_norm_tile(P, lb, r, dr, gb, bb, dgb, y, dy):
    st, dst = lb.stats.next()
    mv, dmv = lb.mv.next()
    rs, drs = lb.rstd.next()
    nm, dnm = lb.nmr.next()
    P.op("dve", lambda e: e.bn_stats(out=st[:, 0, :], in_=r[:, 0:512]), reads=[dr], writes=[dst])
    P.op("dve", lambda e: e.bn_stats(out=st[:, 1, :], in_=r[:, 512:1024]), reads=[dr], writes=[dst], merge=True)
    P.op("dve", lambda e: e.bn_aggr(out=mv[:], in_=st[:]), reads=[dst], writes=[dmv])
    P.op("dve", lambda e: e.tensor_scalar(out=rs[:], in0=mv[:, 1:2], scalar1=EPS, scalar2=None, op0=ALU.add), reads=[dmv], writes=[drs])
    P.op("act", lambda e: e.activation(out=rs[:], in_=rs[:], func=AF.Sqrt), reads=[drs], writes=[drs])
    P.op("dve", lambda e: e.reciprocal(out=rs[:], in_=rs[:]), reads=[drs], writes=[drs])
    P.op("dve", lambda e: e.scalar_tensor_tensor(out=nm[:], in0=mv[:, 0:1], scalar=-1.0, in1=rs[:], op0=ALU.mult, op1=ALU.mult),
         reads=[dmv, drs], writes=[dnm])
    P.op("act", lambda e: e.activation(out=y[:], in_=r[:], func=AF.Identity, bias=nm[:], scale=rs[:]),
         reads=[dr, drs, dnm], writes=[dy])
    P.op("pool", lambda e: e.tensor_tensor(out=y[:], in0=y[:], in1=gb[:], op=ALU.mult), reads=[dy, dgb], writes=[dy])
    P.op("pool", lambda e: e.tensor_tensor(out=y[:], in0=y[:], in1=bb[:], op=ALU.add), reads=[dy, dgb], writes=[dy])


def stage_proj_ln(C, w_name, x_name, lnname, li, out_name):
    P = C.P
    P.begin_stage()
    mixT = C.D("mixT", [1024, 2048], BF16)
    w = C.D(w_name, [1024, 1024], F32)
    x = C.D(x_name, [2048, 1024], F32)
    g_d = C.D(f"{lnname}_g{li}", [128, 1024], F32)
    b_d = C.D(f"{lnname}_b{li}", [128, 1024], F32)
    out = C.D(out_name, [2048, 1024], F32)
    ms = P.sb("ms", [128, 8, 2048], BF16)
    dms = [Dep() for _ in range(8)]
    for kc in range(8):
        P.dma("sp", ms[:, kc, :], mixT[kc * 128:(kc + 1) * 128, :], reads=[C.dd("mixT")], writes=[dms[kc]])
    ws, dws = load_fm_bf16(C, "wout", w, 8, 1024)
    gb = P.sb("gb", [128, 1024], F32)
    bb = P.sb("bb", [128, 1024], F32)
    dgb = Dep()
    P.dma("sp", gb[:], g_d, writes=[dgb])
    P.dma("sp", bb[:], b_d, writes=[dgb], merge=True)
    lb = LNBufs(P, "ln")
    xr = Ring(P, "xr", 3, [128, 1024], F32)
    rr = Ring(P, "rr", 2, [128, 1024], F32)
    yr = Ring(P, "yr", 2, [128, 1024], F32)
    for tt in range(NT):
        xt, dxt = xr.next()
        P.dma("sp", xt[:], x[tt * 128:(tt + 1) * 128, :], reads=[C.dd(x_name)], writes=[dxt])
        r, dr = rr.next()
        for half in range(2):
            bk, dbk = C.bank()
            hs = slice(half * 512, (half + 1) * 512)
            mm_acc(P, bk[:], [(ms[:, kc, tt * 128:(tt + 1) * 128], ws[:, kc, hs]) for kc in range(8)], reads=dms + dws, dwrite=dbk)
            P.op("dve", lambda e, r=r, xt=xt, bk=bk, hs=hs: e.scalar_tensor_tensor(
                out=r[:, hs], in0=xt[:, hs], scalar=ALPHA, in1=bk[:], op0=ALU.mult, op1=ALU.add),
                reads=[dxt, dbk], writes=[dr], merge=(half > 0))
        y, dy = yr.next()
        layer_norm_tile(P, lb, r, dr, gb, bb, dgb, y, dy)
        P.dma("pool", out[tt * 128:(tt + 1) * 128, :], y[:], reads=[dy], writes=[C.dd(out_name)], merge=True)


def stage_moe1(C, li, x_name):
    P = C.P
    P.begin_stage()
    x1 = C.D(x_name, [2048, 1024], F32)
    wr_d = C.D(f"moe_router{li}", [1024, 16], F32)
    ident_d = C.D("ident", [128, 128], F32)
    esel_d = C.D("esel", [16, 16, 128], F32)
    iotac_d = C.D("iota_col", [128, 16], F32)
    xe_all = C.D("xeT_all", [16, 128, 8, 256], BF16)
    idxc_d = C.D("moe_idxc", [128, 2, 16], F32)
    gc_d = C.D("moe_gc", [128, 2, 16], F32)
    wr = P.sb("wr", [128, 8, 16], F32)
    dcst = Dep()
    P.dma("sp", wr[:], wr_d.rearrange("(kc p) e -> p kc e", p=128), writes=[dcst])
    ident = P.sb("ident", [128, 128], F32)
    P.dma("sp", ident[:], ident_d, writes=[dcst], merge=True)
    esel = P.sb("esel", [16, 16, 128], F32)
    P.dma("sp", esel[:], esel_d, writes=[dcst], merge=True)
    iotac = P.sb("iotac", [128, 16], F32)
    P.dma("sp", iotac[:], iotac_d, writes=[dcst], merge=True)
    x1b = P.sb("x1b", [128, NT, 1024], BF16)
    dx1b = [Dep() for _ in range(NT)]
    affT = P.sb("affT", [16, 2048], F32)
    daffT = Dep()
    xr = Ring(P, "m1x", 2, [128, 1024], F32)
    xTr = Ring(P, "m1xT", 2, [128, 8, 128], F32)
    sm_r = Ring(P, "m1sm", 2, [128, 4], F32)
    ex_r = Ring(P, "m1ex", 2, [128, 16], F32)
    af_r = Ring(P, "m1af", 2, [128, 16], F32)
    for tt in range(NT):
        xt, dxt = xr.next()
        P.dma("sp", xt[:], x1[tt * 128:(tt + 1) * 128, :], reads=[C.dd(x_name)], writes=[dxt])
        P.op("act", lambda e, xt=xt, tt=tt: e.copy(out=x1b[:, tt, :], in_=xt[:]), reads=[dxt], writes=[dx1b[tt]])
        xT, dxT = xTr.next()
        for hb in range(2):
            bk, dbk = C.bank()
            for k4 in range(4):
                kc = hb * 4 + k4
                P.op("pe", lambda e, bk=bk, k4=k4, kc=kc, xt=xt: e.transpose(bk[:, k4 * 128:(k4 + 1) * 128], xt[:, kc * 128:(kc + 1) * 128], ident[:]),
                     reads=[dxt, dcst], writes=[dbk], merge=(k4 > 0))
            P.op("dve", lambda e, xT=xT, hb=hb, bk=bk: e.tensor_copy(out=xT[:, hb * 4:(hb + 1) * 4, :], in_=bk[:].rearrange("p (k t) -> p k t", k=4)),
                 reads=[dbk], writes=[dxT], merge=(hb > 0))
        bk, dbk = C.bank()
        mm_acc(P, bk[:, 0:16], [(xT[:, kc, :], wr[:, kc, :]) for kc in range(8)], reads=[dxT, dcst], dwrite=dbk)
        sm, dsm = sm_r.next()
        ex, dex = ex_r.next()
        af, daf = af_r.next()
        P.op("dve", lambda e, sm=sm, bk=bk: e.reduce_max(out=sm[:, 0:1], in_=bk[:, 0:16], axis=AX.X), reads=[dbk], writes=[dsm])
        P.op("dve", lambda e, sm=sm: e.tensor_scalar(out=sm[:, 1:2], in0=sm[:, 0:1], scalar1=-1.0, scalar2=None, op0=ALU.mult),
             reads=[dsm], writes=[dsm])
        P.op("act", lambda e, ex=ex, bk=bk, sm=sm: e.activation(out=ex[:], in_=bk[:, 0:16], func=AF.Exp, bias=sm[:, 1:2], accum_out=sm[:, 2:3]),
             reads=[dbk, dsm], writes=[dex, dsm])
        P.op("dve", lambda e, sm=sm: e.reciprocal(out=sm[:, 3:4], in_=sm[:, 2:3]), reads=[dsm], writes=[dsm])
        P.op("dve", lambda e, af=af, ex=ex, sm=sm: e.tensor_scalar(out=af[:], in0=ex[:], scalar1=sm[:, 3:4], scalar2=None, op0=ALU.mult),
             reads=[dex, dsm], writes=[daf])
        bk2, dbk2 = C.bank()
        P.op("pe", lambda e, bk2=bk2, af=af: e.transpose(bk2[0:16, 0:128], af[:], ident[:]), reads=[daf, dcst], writes=[dbk2])
        P.op("act", lambda e, bk2=bk2, tt=tt: e.copy(out=affT[:, tt * 128:(tt + 1) * 128], in_=bk2[0:16, 0:128]),
             reads=[dbk2], writes=[daffT], merge=(tt > 0))
    work = P.sb("work", [16, 2048], F32)
    dwork = Dep()
    g_all = P.sb("g_all", [16, 256], F32)
    idx_all = P.sb("idx_all", [16, 256], U32)
    dg = Dep()
    di = Dep()
    for r in range(32):
        src, dsrc = (affT, daffT) if r == 0 else (work, dwork)
        sl = slice(r * 8, (r + 1) * 8)
        P.op("dve", lambda e, src=src, sl=sl: e.max(out=g_all[:, sl], in_=src[:]), reads=[dsrc], writes=[dg], merge=(r > 0))
        P.op("dve", lambda e, src=src, sl=sl: e.max_index(out=idx_all[:, sl], in_max=g_all[:, sl], in_values=src[:]),
             reads=[dsrc, dg], writes=[di], merge=(r > 0))
        if r < 31:
            P.op("dve", lambda e, src=src, sl=sl: e.match_replace(out=work[:], in_to_replace=g_all[:, sl], in_values=src[:], imm_value=-1.0),
                 reads=[dsrc, dg], writes=[dwork])
    idxf = P.sb("idxf", [16, 256], F32)
    didxf = Dep()
    P.op("dve", lambda e: e.tensor_copy(out=idxf[:], in_=idx_all[:]), reads=[di], writes=[didxf])
    colt = P.sb("colt", [128, 2, 2, 16], F32)
    dcol = Dep()
    for which, (src, dsrc) in enumerate(((idxf, didxf), (g_all, dg))):
        for cc in range(2):
            bk, dbk = C.bank()
            P.op("pe", lambda e, bk=bk, src=src, cc=cc: e.transpose(bk[:, 0:16], src[:, cc * 128:(cc + 1) * 128], ident[0:16, 0:16]),
                 reads=[dsrc, dcst], writes=[dbk])
            P.op("act", lambda e, bk=bk, which=which, cc=cc: e.copy(out=colt[:, which, cc, :], in_=bk[:, 0:16]),
                 reads=[dbk], writes=[dcol], merge=True)
    P.dma("act", idxc_d, colt[:, 0, :, :], reads=[dcol], writes=[C.dd("moe_idxc")])
    P.dma("act", gc_d, colt[:, 1, :, :], reads=[dcol], writes=[C.dd("moe_gc")])
    sel_r = Ring(P, "sel", 2, [128, NT, 256], BF16)
    xe_r = Ring(P, "xe", 2, [128, 8, 256], BF16)
    for ex_i in range(16):
        bk, dbk = C.bank()
        P.op("pe", lambda e, bk=bk, ex_i=ex_i: e.matmul(bk[:, 0:256], lhsT=esel[:, ex_i, :], rhs=idxf[:], start=True, stop=True),
             reads=[dcst, didxf], writes=[dbk])
        sel, dsel = sel_r.next()
        for tt in range(NT):
            P.op("dve", lambda e, sel=sel, bk=bk, tt=tt: e.tensor_scalar(out=sel[:, tt, :], in0=bk[:, 0:256], scalar1=iotac[:, tt:tt + 1], scalar2=None, op0=ALU.is_equal),
                 reads=[dbk, dcst], writes=[dsel], merge=(tt > 0))
        xe, dxe = xe_r.next()
        for dc in range(8):
            bk2, dbk2 = C.bank()
            mm_acc(P, bk2[:, 0:256], [(x1b[:, tt, dc * 128:(dc + 1) * 128], sel[:, tt, :]) for tt in range(NT)],
                   reads=dx1b + [dsel], dwrite=dbk2)
            if dc % 2 == 0:
                P.op("act", lambda e, xe=xe, dc=dc, bk2=bk2: e.copy(out=xe[:, dc, :], in_=bk2[:, 0:256]), reads=[dbk2], writes=[dxe], merge=(dc > 0))
            else:
                P.op("dve", lambda e, xe=xe, dc=dc, bk2=bk2: e.tensor_copy(out=xe[:, dc, :], in_=bk2[:, 0:256]), reads=[dbk2], writes=[dxe], merge=True)
        P.dma("act", xe_all[ex_i], xe[:], reads=[dxe], writes=[C.dd("xeT_all")], merge=True)


def stage_moe2(C, li, x_name, out_name):
    P = C.P
    P.begin_stage()
    x1 = C.D(x_name, [2048, 1024], F32)
    wg_d = C.D(f"moe_w_gate{li}", [16, 1024, 2048], F32)
    wu_d = C.D(f"moe_w_up{li}", [16, 1024, 2048], F32)
    wd_d = C.D(f"moe_w_down{li}", [16, 2048, 1024], F32)
    xe_all = C.D("xeT_all", [16, 128, 8, 256], BF16)
    idxc_d = C.D("moe_idxc", [128, 2, 16], F32)
    gc_d = C.D("moe_gc", [128, 2, 16], F32)
    iotar_d = C.D("iota_row", [128, 2048], F32)
    ident_d = C.D("ident", [128, 128], F32)
    g_d = C.D(f"ln2_g{li}", [128, 1024], F32)
    b_d = C.D(f"ln2_b{li}", [128, 1024], F32)
    out = C.D(out_name, [2048, 1024], F32)
    outT = C.D(out_name + "T", [1024, 2048], BF16)
    dcst = Dep()
    idxc = P.sb("idxc", [128, 2, 16], F32)
    gc = P.sb("gc", [128, 2, 16], F32)
    iotar = P.sb("iotar", [128, 2048], F32)
    P.dma("sp", idxc[:], idxc_d, reads=[C.dd("moe_idxc")], writes=[dcst])
    P.dma("sp", gc[:], gc_d, reads=[C.dd("moe_gc")], writes=[dcst], merge=True)
    P.dma("sp", iotar[:], iotar_d, writes=[dcst], merge=True)
    f_acc = P.sb("f_acc", [128, NT, 1024], F32)
    dfa = [Dep() for _ in range(NT)]
    xe_r = Ring(P, "m2xe", 2, [128, 8, 256], BF16)
    selT_r = Ring(P, "selT", 2, [128, 2, 2048], BF16)
    wg_r = Ring(P, "wg", 2, [128, 8, 512], BF16)
    wu_r = Ring(P, "wu", 2, [128, 8, 512], BF16)
    wd_r = Ring(P, "wd", 2, [128, 4, 1024], BF16)
    sg_r = Ring(P, "sg", 2, [128, 256], F32)
    hT_r = Ring(P, "hT", 2, [128, 16, 256], BF16)
    ye_r = Ring(P, "ye", 2, [128, 2, 1024], BF16)
    ev = 0
    for e_i in range(16):
        xe, dxe = xe_r.next()
        P.dma("sp", xe[:], xe_all[e_i], reads=[C.dd("xeT_all")], writes=[dxe])
        selT, dselT = selT_r.next()
        for cc in range(2):
            P.op("pool", lambda e, selT=selT, cc=cc, e_i=e_i: e.tensor_scalar(
                out=selT[:, cc, :], in0=iotar[:], scalar1=idxc[:, cc, e_i:e_i + 1], scalar2=gc[:, cc, e_i:e_i + 1],
                op0=ALU.is_equal, op1=ALU.mult), reads=[dcst], writes=[dselT], merge=(cc > 0))
        hT, dhT = hT_r.next()
        wgv = wg_d[e_i].rearrange("(kc p) f -> p kc f", p=128)
        wuv = wu_d[e_i].rearrange("(kc p) f -> p kc f", p=128)
        wdv = wd_d[e_i].rearrange("(fc p) d -> p fc d", p=128)
        for q in range(4):
            wg, dwg = wg_r.next()
            wu, dwu = wu_r.next()
            P.dma("pool", wg[:], wgv[:, :, q * 512:(q + 1) * 512], writes=[dwg])
            P.dma("pool", wu[:], wuv[:, :, q * 512:(q + 1) * 512], writes=[dwu])
            for fcl in range(4):
                fc = q * 4 + fcl
                fs = slice(fcl * 128, (fcl + 1) * 128)
                bg, dbg = C.bank()
                bu, dbu = C.bank()
                mm_acc(P, bg[:, 0:256], [(wg[:, kc, fs], xe[:, kc, :]) for kc in range(8)], reads=[dwg, dxe], dwrite=dbg)
                mm_acc(P, bu[:, 0:256], [(wu[:, kc, fs], xe[:, kc, :]) for kc in range(8)], reads=[dwu, dxe], dwrite=dbu)
                sg, dsg = sg_r.next()
                P.op("act", lambda e, sg=sg, bg=bg: e.activation(out=sg[:], in_=bg[:, 0:256], func=AF.Silu), reads=[dbg], writes=[dsg])
                P.op("dve", lambda e, hT=hT, fc=fc, sg=sg, bu=bu: e.tensor_tensor(out=hT[:, fc, :], in0=sg[:], in1=bu[:, 0:256], op=ALU.mult),
                     reads=[dsg, dbu], writes=[dhT], merge=(fc > 0))
        ye, dye = ye_r.next()
        dbanks = [C.bank() for _ in range(4)]
        for r in range(4):
            wd, dwd = wd_r.next()
            P.dma("pool", wd[:], wdv[:, r * 4:(r + 1) * 4, :], writes=[dwd])
            for ct in range(2):
                for dh in range(2):
                    bk, dbk = dbanks[ct * 2 + dh]
                    for f4 in range(4):
                        fc = r * 4 + f4
                        first = (fc == 0)
                        last = (fc == 15)
                        P.op("pe", lambda e, bk=bk, hT=hT, fc=fc, ct=ct, wd=wd, f4=f4, dh=dh, first=first, last=last: e.matmul(
                            bk[:], lhsT=hT[:, fc, ct * 128:(ct + 1) * 128], rhs=wd[:, f4, dh * 512:(dh + 1) * 512], start=first, stop=last),
                            reads=[dhT, dwd], writes=[dbk], merge=(not first))
        for ct in range(2):
            for dh in range(2):
                bk, dbk = dbanks[ct * 2 + dh]
                P.op("act", lambda e, ye=ye, ct=ct, dh=dh, bk=bk: e.copy(out=ye[:, ct, dh * 512:(dh + 1) * 512], in_=bk[:]),
                     reads=[dbk], writes=[dye], merge=(ct + dh > 0))
        for tt in range(NT):
            for dh in range(2):
                bk, dbk = C.bank()
                ds = slice(dh * 512, (dh + 1) * 512)
                mm_acc(P, bk[:], [(selT[:, ct, tt * 128:(tt + 1) * 128], ye[:, ct, ds]) for ct in range(2)], reads=[dselT, dye], dwrite=dbk)
                if e_i == 0:
                    P.op("dve", lambda e, tt=tt, ds=ds, bk=bk: e.tensor_copy(out=f_acc[:, tt, ds], in_=bk[:]), reads=[dbk], writes=[dfa[tt]], merge=(dh > 0))
                else:
                    P.op("dve", lambda e, tt=tt, ds=ds, bk=bk: e.tensor_tensor(out=f_acc[:, tt, ds], in0=f_acc[:, tt, ds], in1=bk[:], op=ALU.add),
                         reads=[dbk, dfa[tt]], writes=[dfa[tt]])
    gb = P.sb("gb2", [128, 1024], F32)
    bb = P.sb("bb2", [128, 1024], F32)
    ident = P.sb("ident2", [128, 128], F32)
    dgb = Dep()
    P.dma("sp", gb[:], g_d, writes=[dgb])
    P.dma("sp", bb[:], b_d, writes=[dgb], merge=True)
    P.dma("sp", ident[:], ident_d, writes=[dgb], merge=True)
    lb = LNBufs(P, "ln2")
    xr = Ring(P, "m2x", 2, [128, 1024], F32)
    yr = Ring(P, "m2y", 2, [128, 1024], F32)
    yT_r = Ring(P, "m2yT", 2, [128, 8, 128], BF16)
    for tt in range(NT):
        xt, dxt = xr.next()
        P.dma("sp", xt[:], x1[tt * 128:(tt + 1) * 128, :], reads=[C.dd(x_name)], writes=[dxt])
        P.op("dve", lambda e, xt=xt, tt=tt: e.scalar_tensor_tensor(out=xt[:], in0=xt[:], scalar=ALPHA, in1=f_acc[:, tt, :], op0=ALU.mult, op1=ALU.add),
             reads=[dxt, dfa[tt]], writes=[dxt])
        y, dy = yr.next()
        layer_norm_tile(P, lb, xt, dxt, gb, bb, dgb, y, dy)
        P.dma("pool", out[tt * 128:(tt + 1) * 128, :], y[:], reads=[dy], writes=[C.dd(out_name)], merge=True)
        transpose_tile_to_dram(C, y, dy, ident, dgb, yT_r, outT, out_name + "T", tt)


def transpose_tile_to_dram(C, y, dy, ident, dident, yT_r, outT, outT_name, tt):
    P = C.P
    yT, dyT = yT_r.next()
    for hb in range(2):
        bk, dbk = C.bank()
        for k4 in range(4):
            kc = hb * 4 + k4
            P.op("pe", lambda e, bk=bk, k4=k4, kc=kc: e.transpose(bk[:, k4 * 128:(k4 + 1) * 128], y[:, kc * 128:(kc + 1) * 128], ident[:]),
                 reads=[dy, dident], writes=[dbk], merge=(k4 > 0))
        P.op("act", lambda e, hb=hb, bk=bk: e.copy(out=yT[:, hb * 4:(hb + 1) * 4, :], in_=bk[:].rearrange("p (k t) -> p k t", k=4)),
             reads=[dbk], writes=[dyT], merge=(hb > 0))
    P.dma("act", outT.rearrange("(kc p) t -> p kc t", p=128)[:, :, tt * 128:(tt + 1) * 128], yT[:], reads=[dyT], writes=[C.dd(outT_name)], merge=True)


def stage_ple(C, li, x_name, out_name, want_T):
    P = C.P
    P.begin_stage()
    x2 = C.D(x_name, [2048, 1024], F32)
    x2T = C.D(x_name + "T", [1024, 2048], BF16)
    pT_d = C.D(f"pT{li}", [256, 2048], F32)
    wg_d = C.D(f"ple_gate{li}", [1024, 1024], F32)
    wp_d = C.D(f"ple_proj{li}", [256, 1024], F32)
    ident_d = C.D("ident", [128, 128], F32)
    out = C.D(out_name, [2048, 1024], F32)
    xs = P.sb("plx", [128, 8, 2048], BF16)
    dxs = [Dep() for _ in range(8)]
    for kc in range(8):
        P.dma("sp", xs[:, kc, :], x2T[kc * 128:(kc + 1) * 128, :], reads=[C.dd(x_name + "T")], writes=[dxs[kc]])
    ps_, dps_ = load_fm_bf16(C, "plp", pT_d, 2, 2048)
    wg, dwg = load_fm_bf16(C, "plwg", wg_d, 8, 1024)
    wp, dwp = load_fm_bf16(C, "plwp", wp_d, 2, 1024)
    ident = P.sb("ident3", [128, 128], F32)
    dident = Dep()
    P.dma("sp", ident[:], ident_d, writes=[dident])
    if want_T:
        outT = C.D(out_name + "T", [1024, 2048], BF16)
        yT_r = Ring(P, "plyT", 2, [128, 8, 128], BF16)
    xr = Ring(P, "plxr", 2, [128, 1024], F32)
    gr = Ring(P, "plg", 2, [128, 1024], F32)
    yr = Ring(P, "ply", 2, [128, 1024], F32)
    for tt in range(NT):
        ts_ = slice(tt * 128, (tt + 1) * 128)
        xt, dxt = xr.next()
        P.dma("sp", xt[:], x2[ts_, :], reads=[C.dd(x_name)], writes=[dxt])
        gt, dgt = gr.next()
        y, dy = yr.next()
        for half in range(2):
            hs = slice(half * 512, (half + 1) * 512)
            bk, dbk = C.bank()
            mm_acc(P, bk[:], [(xs[:, kc, ts_], wg[:, kc, hs]) for kc in range(8)], reads=dxs + dwg, dwrite=dbk)
            P.op("act", lambda e, gt=gt, hs=hs, bk=bk: e.activation(out=gt[:, hs], in_=bk[:], func=AF.Sigmoid), reads=[dbk], writes=[dgt], merge=(half > 0))
            bk2, dbk2 = C.bank()
            mm_acc(P, bk2[:], [(ps_[:, kc, ts_], wp[:, kc, hs]) for kc in range(2)], reads=dps_ + dwp, dwrite=dbk2)
            P.op("dve", lambda e, gt=gt, hs=hs, bk2=bk2: e.tensor_tensor(out=gt[:, hs], in0=gt[:, hs], in1=bk2[:], op=ALU.mult),
                 reads=[dgt, dbk2], writes=[dgt])
        P.op("pool", lambda e, y=y, xt=xt, gt=gt: e.tensor_tensor(out=y[:], in0=xt[:], in1=gt[:], op=ALU.add), reads=[dxt, dgt], writes=[dy])
        P.dma("pool", out[ts_, :], y[:], reads=[dy], writes=[C.dd(out_name)], merge=True)
        if want_T:
            transpose_tile_to_dram(C, y, dy, ident, dident, yT_r, outT, out_name + "T", tt)


def hyena_consts():
    L, N = 2048, 4096
    R = np.arange(N)
    f = np.where(R <= 2048, R, R - 2048).astype(np.int64)
    is_im = R > 2048
    t = np.arange(L, dtype=np.int64)
    k = (t[:, None] * f[None, :]) % N
    ang = 2.0 * np.pi * k.astype(np.float64) / N
    Wf = np.where(is_im[None, :], -np.sin(ang), np.cos(ang))
    cR = np.full(N, 2.0 / N)
    cR[0] = 1.0 / N
    cR[2048] = 1.0 / N
    WfT = np.ascontiguousarray(Wf.T)
    Wf_d = Wf.reshape(16, 128, 32, 128).transpose(2, 1, 0, 3)
    WA_d = WfT.reshape(32, 128, 16, 128).transpose(2, 1, 0, 3)
    WB_d = WfT.reshape(2, 16, 128, 4, 512).transpose(3, 0, 2, 1, 4)
    tl = np.linspace(0.0, 1.0, L, dtype=np.float32)[:, None]
    w = (2.0 * np.float32(math.pi) * np.arange(L, dtype=np.float32)[:, None] / np.float32(L)).astype(np.float32)
    fb = np.linspace(1e-4, 15, 16, dtype=np.float32)[None, :]
    z = np.concatenate([tl, np.cos(fb * w), -np.sin(fb * w)], axis=-1).astype(np.float32)
    min_decay = math.log(1e-2) / 1.5
    max_decay = math.log(1e-2) / 0.3
    deltas = np.abs(np.linspace(min_decay, max_decay, 512, dtype=np.float32))
    decay = np.exp(-tl * deltas[None, :]).astype(np.float32)
    return {
        "hy_Wf": np.ascontiguousarray(Wf_d).astype(NPBF),
        "hy_WA": np.ascontiguousarray(WA_d).astype(NPBF), "hy_WB": np.ascontiguousarray(WB_d).astype(NPBF),
        "hy_cR": np.ascontiguousarray(cR.reshape(32, 128).T).astype(np.float32),
        "hy_zT": np.ascontiguousarray(z.T), "hy_decay": np.ascontiguousarray(decay.reshape(16, 128, 512)),
    }


TWO_PI = 2.0 * math.pi


def stage_hy_filter(C):
    P = C.P
    P.begin_stage()
    zT_d = C.D("hy_zT", [33, 2048], F32)
    w1_d = C.D("hy_f_w1", [33, 64], F32)
    w2_d = C.D("hy_f_w2", [64, 64], F32)
    w3_d = C.D("hy_f_w3", [64, 2048], F32)
    cols_d = C.D("hy_cols", [64, 3], F32)
    dec_d = C.D("hy_decay", [16, 128, 512], F32)
    Wf_d = C.D("hy_Wf", [32, 128, 16, 128], BF16)
    cR_d = C.D("hy_cR", [128, 32], F32)
    skip_d = C.D("hy_skipb", [2, 128, 512], F32)
    Kf_d = C.D("hy_Kf", [32, 2, 128, 512], F32)
    dc = Dep()
    zT = P.sb("zT", [33, 2048], F32)
    w1 = P.sb("w1", [33, 64], F32)
    w2 = P.sb("w2", [64, 64], F32)
    w3 = P.sb("w3", [64, 2048], BF16)
    cols = P.sb("cols", [64, 8], F32)
    cR = P.sb("cR", [128, 32], F32)
    skipb = P.sb("skipb", [128, 2, 512], F32)
    P.dma("sp", zT[:], zT_d, writes=[dc])
    P.dma("sp", w1[:], w1_d, writes=[dc], merge=True)
    P.dma("sp", w2[:], w2_d, writes=[dc], merge=True)
    P.dma("pool", w3[:], w3_d, writes=[dc], merge=True)
    P.dma("sp", cols[:, 0:3], cols_d, writes=[dc], merge=True)
    P.dma("sp", cR[:], cR_d, writes=[dc], merge=True)
    for o in range(2):
        P.dma("sp", skipb[:, o, :], skip_d[o], writes=[dc], merge=True)
    dcol = Dep()
    P.op("dve", lambda e: e.tensor_tensor(out=cols[:, 3:4], in0=cols[:, 0:1], in1=cols[:, 1:2], op=ALU.mult), reads=[dc], writes=[dcol])
    P.op("dve", lambda e: e.tensor_tensor(out=cols[:, 4:5], in0=cols[:, 2:3], in1=cols[:, 1:2], op=ALU.mult), reads=[dc], writes=[dcol], merge=True)
    P.op("pool", lambda e: e.memset(cols[:, 5:6], -math.pi), reads=[dc], writes=[dcol], merge=True)
    h1T = P.sb("h1T", [64, 2048], F32)
    h2T = P.sb("h2T", [64, 2048], BF16)
    dh1 = Dep()
    dh2 = Dep()
    u_r = Ring(P, "hyu", 2, [64, 512], F32)
    s_r = Ring(P, "hys", 4, [64, 512], F32)
    for layer in range(2):
        for nt in range(4):
            ns = slice(nt * 512, (nt + 1) * 512)
            bk, dbk = C.bank()
            if layer == 0:
                P.op("pe", lambda e, bk=bk, ns=ns: e.matmul(bk[0:64, :], lhsT=w1[:], rhs=zT[:, ns], start=True, stop=True), reads=[dc], writes=[dbk])
            else:
                P.op("pe", lambda e, bk=bk, ns=ns: e.matmul(bk[0:64, :], lhsT=w2[:], rhs=h1T[:, ns], start=True, stop=True), reads=[dc, dh1], writes=[dbk])
            u, du = u_r.next()
            fbc = 3 + layer
            P.op("dve", lambda e, u=u, bk=bk, fbc=fbc: e.tensor_scalar(out=u[:], in0=bk[0:64, :], scalar1=cols[:, 1:2], scalar2=cols[:, fbc:fbc + 1], op0=ALU.mult, op1=ALU.add),
                 reads=[dbk, dcol, dc], writes=[du])
            s2, ds2 = s_r.next()
            s4, ds4 = s_r.next()
            P.op("act", lambda e, u=u, s2=s2: e.activation(out=s2[:], in_=u[:], func=AF.Sin, scale=0.5), reads=[du], writes=[ds2])
            P.op("act", lambda e, u=u, s4=s4: e.activation(out=s4[:], in_=u[:], func=AF.Sin, scale=0.25), reads=[du], writes=[ds4])
            P.op("dve", lambda e, s4=s4: e.tensor_tensor(out=s4[:], in0=s4[:], in1=s4[:], op=ALU.mult), reads=[ds4], writes=[ds4])
            P.op("dve", lambda e, s4=s4: e.tensor_scalar(out=s4[:], in0=s4[:], scalar1=-2.0, scalar2=1.0, op0=ALU.mult, op1=ALU.add), reads=[ds4], writes=[ds4])
            if layer == 0:
                P.op("dve", lambda e, s2=s2, s4=s4, ns=ns: e.scalar_tensor_tensor(out=h1T[:, ns], in0=s2[:], scalar=2.0, in1=s4[:], op0=ALU.mult, op1=ALU.mult),
                     reads=[ds2, ds4], writes=[dh1], merge=(nt > 0))
            else:
                P.op("dve", lambda e, s2=s2, s4=s4, ns=ns: e.scalar_tensor_tensor(out=h2T[:, ns], in0=s2[:], scalar=2.0, in1=s4[:], op0=ALU.mult, op1=ALU.mult),
                     reads=[ds2, ds4], writes=[dh2], merge=(nt > 0))
    Kt = P.sb("Ksd", [128, 16, 4, 512], BF16)
    dKt = [Dep() for _ in range(16)]
    ones = P.sb("onesb", [128, 128], BF16)
    dones = Dep()
    P.op("pool", lambda e: e.memset(ones[:], 1.0), writes=[dones])
    dec_r = Ring(P, "dec", 2, [128, 512], F32)
    sq_r = Ring(P, "sq", 3, [128, 512], BF16)
    kf32_r = Ring(P, "kf32", 2, [128, 512], F32)
    kb32_r = Ring(P, "kb32", 2, [128, 512], F32)
    ssq = [C.bank(hold=True), C.bank(hold=True)]
    for tt in range(16):
        dec, ddec = dec_r.next()
        P.dma("sp", dec[:], dec_d[tt], writes=[ddec])
        for o in range(2):
            bf_, dbf_ = C.bank()
            bb_, dbb_ = C.bank()
            for (bk, dbk, q) in ((bf_, dbf_, 2 * o), (bb_, dbb_, 2 * o + 1)):
                P.op("pe", lambda e, bk=bk, tt=tt, q=q: e.matmul(bk[:], lhsT=h2T[:, tt * 128:(tt + 1) * 128], rhs=w3[:, q * 512:(q + 1) * 512], start=True, stop=True),
                     reads=[dh2, dc], writes=[dbk])
            kf32, dkf32 = kf32_r.next()
            kb32, dkb32 = kb32_r.next()
            P.op("dve", lambda e, kf32=kf32, bf_=bf_, dec=dec: e.tensor_tensor(out=kf32[:], in0=bf_[:], in1=dec[:], op=ALU.mult), reads=[dbf_, ddec], writes=[dkf32])
            P.op("dve", lambda e, kb32=kb32, bb_=bb_, dec=dec: e.tensor_tensor(out=kb32[:], in0=bb_[:], in1=dec[:], op=ALU.mult), reads=[dbb_, ddec], writes=[dkb32])
            if tt == 0:
                P.op("pool", lambda e, kb32=kb32: e.memset(kb32[0:1, :], 0.0), reads=[dkb32], writes=[dkb32])
            P.op("pool", lambda e, tt=tt, o=o, kf32=kf32, kb32=kb32: e.tensor_tensor(out=Kt[:, tt, 2 * o, :], in0=kf32[:], in1=kb32[:], op=ALU.add),
                 reads=[dkf32, dkb32], writes=[dKt[tt]], merge=True)
            P.op("dve", lambda e, tt=tt, o=o, kf32=kf32, kb32=kb32: e.tensor_tensor(out=Kt[:, tt, 2 * o + 1, :], in0=kf32[:], in1=kb32[:], op=ALU.subtract),
                 reads=[dkf32, dkb32], writes=[dKt[tt]], merge=True)
        for q in range(4):
            sq, dsq = sq_r.next()
            P.op("act", lambda e, sq=sq, tt=tt, q=q: e.activation(out=sq[:], in_=Kt[:, tt, q, :], func=AF.Square), reads=[dKt[tt]], writes=[dsq])
            sb_, dsb_ = ssq[q // 2]
            first = (tt == 0 and q % 2 == 0)
            last = (tt == 15 and q % 2 == 1)
            P.op("pe", lambda e, sb_=sb_, sq=sq, first=first, last=last: e.matmul(sb_[:], lhsT=ones[:], rhs=sq[:], start=first, stop=last),
                 reads=[dsq, dones], writes=[dsb_], merge=(not first))
    rs = P.sb("hyrs", [128, 2, 512], F32)
    drs = Dep()
    for o in range(2):
        sb_, dsb_ = ssq[o]
        P.op("dve", lambda e, o=o, sb_=sb_: e.tensor_scalar(out=rs[:, o, :], in0=sb_[:], scalar1=0.5, scalar2=1e-12, op0=ALU.mult, op1=ALU.add), reads=[dsb_], writes=[drs], merge=(o > 0))
    C.release_all()
    P.op("act", lambda e: e.activation(out=rs[:], in_=rs[:], func=AF.Sqrt), reads=[drs], writes=[drs])
    P.op("dve", lambda e: e.reciprocal(out=rs[:], in_=rs[:]), reads=[drs], writes=[drs])
    wf_r = Ring(P, "wf", 2, [128, 16, 128], BF16)
    kf_r = Ring(P, "kf", 3, [128, 512], F32)
    for ft in range(32):
        wf, dwf = wf_r.next()
        P.dma("sp", wf[:], Wf_d[ft], writes=[dwf])
        for o in range(2):
            bk, dbk = C.bank()
            sel = 2 * o if ft < 16 else 2 * o + 1
            mm_acc(P, bk[:], [(wf[:, tt, :], Kt[:, tt, sel, :]) for tt in range(16)], reads=[dwf] + dKt, dwrite=dbk)
            kf, dkf = kf_r.next()
            P.op("dve", lambda e, kf=kf, bk=bk, o=o: e.tensor_tensor(out=kf[:], in0=bk[:], in1=rs[:, o, :], op=ALU.mult), reads=[dbk, drs], writes=[dkf])
            if ft < 16:
                P.op("dve", lambda e, kf=kf, o=o: e.tensor_tensor(out=kf[:], in0=kf[:], in1=skipb[:, o, :], op=ALU.add), reads=[dkf, dc], writes=[dkf])
            elif ft == 16:
                bn, dbn = C.bank()
                mm_acc(P, bn[0:32, :], [(wf[:, tt, 0:32], Kt[:, tt, 2 * o, :]) for tt in range(16)], reads=[dwf] + dKt, dwrite=dbn)
                P.op("dve", lambda e, kf=kf, bn=bn, o=o: e.tensor_tensor(out=kf[0:1, :], in0=bn[0:1, :], in1=rs[0:1, o, :], op=ALU.mult), reads=[dbn, drs, dkf], writes=[dkf])
                P.op("pool", lambda e, kf=kf, o=o: e.tensor_tensor(out=kf[0:1, :], in0=kf[0:1, :], in1=skipb[0:1, o, :], op=ALU.add), reads=[dkf, dc], writes=[dkf])
            P.op("act", lambda e, kf=kf, ft=ft: e.activation(out=kf[:], in_=kf[:], func=AF.Copy, scale=cR[:, ft:ft + 1]), reads=[dkf, dc], writes=[dkf])
            P.dma("act", Kf_d[ft, o], kf[:], reads=[dkf], writes=[C.dd("hy_Kf")], merge=True)


def stage_hy_prep(C):
    P = C.P
    P.begin_stage()
    hbT = C.D("hbT", [1536, 2048], F32)
    cw_d = C.D("hy_cw", [128, 12, 3], F32)
    cb_d = C.D("hy_cb", [128, 12], F32)
    ident_d = C.D("ident", [128, 128], F32)
    hv = C.D("hv_tm", [2048, 512], BF16)
    hx1 = C.D("hx1_tm", [2048, 512], F32)
    hx2T = C.D("hx2T", [512, 2048], F32)
    dc = Dep()
    cw = P.sb("cw", [128, 12, 3], F32)
    cb = P.sb("cb", [128, 12], F32)
    ident = P.sb("identh", [128, 128], F32)
    P.dma("sp", cw[:], cw_d, writes=[dc])
    P.dma("sp", cb[:], cb_d, writes=[dc], merge=True)
    P.dma("sp", ident[:], ident_d, writes=[dc], merge=True)
    xin_r = Ring(P, "hxin", 2, [128, 2048], F32)
    y_r = Ring(P, "hy", 2, [128, 2048], F32)
    sv_r = Ring(P, "hsv", 2, [128, 16, 128], BF16)
    sx_r = Ring(P, "hsx", 2, [128, 16, 128], F32)
    for ch in range(12):
        xin, dxin = xin_r.next()
        P.dma("sp", xin[:], hbT[ch * 128:(ch + 1) * 128, :], reads=[C.dd("hbT")], writes=[dxin])
        y, dy = y_r.next()
        P.op("act", lambda e, y=y, xin=xin, ch=ch: e.activation(out=y[:], in_=xin[:], func=AF.Identity, bias=cb[:, ch:ch + 1], scale=cw[:, ch, 1:2]),
             reads=[dxin, dc], writes=[dy])
        P.op("dve", lambda e, y=y, xin=xin, ch=ch: e.scalar_tensor_tensor(out=y[:, 1:2048], in0=xin[:, 0:2047], scalar=cw[:, ch, 0:1], in1=y[:, 1:2048], op0=ALU.mult, op1=ALU.add),
             reads=[dxin, dc, dy], writes=[dy])
        P.op("dve", lambda e, y=y, xin=xin, ch=ch: e.scalar_tensor_tensor(out=y[:, 0:2047], in0=xin[:, 1:2048], scalar=cw[:, ch, 2:3], in1=y[:, 0:2047], op0=ALU.mult, op1=ALU.add),
             reads=[dxin, dc, dy], writes=[dy])
        if ch >= 8:
            P.dma("act", hx2T[(ch - 8) * 128:(ch - 7) * 128, :], y[:], reads=[dy], writes=[C.dd("hx2T")], merge=True)
            continue
        stg, dstg = (sv_r if ch < 4 else sx_r).next()
        for g in range(4):
            bk, dbk = C.bank()
            for k4 in range(4):
                tt = g * 4 + k4
                P.op("pe", lambda e, bk=bk, k4=k4, tt=tt, y=y: e.transpose(bk[:, k4 * 128:(k4 + 1) * 128], y[:, tt * 128:(tt + 1) * 128], ident[:]),
                     reads=[dy, dc], writes=[dbk], merge=(k4 > 0))
            P.op("act", lambda e, stg=stg, g=g, bk=bk: e.copy(out=stg[:, g * 4:(g + 1) * 4, :], in_=bk[:].rearrange("p (k t) -> p k t", k=4)),
                 reads=[dbk], writes=[dstg], merge=(g > 0))
        if ch < 4:
            P.dma("act", hv.rearrange("(tt p) c -> p tt c", p=128)[:, :, ch * 128:(ch + 1) * 128], stg[:], reads=[dstg], writes=[C.dd("hv_tm")], merge=True)
        else:
            P.dma("act", hx1.rearrange("(tt p) c -> p tt c", p=128)[:, :, (ch - 4) * 128:(ch - 3) * 128], stg[:], reads=[dstg], writes=[C.dd("hx1_tm")], merge=True)


def stage_hy_conv(C):
    P = C.P
    P.begin_stage()
    hv = C.D("hv_tm", [2048, 512], BF16)
    hx1 = C.D("hx1_tm", [2048, 512], F32)
    hx2T = C.D("hx2T", [512, 2048], F32)
    Kf_d = C.D("hy_Kf", [32, 2, 128, 512], F32)
    Wf_d = C.D("hy_Wf", [32, 128, 16, 128], BF16)
    WA_d = C.D("hy_WA", [16, 128, 32, 128], BF16)
    WB_d = C.D("hy_WB", [4, 2, 128, 16, 512], BF16)
    mixT = C.D("mixT", [1024, 2048], BF16)
    ztm = P.sb("ztm", [128, 16, 512], BF16)
    dz = [Dep() for _ in range(16)]
    hvv = hv.rearrange("(tt p) c -> p tt c", p=128)
    for tt in range(16):
        P.dma("sp", ztm[:, tt, :], hvv[:, tt, :], reads=[C.dd("hv_tm")], writes=[dz[tt]])
    Yt = P.sb("Yt", [128, 32, 512], BF16)
    dY = [Dep() for _ in range(32)]
    wf_r = Ring(P, "cwf", 3, [128, 16, 128], BF16)
    kf_r = Ring(P, "ckf", 4, [128, 512], F32)
    t_r = Ring(P, "ct", 4, [128, 512], F32)
    wa_r = Ring(P, "cwa", 2, [128, 32, 128], BF16)
    wb_r = Ring(P, "cwb", 2, [128, 16, 512], BF16)
    x_r = Ring(P, "cx", 3, [128, 512], F32)
    zo_r = Ring(P, "czo", 3, [128, 512], BF16)
    for o in range(2):
        for j in range(16):
            ub = []
            for part in range(2):
                ft = part * 16 + j
                wf, dwf = wf_r.next()
                P.dma("sp", wf[:], Wf_d[ft], writes=[dwf])
                bk, dbk = C.bank()
                mm_acc(P, bk[:], [(wf[:, tt, :], ztm[:, tt, :]) for tt in range(16)], reads=[dwf] + dz, dwrite=dbk)
                ub.append((bk, dbk))
            kre, dkre = kf_r.next()
            kim, dkim = kf_r.next()
            P.dma("sp", kre[:], Kf_d[j, o], reads=[C.dd("hy_Kf")], writes=[dkre])
            P.dma("sp", kim[:], Kf_d[16 + j, o], reads=[C.dd("hy_Kf")], writes=[dkim])
            (ure, dure), (uim, duim) = ub
            t1, dt1 = t_r.next()
            t2, dt2 = t_r.next()
            P.op("dve", lambda e, t1=t1, ure=ure, kre=kre: e.tensor_tensor(out=t1[:], in0=ure[:], in1=kre[:], op=ALU.mult), reads=[dure, dkre], writes=[dt1])
            P.op("dve", lambda e, t2=t2, uim=uim, kim=kim: e.tensor_tensor(out=t2[:], in0=uim[:], in1=kim[:], op=ALU.mult), reads=[duim, dkim], writes=[dt2])
            P.op("pool", lambda e, j=j, t1=t1, t2=t2: e.tensor_tensor(out=Yt[:, j, :], in0=t1[:], in1=t2[:], op=ALU.subtract), reads=[dt1, dt2], writes=[dY[j]])
            if j == 0:
                P.op("pool", lambda e, t1=t1: e.tensor_copy(out=Yt[0:1, 0, :], in_=t1[0:1, :]), reads=[dt1, dY[0]], writes=[dY[0]])
            t3, dt3 = t_r.next()
            t4, dt4 = t_r.next()
            P.op("dve", lambda e, t3=t3, ure=ure, kim=kim: e.tensor_tensor(out=t3[:], in0=ure[:], in1=kim[:], op=ALU.mult), reads=[dure, dkim], writes=[dt3])
            P.op("dve", lambda e, t4=t4, uim=uim, kre=kre: e.tensor_tensor(out=t4[:], in0=uim[:], in1=kre[:], op=ALU.mult), reads=[duim, dkre], writes=[dt4])
            P.op("pool", lambda e, j=j, t3=t3, t4=t4: e.tensor_tensor(out=Yt[:, 16 + j, :], in0=t3[:], in1=t4[:], op=ALU.add), reads=[dt3, dt4], writes=[dY[16 + j]])
            if j == 0:
                P.op("pool", lambda e, t2=t2: e.tensor_copy(out=Yt[0:1, 16, :], in_=t2[0:1, :]), reads=[dt2, dY[16]], writes=[dY[16]])
        if o == 0:
            for tt in range(16):
                wa, dwa = wa_r.next()
                P.dma("sp", wa[:], WA_d[tt], writes=[dwa])
                bk, dbk = C.bank()
                mm_acc(P, bk[:], [(wa[:, kt, :], Yt[:, kt, :]) for kt in range(32)], reads=[dwa] + dY, dwrite=dbk)
                xt, dxt = x_r.next()
                P.dma("sp", xt[:], hx1[tt * 128:(tt + 1) * 128, :], reads=[C.dd("hx1_tm")], writes=[dxt])
                P.op("dve", lambda e, tt=tt, bk=bk, xt=xt: e.tensor_tensor(out=ztm[:, tt, :], in0=bk[:], in1=xt[:], op=ALU.mult), reads=[dbk, dxt], writes=[dz[tt]])
        else:
            for nt in range(4):
                banks = [C.bank() for _ in range(4)]
                for hf in range(2):
                    wb, dwb = wb_r.next()
                    P.dma("sp", wb[:], WB_d[nt, hf], writes=[dwb])
                    for cc in range(4):
                        bk, dbk = banks[cc]
                        for k in range(16):
                            first = (hf == 0 and k == 0)
                            last = (hf == 1 and k == 15)
                            kt = hf * 16 + k
                            P.op("pe", lambda e, bk=bk, kt=kt, cc=cc, wb=wb, k=k, first=first, last=last: e.matmul(
                                bk[:], lhsT=Yt[:, kt, cc * 128:(cc + 1) * 128], rhs=wb[:, k, :], start=first, stop=last),
                                reads=[dY[kt], dwb], writes=[dbk], merge=(not first))
                for cc in range(4):
                    bk, dbk = banks[cc]
                    xt, dxt = x_r.next()
                    P.dma("sp", xt[:], hx2T[cc * 128:(cc + 1) * 128, nt * 512:(nt + 1) * 512], reads=[C.dd("hx2T")], writes=[dxt])
                    zo, dzo = zo_r.next()
                    P.op("dve", lambda e, zo=zo, bk=bk, xt=xt: e.tensor_tensor(out=zo[:], in0=bk[:], in1=xt[:], op=ALU.mult), reads=[dbk, dxt], writes=[dzo])
                    P.dma("act", mixT[512 + cc * 128:512 + (cc + 1) * 128, nt * 512:(nt + 1) * 512], zo[:], reads=[dzo], writes=[C.dd("mixT")], merge=True)


MLA_SCALE = 96.0 ** -0.5


def mla_consts():
    inv = 1.0 / (10000.0 ** (np.arange(0, 32, 2, dtype=np.float32) / 32.0))
    ang = np.arange(2048, dtype=np.float32)[:, None] * inv[None, :].astype(np.float32)
    cos = np.cos(ang).astype(np.float32).T
    sin = np.sin(ang).astype(np.float32).T
    cos2 = np.concatenate([cos, cos], axis=0)
    sin2 = np.concatenate([-sin, sin], axis=0)
    return {"mla_cs2": np.ascontiguousarray(np.stack([cos2, sin2], axis=1)).astype(np.float32)}


def stage_mla1(C, xT_name):
    P = C.P
    P.begin_stage()
    xT = C.D(xT_name, [1024, 2048], BF16)
    wi_d = C.D("mla_w_in", [1024, 672], F32)
    wsw_d = C.D("mla_w_in_sw", [1024, 96], F32)
    gc_d = C.D("mla_gcols", [128, 5], F32)
    cs_d = C.D("mla_cs2", [32, 2, 2048], F32)
    nT_d = C.D("mla_nT", [640, 2048], BF16)
    kr_d = C.D("mla_krT", [32, 2048], BF16)
    xs = P.sb("mxs", [128, 8, 2048], BF16)
    dxs = [Dep() for _ in range(8)]
    for kc in range(8):
        P.dma("sp", xs[:, kc, :], xT[kc * 128:(kc + 1) * 128, :], reads=[C.dd(xT_name)], writes=[dxs[kc]])
    wi, dwi = load_fm_bf16(C, "mwi", wi_d, 8, 672)
    wsw, dwsw = load_fm_bf16(C, "mwsw", wsw_d, 8, 96)
    dc = Dep()
    gcol = P.sb("mgc", [128, 5], F32)
    P.dma("sp", gcol[:], gc_d, writes=[dc])
    cs = P.sb("mcs", [96, 2, 2048], F32)
    P.dma("sp", cs[64:96, :, :], cs_d, writes=[dc], merge=True)
    ones = P.sb("mones", [128, 128], BF16)
    P.op("pool", lambda e: e.memset(ones[:], 1.0), writes=[dc], merge=True)
    hT = P.sb("mhT", [128, 5, 2048], F32)
    nT = P.sb("mnT", [128, 5, 2048], BF16)
    dhT = Dep()
    dnT = [Dep() for _ in range(5)]
    sq_r = Ring(P, "msq", 3, [128, 512], BF16)
    r_r = Ring(P, "mr", 2, [128, 512], F32)
    for (chunks, n) in (((0, 1, 2), 384.0), ((3, 4), 256.0)):
        for nt in range(4):
            ns = slice(nt * 512, (nt + 1) * 512)
            sbk, dsbk = C.bank(hold=True)
            for ci, c in enumerate(chunks):
                bk, dbk = C.bank()
                mm_acc(P, bk[:], [(wi[:, kc, c * 128:(c + 1) * 128], xs[:, kc, ns]) for kc in range(8)], reads=dxs + dwi, dwrite=dbk)
                P.op("act", lambda e, c=c, ns=ns, bk=bk: e.copy(out=hT[:, c, ns], in_=bk[:]), reads=[dbk], writes=[dhT], merge=True)
                sq, dsq = sq_r.next()
                P.op("act", lambda e, sq=sq, bk=bk: e.activation(out=sq[:], in_=bk[:], func=AF.Square), reads=[dbk], writes=[dsq])
                P.op("pe", lambda e, sbk=sbk, sq=sq, ci=ci, chunks=chunks: e.matmul(sbk[:], lhsT=ones[:], rhs=sq[:], start=(ci == 0), stop=(ci == len(chunks) - 1)),
                     reads=[dsq, dc], writes=[dsbk], merge=(ci > 0))
            r, dr = r_r.next()
            P.op("dve", lambda e, r=r, sbk=sbk, n=n: e.tensor_scalar(out=r[:], in0=sbk[:], scalar1=1.0 / n, scalar2=EPS, op0=ALU.mult, op1=ALU.add), reads=[dsbk], writes=[dr])
            C.release_all()
            P.op("act", lambda e, r=r: e.activation(out=r[:], in_=r[:], func=AF.Sqrt), reads=[dr], writes=[dr])
            P.op("dve", lambda e, r=r: e.reciprocal(out=r[:], in_=r[:]), reads=[dr], writes=[dr])
            for c in chunks:
                P.op("dve", lambda e, c=c, ns=ns, r=r: e.scalar_tensor_tensor(out=nT[:, c, ns], in0=hT[:, c, ns], scalar=gcol[:, c:c + 1], in1=r[:], op0=ALU.mult, op1=ALU.mult),
                     reads=[dhT, dr, dc], writes=[dnT[c]], merge=True)
    for c in range(5):
        P.dma("act", nT_d[c * 128:(c + 1) * 128, :], nT[:, c, :], reads=[dnT[c]], writes=[C.dd("mla_nT")], merge=True)
    krT = P.sb("mkr", [96, 2048], BF16)
    dkr = Dep()
    ta_r = Ring(P, "mta", 2, [96, 512], F32)
    tb_r = Ring(P, "mtb", 2, [96, 512], F32)
    for nt in range(4):
        ns = slice(nt * 512, (nt + 1) * 512)
        bk, dbk = C.bank()
        bs, dbs = C.bank()
        mm_acc(P, bk[0:96, :], [(wi[:, kc, 576:672], xs[:, kc, ns]) for kc in range(8)], reads=dxs + dwi, dwrite=dbk)
        mm_acc(P, bs[0:96, :], [(wsw[:, kc, :], xs[:, kc, ns]) for kc in range(8)], reads=dxs + dwsw, dwrite=dbs)
        ta, dta = ta_r.next()
        tb, dtb = tb_r.next()
        P.op("dve", lambda e, ta=ta, bk=bk, ns=ns: e.tensor_tensor(out=ta[64:96, :], in0=bk[64:96, :], in1=cs[64:96, 0, ns], op=ALU.mult), reads=[dbk, dc], writes=[dta])
        P.op("dve", lambda e, tb=tb, bs=bs, ns=ns: e.tensor_tensor(out=tb[64:96, :], in0=bs[64:96, :], in1=cs[64:96, 1, ns], op=ALU.mult), reads=[dbs, dc], writes=[dtb])
        P.op("pool", lambda e, ta=ta, tb=tb, ns=ns: e.tensor_tensor(out=krT[64:96, ns], in0=ta[64:96, :], in1=tb[64:96, :], op=ALU.add), reads=[dta, dtb], writes=[dkr], merge=(nt > 0))
    P.dma("pool", kr_d, krT[64:96, :], reads=[dkr], writes=[C.dd("mla_krT")])


def stage_mla2(C):
    P = C.P
    P.begin_stage()
    nT_d = C.D("mla_nT", [640, 2048], BF16)
    kr_d = C.D("mla_krT", [32, 2048], BF16)
    cs_d = C.D("mla_cs2", [32, 2, 2048], F32)
    wq_d = C.D("mla_w_q_up", [384, 1536], F32)
    wqs_d = C.D("mla_w_q_sw", [384, 1536], F32)
    wk_d = C.D("mla_w_kv_k", [256, 1024], F32)
    wv_d = C.D("mla_w_kv_v", [256, 1024], F32)
    mixT = C.D("mixT", [1024, 2048], BF16)
    nT = P.sb("anT", [128, 5, 2048], BF16)
    dnT = [Dep() for _ in range(5)]
    for c in range(5):
        P.dma("sp", nT[:, c, :], nT_d[c * 128:(c + 1) * 128, :], reads=[C.dd("mla_nT")], writes=[dnT[c]])
    dq = dnT[0:3]
    dkv = dnT[3:5]
    dc = Dep()
    KRT = P.sb("aKRT", [96, 2048], BF16)
    P.dma("sp", KRT[64:96, :], kr_d, reads=[C.dd("mla_krT")], writes=[dc])
    cs = P.sb("acs", [96, 2, 2048], F32)
    P.dma("sp", cs[64:96, :, :], cs_d, writes=[dc], merge=True)
    wq, dwq = load_fm_bf16(C, "awq", wq_d, 3, 1536)
    wqs, dwqs = load_fm_bf16(C, "awqs", wqs_d, 3, 1536)
    wk, dwk = load_fm_bf16(C, "awk", wk_d, 2, 1024)
    wv, dwv = load_fm_bf16(C, "awv", wv_d, 2, 1024)
    onesf = P.sb("aones", [128, 64], F32)
    P.op("pool", lambda e: e.memset(onesf[:], 1.0), writes=[dc], merge=True)
    Vx = P.sb("aVx", [128, 16, 16, 65], BF16)
    dV = Dep()
    P.op("pool", lambda e: e.memset(Vx[:], 1.0), writes=[dV])
    for tt in range(16):
        for half in range(2):
            bk, dbk = C.bank()
            mm_acc(P, bk[:], [(nT[:, 3 + kc, tt * 128:(tt + 1) * 128], wv[:, kc, half * 512:(half + 1) * 512]) for kc in range(2)], reads=dkv + dwv, dwrite=dbk)
            P.op("act", lambda e, tt=tt, half=half, bk=bk: e.copy(out=Vx[:, tt, half * 8:(half + 1) * 8, 0:64], in_=bk[:].rearrange("p (h d) -> p h d", h=8)),
                 reads=[dbk], writes=[dV], merge=True)
    QT_r = Ring(P, "aQT", 2, [96, 2048], BF16)
    KT_r = Ring(P, "aKT", 2, [96, 2048], BF16)
    ta_r = Ring(P, "ata", 2, [96, 512], F32)
    tb_r = Ring(P, "atb", 2, [96, 512], F32)
    p_r = Ring(P, "apT", 4, [128, 512], BF16)
    rd_r = Ring(P, "ard", 2, [65, 512], F32)
    bs_r = Ring(P, "absb", 2, [64, 512], F32)
    yo_r = Ring(P, "ayo", 3, [64, 512], BF16)
    def alloc_head():
        QT, dQT = QT_r.next()
        KT, dKT = KT_r.next()
        return (QT, dQT, KT, dKT)

    def proj_piece(h, nt, hd):
        QT, dQT, KT, dKT = hd
        ns = slice(nt * 512, (nt + 1) * 512)
        bq, dbq = C.bank()
        bs, dbs = C.bank()
        mm_acc(P, bq[0:96, :], [(wq[:, kc, h * 96:(h + 1) * 96], nT[:, kc, ns]) for kc in range(3)], reads=dq + dwq, dwrite=dbq)
        mm_acc(P, bs[0:96, :], [(wqs[:, kc, h * 96:(h + 1) * 96], nT[:, kc, ns]) for kc in range(3)], reads=dq + dwqs, dwrite=dbs)
        P.op("dve", lambda e: e.tensor_copy(out=QT[0:64, ns], in_=bq[0:64, :]), reads=[dbq], writes=[dQT], merge=(nt > 0))
        ta, dta = ta_r.next()
        tb, dtb = tb_r.next()
        P.op("dve", lambda e: e.tensor_tensor(out=ta[64:96, :], in0=bq[64:96, :], in1=cs[64:96, 0, ns], op=ALU.mult), reads=[dbq, dc], writes=[dta])
        P.op("dve", lambda e: e.tensor_tensor(out=tb[64:96, :], in0=bs[64:96, :], in1=cs[64:96, 1, ns], op=ALU.mult), reads=[dbs, dc], writes=[dtb])
        P.op("pool", lambda e: e.tensor_tensor(out=QT[64:96, ns], in0=ta[64:96, :], in1=tb[64:96, :], op=ALU.add), reads=[dta, dtb], writes=[dQT], merge=True)
        bkk, dbkk = C.bank()
        mm_acc(P, bkk[0:64, :], [(wk[:, kc, h * 64:(h + 1) * 64], nT[:, 3 + kc, ns]) for kc in range(2)], reads=dkv + dwk, dwrite=dbkk)
        P.op("dve", lambda e: e.tensor_copy(out=KT[0:64, ns], in_=bkk[0:64, :]), reads=[dbkk], writes=[dKT], merge=(nt > 0))
        if nt == 3:
            P.op("pool", lambda e: e.tensor_copy(out=KT[64:96, :], in_=KRT[64:96, :]), reads=[dc], writes=[dKT], merge=True)

    pendA = []
    pendB = []

    def make_epilogue(accb, h, qs):
        acc, dacc = accb
        rd, drd = rd_r.next()

        def epi_a():
            P.op("dve", lambda e: e.reciprocal(out=rd[64:65, :], in_=acc[64:65, :]), reads=[dacc], writes=[drd])

        def epi_b():
            bb, dbb = C.bank()
            P.op("pe", lambda e: e.matmul(bb[0:64, :], lhsT=onesf[64:65, 0:64], rhs=rd[64:65, :], start=True, stop=True), reads=[drd, dc], writes=[dbb])
            bsb, dbsb = bs_r.next()
            P.op("dve", lambda e: e.tensor_copy(out=bsb[:], in_=bb[0:64, :]), reads=[dbb], writes=[dbsb])
            yo, dyo = yo_r.next()
            P.op("dve", lambda e: e.tensor_tensor(out=yo[:], in0=acc[0:64, :], in1=bsb[:], op=ALU.mult), reads=[dacc, dbsb], writes=[dyo])
            C.release(accb)
            P.dma("pool", mixT[h * 64:(h + 1) * 64, qs], yo[:], reads=[dyo], writes=[C.dd("mixT")], merge=True)
        return epi_a, epi_b

    def attention(h, hd, nxt_hd):
        QT, dQT, KT, dKT = hd
        for qc in range(4):
            qs = slice(qc * 512, (qc + 1) * 512)
            accb = C.bank(hold=True)
            acc, dacc = accb

            def pv(kt, pT, dpT, acc=acc, dacc=dacc, h=h):
                P.op("pe", lambda e, acc=acc, kt=kt, h=h, pT=pT: e.matmul(acc[0:65, :], lhsT=Vx[:, kt, h, :], rhs=pT[:], start=(kt == 0), stop=(kt == 15)),
                     reads=[dV, dpT], writes=[dacc], merge=(kt > 0))
            pend = None
            for kt in range(16):
                sb_, dsb_ = C.bank()
                P.op("pe", lambda e, sb_=sb_, kt=kt, qs=qs: e.matmul(sb_[:], lhsT=KT[0:96, kt * 128:(kt + 1) * 128], rhs=QT[0:96, qs], start=True, stop=True),
                     reads=[dKT, dQT], writes=[dsb_])
                pT, dpT = p_r.next()
                P.op("act", lambda e, pT=pT, sb_=sb_: e.activation(out=pT[:], in_=sb_[:], func=AF.Exp, scale=MLA_SCALE), reads=[dsb_], writes=[dpT])
                if pend is not None:
                    pv(*pend)
                pend = (kt, pT, dpT)
                if kt == 1 and pendA:
                    pendA.pop(0)()
                if kt == 9 and pendB:
                    pendB.pop(0)()
                if kt == 5 and nxt_hd is not None:
                    proj_piece(h + 1, qc, nxt_hd)
            pv(*pend)
            ea, eb = make_epilogue(accb, h, qs)
            pendA.append(ea)
            pendB.append(eb)

    hd = alloc_head()
    for nt in range(4):
        proj_piece(0, nt, hd)
    for h in range(16):
        nxt_hd = alloc_head() if h + 1 < 16 else None
        attention(h, hd, nxt_hd)
        hd = nxt_hd
    while pendA:
        pendA.pop(0)()
    while pendB:
        pendB.pop(0)()


def _rep128(v):
    v = np.asarray(v, np.float32)
    return np.ascontiguousarray(np.broadcast_to(v[None, :], (128, v.shape[0])))


def shared_inputs(inp):
    f32 = lambda a: np.ascontiguousarray(np.asarray(a, np.float32))
    s = {}
    s["ab_w_in"] = f32(inp["ab_w_in"][0])
    s["na_tab"] = na_tables(np.asarray(inp["na_rpb"][0], np.float32))
    s.update(hyena_consts())
    s["hy_f_w1"] = f32(inp["hy_f_w1"][0])
    s["hy_f_w2"] = f32(inp["hy_f_w2"][0])
    s["hy_f_w3"] = f32(inp["hy_f_w3"][0])
    s["hy_cols"] = f32(np.stack([inp["hy_f_b1"][0], inp["hy_f_freq"][0], inp["hy_f_b2"][0]], axis=1))
    s["hy_skipb"] = np.stack([_rep128(inp["hy_skip"][0][0]), _rep128(inp["hy_skip"][0][1])])
    s["hy_cw"] = f32(np.asarray(inp["hy_conv_w"][0]).reshape(3, 12, 128).transpose(2, 1, 0))
    s["hy_cb"] = f32(np.asarray(inp["hy_conv_b"][0]).reshape(12, 128).T)
    s["ident"] = np.eye(128, dtype=np.float32)
    esel = np.zeros((16, 16, 128), np.float32)
    for e in range(16):
        esel[e, e, :] = 1.0
    s["esel"] = esel
    s["iota_col"] = (np.arange(16)[None, :] * 128 + np.arange(128)[:, None]).astype(np.float32)
    s["iota_row"] = _rep128(np.arange(2048, dtype=np.float32))
    s["ab_w_out"] = f32(inp["ab_w_out"][0])
    w_in = np.asarray(inp["mla_w_in"][0], np.float32)
    perm = np.concatenate([np.arange(16, 32), np.arange(0, 16)])
    s["mla_w_in"] = f32(w_in)
    s["mla_w_in_sw"] = f32(np.concatenate([w_in[:, 576:640], w_in[:, 640 + perm]], axis=1))
    wq = np.asarray(inp["mla_w_q_up"][0], np.float32)
    wqs = wq.reshape(384, 16, 96).copy()
    wqs[:, :, 64:] = wqs[:, :, 64 + perm]
    s["mla_w_q_up"] = f32(wq)
    s["mla_w_q_sw"] = f32(wqs.reshape(384, 1536))
    wkv = np.asarray(inp["mla_w_kv_up"][0], np.float32).reshape(256, 16, 128)
    s["mla_w_kv_k"] = f32(wkv[:, :, :64].reshape(256, 1024))
    s["mla_w_kv_v"] = f32(wkv[:, :, 64:].reshape(256, 1024))
    s["mla_gcols"] = f32(np.concatenate([np.asarray(inp["mla_q_norm"][0]).reshape(3, 128).T,
                                         np.asarray(inp["mla_kv_norm"][0]).reshape(2, 128).T], axis=1))
    s.update(mla_consts())
    s["mla_w_out"] = f32(inp["mla_w_out"][0])
    for li in range(2):
        s[f"ln1_g{li}"] = _rep128(inp["ln1_g"][li])
        s[f"ln1_b{li}"] = _rep128(inp["ln1_b"][li])
        s[f"ln2_g{li}"] = _rep128(inp["ln2_g"][li])
        s[f"ln2_b{li}"] = _rep128(inp["ln2_b"][li])
        s[f"moe_router{li}"] = f32(inp["moe_router"][li])
        s[f"moe_w_gate{li}"] = f32(inp["moe_w_gate"][li])
        s[f"moe_w_up{li}"] = f32(inp["moe_w_up"][li])
        s[f"moe_w_down{li}"] = f32(inp["moe_w_down"][li])
        s[f"ple_gate{li}"] = f32(inp["ple_gate"][li])
        s[f"ple_proj{li}"] = f32(inp["ple_proj"][li])
    return s


PER_CORE = ("x_tm", "xT", "pT0", "pT1")


def build_full(shared_names):
    C = Ctx(ext_in=set(shared_names) | set(PER_CORE), ext_out={"out"})
    stage_a1(C)
    stage_a2(C)
    stage_hy_filter(C)
    stage_hy_prep(C)
    stage_hy_conv(C)
    stage_proj_ln(C, "ab_w_out", "x_tm", "ln1", 0, "x1_0")
    stage_moe1(C, 0, "x1_0")
    stage_moe2(C, 0, "x1_0", "x2_0")
    stage_ple(C, 0, "x2_0", "x3_0", True)
    stage_mla1(C, "x3_0T")
    stage_mla2(C)
    stage_proj_ln(C, "mla_w_out", "x3_0", "ln1", 1, "x1_1")
    stage_moe1(C, 1, "x1_1")
    stage_moe2(C, 1, "x1_1", "x2_1")
    stage_ple(C, 1, "x2_1", "out", False)
    C.P.finish()
    return C


def kernel(**inputs):
    inp = {k: np.asarray(v) for k, v in inputs.items()}
    shared = shared_inputs(inp)
    x = np.asarray(inp["x"], np.float32)
    p = np.asarray(inp["p"], np.float32)
    C = build_full(shared.keys())
    used = set(C.dram.keys())
    in_maps = []
    for b in range(8):
        m = {k: v for k, v in shared.items() if k in used}
        m["x_tm"] = np.ascontiguousarray(x[b])
        m["xT"] = np.ascontiguousarray(x[b].T)
        m["pT0"] = np.ascontiguousarray(p[0, b].T)
        m["pT1"] = np.ascontiguousarray(p[1, b].T)
        in_maps.append(m)
    res = run_bass_kernel_spmd(C.nc, in_maps, core_ids=list(range(8)))
    return np.stack([np.asarray(r["out"], np.float32) for r in res.results], axis=0)
```

```python
from contextlib import ExitStack
import math
import numpy as np
import ml_dtypes
import concourse.bass as bass
import concourse.mybir as mybir
from concourse.bass_utils import run_bass_kernel_spmd

F32 = mybir.dt.float32
BF16 = mybir.dt.bfloat16
I32 = mybir.dt.int32
U32 = mybir.dt.uint32
AF = mybir.ActivationFunctionType
ALU = mybir.AluOpType
AX = mybir.AxisListType
NPBF = ml_dtypes.bfloat16

D_MODEL = 1024
SEQ = 2048
NT = SEQ // 128
ALPHA = 4.0 ** 0.25
EPS = 1e-5

ENGS = ("pe", "act", "dve", "pool", "sp")
N_DMA_SEMS = 16


class Dep:
    __slots__ = ("w", "r", "name")

    def __init__(self, name=""):
        self.w = {}
        self.r = {}
        self.name = name


class Prog:
    def __init__(self, nc, strict=True):
        self.nc = nc
        self.es = ExitStack()
        self.q = {e: [] for e in ENGS}
        self.cnt = {e: 0 for e in ENGS}
        self.seen = {e: {} for e in ENGS}
        self.sem = {}
        self.strict = strict
        for e in ENGS:
            self.sem[e] = self.es.enter_context(nc.semaphore("s_" + e))
        self.dma_sems = {}
        self.dma_tot = {}
        self.dma_rr = {}
        for e in ("sp", "pool", "act"):
            self.dma_sems[e] = [self.es.enter_context(nc.semaphore(f"d_{e}{i}")) for i in range(N_DMA_SEMS)]
            self.dma_tot[e] = [0] * N_DMA_SEMS
            self.dma_rr[e] = 0
        self.all_events = {}
        self.n_ops = 0
        self.stage_es = None
        self.uid = 0

    def begin_stage(self):
        self.barrier()
        if self.stage_es is not None:
            self.stage_es.close()
        self.stage_es = ExitStack()

    def sb(self, name, shape, dt):
        self.uid += 1
        t = self.stage_es.enter_context(self.nc.sbuf_tensor(f"{name}_{self.uid}", list(shape), dt))
        return t

    def ps(self, name, shape, dt=F32):
        return self.es.enter_context(self.nc.psum_tensor(name, list(shape), dt))

    def _semobj(self, key):
        if isinstance(key, str):
            return self.sem[key]
        e, i = key
        return self.dma_sems[e][i]

    def _need(self, eng, reads, writes, merge):
        need = {}

        def add(k, v):
            if k == eng and (not self.strict or eng == "pe"):
                return
            if self.seen[eng].get(k, 0) >= v:
                return
            if need.get(k, 0) < v:
                need[k] = v
        for d in reads:
            for k, v in d.w.items():
                add(k, v)
        for d in writes:
            if not merge:
                for k, v in d.w.items():
                    add(k, v)
            for k, v in d.r.items():
                add(k, v)
        for k, v in need.items():
            self.seen[eng][k] = v
        return list(need.items())

    def _commit(self, ev, reads, writes, merge):
        k, v = ev
        for d in reads:
            if d.r.get(k, 0) < v:
                d.r[k] = v
        for d in writes:
            if merge:
                d.w[k] = v
            else:
                d.w = {k: v}
                d.r = {}
        self.all_events[k] = v

    def op(self, eng, fn, reads=(), writes=(), merge=False):
        waits = self._need(eng, reads, writes, merge)
        self.cnt[eng] += 1
        ev = (eng, self.cnt[eng])
        sem = self.sem[eng]
        waitobjs = [(self._semobj(k), v) for k, v in waits]

        def emit(e, fn=fn, waitobjs=waitobjs, sem=sem):
            for s, v in waitobjs:
                e.wait_ge(s, v)
            fn(e).then_inc(sem, 1)
        self.q[eng].append(emit)
        self._commit(ev, reads, writes, merge)
        self.n_ops += 1
        return ev

    def dma(self, eng, out, in_, reads=(), writes=(), merge=False, **kw):
        i = self.dma_rr[eng]
        self.dma_rr[eng] = (i + 1) % N_DMA_SEMS
        key = (eng, i)
        prev = self.dma_tot[eng][i]
        waits = self._need(eng, reads, writes, merge)
        if prev > 0 and self.seen[eng].get(key, 0) < prev:
            waits.append((key, prev))
            self.seen[eng][key] = prev
        self.dma_tot[eng][i] = prev + 16
        ev = (key, prev + 16)
        sem = self.dma_sems[eng][i]
        waitobjs = [(self._semobj(k), v) for k, v in waits]

        def emit(e, waitobjs=waitobjs, sem=sem, out=out, in_=in_, kw=kw):
            for s, v in waitobjs:
                e.wait_ge(s, v)
            e.dma_start(out=out, in_=in_, **kw).then_inc(sem, 16)
        self.q[eng].append(emit)
        self._commit(ev, reads, writes, merge)
        self.n_ops += 1
        return ev

    def coll(self, kind, out, in_, reads=(), writes=()):
        eng = "pool"
        i = self.dma_rr[eng]
        self.dma_rr[eng] = (i + 1) % N_DMA_SEMS
        key = (eng, i)
        prev = self.dma_tot[eng][i]
        waits = self._need(eng, reads, writes, False)
        if prev > 0 and self.seen[eng].get(key, 0) < prev:
            waits.append((key, prev))
            self.seen[eng][key] = prev
        self.dma_tot[eng][i] = prev + 16
        ev = (key, prev + 16)
        sem = self.dma_sems[eng][i]
        waitobjs = [(self._semobj(k), v) for k, v in waits]

        def emit(e, waitobjs=waitobjs, sem=sem, out=out, in_=in_, kind=kind):
            for s_, v in waitobjs:
                e.wait_ge(s_, v)
            e.collective_compute(kind, ALU.bypass, replica_groups=[list(range(8))], ins=[in_], outs=[out]).then_inc(sem, 16)
        self.q[eng].append(emit)
        self._commit(ev, reads, writes, False)
        self.n_ops += 1
        return ev

    def barrier(self):
        snap = dict(self.all_events)
        for eng in ENGS:
            waits = []
            for k, v in snap.items():
                if k == eng:
                    continue
                if self.seen[eng].get(k, 0) >= v:
                    continue
                waits.append((self._semobj(k), v))
                self.seen[eng][k] = v
            if waits:
                def emit(e, waits=waits):
                    for s, v in waits:
                        e.wait_ge(s, v)
                self.q[eng].append(emit)

    def finish(self):
        self.barrier()
        nc = self.nc
        q = self.q
        with nc.Block() as block:
            @block.tensor
            def _(e):
                for f in q["pe"]:
                    f(e)

            @block.scalar
            def _(e):
                for f in q["act"]:
                    f(e)

            @block.vector
            def _(e):
                for f in q["dve"]:
                    f(e)

            @block.gpsimd
            def _(e):
                for f in q["pool"]:
                    f(e)

            @block.sync
            def _(e):
                for f in q["sp"]:
                    f(e)
        if self.stage_es is not None:
            self.stage_es.close()
        self.es.close()


class Ring:
    def __init__(self, P, name, n, shape, dt):
        self.bufs = [(P.sb(f"{name}{i}", shape, dt), Dep(f"{name}{i}")) for i in range(n)]
        self.i = 0

    def next(self):
        b = self.bufs[self.i]
        self.i = (self.i + 1) % len(self.bufs)
        return b


class Ctx:
    def __init__(self, ext_in, ext_out):
        self.nc = bass.Bass("TRN2", target_bir_lowering=False)
        self.P = Prog(self.nc)
        self.ext_in = set(ext_in)
        self.ext_out = set(ext_out)
        self.dram = {}
        self.ddep = {}
        P = self.P
        self.banks = [(P.ps(f"bank{i}", [128, 512], F32), Dep(f"bank{i}")) for i in range(8)]
        self.bank_i = 0
        self.held = set()

    def D(self, name, shape=None, dt=F32):
        if name in self.dram:
            return self.dram[name]
        kind = "Internal"
        if name in self.ext_in:
            kind = "ExternalInput"
        elif name in self.ext_out:
            kind = "ExternalOutput"
        t = self.nc.dram_tensor(name, list(shape), dt, kind=kind).ap()
        self.dram[name] = t
        self.ddep[name] = Dep(name)
        return t

    def dd(self, name):
        return self.ddep[name]

    def bank(self, hold=False):
        for _ in range(8):
            i = self.bank_i
            self.bank_i = (self.bank_i + 1) % 8
            if i not in self.held:
                if hold:
                    self.held.add(i)
                return self.banks[i]
        raise RuntimeError("no free PSUM bank")

    def release_all(self):
        self.held = set()

    def release(self, b):
        for i, bb in enumerate(self.banks):
            if bb[0] is b[0]:
                self.held.discard(i)


def mm_acc(P, out, pairs, reads, dwrite):
    n = len(pairs)
    for i, (l, r) in enumerate(pairs):
        P.op("pe", lambda e, l=l, r=r, i=i: e.matmul(out, lhsT=l, rhs=r, start=(i == 0), stop=(i == n - 1)),
             reads=reads, writes=[dwrite], merge=(i > 0))


def load_fm_bf16(C, name, src, kc_n, width, eng="pool"):
    P = C.P
    t = P.sb(name, [128, kc_n, width], BF16)
    deps = [Dep(f"{name}{k}") for k in range(kc_n)]
    for k in range(kc_n):
        P.dma(eng, t[:, k, :], src[k * 128:(k + 1) * 128, :], writes=[deps[k]])
    return t, deps


def stage_a1(C):
    P = C.P
    P.begin_stage()
    xT = C.D("xT", [1024, 2048], F32)
    w_in = C.D("ab_w_in", [1024, 3072], F32)
    qkT = C.D("qkT", [1024, 2048], BF16)
    v_tm = C.D("v_tm", [2048, 512], BF16)
    hbT = C.D("hbT", [1536, 2048], F32)
    xs, dxs = load_fm_bf16(C, "xTb", xT, 8, 2048)
    ws, dws = load_fm_bf16(C, "winb", w_in, 8, 3072)
    st_b = Ring(P, "a1sb", 3, [128, 2048], BF16)
    st_f = Ring(P, "a1sf", 3, [128, 2048], F32)
    ev_i = 0
    for mc in list(range(8)) + list(range(12, 24)):
        is_hb = mc >= 12
        stg, dstg = (st_f if is_hb else st_b).next()
        for nt in range(4):
            bk, dbk = C.bank()
            mm_acc(P, bk[:], [(ws[:, kc, mc * 128:(mc + 1) * 128], xs[:, kc, nt * 512:(nt + 1) * 512]) for kc in range(8)],
                   reads=dxs + dws, dwrite=dbk)
            eng = "act" if ev_i % 2 == 0 else "dve"
            ev_i += 1
            o = stg[:, nt * 512:(nt + 1) * 512]
            if eng == "act":
                P.op("act", lambda e, o=o, bk=bk: e.copy(out=o, in_=bk[:]), reads=[dbk], writes=[dstg], merge=(nt > 0))
            else:
                P.op("dve", lambda e, o=o, bk=bk: e.tensor_copy(out=o, in_=bk[:]), reads=[dbk], writes=[dstg], merge=(nt > 0))
        if is_hb:
            P.dma("act", hbT[(mc - 12) * 128:(mc - 11) * 128, :], stg[:], reads=[dstg], writes=[C.dd("hbT")], merge=True)
        else:
            P.dma("act", qkT[mc * 128:(mc + 1) * 128, :], stg[:], reads=[dstg], writes=[C.dd("qkT")], merge=True)
    st_v = Ring(P, "a1sv", 3, [128, 512], BF16)
    for tt in range(NT):
        bk, dbk = C.bank()
        mm_acc(P, bk[:], [(xs[:, kc, tt * 128:(tt + 1) * 128], ws[:, kc, 1024:1536]) for kc in range(8)],
               reads=dxs + dws, dwrite=dbk)
        stg, dstg = st_v.next()
        if tt % 2 == 0:
            P.op("act", lambda e, stg=stg, bk=bk: e.copy(out=stg[:], in_=bk[:]), reads=[dbk], writes=[dstg])
        else:
            P.op("dve", lambda e, stg=stg, bk=bk: e.tensor_copy(out=stg[:], in_=bk[:]), reads=[dbk], writes=[dstg])
        P.dma("act", v_tm[tt * 128:(tt + 1) * 128, :], stg[:], reads=[dstg], writes=[C.dd("v_tm")], merge=True)


def na_plan():
    rows, wr = 32, 8
    r0 = np.clip(np.arange(rows) - wr // 2, 0, rows - wr)
    plan = []
    keys = {}
    for i in range(16):
        lo = r0[2 * i] // 2
        hi = (r0[2 * i + 1] + 7) // 2
        lst = []
        for j in range(lo, hi + 1):
            val = []
            for ak in range(2):
                for aq in range(2):
                    r = 2 * i + aq
                    kr = 2 * j + ak
                    val.append(bool(r0[r] <= kr <= r0[r] + 7))
            key = (j - i, tuple(val))
            if key not in keys:
                keys[key] = len(keys)
            lst.append((j, keys[key]))
        plan.append(lst)
    return plan, keys


def na_tables(rpb):
    plan, keys = na_plan()
    c = np.arange(64)
    c0 = np.clip(c - 8, 0, 48)
    col_ok = (c[None, :] >= c0[:, None]) & (c[None, :] < c0[:, None] + 16)
    dc_idx = np.clip(c[None, :] - c[:, None], -15, 15) + 15
    tab = np.full((len(keys), 2, 64, 8, 2, 64), -1e30, np.float32)
    for (delta, val), tid in keys.items():
        vi = 0
        for ak in range(2):
            for aq in range(2):
                ok = val[vi]
                vi += 1
                if not ok:
                    continue
                dr = 2 * delta + ak - aq
                b = rpb[:, dr + 7, :][:, dc_idx]
                b = np.where(col_ok[None], b, np.float32(-1e30))
                tab[tid, ak, :, :, aq, :] = b.transpose(2, 0, 1)
    return tab.reshape(len(keys), 128, 8, 128)


def stage_a2(C):
    P = C.P
    P.begin_stage()
    plan, keys = na_plan()
    ntab = len(keys)
    qkT = C.D("qkT", [1024, 2048], BF16)
    v_tm = C.D("v_tm", [2048, 512], BF16)
    tab_d = C.D("na_tab", [ntab, 128, 8, 128], F32)
    mixT = C.D("mixT", [1024, 2048], BF16)
    QT = P.sb("QT", [128, 4, 2048], BF16)
    KT = P.sb("KT", [128, 4, 2048], BF16)
    dQ = [Dep() for _ in range(4)]
    dK = [Dep() for _ in range(4)]
    for hp in range(4):
        P.dma("sp", QT[:, hp, :], qkT[hp * 128:(hp + 1) * 128, :], reads=[C.dd("qkT")], writes=[dQ[hp]])
        P.dma("sp", KT[:, hp, :], qkT[512 + hp * 128:512 + (hp + 1) * 128, :], reads=[C.dd("qkT")], writes=[dK[hp]])
    tab = P.sb("natab", [128, ntab, 8, 128], F32)
    dtab = Dep()
    for t in range(ntab):
        P.dma("sp", tab[:, t, :, :], tab_d[t], writes=[dtab], merge=True)
    Vx = P.sb("Vx", [128, 8, NT, 128], BF16)
    dV = Dep()
    P.op("pool", lambda e: e.memset(Vx[:], 0.0), writes=[dV])
    vv = v_tm.rearrange("(t p) c -> p t c", p=128)
    for h in range(8):
        a = h % 2
        P.dma("sp", Vx[:, h, :, a * 64:(a + 1) * 64], vv[:, :, h * 64:(h + 1) * 64], reads=[C.dd("v_tm")], writes=[dV], merge=(h > 0))
    ones2 = P.sb("ones2", [128, 2, 128], BF16)
    dones = Dep()
    P.op("pool", lambda e: e.memset(ones2[:], 0.0), writes=[dones])
    P.op("pool", lambda e: e.memset(ones2[:, 0, 0:64], 1.0), writes=[dones])
    P.op("pool", lambda e: e.memset(ones2[:, 1, 64:128], 1.0), writes=[dones])
    yaT = P.sb("yaT", [128, 4, 2048], BF16)
    dya = [Dep() for _ in range(4)]
    s_ring = Ring(P, "na_s", 3, [128, 640], F32)
    p_ring = Ring(P, "na_p", 4, [128, 640], BF16)
    rd_ring = Ring(P, "na_rd", 2, [128, 128], F32)
    units = [(i, hp, a) for i in range(16) for hp in range(4) for a in range(2)]
    pair = {}

    def phase1(u):
        i, hp, a = u
        q0 = i * 128
        h = hp * 2 + a
        pa = slice(a * 64, (a + 1) * 64)
        lst = plan[i]
        nkb = len(lst)
        bA, dbA = C.bank()
        bB, dbB = (C.bank() if nkb > 4 else (None, None))
        ssb, dss = s_ring.next()
        for jj, (j, tid) in enumerate(lst):
            bk, dbk = (bA, dbA) if jj < 4 else (bB, dbB)
            o = bk[:, (jj % 4) * 128:(jj % 4 + 1) * 128]
            P.op("pe", lambda e, o=o, j=j, pa=pa, hp=hp, q0=q0: e.matmul(
                o, lhsT=KT[pa, hp, j * 128:(j + 1) * 128], rhs=QT[pa, hp, q0:q0 + 128], start=True, stop=True),
                reads=[dK[hp], dQ[hp]], writes=[dbk], merge=(jj % 4 > 0))
        for jj, (j, tid) in enumerate(lst):
            bk, dbk = (bA, dbA) if jj < 4 else (bB, dbB)
            o = bk[:, (jj % 4) * 128:(jj % 4 + 1) * 128]
            P.op("dve", lambda e, o=o, jj=jj, tid=tid, h=h, ssb=ssb: e.scalar_tensor_tensor(
                out=ssb[:, jj * 128:(jj + 1) * 128], in0=o, scalar=0.125, in1=tab[:, tid, h, :],
                op0=ALU.mult, op1=ALU.add), reads=[dbk, dtab], writes=[dss], merge=(jj > 0))
        pT, dpT = p_ring.next()
        P.op("act", lambda e, pT=pT, ssb=ssb, nkb=nkb: e.activation(
            out=pT[:, 0:nkb * 128], in_=ssb[:, 0:nkb * 128], func=AF.Exp), reads=[dss], writes=[dpT])
        return (pT, dpT)

    def phase2(u, pp):
        i, hp, a = u
        q0 = i * 128
        h = hp * 2 + a
        pT, dpT = pp
        lst = plan[i]
        nkb = len(lst)
        if a == 0:
            pair[(i, hp)] = (C.bank(hold=True), C.bank(hold=True))
        (bo, dbo), (bd, dbd) = pair[(i, hp)]
        for jj, (j, tid) in enumerate(lst):
            first = (a == 0 and jj == 0)
            last = (a == 1 and jj == nkb - 1)
            P.op("pe", lambda e, bo=bo, h=h, j=j, pT=pT, jj=jj, first=first, last=last: e.matmul(
                bo[:, 0:128], lhsT=Vx[:, h, j, :], rhs=pT[:, jj * 128:(jj + 1) * 128], start=first, stop=last),
                reads=[dV, dpT], writes=[dbo], merge=(not first))
            P.op("pe", lambda e, bd=bd, a=a, pT=pT, jj=jj, first=first, last=last: e.matmul(
                bd[:, 0:128], lhsT=ones2[:, a, :], rhs=pT[:, jj * 128:(jj + 1) * 128], start=first, stop=last),
                reads=[dones, dpT], writes=[dbd], merge=(not first))
        if a == 1:
            rd, drd = rd_ring.next()
            P.op("dve", lambda e, rd=rd, bd=bd: e.reciprocal(out=rd[:], in_=bd[:, 0:128]), reads=[dbd], writes=[drd])
            P.op("dve", lambda e, rd=rd, bo=bo, hp=hp, q0=q0: e.tensor_tensor(
                out=yaT[:, hp, q0:q0 + 128], in0=bo[:, 0:128], in1=rd[:], op=ALU.mult),
                reads=[dbo, drd], writes=[dya[hp]], merge=True)
            C.release(pair[(i, hp)][0])
            C.release(pair[(i, hp)][1])
            del pair[(i, hp)]

    pend = None
    for u in units:
        pp = phase1(u)
        if pend is not None:
            phase2(*pend)
        pend = (u, pp)
    phase2(*pend)
    for hp in range(4):
        P.dma("sp", mixT[hp * 128:(hp + 1) * 128, :], yaT[:, hp, :], reads=[dya[hp]], writes=[C.dd("mixT")], merge=True)


class LNBufs:
    def __init__(self, P, name):
        self.stats = Ring(P, name + "st", 4, [128, 2, 6], F32)
        self.mv = Ring(P, name + "mv", 4, [128, 2], F32)
        self.rstd = Ring(P, name + "rs", 4, [128, 1], F32)
        self.nmr = Ring(P, name + "nm", 4, [128, 1], F32)


def layer_norm_tile(P, lb, r, dr, gb, bb, dgb, y, dy):
    st, dst = lb.stats.next()
    mv, dmv = lb.mv.next()
    rs, drs = lb.rstd.next()
    nm, dnm = lb.nmr.next()
    P.op("dve", lambda e: e.bn_stats(out=st[:, 0, :], in_=r[:, 0:512]), reads=[dr], writes=[dst])
    P.op("dve", lambda e: e.bn_stats(out=st[:, 1, :], in_=r[:, 512:1024]), reads=[dr], writes=[dst], merge=True)
    P.op("dve", lambda e: e.bn_aggr(out=mv[:], in_=st[:]), reads=[dst], writes=[dmv])
    P.op("dve", lambda e: e.tensor_scalar(out=rs[:], in0=mv[:, 1:2], scalar1=EPS, scalar2=None, op0=ALU.add), reads=[dmv], writes=[drs])
    P.op("act", lambda e: e.activation(out=rs[:], in_=rs[:], func=AF.Sqrt), reads=[drs], writes=[drs])
    P.op("dve", lambda e: e.reciprocal(out=rs[:], in_=rs[:]), reads=[drs], writes=[drs])
    P.op("dve", lambda e: e.scalar_tensor_tensor(out=nm[:], in0=mv[:, 0:1], scalar=-1.0, in1=rs[:], op0=ALU.mult, op1=ALU.mult),
         reads=[dmv, drs], writes=[dnm])
    P.op("act", lambda e: e.activation(out=y[:], in_=r[:], func=AF.Identity, bias=nm[:], scale=rs[:]),
         reads=[dr, drs, dnm], writes=[dy])
    P.op("pool", lambda e: e.tensor_tensor(out=y[:], in0=y[:], in1=gb[:], op=ALU.mult), reads=[dy, dgb], writes=[dy])
    P.op("pool", lambda e: e.tensor_tensor(out=y[:], in0=y[:], in1=bb[:], op=ALU.add), reads=[dy, dgb], writes=[dy])


def stage_proj_ln(C, w_name, x_name, lnname, li, out_name):
    P = C.P
    P.begin_stage()
    mixT = C.D("mixT", [1024, 2048], BF16)
    w = C.D(w_name, [1024, 1024], F32)
    x = C.D(x_name, [2048, 1024], F32)
    g_d = C.D(f"{lnname}_g{li}", [128, 1024], F32)
    b_d = C.D(f"{lnname}_b{li}", [128, 1024], F32)
    out = C.D(out_name, [2048, 1024], F32)
    ms = P.sb("ms", [128, 8, 2048], BF16)
    dms = [Dep() for _ in range(8)]
    for kc in range(8):
        P.dma("sp", ms[:, kc, :], mixT[kc * 128:(kc + 1) * 128, :], reads=[C.dd("mixT")], writes=[dms[kc]])
    ws, dws = load_fm_bf16(C, "wout", w, 8, 1024)
    gb = P.sb("gb", [128, 1024], F32)
    bb = P.sb("bb", [128, 1024], F32)
    dgb = Dep()
    P.dma("sp", gb[:], g_d, writes=[dgb])
    P.dma("sp", bb[:], b_d, writes=[dgb], merge=True)
    lb = LNBufs(P, "ln")
    xr = Ring(P, "xr", 4, [128, 1024], F32)
    rr = Ring(P, "rr", 4, [128, 1024], F32)
    yr = Ring(P, "yr", 4, [128, 1024], F32)
    for tt in range(NT):
        xt, dxt = xr.next()
        P.dma("sp", xt[:], x[tt * 128:(tt + 1) * 128, :], reads=[C.dd(x_name)], writes=[dxt])
        r, dr = rr.next()
        for half in range(2):
            bk, dbk = C.bank()
            hs = slice(half * 512, (half + 1) * 512)
            mm_acc(P, bk[:], [(ms[:, kc, tt * 128:(tt + 1) * 128], ws[:, kc, hs]) for kc in range(8)], reads=dms + dws, dwrite=dbk)
            P.op("dve", lambda e, r=r, xt=xt, bk=bk, hs=hs: e.scalar_tensor_tensor(
                out=r[:, hs], in0=xt[:, hs], scalar=ALPHA, in1=bk[:], op0=ALU.mult, op1=ALU.add),
                reads=[dxt, dbk], writes=[dr], merge=(half > 0))
        y, dy = yr.next()
        layer_norm_tile(P, lb, r, dr, gb, bb, dgb, y, dy)
        P.dma("pool", out[tt * 128:(tt + 1) * 128, :], y[:], reads=[dy], writes=[C.dd(out_name)], merge=True)


def stage_moe1(C, li, x_name):
    P = C.P
    P.begin_stage()
    x1 = C.D(x_name, [2048, 1024], F32)
    wr_d = C.D(f"moe_router{li}", [1024, 16], F32)
    ident_d = C.D("ident", [128, 128], F32)
    esel_d = C.D("esel", [16, 16, 128], F32)
    iotac_d = C.D("iota_col", [128, 16], F32)
    xe_all = C.D("xeT_all", [16, 128, 8, 256], BF16)
    idxc_d = C.D("moe_idxc", [128, 2, 16], F32)
    gc_d = C.D("moe_gc", [128, 2, 16], F32)
    wr = P.sb("wr", [128, 8, 16], F32)
    dcst = Dep()
    P.dma("sp", wr[:], wr_d.rearrange("(kc p) e -> p kc e", p=128), writes=[dcst])
    ident = P.sb("ident", [128, 128], F32)
    P.dma("sp", ident[:], ident_d, writes=[dcst], merge=True)
    esel = P.sb("esel", [16, 16, 128], F32)
    P.dma("sp", esel[:], esel_d, writes=[dcst], merge=True)
    iotac = P.sb("iotac", [128, 16], F32)
    P.dma("sp", iotac[:], iotac_d, writes=[dcst], merge=True)
    x1b = P.sb("x1b", [128, NT, 1024], BF16)
    dx1b = [Dep() for _ in range(NT)]
    affT = P.sb("affT", [16, 2048], F32)
    daffT = Dep()
    xr = Ring(P, "m1x", 4, [128, 1024], F32)
    xTr = Ring(P, "m1xT", 4, [128, 8, 128], F32)
    sm_r = Ring(P, "m1sm", 4, [128, 4], F32)
    ex_r = Ring(P, "m1ex", 4, [128, 16], F32)
    af_r = Ring(P, "m1af", 4, [128, 16], F32)
    for tt in range(NT):
        xt, dxt = xr.next()
        P.dma("sp", xt[:], x1[tt * 128:(tt + 1) * 128, :], reads=[C.dd(x_name)], writes=[dxt])
        P.op("act", lambda e, xt=xt, tt=tt: e.copy(out=x1b[:, tt, :], in_=xt[:]), reads=[dxt], writes=[dx1b[tt]])
        xT, dxT = xTr.next()
        for hb in range(2):
            bk, dbk = C.bank()
            for k4 in range(4):
                kc = hb * 4 + k4
                P.op("pe", lambda e, bk=bk, k4=k4, kc=kc, xt=xt: e.transpose(bk[:, k4 * 128:(k4 + 1) * 128], xt[:, kc * 128:(kc + 1) * 128], ident[:]),
                     reads=[dxt, dcst], writes=[dbk], merge=(k4 > 0))
            P.op("dve", lambda e, xT=xT, hb=hb, bk=bk: e.tensor_copy(out=xT[:, hb * 4:(hb + 1) * 4, :], in_=bk[:].rearrange("p (k t) -> p k t", k=4)),
                 reads=[dbk], writes=[dxT], merge=(hb > 0))
        bk, dbk = C.bank()
        mm_acc(P, bk[:, 0:16], [(xT[:, kc, :], wr[:, kc, :]) for kc in range(8)], reads=[dxT, dcst], dwrite=dbk)
        sm, dsm = sm_r.next()
        ex, dex = ex_r.next()
        af, daf = af_r.next()
        P.op("dve", lambda e, sm=sm, bk=bk: e.reduce_max(out=sm[:, 0:1], in_=bk[:, 0:16], axis=AX.X), reads=[dbk], writes=[dsm])
        P.op("dve", lambda e, sm=sm: e.tensor_scalar(out=sm[:, 1:2], in0=sm[:, 0:1], scalar1=-1.0, scalar2=None, op0=ALU.mult),
             reads=[dsm], writes=[dsm])
        P.op("act", lambda e, ex=ex, bk=bk, sm=sm: e.activation(out=ex[:], in_=bk[:, 0:16], func=AF.Exp, bias=sm[:, 1:2], accum_out=sm[:, 2:3]),
             reads=[dbk, dsm], writes=[dex, dsm])
        P.op("dve", lambda e, sm=sm: e.reciprocal(out=sm[:, 3:4], in_=sm[:, 2:3]), reads=[dsm], writes=[dsm])
        P.op("dve", lambda e, af=af, ex=ex, sm=sm: e.tensor_scalar(out=af[:], in0=ex[:], scalar1=sm[:, 3:4], scalar2=None, op0=ALU.mult),
             reads=[dex, dsm], writes=[daf])
        bk2, dbk2 = C.bank()
        P.op("pe", lambda e, bk2=bk2, af=af: e.transpose(bk2[0:16, 0:128], af[:], ident[:]), reads=[daf, dcst], writes=[dbk2])
        P.op("act", lambda e, bk2=bk2, tt=tt: e.copy(out=affT[:, tt * 128:(tt + 1) * 128], in_=bk2[0:16, 0:128]),
             reads=[dbk2], writes=[daffT], merge=(tt > 0))
    work = P.sb("work", [16, 2048], F32)
    dwork = Dep()
    g_all = P.sb("g_all", [16, 256], F32)
    idx_all = P.sb("idx_all", [16, 256], U32)
    dg = Dep()
    di = Dep()
    for r in range(32):
        src, dsrc = (affT, daffT) if r == 0 else (work, dwork)
        sl = slice(r * 8, (r + 1) * 8)
        P.op("dve", lambda e, src=src, sl=sl: e.max(out=g_all[:, sl], in_=src[:]), reads=[dsrc], writes=[dg], merge=(r > 0))
        P.op("dve", lambda e, src=src, sl=sl: e.max_index(out=idx_all[:, sl], in_max=g_all[:, sl], in_values=src[:]),
             reads=[dsrc, dg], writes=[di], merge=(r > 0))
        if r < 31:
            P.op("dve", lambda e, src=src, sl=sl: e.match_replace(out=work[:], in_to_replace=g_all[:, sl], in_values=src[:], imm_value=-1.0),
                 reads=[dsrc, dg], writes=[dwork])
    idxf = P.sb("idxf", [16, 256], F32)
    didxf = Dep()
    P.op("dve", lambda e: e.tensor_copy(out=idxf[:], in_=idx_all[:]), reads=[di], writes=[didxf])
    colt = P.sb("colt", [128, 2, 2, 16], F32)
    dcol = Dep()
    for which, (src, dsrc) in enumerate(((idxf, didxf), (g_all, dg))):
        for cc in range(2):
            bk, dbk = C.bank()
            P.op("pe", lambda e, bk=bk, src=src, cc=cc: e.transpose(bk[:, 0:16], src[:, cc * 128:(cc + 1) * 128], ident[0:16, 0:16]),
                 reads=[dsrc, dcst], writes=[dbk])
            P.op("act", lambda e, bk=bk, which=which, cc=cc: e.copy(out=colt[:, which, cc, :], in_=bk[:, 0:16]),
                 reads=[dbk], writes=[dcol], merge=True)
    P.dma("act", idxc_d, colt[:, 0, :, :], reads=[dcol], writes=[C.dd("moe_idxc")])
    P.dma("act", gc_d, colt[:, 1, :, :], reads=[dcol], writes=[C.dd("moe_gc")])
    sel_r = Ring(P, "sel", 2, [128, NT, 256], BF16)
    xe_r = Ring(P, "xe", 2, [128, 8, 256], BF16)
    for ex_i in range(16):
        bk, dbk = C.bank()
        P.op("pe", lambda e, bk=bk, ex_i=ex_i: e.matmul(bk[:, 0:256], lhsT=esel[:, ex_i, :], rhs=idxf[:], start=True, stop=True),
             reads=[dcst, didxf], writes=[dbk])
        sel, dsel = sel_r.next()
        for tt in range(NT):
            P.op("dve", lambda e, sel=sel, bk=bk, tt=tt: e.tensor_scalar(out=sel[:, tt, :], in0=bk[:, 0:256], scalar1=iotac[:, tt:tt + 1], scalar2=None, op0=ALU.is_equal),
                 reads=[dbk, dcst], writes=[dsel], merge=(tt > 0))
        xe, dxe = xe_r.next()
        for dc in range(8):
            bk2, dbk2 = C.bank()
            mm_acc(P, bk2[:, 0:256], [(x1b[:, tt, dc * 128:(dc + 1) * 128], sel[:, tt, :]) for tt in range(NT)],
                   reads=dx1b + [dsel], dwrite=dbk2)
            if dc % 2 == 0:
                P.op("act", lambda e, xe=xe, dc=dc, bk2=bk2: e.copy(out=xe[:, dc, :], in_=bk2[:, 0:256]), reads=[dbk2], writes=[dxe], merge=(dc > 0))
            else:
                P.op("dve", lambda e, xe=xe, dc=dc, bk2=bk2: e.tensor_copy(out=xe[:, dc, :], in_=bk2[:, 0:256]), reads=[dbk2], writes=[dxe], merge=True)
        P.dma("act", xe_all[ex_i], xe[:], reads=[dxe], writes=[C.dd("xeT_all")], merge=True)


def stage_moe2(C, li, x_name, out_name):
    P = C.P
    P.begin_stage()
    x1 = C.D(x_name, [2048, 1024], F32)
    wg_d = C.D(f"moe_w_gate{li}", [16, 1024, 2048], F32)
    wu_d = C.D(f"moe_w_up{li}", [16, 1024, 2048], F32)
    wd_d = C.D(f"moe_w_down{li}", [16, 2048, 1024], F32)
    xe_all = C.D("xeT_all", [16, 128, 8, 256], BF16)
    idxc_d = C.D("moe_idxc", [128, 2, 16], F32)
    gc_d = C.D("moe_gc", [128, 2, 16], F32)
    iotar_d = C.D("iota_row", [128, 2048], F32)
    ident_d = C.D("ident", [128, 128], F32)
    g_d = C.D(f"ln2_g{li}", [128, 1024], F32)
    b_d = C.D(f"ln2_b{li}", [128, 1024], F32)
    out = C.D(out_name, [2048, 1024], F32)
    outT = C.D(out_name + "T", [1024, 2048], BF16)
    dcst = Dep()
    idxc = P.sb("idxc", [128, 2, 16], F32)
    gc = P.sb("gc", [128, 2, 16], F32)
    iotar = P.sb("iotar", [128, 2048], F32)
    P.dma("sp", idxc[:], idxc_d, reads=[C.dd("moe_idxc")], writes=[dcst])
    P.dma("sp", gc[:], gc_d, reads=[C.dd("moe_gc")], writes=[dcst], merge=True)
    P.dma("sp", iotar[:], iotar_d, writes=[dcst], merge=True)
    f_acc = P.sb("f_acc", [128, NT, 1024], F32)
    dfa = [Dep() for _ in range(NT)]
    xe_r = Ring(P, "m2xe", 2, [128, 8, 256], BF16)
    selT_r = Ring(P, "selT", 2, [128, 2, 2048], BF16)
    wg_r = Ring(P, "wg", 2, [128, 8, 512], BF16)
    wu_r = Ring(P, "wu", 2, [128, 8, 512], BF16)
    wd_r = Ring(P, "wd", 2, [128, 4, 1024], BF16)
    sg_r = Ring(P, "sg", 2, [128, 256], F32)
    hT_r = Ring(P, "hT", 2, [128, 16, 256], BF16)
    ye_r = Ring(P, "ye", 2, [128, 2, 1024], BF16)
    ev = 0
    for e_i in range(16):
        xe, dxe = xe_r.next()
        P.dma("sp", xe[:], xe_all[e_i], reads=[C.dd("xeT_all")], writes=[dxe])
        selT, dselT = selT_r.next()
        for cc in range(2):
            P.op("pool", lambda e, selT=selT, cc=cc, e_i=e_i: e.tensor_scalar(
                out=selT[:, cc, :], in0=iotar[:], scalar1=idxc[:, cc, e_i:e_i + 1], scalar2=gc[:, cc, e_i:e_i + 1],
                op0=ALU.is_equal, op1=ALU.mult), reads=[dcst], writes=[dselT], merge=(cc > 0))
        hT, dhT = hT_r.next()
        wgv = wg_d[e_i].rearrange("(kc p) f -> p kc f", p=128)
        wuv = wu_d[e_i].rearrange("(kc p) f -> p kc f", p=128)
        wdv = wd_d[e_i].rearrange("(fc p) d -> p fc d", p=128)
        for q in range(4):
            wg, dwg = wg_r.next()
            wu, dwu = wu_r.next()
            P.dma("pool", wg[:], wgv[:, :, q * 512:(q + 1) * 512], writes=[dwg])
            P.dma("pool", wu[:], wuv[:, :, q * 512:(q + 1) * 512], writes=[dwu])
            for fcl in range(4):
                fc = q * 4 + fcl
                fs = slice(fcl * 128, (fcl + 1) * 128)
                bg, dbg = C.bank()
                bu, dbu = C.bank()
                mm_acc(P, bg[:, 0:256], [(wg[:, kc, fs], xe[:, kc, :]) for kc in range(8)], reads=[dwg, dxe], dwrite=dbg)
                mm_acc(P, bu[:, 0:256], [(wu[:, kc, fs], xe[:, kc, :]) for kc in range(8)], reads=[dwu, dxe], dwrite=dbu)
                sg, dsg = sg_r.next()
                P.op("act", lambda e, sg=sg, bg=bg: e.activation(out=sg[:], in_=bg[:, 0:256], func=AF.Silu), reads=[dbg], writes=[dsg])
                P.op("dve", lambda e, hT=hT, fc=fc, sg=sg, bu=bu: e.tensor_tensor(out=hT[:, fc, :], in0=sg[:], in1=bu[:, 0:256], op=ALU.mult),
                     reads=[dsg, dbu], writes=[dhT], merge=(fc > 0))
        ye, dye = ye_r.next()
        dbanks = [C.bank() for _ in range(4)]
        for r in range(4):
            wd, dwd = wd_r.next()
            P.dma("pool", wd[:], wdv[:, r * 4:(r + 1) * 4, :], writes=[dwd])
            for ct in range(2):
                for dh in range(2):
                    bk, dbk = dbanks[ct * 2 + dh]
                    for f4 in range(4):
                        fc = r * 4 + f4
                        first = (fc == 0)
                        last = (fc == 15)
                        P.op("pe", lambda e, bk=bk, hT=hT, fc=fc, ct=ct, wd=wd, f4=f4, dh=dh, first=first, last=last: e.matmul(
                            bk[:], lhsT=hT[:, fc, ct * 128:(ct + 1) * 128], rhs=wd[:, f4, dh * 512:(dh + 1) * 512], start=first, stop=last),
                            reads=[dhT, dwd], writes=[dbk], merge=(not first))
        for ct in range(2):
            for dh in range(2):
                bk, dbk = dbanks[ct * 2 + dh]
                P.op("act", lambda e, ye=ye, ct=ct, dh=dh, bk=bk: e.copy(out=ye[:, ct, dh * 512:(dh + 1) * 512], in_=bk[:]),
                     reads=[dbk], writes=[dye], merge=(ct + dh > 0))
        for tt in range(NT):
            for dh in range(2):
                bk, dbk = C.bank()
                ds = slice(dh * 512, (dh + 1) * 512)
                mm_acc(P, bk[:], [(selT[:, ct, tt * 128:(tt + 1) * 128], ye[:, ct, ds]) for ct in range(2)], reads=[dselT, dye], dwrite=dbk)
                if e_i == 0:
                    P.op("dve", lambda e, tt=tt, ds=ds, bk=bk: e.tensor_copy(out=f_acc[:, tt, ds], in_=bk[:]), reads=[dbk], writes=[dfa[tt]], merge=(dh > 0))
                else:
                    P.op("dve", lambda e, tt=tt, ds=ds, bk=bk: e.tensor_tensor(out=f_acc[:, tt, ds], in0=f_acc[:, tt, ds], in1=bk[:], op=ALU.add),
                         reads=[dbk, dfa[tt]], writes=[dfa[tt]])
    gb = P.sb("gb2", [128, 1024], F32)
    bb = P.sb("bb2", [128, 1024], F32)
    ident = P.sb("ident2", [128, 128], F32)
    dgb = Dep()
    P.dma("sp", gb[:], g_d, writes=[dgb])
    P.dma("sp", bb[:], b_d, writes=[dgb], merge=True)
    P.dma("sp", ident[:], ident_d, writes=[dgb], merge=True)
    lb = LNBufs(P, "ln2")
    xr = Ring(P, "m2x", 2, [128, 1024], F32)
    yr = Ring(P, "m2y", 2, [128, 1024], F32)
    yT_r = Ring(P, "m2yT", 2, [128, 8, 128], BF16)
    for tt in range(NT):
        xt, dxt = xr.next()
        P.dma("sp", xt[:], x1[tt * 128:(tt + 1) * 128, :], reads=[C.dd(x_name)], writes=[dxt])
        P.op("dve", lambda e, xt=xt, tt=tt: e.scalar_tensor_tensor(out=xt[:], in0=xt[:], scalar=ALPHA, in1=f_acc[:, tt, :], op0=ALU.mult, op1=ALU.add),
             reads=[dxt, dfa[tt]], writes=[dxt])
        y, dy = yr.next()
        layer_norm_tile(P, lb, xt, dxt, gb, bb, dgb, y, dy)
        P.dma("pool", out[tt * 128:(tt + 1) * 128, :], y[:], reads=[dy], writes=[C.dd(out_name)], merge=True)
        transpose_tile_to_dram(C, y, dy, ident, dgb, yT_r, outT, out_name + "T", tt)


def transpose_tile_to_dram(C, y, dy, ident, dident, yT_r, outT, outT_name, tt):
    P = C.P
    yT, dyT = yT_r.next()
    for hb in range(2):
        bk, dbk = C.bank()
        for k4 in range(4):
            kc = hb * 4 + k4
            P.op("pe", lambda e, bk=bk, k4=k4, kc=kc: e.transpose(bk[:, k4 * 128:(k4 + 1) * 128], y[:, kc * 128:(kc + 1) * 128], ident[:]),
                 reads=[dy, dident], writes=[dbk], merge=(k4 > 0))
        P.op("act", lambda e, hb=hb, bk=bk: e.copy(out=yT[:, hb * 4:(hb + 1) * 4, :], in_=bk[:].rearrange("p (k t) -> p k t", k=4)),
             reads=[dbk], writes=[dyT], merge=(hb > 0))
    P.dma("act", outT.rearrange("(kc p) t -> p kc t", p=128)[:, :, tt * 128:(tt + 1) * 128], yT[:], reads=[dyT], writes=[C.dd(outT_name)], merge=True)


def stage_ple(C, li, x_name, out_name, want_T):
    P = C.P
    P.begin_stage()
    x2 = C.D(x_name, [2048, 1024], F32)
    x2T = C.D(x_name + "T", [1024, 2048], BF16)
    pT_d = C.D(f"pT{li}", [256, 2048], F32)
    wg_d = C.D(f"ple_gate{li}", [1024, 1024], F32)
    wp_d = C.D(f"ple_proj{li}", [256, 1024], F32)
    ident_d = C.D("ident", [128, 128], F32)
    out = C.D(out_name, [2048, 1024], F32)
    xs = P.sb("plx", [128, 8, 2048], BF16)
    dxs = [Dep() for _ in range(8)]
    for kc in range(8):
        P.dma("sp", xs[:, kc, :], x2T[kc * 128:(kc + 1) * 128, :], reads=[C.dd(x_name + "T")], writes=[dxs[kc]])
    ps_, dps_ = load_fm_bf16(C, "plp", pT_d, 2, 2048)
    wg, dwg = load_fm_bf16(C, "plwg", wg_d, 8, 1024)
    wp, dwp = load_fm_bf16(C, "plwp", wp_d, 2, 1024)
    ident = P.sb("ident3", [128, 128], F32)
    dident = Dep()
    P.dma("sp", ident[:], ident_d, writes=[dident])
    if want_T:
        outT = C.D(out_name + "T", [1024, 2048], BF16)
        yT_r = Ring(P, "plyT", 2, [128, 8, 128], BF16)
    xr = Ring(P, "plxr", 3, [128, 1024], F32)
    gr = Ring(P, "plg", 3, [128, 1024], F32)
    yr = Ring(P, "ply", 3, [128, 1024], F32)
    for tt in range(NT):
        ts_ = slice(tt * 128, (tt + 1) * 128)
        xt, dxt = xr.next()
        P.dma("sp", xt[:], x2[ts_, :], reads=[C.dd(x_name)], writes=[dxt])
        gt, dgt = gr.next()
        y, dy = yr.next()
        for half in range(2):
            hs = slice(half * 512, (half + 1) * 512)
            bk, dbk = C.bank()
            mm_acc(P, bk[:], [(xs[:, kc, ts_], wg[:, kc, hs]) for kc in range(8)], reads=dxs + dwg, dwrite=dbk)
            P.op("act", lambda e, gt=gt, hs=hs, bk=bk: e.activation(out=gt[:, hs], in_=bk[:], func=AF.Sigmoid), reads=[dbk], writes=[dgt], merge=(half > 0))
            bk2, dbk2 = C.bank()
            mm_acc(P, bk2[:], [(ps_[:, kc, ts_], wp[:, kc, hs]) for kc in range(2)], reads=dps_ + dwp, dwrite=dbk2)
            P.op("dve", lambda e, gt=gt, hs=hs, bk2=bk2: e.tensor_tensor(out=gt[:, hs], in0=gt[:, hs], in1=bk2[:], op=ALU.mult),
                 reads=[dgt, dbk2], writes=[dgt])
        P.op("pool", lambda e, y=y, xt=xt, gt=gt: e.tensor_tensor(out=y[:], in0=xt[:], in1=gt[:], op=ALU.add), reads=[dxt, dgt], writes=[dy])
        P.dma("pool", out[ts_, :], y[:], reads=[dy], writes=[C.dd(out_name)], merge=True)
        if want_T:
            transpose_tile_to_dram(C, y, dy, ident, dident, yT_r, outT, out_name + "T", tt)


def hyena_consts():
    L, N = 2048, 4096
    R = np.arange(N)
    f = np.where(R <= 2048, R, R - 2048).astype(np.int64)
    is_im = R > 2048
    t = np.arange(L, dtype=np.int64)
    k = (t[:, None] * f[None, :]) % N
    ang = 2.0 * np.pi * k.astype(np.float64) / N
    Wf = np.where(is_im[None, :], -np.sin(ang), np.cos(ang))
    cR = np.full(N, 2.0 / N)
    cR[0] = 1.0 / N
    cR[2048] = 1.0 / N
    WfT = np.ascontiguousarray(Wf.T)
    Wf_d = Wf.reshape(16, 128, 32, 128).transpose(2, 1, 0, 3)
    WA_d = WfT.reshape(32, 128, 16, 128).transpose(2, 1, 0, 3)
    WB_d = WfT.reshape(2, 16, 128, 4, 512).transpose(3, 0, 2, 1, 4)
    tl = np.linspace(0.0, 1.0, L, dtype=np.float32)[:, None]
    w = (2.0 * np.float32(math.pi) * np.arange(L, dtype=np.float32)[:, None] / np.float32(L)).astype(np.float32)
    fb = np.linspace(1e-4, 15, 16, dtype=np.float32)[None, :]
    z = np.concatenate([tl, np.cos(fb * w), -np.sin(fb * w)], axis=-1).astype(np.float32)
    min_decay = math.log(1e-2) / 1.5
    max_decay = math.log(1e-2) / 0.3
    deltas = np.abs(np.linspace(min_decay, max_decay, 512, dtype=np.float32))
    decay = np.exp(-tl * deltas[None, :]).astype(np.float32)
    return {
        "hy_Wf": np.ascontiguousarray(Wf_d).astype(NPBF),
        "hy_WA": np.ascontiguousarray(WA_d).astype(NPBF), "hy_WB": np.ascontiguousarray(WB_d).astype(NPBF),
        "hy_cR": np.ascontiguousarray(cR.reshape(32, 128).T).astype(np.float32),
        "hy_zT": np.ascontiguousarray(z.T), "hy_decay": np.ascontiguousarray(decay.reshape(16, 128, 512)),
    }


TWO_PI = 2.0 * math.pi


def stage_hy_filter(C):
    P = C.P
    P.begin_stage()
    zT_d = C.D("hy_zT", [33, 2048], F32)
    w1_d = C.D("hy_f_w1", [33, 64], F32)
    w2_d = C.D("hy_f_w2", [64, 64], F32)
    w3_d = C.D("hy_f_w3", [64, 2048], F32)
    cols_d = C.D("hy_cols", [64, 3], F32)
    dec_d = C.D("hy_decay", [16, 128, 512], F32)
    Wf_d = C.D("hy_Wf", [32, 128, 16, 128], BF16)
    cR_d = C.D("hy_cR", [128, 32], F32)
    skip_d = C.D("hy_skipb", [2, 128, 512], F32)
    Kf_d = C.D("hy_Kf", [32, 2, 128, 512], F32)
    dc = Dep()
    zT = P.sb("zT", [33, 2048], F32)
    w1 = P.sb("w1", [33, 64], F32)
    w2 = P.sb("w2", [64, 64], F32)
    w3 = P.sb("w3", [64, 2048], BF16)
    cols = P.sb("cols", [64, 8], F32)
    cR = P.sb("cR", [128, 32], F32)
    skipb = P.sb("skipb", [128, 2, 512], F32)
    P.dma("sp", zT[:], zT_d, writes=[dc])
    P.dma("sp", w1[:], w1_d, writes=[dc], merge=True)
    P.dma("sp", w2[:], w2_d, writes=[dc], merge=True)
    P.dma("pool", w3[:], w3_d, writes=[dc], merge=True)
    P.dma("sp", cols[:, 0:3], cols_d, writes=[dc], merge=True)
    P.dma("sp", cR[:], cR_d, writes=[dc], merge=True)
    for o in range(2):
        P.dma("sp", skipb[:, o, :], skip_d[o], writes=[dc], merge=True)
    dcol = Dep()
    P.op("dve", lambda e: e.tensor_tensor(out=cols[:, 3:4], in0=cols[:, 0:1], in1=cols[:, 1:2], op=ALU.mult), reads=[dc], writes=[dcol])
    P.op("dve", lambda e: e.tensor_tensor(out=cols[:, 4:5], in0=cols[:, 2:3], in1=cols[:, 1:2], op=ALU.mult), reads=[dc], writes=[dcol], merge=True)
    P.op("pool", lambda e: e.memset(cols[:, 5:6], -math.pi), reads=[dc], writes=[dcol], merge=True)
    h1T = P.sb("h1T", [64, 2048], F32)
    h2T = P.sb("h2T", [64, 2048], BF16)
    dh1 = Dep()
    dh2 = Dep()
    u_r = Ring(P, "hyu", 2, [64, 512], F32)
    s_r = Ring(P, "hys", 4, [64, 512], F32)
    for layer in range(2):
        for nt in range(4):
            ns = slice(nt * 512, (nt + 1) * 512)
            bk, dbk = C.bank()
            if layer == 0:
                P.op("pe", lambda e, bk=bk, ns=ns: e.matmul(bk[0:64, :], lhsT=w1[:], rhs=zT[:, ns], start=True, stop=True), reads=[dc], writes=[dbk])
            else:
                P.op("pe", lambda e, bk=bk, ns=ns: e.matmul(bk[0:64, :], lhsT=w2[:], rhs=h1T[:, ns], start=True, stop=True), reads=[dc, dh1], writes=[dbk])
            u, du = u_r.next()
            fbc = 3 + layer
            P.op("dve", lambda e, u=u, bk=bk, fbc=fbc: e.tensor_scalar(out=u[:], in0=bk[0:64, :], scalar1=cols[:, 1:2], scalar2=cols[:, fbc:fbc + 1], op0=ALU.mult, op1=ALU.add),
                 reads=[dbk, dcol, dc], writes=[du])
            s2, ds2 = s_r.next()
            s4, ds4 = s_r.next()
            P.op("act", lambda e, u=u, s2=s2: e.activation(out=s2[:], in_=u[:], func=AF.Sin, scale=0.5), reads=[du], writes=[ds2])
            P.op("act", lambda e, u=u, s4=s4: e.activation(out=s4[:], in_=u[:], func=AF.Sin, scale=0.25), reads=[du], writes=[ds4])
            P.op("dve", lambda e, s4=s4: e.tensor_tensor(out=s4[:], in0=s4[:], in1=s4[:], op=ALU.mult), reads=[ds4], writes=[ds4])
            P.op("dve", lambda e, s4=s4: e.tensor_scalar(out=s4[:], in0=s4[:], scalar1=-2.0, scalar2=1.0, op0=ALU.mult, op1=ALU.add), reads=[ds4], writes=[ds4])
            if layer == 0:
                P.op("dve", lambda e, s2=s2, s4=s4, ns=ns: e.scalar_tensor_tensor(out=h1T[:, ns], in0=s2[:], scalar=2.0, in1=s4[:], op0=ALU.mult, op1=ALU.mult),
                     reads=[ds2, ds4], writes=[dh1], merge=(nt > 0))
            else:
                P.op("dve", lambda e, s2=s2, s4=s4, ns=ns: e.scalar_tensor_tensor(out=h2T[:, ns], in0=s2[:], scalar=2.0, in1=s4[:], op0=ALU.mult, op1=ALU.mult),
                     reads=[ds2, ds4], writes=[dh2], merge=(nt > 0))
    Kt = P.sb("Ksd", [128, 16, 4, 512], BF16)
    dKt = [Dep() for _ in range(16)]
    ones = P.sb("onesb", [128, 128], BF16)
    dones = Dep()
    P.op("pool", lambda e: e.memset(ones[:], 1.0), writes=[dones])
    dec_r = Ring(P, "dec", 2, [128, 512], F32)
    sq_r = Ring(P, "sq", 3, [128, 512], BF16)
    kf32_r = Ring(P, "kf32", 2, [128, 512], F32)
    kb32_r = Ring(P, "kb32", 2, [128, 512], F32)
    ssq = [C.bank(hold=True), C.bank(hold=True)]
    for tt in range(16):
        dec, ddec = dec_r.next()
        P.dma("sp", dec[:], dec_d[tt], writes=[ddec])
        for o in range(2):
            bf_, dbf_ = C.bank()
            bb_, dbb_ = C.bank()
            for (bk, dbk, q) in ((bf_, dbf_, 2 * o), (bb_, dbb_, 2 * o + 1)):
                P.op("pe", lambda e, bk=bk, tt=tt, q=q: e.matmul(bk[:], lhsT=h2T[:, tt * 128:(tt + 1) * 128], rhs=w3[:, q * 512:(q + 1) * 512], start=True, stop=True),
                     reads=[dh2, dc], writes=[dbk])
            kf32, dkf32 = kf32_r.next()
            kb32, dkb32 = kb32_r.next()
            P.op("dve", lambda e, kf32=kf32, bf_=bf_, dec=dec: e.tensor_tensor(out=kf32[:], in0=bf_[:], in1=dec[:], op=ALU.mult), reads=[dbf_, ddec], writes=[dkf32])
            P.op("dve", lambda e, kb32=kb32, bb_=bb_, dec=dec: e.tensor_tensor(out=kb32[:], in0=bb_[:], in1=dec[:], op=ALU.mult), reads=[dbb_, ddec], writes=[dkb32])
            if tt == 0:
                P.op("pool", lambda e, kb32=kb32: e.memset(kb32[0:1, :], 0.0), reads=[dkb32], writes=[dkb32])
            P.op("pool", lambda e, tt=tt, o=o, kf32=kf32, kb32=kb32: e.tensor_tensor(out=Kt[:, tt, 2 * o, :], in0=kf32[:], in1=kb32[:], op=ALU.add),
                 reads=[dkf32, dkb32], writes=[dKt[tt]], merge=True)
            P.op("dve", lambda e, tt=tt, o=o, kf32=kf32, kb32=kb32: e.tensor_tensor(out=Kt[:, tt, 2 * o + 1, :], in0=kf32[:], in1=kb32[:], op=ALU.subtract),
                 reads=[dkf32, dkb32], writes=[dKt[tt]], merge=True)
        for q in range(4):
            sq, dsq = sq_r.next()
            P.op("act", lambda e, sq=sq, tt=tt, q=q: e.activation(out=sq[:], in_=Kt[:, tt, q, :], func=AF.Square), reads=[dKt[tt]], writes=[dsq])
            sb_, dsb_ = ssq[q // 2]
            first = (tt == 0 and q % 2 == 0)
            last = (tt == 15 and q % 2 == 1)
            P.op("pe", lambda e, sb_=sb_, sq=sq, first=first, last=last: e.matmul(sb_[:], lhsT=ones[:], rhs=sq[:], start=first, stop=last),
                 reads=[dsq, dones], writes=[dsb_], merge=(not first))
    rs = P.sb("hyrs", [128, 2, 512], F32)
    drs = Dep()
    for o in range(2):
        sb_, dsb_ = ssq[o]
        P.op("dve", lambda e, o=o, sb_=sb_: e.tensor_scalar(out=rs[:, o, :], in0=sb_[:], scalar1=0.5, scalar2=1e-12, op0=ALU.mult, op1=ALU.add), reads=[dsb_], writes=[drs], merge=(o > 0))
    C.release_all()
    P.op("act", lambda e: e.activation(out=rs[:], in_=rs[:], func=AF.Sqrt), reads=[drs], writes=[drs])
    P.op("dve", lambda e: e.reciprocal(out=rs[:], in_=rs[:]), reads=[drs], writes=[drs])
    wf_r = Ring(P, "wf", 2, [128, 16, 128], BF16)
    kf_r = Ring(P, "kf", 3, [128, 512], F32)
    for ft in range(32):
        wf, dwf = wf_r.next()
        P.dma("sp", wf[:], Wf_d[ft], writes=[dwf])
        for o in range(2):
            bk, dbk = C.bank()
            sel = 2 * o if ft < 16 else 2 * o + 1
            mm_acc(P, bk[:], [(wf[:, tt, :], Kt[:, tt, sel, :]) for tt in range(16)], reads=[dwf] + dKt, dwrite=dbk)
            kf, dkf = kf_r.next()
            P.op("dve", lambda e, kf=kf, bk=bk, o=o: e.tensor_tensor(out=kf[:], in0=bk[:], in1=rs[:, o, :], op=ALU.mult), reads=[dbk, drs], writes=[dkf])
            if ft < 16:
                P.op("dve", lambda e, kf=kf, o=o: e.tensor_tensor(out=kf[:], in0=kf[:], in1=skipb[:, o, :], op=ALU.add), reads=[dkf, dc], writes=[dkf])
            elif ft == 16:
                bn, dbn = C.bank()
                mm_acc(P, bn[0:32, :], [(wf[:, tt, 0:32], Kt[:, tt, 2 * o, :]) for tt in range(16)], reads=[dwf] + dKt, dwrite=dbn)
                P.op("dve", lambda e, kf=kf, bn=bn, o=o: e.tensor_tensor(out=kf[0:1, :], in0=bn[0:1, :], in1=rs[0:1, o, :], op=ALU.mult), reads=[dbn, drs, dkf], writes=[dkf])
                P.op("pool", lambda e, kf=kf, o=o: e.tensor_tensor(out=kf[0:1, :], in0=kf[0:1, :], in1=skipb[0:1, o, :], op=ALU.add), reads=[dkf, dc], writes=[dkf])
            P.op("act", lambda e, kf=kf, ft=ft: e.activation(out=kf[:], in_=kf[:], func=AF.Copy, scale=cR[:, ft:ft + 1]), reads=[dkf, dc], writes=[dkf])
            P.dma("act", Kf_d[ft, o], kf[:], reads=[dkf], writes=[C.dd("hy_Kf")], merge=True)


def stage_hy_prep(C):
    P = C.P
    P.begin_stage()
    hbT = C.D("hbT", [1536, 2048], F32)
    cw_d = C.D("hy_cw", [128, 12, 3], F32)
    cb_d = C.D("hy_cb", [128, 12], F32)
    ident_d = C.D("ident", [128, 128], F32)
    hv = C.D("hv_tm", [2048, 512], BF16)
    hx1 = C.D("hx1_tm", [2048, 512], F32)
    hx2T = C.D("hx2T", [512, 2048], F32)
    dc = Dep()
    cw = P.sb("cw", [128, 12, 3], F32)
    cb = P.sb("cb", [128, 12], F32)
    ident = P.sb("identh", [128, 128], F32)
    P.dma("sp", cw[:], cw_d, writes=[dc])
    P.dma("sp", cb[:], cb_d, writes=[dc], merge=True)
    P.dma("sp", ident[:], ident_d, writes=[dc], merge=True)
    xin_r = Ring(P, "hxin", 2, [128, 2048], F32)
    y_r = Ring(P, "hy", 2, [128, 2048], F32)
    sv_r = Ring(P, "hsv", 2, [128, 16, 128], BF16)
    sx_r = Ring(P, "hsx", 2, [128, 16, 128], F32)
    for ch in range(12):
        xin, dxin = xin_r.next()
        P.dma("sp", xin[:], hbT[ch * 128:(ch + 1) * 128, :], reads=[C.dd("hbT")], writes=[dxin])
        y, dy = y_r.next()
        P.op("act", lambda e, y=y, xin=xin, ch=ch: e.activation(out=y[:], in_=xin[:], func=AF.Identity, bias=cb[:, ch:ch + 1], scale=cw[:, ch, 1:2]),
             reads=[dxin, dc], writes=[dy])
        P.op("dve", lambda e, y=y, xin=xin, ch=ch: e.scalar_tensor_tensor(out=y[:, 1:2048], in0=xin[:, 0:2047], scalar=cw[:, ch, 0:1], in1=y[:, 1:2048], op0=ALU.mult, op1=ALU.add),
             reads=[dxin, dc, dy], writes=[dy])
        P.op("dve", lambda e, y=y, xin=xin, ch=ch: e.scalar_tensor_tensor(out=y[:, 0:2047], in0=xin[:, 1:2048], scalar=cw[:, ch, 2:3], in1=y[:, 0:2047], op0=ALU.mult, op1=ALU.add),
             reads=[dxin, dc, dy], writes=[dy])
        if ch >= 8:
            P.dma("act", hx2T[(ch - 8) * 128:(ch - 7) * 128, :], y[:], reads=[dy], writes=[C.dd("hx2T")], merge=True)
            continue
        stg, dstg = (sv_r if ch < 4 else sx_r).next()
        for g in range(4):
            bk, dbk = C.bank()
            for k4 in range(4):
                tt = g * 4 + k4
                P.op("pe", lambda e, bk=bk, k4=k4, tt=tt, y=y: e.transpose(bk[:, k4 * 128:(k4 + 1) * 128], y[:, tt * 128:(tt + 1) * 128], ident[:]),
                     reads=[dy, dc], writes=[dbk], merge=(k4 > 0))
            P.op("act", lambda e, stg=stg, g=g, bk=bk: e.copy(out=stg[:, g * 4:(g + 1) * 4, :], in_=bk[:].rearrange("p (k t) -> p k t", k=4)),
                 reads=[dbk], writes=[dstg], merge=(g > 0))
        if ch < 4:
            P.dma("act", hv.rearrange("(tt p) c -> p tt c", p=128)[:, :, ch * 128:(ch + 1) * 128], stg[:], reads=[dstg], writes=[C.dd("hv_tm")], merge=True)
        else:
            P.dma("act", hx1.rearrange("(tt p) c -> p tt c", p=128)[:, :, (ch - 4) * 128:(ch - 3) * 128], stg[:], reads=[dstg], writes=[C.dd("hx1_tm")], merge=True)


def stage_hy_conv(C):
    P = C.P
    P.begin_stage()
    hv = C.D("hv_tm", [2048, 512], BF16)
    hx1 = C.D("hx1_tm", [2048, 512], F32)
    hx2T = C.D("hx2T", [512, 2048], F32)
    Kf_d = C.D("hy_Kf", [32, 2, 128, 512], F32)
    Wf_d = C.D("hy_Wf", [32, 128, 16, 128], BF16)
    WA_d = C.D("hy_WA", [16, 128, 32, 128], BF16)
    WB_d = C.D("hy_WB", [4, 2, 128, 16, 512], BF16)
    mixT = C.D("mixT", [1024, 2048], BF16)
    ztm = P.sb("ztm", [128, 16, 512], BF16)
    dz = [Dep() for _ in range(16)]
    hvv = hv.rearrange("(tt p) c -> p tt c", p=128)
    for tt in range(16):
        P.dma("sp", ztm[:, tt, :], hvv[:, tt, :], reads=[C.dd("hv_tm")], writes=[dz[tt]])
    Yt = P.sb("Yt", [128, 32, 512], BF16)
    dY = [Dep() for _ in range(32)]
    wf_r = Ring(P, "cwf", 3, [128, 16, 128], BF16)
    kf_r = Ring(P, "ckf", 4, [128, 512], F32)
    t_r = Ring(P, "ct", 4, [128, 512], F32)
    wa_r = Ring(P, "cwa", 2, [128, 32, 128], BF16)
    wb_r = Ring(P, "cwb", 2, [128, 16, 512], BF16)
    x_r = Ring(P, "cx", 3, [128, 512], F32)
    zo_r = Ring(P, "czo", 3, [128, 512], BF16)
    for o in range(2):
        for j in range(16):
            ub = []
            for part in range(2):
                ft = part * 16 + j
                wf, dwf = wf_r.next()
                P.dma("sp", wf[:], Wf_d[ft], writes=[dwf])
                bk, dbk = C.bank()
                mm_acc(P, bk[:], [(wf[:, tt, :], ztm[:, tt, :]) for tt in range(16)], reads=[dwf] + dz, dwrite=dbk)
                ub.append((bk, dbk))
            kre, dkre = kf_r.next()
            kim, dkim = kf_r.next()
            P.dma("sp", kre[:], Kf_d[j, o], reads=[C.dd("hy_Kf")], writes=[dkre])
            P.dma("sp", kim[:], Kf_d[16 + j, o], reads=[C.dd("hy_Kf")], writes=[dkim])
            (ure, dure), (uim, duim) = ub
            t1, dt1 = t_r.next()
            t2, dt2 = t_r.next()
            P.op("dve", lambda e, t1=t1, ure=ure, kre=kre: e.tensor_tensor(out=t1[:], in0=ure[:], in1=kre[:], op=ALU.mult), reads=[dure, dkre], writes=[dt1])
            P.op("dve", lambda e, t2=t2, uim=uim, kim=kim: e.tensor_tensor(out=t2[:], in0=uim[:], in1=kim[:], op=ALU.mult), reads=[duim, dkim], writes=[dt2])
            P.op("pool", lambda e, j=j, t1=t1, t2=t2: e.tensor_tensor(out=Yt[:, j, :], in0=t1[:], in1=t2[:], op=ALU.subtract), reads=[dt1, dt2], writes=[dY[j]])
            if j == 0:
                P.op("pool", lambda e, t1=t1: e.tensor_copy(out=Yt[0:1, 0, :], in_=t1[0:1, :]), reads=[dt1, dY[0]], writes=[dY[0]])
            t3, dt3 = t_r.next()
            t4, dt4 = t_r.next()
            P.op("dve", lambda e, t3=t3, ure=ure, kim=kim: e.tensor_tensor(out=t3[:], in0=ure[:], in1=kim[:], op=ALU.mult), reads=[dure, dkim], writes=[dt3])
            P.op("dve", lambda e, t4=t4, uim=uim, kre=kre: e.tensor_tensor(out=t4[:], in0=uim[:], in1=kre[:], op=ALU.mult), reads=[duim, dkre], writes=[dt4])
            P.op("pool", lambda e, j=j, t3=t3, t4=t4: e.tensor_tensor(out=Yt[:, 16 + j, :], in0=t3[:], in1=t4[:], op=ALU.add), reads=[dt3, dt4], writes=[dY[16 + j]])
            if j == 0:
                P.op("pool", lambda e, t2=t2: e.tensor_copy(out=Yt[0:1, 16, :], in_=t2[0:1, :]), reads=[dt2, dY[16]], writes=[dY[16]])
        if o == 0:
            for tt in range(16):
                wa, dwa = wa_r.next()
                P.dma("sp", wa[:], WA_d[tt], writes=[dwa])
                bk, dbk = C.bank()
                mm_acc(P, bk[:], [(wa[:, kt, :], Yt[:, kt, :]) for kt in range(32)], reads=[dwa] + dY, dwrite=dbk)
                xt, dxt = x_r.next()
                P.dma("sp", xt[:], hx1[tt * 128:(tt + 1) * 128, :], reads=[C.dd("hx1_tm")], writes=[dxt])
                P.op("dve", lambda e, tt=tt, bk=bk, xt=xt: e.tensor_tensor(out=ztm[:, tt, :], in0=bk[:], in1=xt[:], op=ALU.mult), reads=[dbk, dxt], writes=[dz[tt]])
        else:
            for nt in range(4):
                banks = [C.bank() for _ in range(4)]
                for hf in range(2):
                    wb, dwb = wb_r.next()
                    P.dma("sp", wb[:], WB_d[nt, hf], writes=[dwb])
                    for cc in range(4):
                        bk, dbk = banks[cc]
                        for k in range(16):
                            first = (hf == 0 and k == 0)
                            last = (hf == 1 and k == 15)
                            kt = hf * 16 + k
                            P.op("pe", lambda e, bk=bk, kt=kt, cc=cc, wb=wb, k=k, first=first, last=last: e.matmul(
                                bk[:], lhsT=Yt[:, kt, cc * 128:(cc + 1) * 128], rhs=wb[:, k, :], start=first, stop=last),
                                reads=[dY[kt], dwb], writes=[dbk], merge=(not first))
                for cc in range(4):
                    bk, dbk = banks[cc]
                    xt, dxt = x_r.next()
                    P.dma("sp", xt[:], hx2T[cc * 128:(cc + 1) * 128, nt * 512:(nt + 1) * 512], reads=[C.dd("hx2T")], writes=[dxt])
                    zo, dzo = zo_r.next()
                    P.op("dve", lambda e, zo=zo, bk=bk, xt=xt: e.tensor_tensor(out=zo[:], in0=bk[:], in1=xt[:], op=ALU.mult), reads=[dbk, dxt], writes=[dzo])
                    P.dma("act", mixT[512 + cc * 128:512 + (cc + 1) * 128, nt * 512:(nt + 1) * 512], zo[:], reads=[dzo], writes=[C.dd("mixT")], merge=True)


MLA_SCALE = 96.0 ** -0.5


def mla_consts():
    inv = 1.0 / (10000.0 ** (np.arange(0, 32, 2, dtype=np.float32) / 32.0))
    ang = np.arange(2048, dtype=np.float32)[:, None] * inv[None, :].astype(np.float32)
    cos = np.cos(ang).astype(np.float32).T
    sin = np.sin(ang).astype(np.float32).T
    cos2 = np.concatenate([cos, cos], axis=0)
    sin2 = np.concatenate([-sin, sin], axis=0)
    return {"mla_cs2": np.ascontiguousarray(np.stack([cos2, sin2], axis=1)).astype(np.float32)}


def stage_mla1(C, xT_name):
    P = C.P
    P.begin_stage()
    xT = C.D(xT_name, [1024, 2048], BF16)
    wi_d = C.D("mla_w_in", [1024, 672], F32)
    wsw_d = C.D("mla_w_in_sw", [1024, 96], F32)
    gc_d = C.D("mla_gcols", [128, 5], F32)
    cs_d = C.D("mla_cs2", [32, 2, 2048], F32)
    nT_d = C.D("mla_nT", [640, 2048], BF16)
    kr_d = C.D("mla_krT", [32, 2048], BF16)
    xs = P.sb("mxs", [128, 8, 2048], BF16)
    dxs = [Dep() for _ in range(8)]
    for kc in range(8):
        P.dma("sp", xs[:, kc, :], xT[kc * 128:(kc + 1) * 128, :], reads=[C.dd(xT_name)], writes=[dxs[kc]])
    wi, dwi = load_fm_bf16(C, "mwi", wi_d, 8, 672)
    wsw, dwsw = load_fm_bf16(C, "mwsw", wsw_d, 8, 96)
    dc = Dep()
    gcol = P.sb("mgc", [128, 5], F32)
    P.dma("sp", gcol[:], gc_d, writes=[dc])
    cs = P.sb("mcs", [96, 2, 2048], F32)
    P.dma("sp", cs[64:96, :, :], cs_d, writes=[dc], merge=True)
    ones = P.sb("mones", [128, 128], BF16)
    P.op("pool", lambda e: e.memset(ones[:], 1.0), writes=[dc], merge=True)
    hT = P.sb("mhT", [128, 5, 2048], F32)
    nT = P.sb("mnT", [128, 5, 2048], BF16)
    dhT = Dep()
    dnT = [Dep() for _ in range(5)]
    sq_r = Ring(P, "msq", 3, [128, 512], BF16)
    r_r = Ring(P, "mr", 2, [128, 512], F32)
    for (chunks, n) in (((0, 1, 2), 384.0), ((3, 4), 256.0)):
        for nt in range(4):
            ns = slice(nt * 512, (nt + 1) * 512)
            sbk, dsbk = C.bank(hold=True)
            for ci, c in enumerate(chunks):
                bk, dbk = C.bank()
                mm_acc(P, bk[:], [(wi[:, kc, c * 128:(c + 1) * 128], xs[:, kc, ns]) for kc in range(8)], reads=dxs + dwi, dwrite=dbk)
                P.op("act", lambda e, c=c, ns=ns, bk=bk: e.copy(out=hT[:, c, ns], in_=bk[:]), reads=[dbk], writes=[dhT], merge=True)
                sq, dsq = sq_r.next()
                P.op("act", lambda e, sq=sq, bk=bk: e.activation(out=sq[:], in_=bk[:], func=AF.Square), reads=[dbk], writes=[dsq])
                P.op("pe", lambda e, sbk=sbk, sq=sq, ci=ci, chunks=chunks: e.matmul(sbk[:], lhsT=ones[:], rhs=sq[:], start=(ci == 0), stop=(ci == len(chunks) - 1)),
                     reads=[dsq, dc], writes=[dsbk], merge=(ci > 0))
            r, dr = r_r.next()
            P.op("dve", lambda e, r=r, sbk=sbk, n=n: e.tensor_scalar(out=r[:], in0=sbk[:], scalar1=1.0 / n, scalar2=EPS, op0=ALU.mult, op1=ALU.add), reads=[dsbk], writes=[dr])
            C.release_all()
            P.op("act", lambda e, r=r: e.activation(out=r[:], in_=r[:], func=AF.Sqrt), reads=[dr], writes=[dr])
            P.op("dve", lambda e, r=r: e.reciprocal(out=r[:], in_=r[:]), reads=[dr], writes=[dr])
            for c in chunks:
                P.op("dve", lambda e, c=c, ns=ns, r=r: e.scalar_tensor_tensor(out=nT[:, c, ns], in0=hT[:, c, ns], scalar=gcol[:, c:c + 1], in1=r[:], op0=ALU.mult, op1=ALU.mult),
                     reads=[dhT, dr, dc], writes=[dnT[c]], merge=True)
    for c in range(5):
        P.dma("act", nT_d[c * 128:(c + 1) * 128, :], nT[:, c, :], reads=[dnT[c]], writes=[C.dd("mla_nT")], merge=True)
    krT = P.sb("mkr", [96, 2048], BF16)
    dkr = Dep()
    ta_r = Ring(P, "mta", 2, [96, 512], F32)
    tb_r = Ring(P, "mtb", 2, [96, 512], F32)
    for nt in range(4):
        ns = slice(nt * 512, (nt + 1) * 512)
        bk, dbk = C.bank()
        bs, dbs = C.bank()
        mm_acc(P, bk[0:96, :], [(wi[:, kc, 576:672], xs[:, kc, ns]) for kc in range(8)], reads=dxs + dwi, dwrite=dbk)
        mm_acc(P, bs[0:96, :], [(wsw[:, kc, :], xs[:, kc, ns]) for kc in range(8)], reads=dxs + dwsw, dwrite=dbs)
        ta, dta = ta_r.next()
        tb, dtb = tb_r.next()
        P.op("dve", lambda e, ta=ta, bk=bk, ns=ns: e.tensor_tensor(out=ta[64:96, :], in0=bk[64:96, :], in1=cs[64:96, 0, ns], op=ALU.mult), reads=[dbk, dc], writes=[dta])
        P.op("dve", lambda e, tb=tb, bs=bs, ns=ns: e.tensor_tensor(out=tb[64:96, :], in0=bs[64:96, :], in1=cs[64:96, 1, ns], op=ALU.mult), reads=[dbs, dc], writes=[dtb])
        P.op("pool", lambda e, ta=ta, tb=tb, ns=ns: e.tensor_tensor(out=krT[64:96, ns], in0=ta[64:96, :], in1=tb[64:96, :], op=ALU.add), reads=[dta, dtb], writes=[dkr], merge=(nt > 0))
    P.dma("pool", kr_d, krT[64:96, :], reads=[dkr], writes=[C.dd("mla_krT")])


def stage_mla2(C):
    P = C.P
    P.begin_stage()
    nT_d = C.D("mla_nT", [640, 2048], BF16)
    kr_d = C.D("mla_krT", [32, 2048], BF16)
    cs_d = C.D("mla_cs2", [32, 2, 2048], F32)
    wq_d = C.D("mla_w_q_up", [384, 1536], F32)
    wqs_d = C.D("mla_w_q_sw", [384, 1536], F32)
    wk_d = C.D("mla_w_kv_k", [256, 1024], F32)
    wv_d = C.D("mla_w_kv_v", [256, 1024], F32)
    mixT = C.D("mixT", [1024, 2048], BF16)
    nT = P.sb("anT", [128, 5, 2048], BF16)
    dnT = [Dep() for _ in range(5)]
    for c in range(5):
        P.dma("sp", nT[:, c, :], nT_d[c * 128:(c + 1) * 128, :], reads=[C.dd("mla_nT")], writes=[dnT[c]])
    dq = dnT[0:3]
    dkv = dnT[3:5]
    dc = Dep()
    KRT = P.sb("aKRT", [96, 2048], BF16)
    P.dma("sp", KRT[64:96, :], kr_d, reads=[C.dd("mla_krT")], writes=[dc])
    cs = P.sb("acs", [96, 2, 2048], F32)
    P.dma("sp", cs[64:96, :, :], cs_d, writes=[dc], merge=True)
    wq, dwq = load_fm_bf16(C, "awq", wq_d, 3, 1536)
    wqs, dwqs = load_fm_bf16(C, "awqs", wqs_d, 3, 1536)
    wk, dwk = load_fm_bf16(C, "awk", wk_d, 2, 1024)
    wv, dwv = load_fm_bf16(C, "awv", wv_d, 2, 1024)
    onesf = P.sb("aones", [128, 64], F32)
    P.op("pool", lambda e: e.memset(onesf[:], 1.0), writes=[dc], merge=True)
    Vx = P.sb("aVx", [128, 16, 16, 65], BF16)
    dV = Dep()
    P.op("pool", lambda e: e.memset(Vx[:], 1.0), writes=[dV])
    for tt in range(16):
        for half in range(2):
            bk, dbk = C.bank()
            mm_acc(P, bk[:], [(nT[:, 3 + kc, tt * 128:(tt + 1) * 128], wv[:, kc, half * 512:(half + 1) * 512]) for kc in range(2)], reads=dkv + dwv, dwrite=dbk)
            P.op("act", lambda e, tt=tt, half=half, bk=bk: e.copy(out=Vx[:, tt, half * 8:(half + 1) * 8, 0:64], in_=bk[:].rearrange("p (h d) -> p h d", h=8)),
                 reads=[dbk], writes=[dV], merge=True)
    QT_r = Ring(P, "aQT", 2, [96, 2048], BF16)
    KT_r = Ring(P, "aKT", 2, [96, 2048], BF16)
    ta_r = Ring(P, "ata", 2, [96, 512], F32)
    tb_r = Ring(P, "atb", 2, [96, 512], F32)
    p_r = Ring(P, "apT", 4, [128, 512], BF16)
    rd_r = Ring(P, "ard", 2, [65, 512], F32)
    bs_r = Ring(P, "absb", 2, [64, 512], F32)
    yo_r = Ring(P, "ayo", 3, [64, 512], BF16)
    def alloc_head():
        QT, dQT = QT_r.next()
        KT, dKT = KT_r.next()
        return (QT, dQT, KT, dKT)

    def proj_piece(h, nt, hd):
        QT, dQT, KT, dKT = hd
        ns = slice(nt * 512, (nt + 1) * 512)
        bq, dbq = C.bank()
        bs, dbs = C.bank()
        mm_acc(P, bq[0:96, :], [(wq[:, kc, h * 96:(h + 1) * 96], nT[:, kc, ns]) for kc in range(3)], reads=dq + dwq, dwrite=dbq)
        mm_acc(P, bs[0:96, :], [(wqs[:, kc, h * 96:(h + 1) * 96], nT[:, kc, ns]) for kc in range(3)], reads=dq + dwqs, dwrite=dbs)
        P.op("dve", lambda e: e.tensor_copy(out=QT[0:64, ns], in_=bq[0:64, :]), reads=[dbq], writes=[dQT], merge=(nt > 0))
        ta, dta = ta_r.next()
        tb, dtb = tb_r.next()
        P.op("dve", lambda e: e.tensor_tensor(out=ta[64:96, :], in0=bq[64:96, :], in1=cs[64:96, 0, ns], op=ALU.mult), reads=[dbq, dc], writes=[dta])
        P.op("dve", lambda e: e.tensor_tensor(out=tb[64:96, :], in0=bs[64:96, :], in1=cs[64:96, 1, ns], op=ALU.mult), reads=[dbs, dc], writes=[dtb])
        P.op("pool", lambda e: e.tensor_tensor(out=QT[64:96, ns], in0=ta[64:96, :], in1=tb[64:96, :], op=ALU.add), reads=[dta, dtb], writes=[dQT], merge=True)
        bkk, dbkk = C.bank()
        mm_acc(P, bkk[0:64, :], [(wk[:, kc, h * 64:(h + 1) * 64], nT[:, 3 + kc, ns]) for kc in range(2)], reads=dkv + dwk, dwrite=dbkk)
        P.op("dve", lambda e: e.tensor_copy(out=KT[0:64, ns], in_=bkk[0:64, :]), reads=[dbkk], writes=[dKT], merge=(nt > 0))
        if nt == 3:
            P.op("pool", lambda e: e.tensor_copy(out=KT[64:96, :], in_=KRT[64:96, :]), reads=[dc], writes=[dKT], merge=True)

    pendA = []
    pendB = []

    def make_epilogue(accb, h, qs):
        acc, dacc = accb
        rd, drd = rd_r.next()

        def epi_a():
            P.op("dve", lambda e: e.reciprocal(out=rd[64:65, :], in_=acc[64:65, :]), reads=[dacc], writes=[drd])

        def epi_b():
            bb, dbb = C.bank()
            P.op("pe", lambda e: e.matmul(bb[0:64, :], lhsT=onesf[64:65, 0:64], rhs=rd[64:65, :], start=True, stop=True), reads=[drd, dc], writes=[dbb])
            bsb, dbsb = bs_r.next()
            P.op("dve", lambda e: e.tensor_copy(out=bsb[:], in_=bb[0:64, :]), reads=[dbb], writes=[dbsb])
            yo, dyo = yo_r.next()
            P.op("dve", lambda e: e.tensor_tensor(out=yo[:], in0=acc[0:64, :], in1=bsb[:], op=ALU.mult), reads=[dacc, dbsb], writes=[dyo])
            C.release(accb)
            P.dma("pool", mixT[h * 64:(h + 1) * 64, qs], yo[:], reads=[dyo], writes=[C.dd("mixT")], merge=True)
        return epi_a, epi_b

    def attention(h, hd, nxt_hd):
        QT, dQT, KT, dKT = hd
        for qc in range(4):
            qs = slice(qc * 512, (qc + 1) * 512)
            accb = C.bank(hold=True)
            acc, dacc = accb

            def pv(kt, pT, dpT, acc=acc, dacc=dacc, h=h):
                P.op("pe", lambda e, acc=acc, kt=kt, h=h, pT=pT: e.matmul(acc[0:65, :], lhsT=Vx[:, kt, h, :], rhs=pT[:], start=(kt == 0), stop=(kt == 15)),
                     reads=[dV, dpT], writes=[dacc], merge=(kt > 0))
            pend = None
            for kt in range(16):
                sb_, dsb_ = C.bank()
                P.op("pe", lambda e, sb_=sb_, kt=kt, qs=qs: e.matmul(sb_[:], lhsT=KT[0:96, kt * 128:(kt + 1) * 128], rhs=QT[0:96, qs], start=True, stop=True),
                     reads=[dKT, dQT], writes=[dsb_])
                pT, dpT = p_r.next()
                P.op("act", lambda e, pT=pT, sb_=sb_: e.activation(out=pT[:], in_=sb_[:], func=AF.Exp, scale=MLA_SCALE), reads=[dsb_], writes=[dpT])
                if pend is not None:
                    pv(*pend)
                pend = (kt, pT, dpT)
                if kt == 1 and pendA:
                    pendA.pop(0)()
                if kt == 9 and pendB:
                    pendB.pop(0)()
                if kt == 5 and nxt_hd is not None:
                    proj_piece(h + 1, qc, nxt_hd)
            pv(*pend)
            ea, eb = make_epilogue(accb, h, qs)
            pendA.append(ea)
            pendB.append(eb)

    hd = alloc_head()
    for nt in range(4):
        proj_piece(0, nt, hd)
    for h in range(16):
        nxt_hd = alloc_head() if h + 1 < 16 else None
        attention(h, hd, nxt_hd)
        hd = nxt_hd
    while pendA:
        pendA.pop(0)()
    while pendB:
        pendB.pop(0)()


def _rep128(v):
    v = np.asarray(v, np.float32)
    return np.ascontiguousarray(np.broadcast_to(v[None, :], (128, v.shape[0])))


def shared_inputs(inp):
    f32 = lambda a: np.ascontiguousarray(np.asarray(a, np.float32))
    s = {}
    s["ab_w_in"] = f32(inp["ab_w_in"][0])
    s["na_tab"] = na_tables(np.asarray(inp["na_rpb"][0], np.float32))
    s.update(hyena_consts())
    s["hy_f_w1"] = f32(inp["hy_f_w1"][0])
    s["hy_f_w2"] = f32(inp["hy_f_w2"][0])
    s["hy_f_w3"] = f32(inp["hy_f_w3"][0])
    s["hy_cols"] = f32(np.stack([inp["hy_f_b1"][0], inp["hy_f_freq"][0], inp["hy_f_b2"][0]], axis=1))
    s["hy_skipb"] = np.stack([_rep128(inp["hy_skip"][0][0]), _rep128(inp["hy_skip"][0][1])])
    s["hy_cw"] = f32(np.asarray(inp["hy_conv_w"][0]).reshape(3, 12, 128).transpose(2, 1, 0))
    s["hy_cb"] = f32(np.asarray(inp["hy_conv_b"][0]).reshape(12, 128).T)
    s["ident"] = np.eye(128, dtype=np.float32)
    esel = np.zeros((16, 16, 128), np.float32)
    for e in range(16):
        esel[e, e, :] = 1.0
    s["esel"] = esel
    s["iota_col"] = (np.arange(16)[None, :] * 128 + np.arange(128)[:, None]).astype(np.float32)
    s["iota_row"] = _rep128(np.arange(2048, dtype=np.float32))
    s["ab_w_out"] = f32(inp["ab_w_out"][0])
    w_in = np.asarray(inp["mla_w_in"][0], np.float32)
    perm = np.concatenate([np.arange(16, 32), np.arange(0, 16)])
    s["mla_w_in"] = f32(w_in)
    s["mla_w_in_sw"] = f32(np.concatenate([w_in[:, 576:640], w_in[:, 640 + perm]], axis=1))
    wq = np.asarray(inp["mla_w_q_up"][0], np.float32)
    wqs = wq.reshape(384, 16, 96).copy()
    wqs[:, :, 64:] = wqs[:, :, 64 + perm]
    s["mla_w_q_up"] = f32(wq)
    s["mla_w_q_sw"] = f32(wqs.reshape(384, 1536))
    wkv = np.asarray(inp["mla_w_kv_up"][0], np.float32).reshape(256, 16, 128)
    s["mla_w_kv_k"] = f32(wkv[:, :, :64].reshape(256, 1024))
    s["mla_w_kv_v"] = f32(wkv[:, :, 64:].reshape(256, 1024))
    s["mla_gcols"] = f32(np.concatenate([np.asarray(inp["mla_q_norm"][0]).reshape(3, 128).T,
                                         np.asarray(inp["mla_kv_norm"][0]).reshape(2, 128).T], axis=1))
    s.update(mla_consts())
    s["mla_w_out"] = f32(inp["mla_w_out"][0])
    for li in range(2):
        s[f"ln1_g{li}"] = _rep128(inp["ln1_g"][li])
        s[f"ln1_b{li}"] = _rep128(inp["ln1_b"][li])
        s[f"ln2_g{li}"] = _rep128(inp["ln2_g"][li])
        s[f"ln2_b{li}"] = _rep128(inp["ln2_b"][li])
        s[f"moe_router{li}"] = f32(inp["moe_router"][li])
        s[f"moe_w_gate{li}"] = f32(inp["moe_w_gate"][li])
        s[f"moe_w_up{li}"] = f32(inp["moe_w_up"][li])
        s[f"moe_w_down{li}"] = f32(inp["moe_w_down"][li])
        s[f"ple_gate{li}"] = f32(inp["ple_gate"][li])
        s[f"ple_proj{li}"] = f32(inp["ple_proj"][li])
    return s


PER_CORE = ("x_tm", "xT", "pT0", "pT1")


def build_full(shared_names):
    C = Ctx(ext_in=set(shared_names) | set(PER_CORE), ext_out={"out"})
    stage_a1(C)
    stage_a2(C)
    stage_hy_filter(C)
    stage_hy_prep(C)
    stage_hy_conv(C)
    stage_proj_ln(C, "ab_w_out", "x_tm", "ln1", 0, "x1_0")
    stage_moe1(C, 0, "x1_0")
    stage_moe2(C, 0, "x1_0", "x2_0")
    stage_ple(C, 0, "x2_0", "x3_0", True)
    stage_mla1(C, "x3_0T")
    stage_mla2(C)
    stage_proj_ln(C, "mla_w_out", "x3_0", "ln1", 1, "x1_1")
    stage_moe1(C, 1, "x1_1")
    stage_moe2(C, 1, "x1_1", "x2_1")
    stage_ple(C, 1, "x2_1", "out", False)
    C.P.finish()
    return C


def kernel(**inputs):
    inp = {k: np.asarray(v) for k, v in inputs.items()}
    shared = shared_inputs(inp)
    x = np.asarray(inp["x"], np.float32)
    p = np.asarray(inp["p"], np.float32)
    C = build_full(shared.keys())
    used = set(C.dram.keys())
    in_maps = []
    for b in range(8):
        m = {k: v for k, v in shared.items() if k in used}
        m["x_tm"] = np.ascontiguousarray(x[b])
        m["xT"] = np.ascontiguousarray(x[b].T)
        m["pT0"] = np.ascontiguousarray(p[0, b].T)
        m["pT1"] = np.ascontiguousarray(p[1, b].T)
        in_maps.append(m)
    res = run_bass_kernel_spmd(C.nc, in_maps, core_ids=list(range(8)))
    return np.stack([np.asarray(r["out"], np.float32) for r in res.results], axis=0)
```
